# Optimizing a Trainium2 kernel written in Bass

```python
import math
import jax
import jax.numpy as jnp
from jax import lax
import numpy as np

D_MODEL = 2048
BATCH = 4
SEQ = 8192
DEPTH = 4

GRID_W = 64
CTX_LEN = 256
N_BRANCH = 4
D_BRANCH = D_MODEL // N_BRANCH
CHUNK = 128
GMLP_GROUPS = 4
GMLP_GW = D_BRANCH // GMLP_GROUPS
CONV_W = 31
HEAD_DIM = 64
N_Q_HEADS = D_BRANCH // HEAD_DIM
N_KV_HEADS = 2
Q_PER_KV = N_Q_HEADS // N_KV_HEADS
WINDOW = 128
BLOCK = 128
ROPE_BASE = 10000.0
S5_GW = 16
S5_GROUPS = D_BRANCH // S5_GW
S5_STATE = 64
D_FF = 4 * D_MODEL
N_MOD = 6
EPS = 1e-6
NEG_INF = -1e30
COLS_A = 2 * D_BRANCH
COLS_B = 2 * D_BRANCH
COLS_Q = N_Q_HEADS * HEAD_DIM
COLS_KV = N_KV_HEADS * HEAD_DIM
COLS_D = D_BRANCH
IN_COLS = COLS_A + COLS_B + COLS_Q + 2 * COLS_KV + COLS_D

kernel_name = 'hybrid_parallel_gmlp_conformer_swa_s5_dit'


def rms_norm(x, g):
    xf = x.astype(jnp.float32)
    y = xf * lax.rsqrt(jnp.mean(xf * xf, axis=-1, keepdims=True) + EPS)
    return (y * g.astype(jnp.float32)).astype(x.dtype)


def layer_norm(x, g, b):
    xf = x.astype(jnp.float32)
    xc = xf - jnp.mean(xf, axis=-1, keepdims=True)
    var = jnp.mean(xc * xc, axis=-1, keepdims=True)
    return (xc * lax.rsqrt(var + EPS) * g.astype(jnp.float32) + b.astype(jnp.float32)).astype(x.dtype)


def rope_1d(x, pos):
    d = x.shape[-1]
    inv = ROPE_BASE ** (-jnp.arange(0, d, 2, dtype=jnp.float32) / d)
    ang = pos.astype(jnp.float32)[:, None] * inv[None, :]
    cos = jnp.cos(ang)[:, None, :]
    sin = jnp.sin(ang)[:, None, :]
    xf = x.astype(jnp.float32)
    x1, x2 = xf[..., : d // 2], xf[..., d // 2:]
    return jnp.concatenate([x1 * cos - x2 * sin, x1 * sin + x2 * cos], axis=-1).astype(x.dtype)


def axial_rope(x, row, col):
    half = HEAD_DIM // 2
    return jnp.concatenate([rope_1d(x[..., :half], row), rope_1d(x[..., half:], col)], axis=-1)


def split_in(z):
    parts = []
    start = 0
    for width in (COLS_A, COLS_B, COLS_Q, COLS_KV, COLS_KV, COLS_D):
        parts.append(z[..., start:start + width])
        start += width
    return parts


def gmlp_chunk_mix(za, ln_g, ln_b, w_s, b_s):
    za = jax.nn.gelu(za)
    u, v = jnp.split(za, 2, axis=-1)
    v = layer_norm(v, ln_g, ln_b)
    B, L, _ = v.shape
    vb = v.reshape(B, L // CHUNK, CHUNK, GMLP_GROUPS, GMLP_GW)
    mixed = jnp.einsum('gpq,bnqgc->bnpgc', w_s, vb) + b_s.T[None, None, :, :, None]
    return u * mixed.reshape(B, L, D_BRANCH)


def conformer_conv(zb, w_dw, b_dw, ln_g, ln_b):
    a, g = jnp.split(zb, 2, axis=-1)
    y = a * jax.nn.sigmoid(g)
    y = lax.conv_general_dilated(
        y, w_dw[:, None, :], window_strides=(1,), padding=[(CONV_W // 2, CONV_W // 2)],
        dimension_numbers=('NWC', 'WIO', 'NWC'), feature_group_count=D_BRANCH) + b_dw
    return jax.nn.silu(layer_norm(y, ln_g, ln_b))


def windowed_attention(q, k, v, kc, vc, sink):
    B, S = q.shape[:2]
    nb = S // BLOCK
    scale = HEAD_DIM ** -0.5
    qb = q.reshape(B, nb, BLOCK, N_KV_HEADS, Q_PER_KV, HEAD_DIM)

    def band(t):
        tp = jnp.pad(t, ((0, 0), (BLOCK, BLOCK), (0, 0), (0, 0)))
        tp = tp.reshape(B, nb + 2, BLOCK, N_KV_HEADS, HEAD_DIM)
        return jnp.concatenate([tp[:, :-2], tp[:, 1:-1], tp[:, 2:]], axis=2)

    kb, vb = band(k), band(v)
    blk = jnp.arange(nb)[:, None, None]
    qpos = blk * BLOCK + jnp.arange(BLOCK)[None, :, None]
    kpos = (blk - 1) * BLOCK + jnp.arange(3 * BLOCK)[None, None, :]
    allowed = (jnp.abs(qpos - kpos) <= WINDOW) & (kpos >= 0) & (kpos < S)
    s_loc = jnp.einsum('bnqhgd,bnkhd->bnhgqk', qb, kb, preferred_element_type=jnp.float32) * scale
    s_loc = jnp.where(allowed[None, :, None, None], s_loc, NEG_INF)
    s_ctx = jnp.einsum('bnqhgd,bchd->bnhgqc', qb, kc, preferred_element_type=jnp.float32) * scale
    sink_col = jnp.broadcast_to(sink.astype(jnp.float32).reshape(1, 1, N_KV_HEADS, Q_PER_KV, 1, 1),
                                s_loc.shape[:-1] + (1,))
    p = jax.nn.softmax(jnp.concatenate([s_loc, s_ctx, sink_col], axis=-1), axis=-1)
    n_loc = 3 * BLOCK
    n_ctx = kc.shape[1]
    p_loc = p[..., :n_loc].astype(v.dtype)
    p_ctx = p[..., n_loc:n_loc + n_ctx].astype(v.dtype)
    o = (jnp.einsum('bnhgqk,bnkhd->bnqhgd', p_loc, vb)
         + jnp.einsum('bnhgqc,bchd->bnqhgd', p_ctx, vc))
    return o.reshape(B, S, N_Q_HEADS * HEAD_DIM)


def context_attention(qc, kc, vc, sink):
    B, C = qc.shape[:2]
    scale = HEAD_DIM ** -0.5
    qg = qc.reshape(B, C, N_KV_HEADS, Q_PER_KV, HEAD_DIM)
    s = jnp.einsum('bqhgd,bkhd->bhgqk', qg, kc, preferred_element_type=jnp.float32) * scale
    sink_col = jnp.broadcast_to(sink.astype(jnp.float32).reshape(1, N_KV_HEADS, Q_PER_KV, 1, 1),
                                s.shape[:-1] + (1,))
    p = jax.nn.softmax(jnp.concatenate([s, sink_col], axis=-1), axis=-1)
    o = jnp.einsum('bhgqk,bkhd->bqhgd', p[..., :C].astype(vc.dtype), vc)
    return o.reshape(B, C, N_Q_HEADS * HEAD_DIM)


def s5_discretise(a_re, a_im, log_step, b_re, b_im):
    lam = lax.complex(a_re.astype(jnp.float32), a_im.astype(jnp.float32))
    dt = jnp.exp(log_step.astype(jnp.float32))[:, None]
    log_lbar = lam * dt
    lbar = jnp.exp(log_lbar)
    b = lax.complex(b_re.astype(jnp.float32), b_im.astype(jnp.float32))
    bbar = ((lbar - 1.0) / lam)[..., None] * b
    return log_lbar, lbar, bbar


def _ssm_combine(left, right):
    a1, b1 = left
    a2, b2 = right
    return a1 * a2, a2 * b1 + b2


def s5_scan(u, disc, s0):
    log_lbar, lbar, bbar = disc
    bu = jnp.einsum('blgc,gpc->blgp', u.astype(jnp.float32).astype(jnp.complex64), bbar)
    a = jnp.broadcast_to(lbar, bu.shape)
    _, s = lax.associative_scan(_ssm_combine, (a, bu), axis=1)
    if s0 is not None:
        steps = jnp.arange(1, u.shape[1] + 1, dtype=jnp.float32)
        s = s + jnp.exp(log_lbar[None] * steps[:, None, None])[None] * s0[:, None]
    return s


def s5_readout(u, s_f, s_b, c_re, c_im, d_skip, w_glu):
    cf = lax.complex(c_re[0].astype(jnp.float32), c_im[0].astype(jnp.float32))
    cb = lax.complex(c_re[1].astype(jnp.float32), c_im[1].astype(jnp.float32))
    y = (jnp.einsum('gcp,blgp->blgc', cf, s_f) + jnp.einsum('gcp,blgp->blgc', cb, s_b)).real
    B, L = u.shape[:2]
    y = y.reshape(B, L, D_BRANCH) + d_skip.astype(jnp.float32) * u.reshape(B, L, D_BRANCH).astype(jnp.float32)
    y = jax.nn.gelu(y).astype(u.dtype)
    a, g = jnp.split(y @ w_glu, 2, axis=-1)
    return a * jax.nn.sigmoid(g)


def merge_branches(h, branches, w_br, w_gate, b_gate, w_out):
    merged = None
    for kb, br in enumerate(branches):
        term = jax.nn.sigmoid(h @ w_gate[kb] + b_gate[kb]) * (br @ w_br[kb])
        merged = term if merged is None else merged + term
    return merged @ w_out


def sq_relu_mlp(h, w1, w2):
    return jnp.square(jax.nn.relu(h @ w1)) @ w2


def setup_inputs(seed: int = 0) -> dict:
    key = jax.random.key(seed)
    keys = jax.random.split(key, 40)

    def nrm(i, shape, s):
        return jax.random.normal(keys[i], shape, jnp.float32) * s

    L, D, G, P = DEPTH, D_MODEL, S5_GROUPS, S5_STATE
    n_idx = jnp.arange(P, dtype=jnp.float32)
    return {
        'x': nrm(0, (BATCH, SEQ, D), 1.0),
        'c': nrm(1, (BATCH, D), 1.0),
        'ctx': nrm(2, (BATCH, CTX_LEN, D), 1.0),
        'c_ctx': nrm(3, (D,), 1.0),
        'w_mod': nrm(4, (L, D, N_MOD * D), 0.5 * D ** -0.5),
        'b_mod': nrm(5, (L, N_MOD * D), 0.02),
        'norm1_g': 1.0 + nrm(6, (L, D), 0.02),
        'norm2_g': 1.0 + nrm(7, (L, D), 0.02),
        'w_in': nrm(8, (L, D, IN_COLS), D ** -0.5),
        'gmlp_ln_g': 1.0 + nrm(9, (L, D_BRANCH), 0.02),
        'gmlp_ln_b': nrm(10, (L, D_BRANCH), 0.02),
        'gmlp_ws': nrm(11, (L, GMLP_GROUPS, CHUNK, CHUNK), CHUNK ** -0.5),
        'gmlp_bs': 1.0 + nrm(12, (L, GMLP_GROUPS, CHUNK), 0.02),
        'conv_w': nrm(13, (L, CONV_W, D_BRANCH), CONV_W ** -0.5),
        'conv_b': nrm(14, (L, D_BRANCH), 0.02),
        'conv_ln_g': 1.0 + nrm(15, (L, D_BRANCH), 0.02),
        'conv_ln_b': nrm(16, (L, D_BRANCH), 0.02),
        'attn_sink': nrm(17, (L, N_Q_HEADS), 0.5),
        's5_a_re': -0.5 + nrm(18, (L, 2, G, P), 0.01),
        's5_a_im': math.pi * n_idx + nrm(19, (L, 2, G, P), 0.01),
        's5_log_step': jax.random.uniform(keys[20], (L, 2, G), jnp.float32, math.log(1e-3), math.log(1e-1)),
        's5_b_re': nrm(21, (L, G, P, S5_GW), (2 * S5_GW) ** -0.5),
        's5_b_im': nrm(22, (L, G, P, S5_GW), (2 * S5_GW) ** -0.5),
        's5_c_re': nrm(23, (L, 2, G, S5_GW, P), 0.25),
        's5_c_im': nrm(24, (L, 2, G, S5_GW, P), 0.25),
        's5_d': nrm(25, (L, D_BRANCH), 0.5),
        's5_w_glu': nrm(26, (L, D_BRANCH, 2 * D_BRANCH), D_BRANCH ** -0.5),
        'w_branch': nrm(27, (L, N_BRANCH, D_BRANCH, D), D_BRANCH ** -0.5),
        'w_gate': nrm(28, (L, N_BRANCH, D, D), D ** -0.5),
        'b_gate': nrm(29, (L, N_BRANCH, D), 0.02),
        'w_out': nrm(30, (L, D, D), D ** -0.5),
        'w_ff1': nrm(31, (L, D, D_FF), D ** -0.5),
        'w_ff2': nrm(32, (L, D_FF, D), D_FF ** -0.5),
        'final_g': 1.0 + nrm(33, (D,), 0.02),
    }


def reference(x, c, ctx, c_ctx, w_mod, b_mod, norm1_g, norm2_g, w_in,
              gmlp_ln_g, gmlp_ln_b, gmlp_ws, gmlp_bs,
              conv_w, conv_b, conv_ln_g, conv_ln_b,
              attn_sink,
              s5_a_re, s5_a_im, s5_log_step, s5_b_re, s5_b_im, s5_c_re, s5_c_im, s5_d, s5_w_glu,
              w_branch, w_gate, b_gate, w_out, w_ff1, w_ff2, final_g):
    B, S, _ = x.shape
    n_ctx = ctx.shape[1]
    rows = S // GRID_W
    row = jnp.repeat(jnp.arange(rows), GRID_W)
    col = jnp.tile(jnp.arange(GRID_W), rows)
    cond_x = jax.nn.silu(c)
    cond_c = jax.nn.silu(c_ctx)
    for l in range(DEPTH):
        mod_x = (cond_x @ w_mod[l] + b_mod[l])[:, None, :]
        mod_c = cond_c @ w_mod[l] + b_mod[l]
        shift1, scale1, gate1, shift2, scale2, gate2 = jnp.split(mod_x, N_MOD, axis=-1)
        cshift1, cscale1, cgate1, cshift2, cscale2, cgate2 = jnp.split(mod_c, N_MOD, axis=-1)
        disc_f = s5_discretise(s5_a_re[l, 0], s5_a_im[l, 0], s5_log_step[l, 0], s5_b_re[l], s5_b_im[l])
        disc_b = s5_discretise(s5_a_re[l, 1], s5_a_im[l, 1], s5_log_step[l, 1], s5_b_re[l], s5_b_im[l])

        hc = rms_norm(ctx, norm1_g[l]) * (1.0 + cscale1) + cshift1
        zc_a, zc_b, qc, kc, vc, zc_d = split_in(hc @ w_in[l])
        kc = kc.reshape(B, n_ctx, N_KV_HEADS, HEAD_DIM)
        vc = vc.reshape(B, n_ctx, N_KV_HEADS, HEAD_DIM)
        uc = zc_d.reshape(B, n_ctx, S5_GROUPS, S5_GW)
        sc_f = s5_scan(uc, disc_f, None)
        sc_b_rev = s5_scan(jnp.flip(uc, 1), disc_b, None)

        h = rms_norm(x, norm1_g[l]) * (1.0 + scale1) + shift1
        z_a, z_b, q, k, v, z_d = split_in(h @ w_in[l])
        q = axial_rope(q.reshape(B, S, N_Q_HEADS, HEAD_DIM), row, col)
        k = axial_rope(k.reshape(B, S, N_KV_HEADS, HEAD_DIM), row, col)
        v = v.reshape(B, S, N_KV_HEADS, HEAD_DIM)
        u = z_d.reshape(B, S, S5_GROUPS, S5_GW)
        s_f = s5_scan(u, disc_f, sc_f[:, -1])
        s_b = jnp.flip(s5_scan(jnp.flip(u, 1), disc_b, sc_b_rev[:, -1]), 1)
        branches = (
            gmlp_chunk_mix(z_a, gmlp_ln_g[l], gmlp_ln_b[l], gmlp_ws[l], gmlp_bs[l]),
            conformer_conv(z_b, conv_w[l], conv_b[l], conv_ln_g[l], conv_ln_b[l]),
            windowed_attention(q, k, v, kc, vc, attn_sink[l]),
            s5_readout(u, s_f, s_b, s5_c_re[l], s5_c_im[l], s5_d[l], s5_w_glu[l]),
        )
        x = x + gate1 * merge_branches(h, branches, w_branch[l], w_gate[l], b_gate[l], w_out[l])
        h2 = rms_norm(x, norm2_g[l]) * (1.0 + scale2) + shift2
        x = x + gate2 * sq_relu_mlp(h2, w_ff1[l], w_ff2[l])

        if l < DEPTH - 1:
            branches_c = (
                gmlp_chunk_mix(zc_a, gmlp_ln_g[l], gmlp_ln_b[l], gmlp_ws[l], gmlp_bs[l]),
                conformer_conv(zc_b, conv_w[l], conv_b[l], conv_ln_g[l], conv_ln_b[l]),
                context_attention(qc, kc, vc, attn_sink[l]),
                s5_readout(uc, sc_f, jnp.flip(sc_b_rev, 1), s5_c_re[l], s5_c_im[l], s5_d[l], s5_w_glu[l]),
            )
            ctx = ctx + cgate1 * merge_branches(hc, branches_c, w_branch[l], w_gate[l], b_gate[l], w_out[l])
            hc2 = rms_norm(ctx, norm2_g[l]) * (1.0 + cscale2) + cshift2
            ctx = ctx + cgate2 * sq_relu_mlp(hc2, w_ff1[l], w_ff2[l])
    return rms_norm(x, final_g)
```

```python
import math
import numpy as np
import concourse.bass as bass
import concourse.mybir as mybir
from concourse.bass_utils import run_bass_kernel_spmd

F32 = mybir.dt.float32
BF16 = mybir.dt.bfloat16
AF = mybir.ActivationFunctionType
ALU = mybir.AluOpType
AX = mybir.AxisListType


class Cfg:
    def __init__(self, D=2048, SEQ=8192, CTX=256, DEPTH=4, GRID_W=64, T=512, BATCH=4):
        self.D, self.SEQ, self.CTX, self.DEPTH, self.GRID_W, self.T, self.BATCH = D, SEQ, CTX, DEPTH, GRID_W, T, BATCH
        self.DB = D // 4
        self.KC = D // 128
        self.BC = self.DB // 128
        self.FF = 4 * D
        self.FFC = self.FF // 128
        self.HQ = self.DB // 64
        self.QPK = self.HQ // 2
        self.G = self.DB // 16
        self.NPAIR = self.G // 2
        self.NT = CTX + SEQ
        self.IN_COLS = 2 * self.DB + 2 * self.DB + self.DB + 256 + self.DB
        self.CONV_W = 31
        self.EPS = 1e-6
        BC = self.BC
        self.c_u = 0
        self.c_v = BC
        self.c_a = 2 * BC
        self.c_g = 3 * BC
        self.c_q = 4 * BC
        self.c_k = 5 * BC
        self.c_vv = 5 * BC + 1
        self.c_d = 5 * BC + 2
        self.NCI = 6 * BC + 2
        tiles = []
        s = 0
        while s < CTX:
            w = min(T, CTX - s)
            tiles.append((s, w, True))
            s += w
        while s < self.NT:
            w = min(T, self.NT - s)
            tiles.append((s, w, False))
            s += w
        self.tiles = tiles
        self.YPAD = 16
        self.NY = self.NT + 4 * self.YPAD


class Buf:
    __slots__ = ("name", "w", "r", "dsem", "dcnt", "dq")

    def __init__(self, name):
        self.name = name
        self.w = None
        self.r = []
        self.dsem = None
        self.dcnt = 0


class Eng:
    def __init__(self, K, eng, name, sem):
        self.K, self.eng, self.name, self.sem = K, eng, name, sem
        self.count = 0
        self.waited = {}

    def _wait(self, ev):
        if ev is None:
            return
        sem, val = ev
        if sem is self.sem and not self.K.same_engine_sync:
            return
        if sem is self.sem and self.name == "pe":
            return
        if self.waited.get(id(sem), 0) >= val:
            return
        self.eng.wait_ge(sem, val)
        self.waited[id(sem)] = val

    def deps(self, reads, writes):
        evs = {}

        def add(ev):
            if ev is None:
                return
            k = id(ev[0])
            if k not in evs or evs[k][1] < ev[1]:
                evs[k] = ev
        for b in reads:
            add(b.w)
        for b in writes:
            add(b.w)
            for ev in b.r:
                add(ev)
        for ev in evs.values():
            self._wait(ev)

    def op(self, fn, reads=(), writes=()):
        self.deps(reads, writes)
        ins = fn(self.eng)
        self.count += 1
        ins.then_inc(self.sem, 1)
        ev = (self.sem, self.count)
        for b in reads:
            b.r = [x for x in b.r if x[0] is not ev[0]] + [ev]
        for b in writes:
            b.w = ev
            b.r = []
        return ins

    def dma(self, out, in_, sbuf, reads=(), writes=(), load=True, **kw):
        K = self.K
        if sbuf.dsem is None:
            fl = K.free_sems.setdefault(self.name, [])
            if fl:
                sbuf.dsem, sbuf.dcnt = fl.pop()
            else:
                sbuf.dsem, sbuf.dcnt = K.new_sem(f"dsem{len(K._stack)}"), 0
            sbuf.dq = self.name
        assert sbuf.dq == self.name, "a buffer's DMAs must stay on one queue"
        rd = list(reads) + ([] if load else [sbuf])
        wr = list(writes) + ([sbuf] if load else [])
        self.deps(rd, wr)
        ins = self.eng.dma_start(out=out, in_=in_, **kw)
        sbuf.dcnt += 1
        ins.then_inc(sbuf.dsem, 16)
        ev = (sbuf.dsem, 16 * sbuf.dcnt)
        for b in rd:
            b.r = [x for x in b.r if x[0] is not ev[0]] + [ev]
        for b in wr:
            b.w = ev
            b.r = []
        K.dma_bufs[id(sbuf)] = sbuf
        return ins


class Kern:
    def __init__(self, nc, same_engine_sync=True):
        self.nc = nc
        self.same_engine_sync = same_engine_sync
        self.sems = []
        self.dma_bufs = {}
        self.free_sems = {}
        self._stack = []

    def new_sem(self, name):
        cm = self.nc.semaphore(name)
        h = cm.__enter__()
        self._stack.append(cm)
        return h

    def make_engines(self, block_engs):
        self.pe = Eng(self, block_engs["tensor"], "pe", self.new_sem("s_pe"))
        self.act = Eng(self, block_engs["scalar"], "act", self.new_sem("s_act"))
        self.dve = Eng(self, block_engs["vector"], "dve", self.new_sem("s_dve"))
        self.pool = Eng(self, block_engs["gpsimd"], "pool", self.new_sem("s_pool"))
        self.sp = Eng(self, block_engs["sync"], "sp", self.new_sem("s_sp"))
        self.engs = [self.pe, self.act, self.dve, self.pool, self.sp]

    def barrier(self):
        for e in self.engs:
            for o in self.engs:
                if o is not e and o.count > 0:
                    e._wait((o.sem, o.count))
            for b in self.dma_bufs.values():
                if b.dcnt:
                    e._wait((b.dsem, 16 * b.dcnt))
        for e in self.engs:
            if e.count > 20000:
                e.sem = self.new_sem(f"s_{e.name}_{len(self._stack)}")
                e.count = 0
        for b in self.dma_bufs.values():
            self.free_sems[b.dq].append((b.dsem, b.dcnt))
            b.dsem = None
        self.dma_bufs = {}

    def close(self):
        for cm in reversed(self._stack):
            cm.__exit__(None, None, None)


def lhsT_chunks(W, ncols=128):
    K, N = W.shape
    return np.ascontiguousarray(W.reshape(K // 128, 128, N // ncols, ncols).transpose(1, 2, 0, 3))


def weight_plan(cfg):
    KC, BC, FFC, DB = cfg.KC, cfg.BC, cfg.FFC, cfg.DB
    plan = {}
    off = 0

    def add(name, n):
        nonlocal off
        plan[name] = off
        off += n
    add("inF", (6 * BC + 2) * KC * 128)
    add("inV1", KC * DB)
    add("inV2", KC * 128)
    add("wsT", BC * 128)
    add("glu", 2 * BC * BC * 128)
    add("gb", KC * (4 * KC * 128 + 4 * BC * 128))
    add("out", KC * KC * 128)
    add("ff1", FFC * KC * 128)
    add("ff2", KC * FFC * 128)
    tot = off
    tot = (tot + 8191) // 8192 * 8192
    plan["_total"] = tot
    return plan


def swap16(W):
    K, N = W.shape
    return np.ascontiguousarray(W.reshape(K, N // 32, 2, 16)[:, :, ::-1, :].reshape(K, N))


def pack_layer_weights(cfg, inp, l):
    KC, BC, FFC, DB, D = cfg.KC, cfg.BC, cfg.FFC, cfg.DB, cfg.D
    plan = weight_plan(cfg)
    flat = np.zeros((128, plan["_total"]), np.float32)

    def put(name, arr):
        a = arr.reshape(128, -1)
        flat[:, plan[name]:plan[name] + a.shape[1]] = a
    w_in = inp["w_in"][l]
    cA, cB, cQ, cK, cV, cD = 0, 2 * DB, 4 * DB, 5 * DB, 5 * DB + 128, 5 * DB + 256
    cols = []
    for i in range(BC):
        cols.append(w_in[:, cA + i * 128: cA + (i + 1) * 128])
    for i in range(BC):
        cols.append(w_in[:, cB + DB + i * 128: cB + DB + (i + 1) * 128])
        cols.append(w_in[:, cB + i * 128: cB + (i + 1) * 128])
    wq = w_in[:, cQ:cQ + DB]
    wqs = swap16(wq)
    for i in range(BC):
        cols.append(wq[:, i * 128:(i + 1) * 128])
        cols.append(wqs[:, i * 128:(i + 1) * 128])
    wk = w_in[:, cK:cK + 128]
    cols.append(wk)
    cols.append(swap16(wk))
    for i in range(BC):
        cols.append(w_in[:, cD + i * 128: cD + (i + 1) * 128])
    put("inF", lhsT_chunks(np.concatenate(cols, axis=1)))
    put("inV1", lhsT_chunks(w_in[:, cA + DB: cA + 2 * DB], ncols=DB))
    put("inV2", lhsT_chunks(w_in[:, cV:cV + 128]))
    put("wsT", np.ascontiguousarray(inp["gmlp_ws"][l].transpose(2, 0, 1)))
    put("glu", lhsT_chunks(inp["s5_w_glu"][l]))
    g4 = np.stack([lhsT_chunks(inp["w_gate"][l, k]) for k in range(4)], axis=2)
    b4 = np.stack([lhsT_chunks(inp["w_branch"][l, k]) for k in range(4)], axis=2)
    gb = np.concatenate([g4.reshape(128, KC, -1), b4.reshape(128, KC, -1)], axis=2)
    put("gb", gb)
    put("out", lhsT_chunks(inp["w_out"][l]))
    put("ff1", lhsT_chunks(inp["w_ff1"][l]))
    put("ff2", lhsT_chunks(inp["w_ff2"][l]))
    return flat


def fm(v):
    return np.ascontiguousarray(v.reshape(-1, 128).T)


def pvec_plan(cfg):
    KC, BC = cfg.KC, cfg.BC
    plan = {}
    off = 0
    for name, n in (("b_mod", 6 * KC), ("n1g", KC), ("n2g", KC), ("b_gate", 4 * KC), ("conv_w", BC * 31),
                    ("conv_b", BC), ("cln_g", BC), ("cln_b", BC), ("s5_d", BC)):
        plan[name] = (off, n)
        off += n
    plan["_total"] = off
    return plan


def rowv_plan(cfg):
    DB, BC = cfg.DB, cfg.BC
    plan = {}
    off = 0
    for name, n in (("gln_g", DB), ("gln_b", DB), ("gbs", BC * 128), ("sink", cfg.HQ)):
        plan[name] = (off, n)
        off += n
    plan["_total"] = off
    return plan


def prep_inputs(cfg, inp, core):
    L, KC, BC, D, DB = cfg.DEPTH, cfg.KC, cfg.BC, cfg.D, cfg.DB
    b = core % cfg.BATCH
    m = {}
    xcat = np.concatenate([inp["ctx"][b], inp["x"][b]], axis=0)
    m["xT"] = np.ascontiguousarray(xcat.T.reshape(KC, 128, cfg.NT))
    cond = np.stack([fm(inp["c_ctx"]), fm(inp["c"][b])], axis=2)
    m["cond"] = np.ascontiguousarray(cond)
    return m


def prep_shared(cfg, inp):
    L, KC, BC, D, DB = cfg.DEPTH, cfg.KC, cfg.BC, cfg.D, cfg.DB
    m = {}
    m["wall"] = np.stack([pack_layer_weights(cfg, inp, l) for l in range(L)], axis=0)
    m["wmod"] = np.stack([lhsT_chunks(inp["w_mod"][l]) for l in range(L)], axis=0)
    pp = pvec_plan(cfg)
    pv = np.zeros((128, L + 1, pp["_total"]), np.float32)
    for l in range(L):
        def put(name, a):
            o, n = pp[name]
            pv[:, l, o:o + n] = a.reshape(128, n)
        put("b_mod", fm(inp["b_mod"][l]))
        put("n1g", fm(inp["norm1_g"][l]))
        put("n2g", fm(inp["norm2_g"][l]))
        put("b_gate", np.stack([fm(inp["b_gate"][l, k]) for k in range(4)], axis=1))
        cw = inp["conv_w"][l]
        put("conv_w", np.ascontiguousarray(cw.T.reshape(BC, 128, 31).transpose(1, 0, 2)))
        put("conv_b", fm(inp["conv_b"][l]))
        put("cln_g", fm(inp["conv_ln_g"][l]))
        put("cln_b", fm(inp["conv_ln_b"][l]))
        put("s5_d", fm(inp["s5_d"][l]))
    o, n = pp["n1g"]
    pv[:, L, o:o + n] = fm(inp["final_g"])
    m["pvec"] = pv
    rp = rowv_plan(cfg)
    rv = np.zeros((128, L, rp["_total"]), np.float32)
    for l in range(L):
        for name, a in (("gln_g", inp["gmlp_ln_g"][l]), ("gln_b", inp["gmlp_ln_b"][l]),
                        ("gbs", inp["gmlp_bs"][l].reshape(-1)), ("sink", inp["attn_sink"][l])):
            o, n = rp[name]
            rv[:, l, o:o + n] = np.broadcast_to(a.reshape(1, n), (128, n))
    m["rowv"] = rv
    half = 32
    inv = (10000.0 ** (-np.arange(0, half, 2, dtype=np.float32) / half)).astype(np.float32)
    pos = np.arange(cfg.SEQ)
    row = (pos // cfg.GRID_W).astype(np.float32)
    col = (pos % cfg.GRID_W).astype(np.float32)
    cosT = np.ones((128, cfg.NT), np.float32)
    sinT = np.zeros((128, cfg.NT), np.float32)
    for p in range(128):
        d = p % 64
        blk, j = d // 32, d % 32
        ang = ((row if blk == 0 else col) * inv[j % 16]).astype(np.float32)
        cosT[p, cfg.CTX:] = np.cos(ang)
        sinT[p, cfg.CTX:] = np.sin(ang) * (-1.0 if j < 16 else 1.0)
    m["ropec"] = cosT
    m["ropes"] = sinT
    kl = np.arange(128)[:, None]
    ql = np.arange(128)[None, :]
    m["mask_prev"] = (kl >= ql).astype(np.float32)
    m["mask_next"] = (kl <= ql).astype(np.float32)
    m["ident"] = np.eye(128, dtype=np.float32)
    NP = cfg.NPAIR
    A = np.zeros((128, L, 2, NP, 3), np.float32)
    Bm = np.zeros((128, L, 2, NP, 32), np.float32)
    Cm = np.zeros((128, L, 2, NP, 2, 64), np.float32)
    for g2 in range(2):
        rows = slice(g2 * 64, (g2 + 1) * 64)
        gidx = 2 * np.arange(NP) + g2
        A[rows, :, :, :, 0] = inp["s5_a_re"][:, :, gidx, :].transpose(3, 0, 1, 2)
        A[rows, :, :, :, 1] = inp["s5_a_im"][:, :, gidx, :].transpose(3, 0, 1, 2)
        A[rows, :, :, :, 2] = np.broadcast_to(inp["s5_log_step"][:, :, gidx][None], (64, L, 2, NP))
        cs_ = slice(g2 * 16, (g2 + 1) * 16)
        Bm[rows, :, 0, :, cs_] = inp["s5_b_re"][:, gidx, :, :].transpose(2, 0, 1, 3)
        Bm[rows, :, 1, :, cs_] = inp["s5_b_im"][:, gidx, :, :].transpose(2, 0, 1, 3)
        for k in range(NP):
            cc_ = slice(32 * (k % 2) + g2 * 16, 32 * (k % 2) + (g2 + 1) * 16)
            Cm[rows, :, :, k, 0, cc_] = inp["s5_c_re"][:, :, gidx[k], :, :].transpose(3, 0, 1, 2)
            Cm[rows, :, :, k, 1, cc_] = inp["s5_c_im"][:, :, gidx[k], :, :].transpose(3, 0, 1, 2)
    m["s5A"], m["s5B"], m["s5C"] = A, Bm, Cm
    return m


from contextlib import ExitStack

WSLOT = 8192
NWSLOT = 3


class Prog:
    def __init__(self, cfg, debug=False, n_layers=None, stop_after=None, same_engine_sync=True):
        self.cfg = cfg
        self.debug = debug
        self.L = cfg.DEPTH if n_layers is None else n_layers
        self.stop_after = stop_after
        nc = bass.Bass("TRN2", target_bir_lowering=False)
        self.nc = nc
        self.K = Kern(nc, same_engine_sync)
        self.K.make_engines({"tensor": nc.tensor, "scalar": nc.scalar, "vector": nc.vector,
                             "gpsimd": nc.gpsimd, "sync": nc.sync})
        self.wplan = weight_plan(cfg)
        self.pplan = pvec_plan(cfg)
        self.rplan = rowv_plan(cfg)
        self.psum_i = 0

    def din(self, name, shape, dt=F32):
        return self.nc.dram_tensor(name, list(shape), dt, kind="ExternalInput").ap()

    def dscratch(self, name, shape, dt):
        kind = "ExternalOutput" if self.debug else "Internal"
        return self.nc.dram_tensor(name, list(shape), dt, kind=kind).ap()

    def sb(self, st, name, shape, dt):
        self._uid = getattr(self, "_uid", 0) + 1
        name = f"{name}_{self._uid}"
        t = st.enter_context(self.nc.sbuf_tensor(name, list(shape), dt))
        return t, Buf(name)

    def next_psum(self):
        i = self.psum_i
        self.psum_i = (i + 1) % 6
        return self.ps[i], self.psb[i]

    def declare(self):
        cfg, L = self.cfg, self.cfg.DEPTH
        KC, BC, NT = cfg.KC, cfg.BC, cfg.NT
        self.xT = self.din("xT", [KC, 128, NT])
        self.cond = self.din("cond", [128, KC, 2])
        self.wall = self.din("wall", [L, 128, self.wplan["_total"]])
        self.wmod = self.din("wmod", [L, 128, 6 * KC, KC, 128])
        self.pvec = self.din("pvec", [128, L + 1, self.pplan["_total"]])
        self.rowv = self.din("rowv", [128, L, self.rplan["_total"]])
        self.ropec = self.din("ropec", [128, NT])
        self.ropes = self.din("ropes", [128, NT])
        self.mask_prev = self.din("mask_prev", [128, 128])
        self.mask_next = self.din("mask_next", [128, 128])
        self.ident = self.din("ident", [128, 128])
        self.s5A = self.din("s5A", [128, L, 2, cfg.NPAIR, 3])
        self.s5B = self.din("s5B", [128, L, 2, cfg.NPAIR, 32])
        self.s5C = self.din("s5C", [128, L, 2, cfg.NPAIR, 2, 64])
        self.out = self.nc.dram_tensor("out", [KC, 128, cfg.SEQ], F32, kind="ExternalOutput").ap()
        self.wb = [self.dscratch(f"wb{l}", [128, self.wplan["_total"]], BF16) for l in range(L)]
        self.X = self.dscratch("X", [KC, 128, NT], F32)
        self.H = self.dscratch("H", [KC, 128, NT], BF16)
        self.BR = self.dscratch("BR", [4, BC, 128, NT], BF16)
        self.Y = self.dscratch("Y", [BC, 128, cfg.NY], F32)
        self.Q = self.dscratch("Q", [BC, 128, NT], BF16)
        self.KT = self.dscratch("KT", [2, 128, NT], BF16)
        self.V = self.dscratch("V", [NT, 2, 128], BF16)
        self.U = self.dscratch("U", [BC, 128, NT], F32)
        self.YB = self.dscratch("YB", [BC, 128, NT], F32)

    def ypos(self, t):
        cfg = self.cfg
        return t + cfg.YPAD if t < cfg.CTX else t + 3 * cfg.YPAD

    def wload(self, l, off, n):
        i = self.wslot_i
        self.wslot_i = (i + 1) % len(self.wslots)
        t, b = self.wslots[i]
        self.K.sp.dma(t[:, 0:n], self.wb[l][:, off:off + n], b)
        return t, b

    def build(self):
        cfg, K, nc = self.cfg, self.K, self.nc
        self.declare()
        with ExitStack() as st:
            self.ps, self.psb = [], []
            for i in range(8):
                p = st.enter_context(nc.psum_tensor(f"ps{i}", [128, 512], F32))
                self.ps.append(p)
                self.psb.append(Buf(f"ps{i}"))
            self.pv, self.pvb = self.sb(st, "pv", [128, cfg.DEPTH + 1, self.pplan["_total"]], F32)
            self.modv, self.modb = self.sb(st, "modv", [128, cfg.DEPTH, 6 * cfg.KC, 2], F32)
            self.ones, self.onesb = self.sb(st, "ones", [128, 128], F32)
            self.identt, self.identb = self.sb(st, "identt", [128, 128], F32)
            K.pool.dma(self.pv[:], self.pvec[:, :, :], self.pvb)
            K.pool.dma(self.identt[:], self.ident[:, :], self.identb)
            K.dve.op(lambda e: e.memset(self.ones[:], 1.0), writes=[self.onesb])
            self.phase_w()
            K.barrier()
            self.phase_m()
            K.barrier()
            for l in range(self.L):
                self.phase1(l)
                K.barrier()
                if self.stop_after == ("p1", l):
                    break
                self.phase_s(l)
                K.barrier()
                if self.stop_after == ("ps", l):
                    break
                self.phase2(l)
                K.barrier()
            else:
                self.phase_final()
                K.barrier()
        K.close()
        return nc

    def phase_w(self):
        cfg, K, nc = self.cfg, self.K, self.nc
        tot = self.wplan["_total"]
        CH = 8192
        with ExitStack() as st:
            s32 = [self.sb(st, f"w32_{i}", [128, CH], F32) for i in range(2)]
            s16 = [self.sb(st, f"w16_{i}", [128, CH], BF16) for i in range(2)]
            it = 0
            for l in range(self.L):
                for off in range(0, tot, CH):
                    a, ab = s32[it % 2]
                    o, ob = s16[it % 2]
                    K.sp.dma(a[:], self.wall[l, :, off:off + CH], ab)
                    eng = (K.dve, K.act, K.pool)[it % 3]
                    if eng is K.act:
                        eng.op(lambda e: e.copy(out=o[:], in_=a[:]), reads=[ab], writes=[ob])
                    else:
                        eng.op(lambda e: e.tensor_copy(out=o[:], in_=a[:]), reads=[ab], writes=[ob])
                    K.pool.dma(self.wb[l][:, off:off + CH], o[:], ob, load=False)
                    it += 1

    def phase_m(self):
        cfg, K, nc = self.cfg, self.K, self.nc
        KC = cfg.KC
        NJ = 6 * KC
        JB = max(d for d in range(1, NJ + 1) if NJ % d == 0 and d * KC * 128 <= 8192)
        with ExitStack() as st:
            ct, cb = self.sb(st, "condt", [128, KC, 2], F32)
            sc, scb = self.sb(st, "scond", [128, KC, 2], F32)
            ws = [self.sb(st, f"wm_{i}", [128, JB, KC, 128], F32) for i in range(2)]
            K.pool.dma(ct[:], self.cond[:, :, :], cb)
            K.act.op(lambda e: e.activation(out=sc[:], in_=ct[:], func=AF.Silu), reads=[cb], writes=[scb])
            it = 0
            o_b, _ = self.pplan["b_mod"]
            for l in range(self.L):
                for j0 in range(0, NJ, JB):
                    w, wbuf = ws[it % 2]
                    it += 1
                    K.sp.dma(w[:], self.wmod[l, :, j0:j0 + JB, :, :], wbuf)
                    for j in range(j0, j0 + JB):
                        ps, psb = self.next_psum()
                        for kc in range(KC):
                            K.pe.op(lambda e, kc=kc, j=j: e.matmul(ps[:, 0:2], lhsT=w[:, j - j0, kc, :], rhs=sc[:, kc, :],
                                                                    start=(kc == 0), stop=(kc == KC - 1)),
                                    reads=[wbuf, scb], writes=[psb])
                        K.dve.op(lambda e, j=j: e.tensor_tensor(
                            out=self.modv[:, l, j, :], in0=ps[:, 0:2],
                            in1=self.pv[:, l, o_b + j:o_b + j + 1].to_broadcast([128, 2]), op=ALU.add),
                            reads=[psb, self.pvb], writes=[self.modb])

    def rms_to_h(self, st_tiles, xt, xtb, w, mod_scale_idx, mod_shift_idx, gname, l, which, ht, htb):
        cfg, K = self.cfg, self.K
        KC = cfg.KC
        sq = st_tiles["sq"]
        rs, rsb = st_tiles["rstd"]
        ab, abb = st_tiles["ab"]
        hf = st_tiles["hf"]
        o_g, _ = self.pplan[gname]
        K.dve.op(lambda e: e.tensor_scalar(out=ab[:, :, 0], in0=self.modv[:, l, mod_scale_idx * KC:(mod_scale_idx + 1) * KC, which],
                                           scalar1=1.0, scalar2=None, op0=ALU.add),
                 reads=[self.modb], writes=[abb])
        K.dve.op(lambda e: e.tensor_tensor(out=ab[:, :, 0], in0=ab[:, :, 0], in1=self.pv[:, l, o_g:o_g + KC], op=ALU.mult),
                 reads=[abb, self.pvb], writes=[abb])
        K.dve.op(lambda e: e.tensor_copy(out=ab[:, :, 1], in_=self.modv[:, l, mod_shift_idx * KC:(mod_shift_idx + 1) * KC, which]),
                 reads=[self.modb], writes=[abb])
        ps, psb = self.next_psum()
        for kc in range(KC):
            s, sb_ = sq[kc % 2]
            K.act.op(lambda e, kc=kc, s=s: e.activation(out=s[:, :w], in_=xt[:, kc, :w], func=AF.Square),
                     reads=[xtb], writes=[sb_])
            K.pe.op(lambda e, kc=kc, s=s: e.matmul(ps[:, :w], lhsT=self.ones[:], rhs=s[:, :w],
                                                   start=(kc == 0), stop=(kc == KC - 1)),
                    reads=[self.onesb, sb_], writes=[psb])
        K.act.op(lambda e: e.activation(out=rs[:, :w], in_=ps[:, :w], func=AF.Sqrt, scale=1.0 / cfg.D, bias=self.epsc[:, 0:1]),
                 reads=[psb, self.epsb], writes=[rsb])
        K.dve.op(lambda e: e.reciprocal(out=rs[:, :w], in_=rs[:, :w]), reads=[rsb], writes=[rsb])
        for kc in range(KC):
            f, fb = hf[kc % 2]
            K.dve.op(lambda e, kc=kc, f=f: e.tensor_tensor(out=f[:, :w], in0=xt[:, kc, :w], in1=rs[:, :w], op=ALU.mult),
                     reads=[xtb, rsb], writes=[fb])
            K.act.op(lambda e, kc=kc, f=f: e.activation(out=ht[:, kc, :w], in_=f[:, :w], func=AF.Identity,
                                                        scale=ab[:, kc, 0:1], bias=ab[:, kc, 1:2]),
                     reads=[fb, abb], writes=[htb])

    def mm_group(self, ps_ap, psb, lhs_list, rhs_list, reads):
        n = len(lhs_list)
        for i in range(n):
            self.K.pe.op(lambda e, i=i: e.matmul(ps_ap, lhsT=lhs_list[i], rhs=rhs_list[i], start=(i == 0), stop=(i == n - 1)),
                         reads=reads, writes=[psb])

    def phase1(self, l):
        cfg, K, nc = self.cfg, self.K, self.nc
        KC, BC, DB, T = cfg.KC, cfg.BC, cfg.DB, cfg.T
        xsrc = self.xT if l == 0 else self.X
        wp = self.wplan
        CW = KC * 128
        with ExitStack() as st:
            xt, xtb = self.sb(st, "p1_xt", [128, KC, T], F32)
            ht, htb = self.sb(st, "p1_ht", [128, KC, T], BF16)
            tl = {
                "sq": [self.sb(st, f"p1_sq{i}", [128, T], F32) for i in range(2)],
                "hf": [self.sb(st, f"p1_hf{i}", [128, T], F32) for i in range(2)],
                "rstd": self.sb(st, "p1_rstd", [128, T], F32),
                "ab": self.sb(st, "p1_ab", [128, KC, 2], F32),
            }
            self.epsc, self.epsb = self.sb(st, "p1_eps", [128, 1], F32)
            K.dve.op(lambda e: e.memset(self.epsc[:], cfg.EPS), writes=[self.epsb])
            wv1, wv1b = self.sb(st, "p1_wv1", [128, KC, DB], BF16)
            wv2, wv2b = self.sb(st, "p1_wv2", [128, KC, 128], BF16)
            wst, wstb = self.sb(st, "p1_wst", [128, BC, 128], BF16)
            self.wslots = [self.sb(st, f"p1_ws{i}", [128, WSLOT], BF16) for i in range(NWSLOT)]
            self.wslot_i = 0
            rv, rvb = self.sb(st, "p1_rv", [128, self.rplan["_total"]], F32)
            ut, utb = self.sb(st, "p1_ut", [128, BC, T], F32)
            yt, ytb = self.sb(st, "p1_yt", [128, BC, T], F32)
            qt, qtb = self.sb(st, "p1_qt", [128, BC, T], BF16)
            kt, ktb = self.sb(st, "p1_kt", [128, T], BF16)
            dt_, dtb = self.sb(st, "p1_dt", [128, BC, T], F32)
            bra, brab = self.sb(st, "p1_bra", [128, BC, T], BF16)
            cs, csb = self.sb(st, "p1_cos", [128, T], F32)
            sn, snb = self.sb(st, "p1_sin", [128, T], F32)
            sg, sgb = self.sb(st, "p1_sg", [128, T], F32)
            t1 = [self.sb(st, f"p1_t1{i}", [128, T], F32) for i in range(2)]
            t2 = [self.sb(st, f"p1_t2{i}", [128, T], F32) for i in range(2)]
            vg, vgb = self.sb(st, "p1_vg", [128, DB], F32)
            vn, vnb = self.sb(st, "p1_vn", [128, DB], F32)
            vnh, vnhb = self.sb(st, "p1_vnh", [128, DB], BF16)
            stt, sttb = self.sb(st, "p1_stats", [128, 8, 6], F32)
            mv, mvb = self.sb(st, "p1_mv", [128, 4], F32)
            mx, mxb = self.sb(st, "p1_mx", [128, BC, 128], F32)
            vv, vvb = self.sb(st, "p1_vv", [128, 2, 2, 64], BF16)
            o_glg, _ = self.rplan["gln_g"]
            o_glb, _ = self.rplan["gln_b"]
            o_gbs, _ = self.rplan["gbs"]
            K.pool.dma(rv[:], self.rowv[:, l, :], rvb)
            K.sp.dma(wv1[:], self.wb[l][:, wp["inV1"]:wp["inV1"] + KC * DB], wv1b)
            K.sp.dma(wv2[:], self.wb[l][:, wp["inV2"]:wp["inV2"] + KC * 128], wv2b)
            K.sp.dma(wst[:], self.wb[l][:, wp["wsT"]:wp["wsT"] + BC * 128], wstb)
            for (t0, w, is_ctx) in cfg.tiles:
                which = 0 if is_ctx else 1
                K.pool.dma(xt[:, :, :w], xsrc[:, :, t0:t0 + w].rearrange("c p t -> p c t"), xtb)
                K.pool.dma(cs[:, :w], self.ropec[:, t0:t0 + w], csb)
                K.pool.dma(sn[:, :w], self.ropes[:, t0:t0 + w], snb)
                self.rms_to_h(tl, xt, xtb, w, 1, 0, "n1g", l, which, ht, htb)
                K.pool.dma(self.H[:, :, t0:t0 + w].rearrange("c p t -> p c t"), ht[:, :, :w], htb, load=False)
                nchunks = 6 * BC + 2
                chunk_kind = []
                for i in range(BC):
                    chunk_kind.append(("u", i))
                for i in range(BC):
                    chunk_kind.append(("g", i))
                    chunk_kind.append(("a", i))
                for i in range(BC):
                    chunk_kind.append(("q", i))
                    chunk_kind.append(("qs", i))
                chunk_kind.append(("k", 0))
                chunk_kind.append(("ks", 0))
                for i in range(BC):
                    chunk_kind.append(("d", i))
                CPS = max(1, WSLOT // CW)
                wt = wtb = None
                pending = {}
                for ci, (kind, i) in enumerate(chunk_kind):
                    if ci % CPS == 0:
                        n = min(CPS, nchunks - ci)
                        wt, wtb = self.wload(l, wp["inF"] + ci * CW, n * CW)
                    base = (ci % CPS) * CW
                    ps, psb = self.next_psum()
                    self.mm_group(ps[:, :w], psb,
                                  [wt[:, base + kc * 128: base + (kc + 1) * 128] for kc in range(KC)],
                                  [ht[:, kc, :w] for kc in range(KC)], [wtb, htb])
                    if kind == "u":
                        K.act.op(lambda e, i=i, ps=ps: e.activation(out=ut[:, i, :w], in_=ps[:, :w], func=AF.Gelu_apprx_tanh),
                                 reads=[psb], writes=[utb])
                    elif kind == "g":
                        K.act.op(lambda e, ps=ps: e.activation(out=sg[:, :w], in_=ps[:, :w], func=AF.Sigmoid),
                                 reads=[psb], writes=[sgb])
                    elif kind == "a":
                        K.dve.op(lambda e, i=i, ps=ps: e.tensor_tensor(out=yt[:, i, :w], in0=ps[:, :w], in1=sg[:, :w], op=ALU.mult),
                                 reads=[psb, sgb], writes=[ytb])
                    elif kind in ("q", "k"):
                        a, ab_ = t1[ci % 2 if False else (ci // 2) % 2]
                        K.dve.op(lambda e, ps=ps, a=a: e.tensor_tensor(out=a[:, :w], in0=ps[:, :w], in1=cs[:, :w], op=ALU.mult),
                                 reads=[psb, csb], writes=[ab_])
                        pending["t1"] = (a, ab_)
                    elif kind in ("qs", "ks"):
                        a, ab_ = pending["t1"]
                        b2, b2b = t2[(ci // 2) % 2]
                        K.dve.op(lambda e, ps=ps, b2=b2: e.tensor_tensor(out=b2[:, :w], in0=ps[:, :w], in1=sn[:, :w], op=ALU.mult),
                                 reads=[psb, snb], writes=[b2b])
                        if kind == "qs":
                            K.pool.op(lambda e, i=i, a=a, b2=b2: e.tensor_tensor(out=qt[:, i, :w], in0=a[:, :w], in1=b2[:, :w], op=ALU.add),
                                      reads=[ab_, b2b], writes=[qtb])
                        else:
                            K.pool.op(lambda e, a=a, b2=b2: e.tensor_tensor(out=kt[:, :w], in0=a[:, :w], in1=b2[:, :w], op=ALU.add),
                                      reads=[ab_, b2b], writes=[ktb])
                    elif kind == "d":
                        K.act.op(lambda e, i=i, ps=ps: e.copy(out=dt_[:, i, :w], in_=ps[:, :w]), reads=[psb], writes=[dtb])
                y0 = self.ypos(t0)
                K.pool.dma(self.Y[:, :, y0:y0 + w].rearrange("c p t -> p c t"), yt[:, :, :w], ytb, load=False)
                K.pool.dma(self.Q[:, :, t0:t0 + w].rearrange("c p t -> p c t"), qt[:, :, :w], qtb, load=False)
                for hk in range(2):
                    for dup in range(2):
                        K.pool.dma(self.KT[hk, dup * 64:(dup + 1) * 64, t0:t0 + w], kt[hk * 64:(hk + 1) * 64, :w], ktb, load=False)
                K.pool.dma(self.U[:, :, t0:t0 + w].rearrange("c p t -> p c t"), dt_[:, :, :w], dtb, load=False)
                for j in range(w // 128):
                    tk = slice(j * 128, (j + 1) * 128)
                    ps, psb = self.next_psum()
                    self.mm_group(ps[:, :DB], psb, [ht[:, kc, tk] for kc in range(KC)],
                                  [wv1[:, kc, :] for kc in range(KC)], [htb, wv1b])
                    K.act.op(lambda e, ps=ps: e.activation(out=vg[:, :], in_=ps[:, :DB], func=AF.Gelu_apprx_tanh),
                             reads=[psb], writes=[vgb])
                    FMAX = 512
                    nst = (DB + FMAX - 1) // FMAX
                    for s_ in range(nst):
                        K.dve.op(lambda e, s_=s_: e.bn_stats(out=stt[:, s_, :], in_=vg[:, s_ * FMAX:min(DB, (s_ + 1) * FMAX)]),
                                 reads=[vgb], writes=[sttb])
                    K.dve.op(lambda e: e.bn_aggr(out=mv[:, 0:2], in_=stt[:, 0:nst, :]), reads=[sttb], writes=[mvb])
                    K.act.op(lambda e: e.activation(out=mv[:, 2:3], in_=mv[:, 1:2], func=AF.Sqrt, bias=self.epsc[:, 0:1]),
                             reads=[mvb, self.epsb], writes=[mvb])
                    K.dve.op(lambda e: e.reciprocal(out=mv[:, 2:3], in_=mv[:, 2:3]), reads=[mvb], writes=[mvb])
                    K.dve.op(lambda e: e.tensor_scalar(out=vn[:, :], in0=vg[:, :], scalar1=mv[:, 0:1], scalar2=mv[:, 2:3],
                                                       op0=ALU.subtract, op1=ALU.mult),
                             reads=[vgb, mvb], writes=[vnb])
                    K.pool.op(lambda e: e.tensor_tensor(out=vn[:, :], in0=vn[:, :], in1=rv[:, o_glg:o_glg + DB], op=ALU.mult),
                              reads=[vnb, rvb], writes=[vnb])
                    K.pool.op(lambda e: e.tensor_tensor(out=vnh[:, :], in0=vn[:, :], in1=rv[:, o_glb:o_glb + DB], op=ALU.add),
                              reads=[vnb, rvb], writes=[vnhb])
                    ps2, ps2b = self.next_psum()
                    for gi in range(BC):
                        K.pe.op(lambda e, gi=gi, ps2=ps2: e.matmul(ps2[:, gi * 128:(gi + 1) * 128], lhsT=vnh[:, gi * 128:(gi + 1) * 128],
                                                                  rhs=wst[:, gi, :], start=True, stop=True),
                                reads=[vnhb, wstb], writes=[ps2b])
                    K.dve.op(lambda e, ps2=ps2: e.tensor_tensor(out=mx[:, :, :], in0=ps2[:, :BC * 128].rearrange("p (g q) -> p g q", g=BC),
                                                               in1=rv[:, o_gbs:o_gbs + BC * 128].rearrange("p (g q) -> p g q", g=BC), op=ALU.add),
                             reads=[ps2b, rvb], writes=[mxb])
                    K.pool.op(lambda e, tk=tk: e.tensor_tensor(out=bra[:, :, tk], in0=mx[:, :, :], in1=ut[:, :, tk], op=ALU.mult),
                              reads=[mxb, utb], writes=[brab])
                    ps3, ps3b = self.next_psum()
                    self.mm_group(ps3[:, :128], ps3b, [ht[:, kc, tk] for kc in range(KC)],
                                  [wv2[:, kc, :] for kc in range(KC)], [htb, wv2b])
                    for dup in range(2):
                        K.act.op(lambda e, ps3=ps3, dup=dup: e.copy(out=vv[:, :, dup, :], in_=ps3[:, :128].rearrange("p (h d) -> p h d", h=2)),
                                 reads=[ps3b], writes=[vvb])
                    K.pool.dma(self.V[t0 + j * 128:t0 + (j + 1) * 128, :, :], vv[:].rearrange("p h u d -> p h (u d)"), vvb, load=False)
                K.pool.dma(self.BR[0, :, :, t0:t0 + w].rearrange("c p t -> p c t"), bra[:, :, :w], brab, load=False)


    def phase2(self, l):
        cfg, K, nc = self.cfg, self.K, self.nc
        KC, BC, DB, T, FFC, CTX, NT = cfg.KC, cfg.BC, cfg.DB, cfg.T, cfg.FFC, cfg.CTX, cfg.NT
        QPK = cfg.QPK
        xsrc = self.xT if l == 0 else self.X
        wp = self.wplan
        pp = self.pplan
        last = (l == cfg.DEPTH - 1)
        NCC = CTX // 128
        HC = min(FFC, KC)
        GW_ = 4 * KC * 128
        BW_ = 4 * BC * 128
        with ExitStack() as st:
            xt, xtb = self.sb(st, "p2_xt", [128, KC, T], F32)
            ht, htb = self.sb(st, "p2_ht", [128, KC, T], BF16)
            big, bigb = self.sb(st, "p2_big", [128, HC, T], BF16)
            tl = {
                "sq": [self.sb(st, f"p2_sq{i}", [128, T], F32) for i in range(2)],
                "hf": [self.sb(st, f"p2_hf{i}", [128, T], F32) for i in range(2)],
                "rstd": self.sb(st, "p2_rstd", [128, T], F32),
                "ab": self.sb(st, "p2_ab", [128, KC, 2], F32),
            }
            self.epsc, self.epsb = self.sb(st, "p2_eps", [128, 1], F32)
            K.dve.op(lambda e: e.memset(self.epsc[:], cfg.EPS), writes=[self.epsb])
            self.wslots = [self.sb(st, f"p2_ws{i}", [128, 8192], BF16) for i in range(3)]
            self.wslot_i = 0
            bws = [self.sb(st, f"p2_bw{i}", [128, BW_], BF16) for i in range(2)]
            brt = [self.sb(st, f"p2_br{k}", [128, BC, T], BF16) for k in range(4)]
            rv, rvb = self.sb(st, "p2_rv", [128, cfg.HQ], F32)
            esk, eskb = self.sb(st, "p2_esk", [128, cfg.HQ], F32)
            ywin, ywinb = self.sb(st, "p2_ywin", [128, BC, T + 32], F32)
            acc, accb = self.sb(st, "p2_acc", [128, BC, T], F32)
            cst = [self.sb(st, f"p2_cst{i}", [128, T], F32) for i in range(3)]
            qt, qtb = self.sb(st, "p2_qt", [128, BC, T], BF16)
            kwin, kwinb = self.sb(st, "p2_kwin", [128, 2, T + 256], BF16)
            vwin, vwinb = self.sb(st, "p2_vwin", [128, (T + 256) // 128, 2, 128], BF16)
            kctx, kctxb = self.sb(st, "p2_kctx", [128, 2, CTX], BF16)
            vctx, vctxb = self.sb(st, "p2_vctx", [128, NCC, 2, 128], BF16)
            mk = [self.sb(st, f"p2_mk{i}", [128, 128], BF16) for i in range(2)]
            mk32, mk32b = self.sb(st, "p2_mk32", [128, 2, 128], F32)
            onesh, oneshb = self.sb(st, "p2_onesh", [128, 128], BF16)
            pts = [self.sb(st, f"p2_pt{i}", [128, QPK * 128], BF16) for i in range(3)]
            rden, rdenb = self.sb(st, "p2_rden", [128, QPK * 128], F32)
            sgs = tl["sq"]
            prods = [self.sb(st, f"p2_pr{i}", [128, T], F32) for i in range(2)] + tl["hf"]
            rl = tl["sq"]
            zt, ztb = self.sb(st, "p2_zero", [128, 32], F32)
            o_sk, _ = self.rplan["sink"]
            K.pool.dma(rv[:], self.rowv[:, l, o_sk:o_sk + cfg.HQ], rvb)
            K.act.op(lambda e: e.activation(out=esk[:], in_=rv[:, :], func=AF.Exp), reads=[rvb], writes=[eskb])
            K.pool.dma(mk32[:, 0, :], self.mask_prev[:, :], mk32b)
            K.pool.dma(mk32[:, 1, :], self.mask_next[:, :], mk32b)
            for i in range(2):
                K.dve.op(lambda e, i=i: e.tensor_copy(out=mk[i][0][:], in_=mk32[:, i, :]), reads=[mk32b], writes=[mk[i][1]])
            K.dve.op(lambda e: e.memset(onesh[:], 1.0), writes=[oneshb])
            K.dve.op(lambda e: e.memset(zt[:], 0.0), writes=[ztb])
            P_ = cfg.YPAD
            for c0 in (0, P_ + CTX, 2 * P_ + CTX, 3 * P_ + NT):
                for c in range(BC):
                    K.pool.dma(self.Y[c, :, c0:c0 + P_], zt[:, 0:P_], ztb, load=False)
            K.pool.dma(kctx[:], self.KT[:, :, 0:CTX].rearrange("h p t -> p h t"), kctxb)
            K.pool.dma(vctx[:], self.V[0:CTX, :, :].rearrange("(c p) h d -> p c h d", p=128), vctxb)
            K.barrier()
            o_cw, _ = pp["conv_w"]
            o_cb, _ = pp["conv_b"]
            o_lg, _ = pp["cln_g"]
            o_lb, _ = pp["cln_b"]
            o_bg, _ = pp["b_gate"]
            for (t0, w, is_ctx) in cfg.tiles:
                if is_ctx and last:
                    continue
                which = 0 if is_ctx else 1
                K.pool.dma(xt[:, :, :w], xsrc[:, :, t0:t0 + w].rearrange("c p t -> p c t"), xtb)
                K.pool.dma(ht[:, :, :w], self.H[:, :, t0:t0 + w].rearrange("c p t -> p c t"), htb)
                K.pool.dma(brt[0][0][:, :, :w], self.BR[0, :, :, t0:t0 + w].rearrange("c p t -> p c t"), brt[0][1])
                K.pool.dma(brt[3][0][:, :, :w], self.BR[3, :, :, t0:t0 + w].rearrange("c p t -> p c t"), brt[3][1])
                y0 = self.ypos(t0)
                K.pool.dma(ywin[:, :, :w + 30], self.Y[:, :, y0 - 15:y0 + w + 15].rearrange("c p t -> p c t"), ywinb)
                K.pool.dma(qt[:, :, :w], self.Q[:, :, t0:t0 + w].rearrange("c p t -> p c t"), qtb)
                if not is_ctx:
                    k_lo = max(CTX, t0 - 128)
                    k_hi = min(NT, t0 + w + 128)
                    K.pool.dma(kwin[:, :, :k_hi - k_lo], self.KT[:, :, k_lo:k_hi].rearrange("h p t -> p h t"), kwinb)
                    K.pool.dma(vwin[:, :(k_hi - k_lo) // 128, :, :],
                               self.V[k_lo:k_hi, :, :].rearrange("(c p) h d -> p c h d", p=128), vwinb)
                for c in range(BC):
                    eng = K.dve
                    eng.op(lambda e, c=c: e.tensor_scalar(out=acc[:, c, :w], in0=ywin[:, c, 0:w],
                                                          scalar1=self.pv[:, l, o_cw + c * 31:o_cw + c * 31 + 1],
                                                          scalar2=self.pv[:, l, o_cb + c:o_cb + c + 1], op0=ALU.mult, op1=ALU.add),
                           reads=[ywinb, self.pvb], writes=[accb])
                    for j in range(1, 31):
                        eng.op(lambda e, c=c, j=j: e.scalar_tensor_tensor(
                            out=acc[:, c, :w], in0=ywin[:, c, j:j + w], scalar=self.pv[:, l, o_cw + c * 31 + j:o_cw + c * 31 + j + 1],
                            in1=acc[:, c, :w], op0=ALU.mult, op1=ALU.add), reads=[ywinb, self.pvb, accb], writes=[accb])
                ps1, ps1b = self.next_psum()
                ps2, ps2b = self.next_psum()
                for c in range(BC):
                    s_, sb_ = tl["sq"][c % 2]
                    K.pe.op(lambda e, c=c: e.matmul(ps1[:, :w], lhsT=self.ones[:], rhs=acc[:, c, :w], start=(c == 0), stop=(c == BC - 1)),
                            reads=[self.onesb, accb], writes=[ps1b])
                    K.act.op(lambda e, c=c, s_=s_: e.activation(out=s_[:, :w], in_=acc[:, c, :w], func=AF.Square), reads=[accb], writes=[sb_])
                    K.pe.op(lambda e, c=c, s_=s_: e.matmul(ps2[:, :w], lhsT=self.ones[:], rhs=s_[:, :w], start=(c == 0), stop=(c == BC - 1)),
                            reads=[self.onesb, sb_], writes=[ps2b])
                mean, meanb = cst[0]
                msq, msqb = cst[1]
                var, varb = cst[2]
                K.act.op(lambda e: e.mul(out=mean[:, :w], in_=ps1[:, :w], mul=1.0 / DB), reads=[ps1b], writes=[meanb])
                K.dve.op(lambda e: e.tensor_tensor(out=msq[:, :w], in0=mean[:, :w], in1=mean[:, :w], op=ALU.mult), reads=[meanb], writes=[msqb])
                K.dve.op(lambda e: e.scalar_tensor_tensor(out=var[:, :w], in0=ps2[:, :w], scalar=1.0 / DB, in1=msq[:, :w],
                                                          op0=ALU.mult, op1=ALU.subtract), reads=[ps2b, msqb], writes=[varb])
                K.act.op(lambda e: e.activation(out=var[:, :w], in_=var[:, :w], func=AF.Sqrt, bias=self.epsc[:, 0:1]),
                         reads=[varb, self.epsb], writes=[varb])
                K.dve.op(lambda e: e.reciprocal(out=var[:, :w], in_=var[:, :w]), reads=[varb], writes=[varb])
                for c in range(BC):
                    eng = K.dve if c % 2 == 0 else K.pool
                    eng.op(lambda e, c=c: e.tensor_tensor(out=acc[:, c, :w], in0=acc[:, c, :w], in1=mean[:, :w], op=ALU.subtract),
                           reads=[accb, meanb], writes=[accb])
                    eng.op(lambda e, c=c: e.tensor_tensor(out=acc[:, c, :w], in0=acc[:, c, :w], in1=var[:, :w], op=ALU.mult),
                           reads=[accb, varb], writes=[accb])
                    K.act.op(lambda e, c=c: e.activation(out=brt[1][0][:, c, :w], in_=acc[:, c, :w], func=AF.Silu,
                                                         scale=self.pv[:, l, o_lg + c:o_lg + c + 1], bias=self.pv[:, l, o_lb + c:o_lb + c + 1]),
                             reads=[accb, self.pvb], writes=[brt[1][1]])
                brc, brcb = brt[2]
                for j in range(w // 128):
                    tq0 = t0 + j * 128
                    qs = slice(j * 128, (j + 1) * 128)
                    chunks = []
                    if not is_ctx:
                        for rel_, mi in ((-128, 0), (0, None), (128, 1)):
                            ks = tq0 + rel_
                            if ks < CTX or ks >= NT:
                                continue
                            o = ks - k_lo
                            chunks.append((lambda hk, par, o=o: kwin[par * 64:(par + 1) * 64, hk, o:o + 128],
                                           lambda hk, o=o: vwin[:, o // 128, hk, :], mi, [kwinb], [vwinb]))
                    for cc in range(NCC):
                        chunks.append((lambda hk, par, cc=cc: kctx[par * 64:(par + 1) * 64, hk, cc * 128:(cc + 1) * 128],
                                       lambda hk, cc=cc: vctx[:, cc, hk, :], None, [kctxb], [vctxb]))
                    for hk in range(2):
                        pso, psob = self.ps[6], self.psb[6]
                        psd, psdb = self.ps[7], self.psb[7]
                        for ci, (kf, vf, mi, krd, vrd) in enumerate(chunks):
                            pt, ptb = pts[ci % 3]
                            for par in range(2):
                                heads = [i for i in range(QPK) if (hk * QPK + i) % 2 == par]
                                if not heads:
                                    continue
                                pss, pssb = self.next_psum()
                                for i in heads:
                                    ch = (hk * QPK + i) // 2
                                    K.pe.op(lambda e, i=i, par=par, ch=ch, kf=kf, pss=pss: e.matmul(
                                        pss[:, i * 128:(i + 1) * 128], lhsT=kf(hk, par), rhs=qt[par * 64:(par + 1) * 64, ch, qs],
                                        start=True, stop=True), reads=krd + [qtb], writes=[pssb])
                                for i in heads:
                                    K.act.op(lambda e, pss=pss, pt=pt, i=i: e.activation(out=pt[:, i * 128:(i + 1) * 128], in_=pss[:, i * 128:(i + 1) * 128],
                                                                                    func=AF.Exp, scale=0.125), reads=[pssb], writes=[ptb])
                            if mi is not None:
                                K.pool.op(lambda e, pt=pt, mi=mi: e.tensor_tensor(
                                    out=pt[:, :].rearrange("p (h q) -> p h q", h=QPK), in0=pt[:, :].rearrange("p (h q) -> p h q", h=QPK),
                                    in1=mk[mi][0][:, :].unsqueeze(1).to_broadcast([128, QPK, 128]), op=ALU.mult),
                                    reads=[ptb, mk[mi][1]], writes=[ptb])
                            first, lastc = (ci == 0), (ci == len(chunks) - 1)
                            K.pe.op(lambda e, vf=vf, pt=pt, first=first, lastc=lastc: e.matmul(
                                pso[:, :QPK * 128], lhsT=vf(hk), rhs=pt[:, :], start=first, stop=lastc), reads=vrd + [ptb], writes=[psob])
                            K.pe.op(lambda e, pt=pt, first=first, lastc=lastc: e.matmul(
                                psd[:, :QPK * 128], lhsT=onesh[:, :], rhs=pt[:, :], start=first, stop=lastc), reads=[oneshb, ptb], writes=[psdb])
                        K.dve.op(lambda e: e.tensor_tensor(
                            out=rden[:, :].rearrange("p (h q) -> p h q", h=QPK), in0=psd[:, :QPK * 128].rearrange("p (h q) -> p h q", h=QPK),
                            in1=esk[:, hk * QPK:(hk + 1) * QPK].unsqueeze(2).to_broadcast([128, QPK, 128]), op=ALU.add),
                            reads=[psdb, eskb], writes=[rdenb])
                        K.dve.op(lambda e: e.reciprocal(out=rden[:, :], in_=rden[:, :]), reads=[rdenb], writes=[rdenb])
                        for i in range(QPK):
                            hq = hk * QPK + i
                            par, ch = hq % 2, hq // 2
                            pr = slice(par * 64, (par + 1) * 64)
                            K.dve.op(lambda e, i=i, pr=pr, ch=ch: e.tensor_tensor(
                                out=brc[pr, ch, qs], in0=pso[pr, i * 128:(i + 1) * 128], in1=rden[pr, i * 128:(i + 1) * 128], op=ALU.mult),
                                reads=[psob, rdenb], writes=[brcb])
                for oc in range(KC):
                    gw, gwb = self.wload(l, wp["gb"] + oc * (GW_ + BW_), GW_)
                    bw, bwb = bws[oc % 2]
                    K.sp.dma(bw[:, :], self.wb[l][:, wp["gb"] + oc * (GW_ + BW_) + GW_: wp["gb"] + (oc + 1) * (GW_ + BW_)], bwb)
                    for k in range(4):
                        psg, psgb = self.next_psum()
                        self.mm_group(psg[:, :w], psgb, [gw[:, (k * KC + kc) * 128:(k * KC + kc + 1) * 128] for kc in range(KC)],
                                      [ht[:, kc, :w] for kc in range(KC)], [gwb, htb])
                        psb_, psbb = self.next_psum()
                        self.mm_group(psb_[:, :w], psbb, [bw[:, (k * BC + bc) * 128:(k * BC + bc + 1) * 128] for bc in range(BC)],
                                      [brt[k][0][:, bc, :w] for bc in range(BC)], [bwb, brt[k][1]])
                        sg, sgb = sgs[k % 2]
                        K.act.op(lambda e, psg=psg, sg=sg, k=k: e.activation(out=sg[:, :w], in_=psg[:, :w], func=AF.Sigmoid,
                                                                        bias=self.pv[:, l, o_bg + k * KC + oc:o_bg + k * KC + oc + 1]),
                                 reads=[psgb, self.pvb], writes=[sgb])
                        pr_, prb = prods[k]
                        K.dve.op(lambda e, psb_=psb_, sg=sg, pr_=pr_: e.tensor_tensor(out=pr_[:, :w], in0=psb_[:, :w], in1=sg[:, :w], op=ALU.mult),
                                 reads=[psbb, sgb], writes=[prb])
                    K.pool.op(lambda e: e.tensor_tensor(out=prods[0][0][:, :w], in0=prods[0][0][:, :w], in1=prods[1][0][:, :w], op=ALU.add),
                              reads=[prods[0][1], prods[1][1]], writes=[prods[0][1]])
                    K.pool.op(lambda e: e.tensor_tensor(out=prods[2][0][:, :w], in0=prods[2][0][:, :w], in1=prods[3][0][:, :w], op=ALU.add),
                              reads=[prods[2][1], prods[3][1]], writes=[prods[2][1]])
                    K.pool.op(lambda e, oc=oc: e.tensor_tensor(out=big[:, oc, :w], in0=prods[0][0][:, :w], in1=prods[2][0][:, :w], op=ALU.add),
                              reads=[prods[0][1], prods[2][1]], writes=[bigb])
                CW = KC * 128
                CPS = max(1, 8192 // CW)
                for oc in range(KC):
                    if oc % CPS == 0:
                        n = min(CPS, KC - oc)
                        wt, wtb = self.wload(l, wp["out"] + oc * CW, n * CW)
                    base = (oc % CPS) * CW
                    ps, psb = self.next_psum()
                    self.mm_group(ps[:, :w], psb, [wt[:, base + kc * 128:base + (kc + 1) * 128] for kc in range(KC)],
                                  [big[:, kc, :w] for kc in range(KC)], [wtb, bigb])
                    K.dve.op(lambda e, oc=oc, ps=ps: e.scalar_tensor_tensor(
                        out=xt[:, oc, :w], in0=ps[:, :w], scalar=self.modv[:, l, 2 * KC + oc, which:which + 1], in1=xt[:, oc, :w],
                        op0=ALU.mult, op1=ALU.add), reads=[psb, self.modb, xtb], writes=[xtb])
                self.rms_to_h(tl, xt, xtb, w, 4, 3, "n2g", l, which, ht, htb)
                for hp in range(FFC // HC):
                    for fcl in range(HC):
                        fc = hp * HC + fcl
                        if fcl % CPS == 0:
                            n = min(CPS, HC - fcl)
                            wt, wtb = self.wload(l, wp["ff1"] + fc * CW, n * CW)
                        base = (fcl % CPS) * CW
                        ps, psb = self.next_psum()
                        self.mm_group(ps[:, :w], psb, [wt[:, base + kc * 128:base + (kc + 1) * 128] for kc in range(KC)],
                                      [ht[:, kc, :w] for kc in range(KC)], [wtb, htb])
                        r_, rb = rl[fc % 2]
                        K.act.op(lambda e, ps=ps, r_=r_: e.activation(out=r_[:, :w], in_=ps[:, :w], func=AF.Relu), reads=[psb], writes=[rb])
                        eng = K.pool if fc % 2 == 0 else K.dve
                        eng.op(lambda e, r_=r_, fcl=fcl: e.tensor_tensor(out=big[:, fcl, :w], in0=r_[:, :w], in1=r_[:, :w], op=ALU.mult),
                               reads=[rb], writes=[bigb])
                    FW = FFC * 128
                    for oc in range(KC):
                        ps, psb = self.next_psum()
                        nload = HC * 128
                        for s0 in range(0, nload, 8192):
                            n = min(8192, nload - s0)
                            wt, wtb = self.wload(l, wp["ff2"] + oc * FW + hp * HC * 128 + s0, n)
                            nk = n // 128
                            for kk in range(nk):
                                fcl = s0 // 128 + kk
                                K.pe.op(lambda e, wt=wt, kk=kk, fcl=fcl, ps=ps: e.matmul(
                                    ps[:, :w], lhsT=wt[:, kk * 128:(kk + 1) * 128], rhs=big[:, fcl, :w],
                                    start=(fcl == 0), stop=(fcl == HC - 1)), reads=[wtb, bigb], writes=[psb])
                        K.dve.op(lambda e, oc=oc, ps=ps: e.scalar_tensor_tensor(
                            out=xt[:, oc, :w], in0=ps[:, :w], scalar=self.modv[:, l, 5 * KC + oc, which:which + 1], in1=xt[:, oc, :w],
                            op0=ALU.mult, op1=ALU.add), reads=[psb, self.modb, xtb], writes=[xtb])
                K.pool.dma(self.X[:, :, t0:t0 + w].rearrange("c p t -> p c t"), xt[:, :, :w], xtb, load=False)

    def phase_s(self, l):
        cfg, K, nc = self.cfg, self.K, self.nc
        BC, NP, NT, CTX = cfg.BC, cfg.NPAIR, cfg.NT, cfg.CTX
        TS = 256
        NQ = NP // 4
        PI = math.pi
        wp, pp = self.wplan, self.pplan
        LOG = int(math.log2(TS))
        with ExitStack() as st:
            W, Wb = self.sb(st, "s_W", [128, 24, 2, NP], F32)
            Bt, Btb = self.sb(st, "s_Bt", [128, 2, 2, 2, NQ, 128], BF16)
            Ct, Ctb = self.sb(st, "s_Ct", [128, 2, NP, 2, 64], BF16)
            R, Rb = self.sb(st, "s_R", [128, 2 * NP, 2, TS], F32)
            carry, carryb_ = self.sb(st, "s_carry", [128, 2, NP, 2], F32)
            st2 = ExitStack()
            At, Atb = self.sb(st2, "s_A", [128, 2, NP, 3], F32)
            Bt32, Bt32b = self.sb(st2, "s_B32", [128, 2, NP, 32], F32)
            Ct32, Ct32b = self.sb(st2, "s_C32", [128, 2, NP, 2, 64], F32)
            Wi, Wib = self.sb(st2, "s_Wi", [128, 2, NP], mybir.dt.int32)
            bbar, bbarb = self.sb(st2, "s_bbar", [128, 2, 2, NP, 32], F32)
            tb, tbb = self.sb(st2, "s_tb", [128, 2, NP, 32], F32)
            bbv, bbvb = self.sb(st2, "s_bbv", [128, 2, 2, 2, NP, 32], F32)
            E, Eb = self.sb(st2, "s_E", [128, 4, 2 * NP], F32)
            rt1, rt1b = self.sb(st2, "s_rt1", [128, 2 * NP, TS // 2], F32)
            rt2, rt2b = self.sb(st2, "s_rt2", [128, 2 * NP, TS // 2], F32)
            carryb = [[Buf(f"carry{d}_{k}") for k in range(NP)] for d in range(2)]
            K.pool.dma(At[:], self.s5A[:, l], Atb)
            K.pool.dma(Bt32[:], self.s5B[:, l], Bt32b)
            K.pool.dma(Ct32[:], self.s5C[:, l], Ct32b)
            a_re, a_im, ls = At[:, :, :, 0], At[:, :, :, 1], At[:, :, :, 2]
            (DT, XR, XI, MAG, T0, TF, RS, RC, SIN, COS, LR, LI, NR, DEN, CR, CI, TA, TB_, XC) = [W[:, i] for i in range(19)]

            def dv(fn, rd=(Atb, Wb), wr=(Wb,)):
                K.dve.op(fn, reads=list(rd), writes=list(wr))

            def ac(fn, rd=(Atb, Wb), wr=(Wb,)):
                K.act.op(fn, reads=list(rd), writes=list(wr))
            ac(lambda e: e.activation(out=DT, in_=ls, func=AF.Exp))
            dv(lambda e: e.tensor_tensor(out=XR, in0=a_re, in1=DT, op=ALU.mult))
            dv(lambda e: e.tensor_tensor(out=XI, in0=a_im, in1=DT, op=ALU.mult))
            ac(lambda e: e.activation(out=MAG, in_=XR, func=AF.Exp))

            def reduce_sin(dst, x_ap):
                dv(lambda e: e.tensor_scalar(out=T0, in0=x_ap, scalar1=1.0 / (2 * PI), scalar2=None, op0=ALU.mult))
                dv(lambda e: e.tensor_copy(out=Wi[:], in_=T0), wr=(Wib,))
                dv(lambda e: e.tensor_copy(out=TF, in_=Wi[:]), rd=(Wib,))
                dv(lambda e: e.scalar_tensor_tensor(out=RS, in0=TF, scalar=-2 * PI, in1=x_ap, op0=ALU.mult, op1=ALU.add))
                dv(lambda e: e.tensor_scalar(out=RS, in0=RS, scalar1=-PI, scalar2=PI, op0=ALU.max, op1=ALU.min))
                ac(lambda e: e.activation(out=dst, in_=RS, func=AF.Sin))
            reduce_sin(SIN, XI)
            dv(lambda e: e.tensor_scalar(out=XC, in0=XI, scalar1=PI / 2, scalar2=None, op0=ALU.add))
            reduce_sin(COS, XC)
            dv(lambda e: e.tensor_tensor(out=LR, in0=MAG, in1=COS, op=ALU.mult))
            dv(lambda e: e.tensor_tensor(out=LI, in0=MAG, in1=SIN, op=ALU.mult))
            dv(lambda e: e.tensor_scalar(out=NR, in0=LR, scalar1=-1.0, scalar2=None, op0=ALU.add))
            dv(lambda e: e.tensor_tensor(out=DEN, in0=a_re, in1=a_re, op=ALU.mult))
            dv(lambda e: e.tensor_tensor(out=TA, in0=a_im, in1=a_im, op=ALU.mult))
            dv(lambda e: e.tensor_tensor(out=DEN, in0=DEN, in1=TA, op=ALU.add))
            dv(lambda e: e.reciprocal(out=DEN, in_=DEN))
            dv(lambda e: e.tensor_tensor(out=TA, in0=NR, in1=a_re, op=ALU.mult))
            dv(lambda e: e.tensor_tensor(out=TB_, in0=LI, in1=a_im, op=ALU.mult))
            dv(lambda e: e.tensor_tensor(out=CR, in0=TA, in1=TB_, op=ALU.add))
            dv(lambda e: e.tensor_tensor(out=CR, in0=CR, in1=DEN, op=ALU.mult))
            dv(lambda e: e.tensor_tensor(out=TA, in0=LI, in1=a_re, op=ALU.mult))
            dv(lambda e: e.tensor_tensor(out=TB_, in0=NR, in1=a_im, op=ALU.mult))
            dv(lambda e: e.tensor_tensor(out=CI, in0=TA, in1=TB_, op=ALU.subtract))
            dv(lambda e: e.tensor_tensor(out=CI, in0=CI, in1=DEN, op=ALU.mult))
            b_re, b_im = Bt32[:, 0], Bt32[:, 1]
            for d in range(2):
                crb = CR[:, d, :].unsqueeze(2).to_broadcast([128, NP, 32])
                cib = CI[:, d, :].unsqueeze(2).to_broadcast([128, NP, 32])
                rd = (Bt32b, Wb, bbarb, tbb)
                dv(lambda e, d=d, crb=crb: e.tensor_tensor(out=bbar[:, d, 0], in0=b_re, in1=crb, op=ALU.mult), rd=rd, wr=(bbarb,))
                dv(lambda e, cib=cib: e.tensor_tensor(out=tb[:, 0], in0=b_im, in1=cib, op=ALU.mult), rd=rd, wr=(tbb,))
                dv(lambda e, d=d: e.tensor_tensor(out=bbar[:, d, 0], in0=bbar[:, d, 0], in1=tb[:, 0], op=ALU.subtract), rd=rd, wr=(bbarb,))
                dv(lambda e, d=d, crb=crb: e.tensor_tensor(out=bbar[:, d, 1], in0=b_im, in1=crb, op=ALU.mult), rd=rd, wr=(bbarb,))
                dv(lambda e, cib=cib: e.tensor_tensor(out=tb[:, 1], in0=b_re, in1=cib, op=ALU.mult), rd=rd, wr=(tbb,))
                dv(lambda e, d=d: e.tensor_tensor(out=bbar[:, d, 1], in0=bbar[:, d, 1], in1=tb[:, 1], op=ALU.add), rd=rd, wr=(bbarb,))
            K.dve.op(lambda e: e.memset(bbv[:], 0.0), writes=[bbvb])
            for v in range(2):
                K.dve.op(lambda e, v=v: e.tensor_copy(out=bbv[:, v, :, :, v::2, :], in_=bbar[:, :, :, v::2, :]), reads=[bbarb, bbvb], writes=[bbvb])
            for v in range(2):
              for d in range(2):
                for ri in range(2):
                    for q in range(NQ):
                        ps, psb = self.next_psum()
                        K.pe.op(lambda e, v=v, d=d, ri=ri, q=q, ps=ps: e.transpose(
                            out=ps[:, :128], in_=bbv[:, v, d, ri, q * 4:(q + 1) * 4, :].rearrange("p a b -> p (a b)"), identity=self.identt[:]),
                            reads=[bbvb, self.identb], writes=[psb])
                        K.act.op(lambda e, v=v, d=d, ri=ri, q=q, ps=ps: e.copy(out=Bt[:, v, d, ri, q, :], in_=ps[:, :128]), reads=[psb], writes=[Btb])
            K.act.op(lambda e: e.copy(out=Ct[:, :, :, 0, :], in_=Ct32[:, :, :, 0, :]), reads=[Ct32b], writes=[Ctb])
            K.act.op(lambda e: e.mul(out=Ct[:, :, :, 1, :], in_=Ct32[:, :, :, 1, :], mul=-1.0), reads=[Ct32b], writes=[Ctb])
            N2 = 2 * NP
            cosf = COS.rearrange("p d k -> p (d k)")
            sinf = SIN.rearrange("p d k -> p (d k)")
            dv(lambda e: e.tensor_copy(out=E[:, 0, :], in_=cosf), wr=(Eb,))
            dv(lambda e: e.tensor_copy(out=E[:, 1, :], in_=sinf), wr=(Eb,))
            dv(lambda e: e.tensor_copy(out=R[:, :, 0, 0], in_=cosf), wr=(Rb,))
            dv(lambda e: e.tensor_copy(out=R[:, :, 1, 0], in_=sinf), wr=(Rb,))
            rdR = (Rb, Eb, rt1b, rt2b)
            for kk in range(LOG):
                n = 1 << kk
                er = E[:, 0, :].unsqueeze(2).to_broadcast([128, N2, n])
                ei = E[:, 1, :].unsqueeze(2).to_broadcast([128, N2, n])
                sre, sim_ = R[:, :, 0, 0:n], R[:, :, 1, 0:n]
                dre, dim_ = R[:, :, 0, n:2 * n], R[:, :, 1, n:2 * n]
                a1, a2 = rt1[:, :, 0:n], rt2[:, :, 0:n]
                dv(lambda e, a1=a1, sre=sre, er=er: e.tensor_tensor(out=a1, in0=sre, in1=er, op=ALU.mult), rd=rdR, wr=(rt1b,))
                dv(lambda e, a2=a2, sim_=sim_, ei=ei: e.tensor_tensor(out=a2, in0=sim_, in1=ei, op=ALU.mult), rd=rdR, wr=(rt2b,))
                dv(lambda e, a1=a1, a2=a2, dre=dre: e.tensor_tensor(out=dre, in0=a1, in1=a2, op=ALU.subtract), rd=rdR, wr=(Rb,))
                dv(lambda e, a1=a1, sre=sre, ei=ei: e.tensor_tensor(out=a1, in0=sre, in1=ei, op=ALU.mult), rd=rdR, wr=(rt1b,))
                dv(lambda e, a2=a2, sim_=sim_, er=er: e.tensor_tensor(out=a2, in0=sim_, in1=er, op=ALU.mult), rd=rdR, wr=(rt2b,))
                dv(lambda e, a1=a1, a2=a2, dim_=dim_: e.tensor_tensor(out=dim_, in0=a1, in1=a2, op=ALU.add), rd=rdR, wr=(Rb,))
                dv(lambda e: e.tensor_tensor(out=E[:, 2, :], in0=E[:, 0, :], in1=E[:, 0, :], op=ALU.mult), rd=(Eb,), wr=(Eb,))
                dv(lambda e: e.tensor_tensor(out=E[:, 3, :], in0=E[:, 1, :], in1=E[:, 1, :], op=ALU.mult), rd=(Eb,), wr=(Eb,))
                dv(lambda e: e.tensor_tensor(out=E[:, 1, :], in0=E[:, 0, :], in1=E[:, 1, :], op=ALU.mult), rd=(Eb,), wr=(Eb,))
                dv(lambda e: e.tensor_scalar(out=E[:, 1, :], in0=E[:, 1, :], scalar1=2.0, scalar2=None, op0=ALU.mult), rd=(Eb,), wr=(Eb,))
                dv(lambda e: e.tensor_tensor(out=E[:, 0, :], in0=E[:, 2, :], in1=E[:, 3, :], op=ALU.subtract), rd=(Eb,), wr=(Eb,))
            K.dve.op(lambda e: e.memset(carry[:], 0.0), writes=[b for row in carryb for b in row])
            K.barrier()
            st2.close()
            wglu, wglub = self.sb(st, "s_wglu", [128, 2 * BC, BC, 128], BF16)
            u32, u32b = self.sb(st, "s_u32", [128, BC, TS], F32)
            ubf, ubfb = self.sb(st, "s_ubf", [128, BC, TS], BF16)
            ybt, ybtb = self.sb(st, "s_yb", [128, BC, TS], F32)
            yt, ytb = self.sb(st, "s_yt", [128, BC, TS], F32)
            yg, ygb = self.sb(st, "s_yg", [128, BC, TS], BF16)
            brd, brdb = self.sb(st, "s_brd", [128, BC, TS], BF16)
            sig, sigb = self.sb(st, "s_sig", [128, TS], F32)
            sets = []
            for i in range(2):
                d = {}
                for nm in ("t1", "t2", "bt", "st", "t3", "t4", "s32"):
                    d[nm] = self.sb(st, f"s_{nm}{i}", [128, 2, TS], F32)
                d["sbf"] = self.sb(st, f"s_sbf{i}", [128, 2, TS], BF16)
                sets.append(d)
            K.sp.dma(wglu[:], self.wb[l][:, wp["glu"]:wp["glu"] + 2 * BC * BC * 128], wglub)
            ctx_subs = [(t, TS) for t in range(0, CTX, TS)]
            lat_subs = [(t, TS) for t in range(CTX, NT, TS)]
            o_d, _ = pp["s5_d"]
            for dirn in (1, 0):
                order = (ctx_subs + lat_subs) if dirn == 0 else (ctx_subs[::-1] + lat_subs[::-1])
                for (t0, w) in order:
                    K.pool.dma(u32[:], self.U[:, :, t0:t0 + w].rearrange("c p t -> p c t"), u32b)
                    if dirn == 0:
                        K.pool.dma(ybt[:], self.YB[:, :, t0:t0 + w].rearrange("c p t -> p c t"), ybtb)
                        K.act.op(lambda e: e.copy(out=ubf[:], in_=u32[:]), reads=[u32b], writes=[ubfb])
                    else:
                        K.act.op(lambda e: e.copy(out=ubf[:], in_=u32[:, :, ::-1]), reads=[u32b], writes=[ubfb])
                    psy = [self.ps[6], self.ps[7]]
                    psyb = [self.psb[6], self.psb[7]]
                    for k in range(NP):
                        q, slot = k // 4, k % 4
                        rows = slice(64 * (slot // 2), 64 * (slot // 2) + 64)
                        S_ = sets[k % 2]
                        ps, psb = self.next_psum()
                        for ri in range(2):
                            K.pe.op(lambda e, ri=ri, ps=ps, q=q, rows=rows, slot=slot: e.matmul(
                                ps[:, ri * TS:(ri + 1) * TS], lhsT=Bt[rows, slot % 2, dirn, ri, q, :], rhs=ubf[rows, q, :], start=True, stop=True),
                                reads=[Btb, ubfb], writes=[psb])
                        bu = ps[:, :2 * TS].rearrange("p (r t) -> p r t", r=2)
                        rr = R[:, dirn * NP + k, 0, :].unsqueeze(1).to_broadcast([128, 2, TS])
                        rim = R[:, dirn * NP + k, 1, :].unsqueeze(1).to_broadcast([128, 2, TS])
                        (t1, t1b), (t2, t2b), (bt, btb), (stt, sttb) = S_["t1"], S_["t2"], S_["bt"], S_["st"]
                        (t3, t3b), (t4, t4b), (s32, s32b), (sbf, sbfb) = S_["t3"], S_["t4"], S_["s32"], S_["sbf"]
                        K.dve.op(lambda e, t1=t1, bu=bu, rr=rr: e.tensor_tensor(out=t1[:], in0=bu, in1=rr, op=ALU.mult), reads=[psb, Rb], writes=[t1b])
                        K.dve.op(lambda e, t2=t2, bu=bu, rim=rim: e.tensor_tensor(out=t2[:], in0=bu[:, ::-1, :], in1=rim, op=ALU.mult),
                                 reads=[psb, Rb], writes=[t2b])
                        K.pool.op(lambda e, bt=bt, t1=t1, t2=t2: e.tensor_tensor(out=bt[:, 0, :], in0=t1[:, 0, :], in1=t2[:, 0, :], op=ALU.add),
                                  reads=[t1b, t2b], writes=[btb])
                        K.pool.op(lambda e, bt=bt, t1=t1, t2=t2: e.tensor_tensor(out=bt[:, 1, :], in0=t1[:, 1, :], in1=t2[:, 1, :], op=ALU.subtract),
                                  reads=[t1b, t2b], writes=[btb])
                        dec = W[:, 3, dirn, k:k + 1].to_broadcast([128, TS])
                        for ri in range(2):
                            K.dve.op(lambda e, ri=ri, stt=stt, bt=bt, dec=dec, k=k: e.tensor_tensor_scan(
                                out=stt[:, ri, :], data0=dec, data1=bt[:, ri, :], initial=carry[:, dirn, k, ri:ri + 1], op0=ALU.mult, op1=ALU.add),
                                reads=[btb, Wb, carryb[dirn][k]], writes=[sttb])
                        K.pool.op(lambda e, t3=t3, stt=stt, rr=rr: e.tensor_tensor(out=t3[:], in0=stt[:], in1=rr, op=ALU.mult), reads=[sttb, Rb], writes=[t3b])
                        K.pool.op(lambda e, t4=t4, stt=stt, rim=rim: e.tensor_tensor(out=t4[:], in0=stt[:, ::-1, :], in1=rim, op=ALU.mult),
                                  reads=[sttb, Rb], writes=[t4b])
                        K.pool.op(lambda e, s32=s32, t3=t3, t4=t4: e.tensor_tensor(out=s32[:, 0, :], in0=t3[:, 0, :], in1=t4[:, 0, :], op=ALU.subtract),
                                  reads=[t3b, t4b], writes=[s32b])
                        K.pool.op(lambda e, s32=s32, t3=t3, t4=t4: e.tensor_tensor(out=s32[:, 1, :], in0=t3[:, 1, :], in1=t4[:, 1, :], op=ALU.add),
                                  reads=[t3b, t4b], writes=[s32b])
                        K.act.op(lambda e, s32=s32, k=k: e.copy(out=carry[:, dirn, k, :], in_=s32[:, :, TS - 1]), reads=[s32b], writes=[carryb[dirn][k]])
                        if dirn == 0:
                            K.act.op(lambda e, sbf=sbf, s32=s32: e.copy(out=sbf[:], in_=s32[:]), reads=[s32b], writes=[sbfb])
                        else:
                            K.act.op(lambda e, sbf=sbf, s32=s32: e.copy(out=sbf[:], in_=s32[:, :, ::-1]), reads=[s32b], writes=[sbfb])
                        py, pyb = psy[q // 2], psyb[q // 2]
                        c0 = (q % 2) * TS
                        for ri in range(2):
                            K.pe.op(lambda e, ri=ri, py=py, c0=c0, rows=rows, sbf=sbf, k=k, slot=slot: e.matmul(
                                py[rows, c0:c0 + TS], lhsT=Ct[:, dirn, k, ri, :], rhs=sbf[:, ri, :], start=(ri == 0 and slot % 2 == 0), stop=(ri == 1 and slot % 2 == 1)),
                                reads=[Ctb, sbfb], writes=[pyb])
                    if dirn == 1:
                        for q in range(BC):
                            K.act.op(lambda e, q=q: e.copy(out=ybt[:, q, :], in_=psy[q // 2][:, (q % 2) * TS:(q % 2 + 1) * TS]),
                                     reads=[psyb[q // 2]], writes=[ybtb])
                        K.pool.dma(self.YB[:, :, t0:t0 + w].rearrange("c p t -> p c t"), ybt[:], ybtb, load=False)
                    else:
                        for q in range(BC):
                            K.dve.op(lambda e, q=q: e.tensor_tensor(out=yt[:, q, :], in0=psy[q // 2][:, (q % 2) * TS:(q % 2 + 1) * TS],
                                                                    in1=ybt[:, q, :], op=ALU.add), reads=[psyb[q // 2], ybtb], writes=[ytb])
                            K.dve.op(lambda e, q=q: e.scalar_tensor_tensor(out=yt[:, q, :], in0=u32[:, q, :], scalar=self.pv[:, l, o_d + q:o_d + q + 1],
                                                                           in1=yt[:, q, :], op0=ALU.mult, op1=ALU.add),
                                     reads=[u32b, self.pvb, ytb], writes=[ytb])
                            K.act.op(lambda e, q=q: e.activation(out=yg[:, q, :], in_=yt[:, q, :], func=AF.Gelu_apprx_tanh), reads=[ytb], writes=[ygb])
                        for oc in range(BC):
                            psg, psgb = self.next_psum()
                            self.mm_group(psg[:, :TS], psgb, [wglu[:, BC + oc, bc, :] for bc in range(BC)], [yg[:, bc, :] for bc in range(BC)], [wglub, ygb])
                            K.act.op(lambda e, psg=psg: e.activation(out=sig[:, :], in_=psg[:, :TS], func=AF.Sigmoid), reads=[psgb], writes=[sigb])
                            psa, psab = self.next_psum()
                            self.mm_group(psa[:, :TS], psab, [wglu[:, oc, bc, :] for bc in range(BC)], [yg[:, bc, :] for bc in range(BC)], [wglub, ygb])
                            K.dve.op(lambda e, oc=oc, psa=psa: e.tensor_tensor(out=brd[:, oc, :], in0=psa[:, :TS], in1=sig[:, :], op=ALU.mult),
                                     reads=[psab, sigb], writes=[brdb])
                        K.pool.dma(self.BR[3, :, :, t0:t0 + w].rearrange("c p t -> p c t"), brd[:], brdb, load=False)

    def phase_final(self):
        cfg, K, nc = self.cfg, self.K, self.nc
        KC, T, CTX = cfg.KC, cfg.T, cfg.CTX
        L = cfg.DEPTH
        o_g, _ = self.pplan["n1g"]
        with ExitStack() as st:
            xt, xtb = self.sb(st, "f_xt", [128, KC, T], F32)
            ot, otb = self.sb(st, "f_ot", [128, KC, T], F32)
            sq = [self.sb(st, f"f_sq{i}", [128, T], F32) for i in range(2)]
            rs, rsb = self.sb(st, "f_rs", [128, T], F32)
            eps, epsb = self.sb(st, "f_eps", [128, 1], F32)
            K.dve.op(lambda e: e.memset(eps[:], cfg.EPS), writes=[epsb])
            for (t0, w, is_ctx) in cfg.tiles:
                if is_ctx:
                    continue
                K.pool.dma(xt[:, :, :w], self.X[:, :, t0:t0 + w].rearrange("c p t -> p c t"), xtb)
                ps, psb = self.next_psum()
                for kc in range(KC):
                    s_, sb_ = sq[kc % 2]
                    K.act.op(lambda e, kc=kc, s_=s_: e.activation(out=s_[:, :w], in_=xt[:, kc, :w], func=AF.Square), reads=[xtb], writes=[sb_])
                    K.pe.op(lambda e, kc=kc, s_=s_: e.matmul(ps[:, :w], lhsT=self.ones[:], rhs=s_[:, :w], start=(kc == 0), stop=(kc == KC - 1)),
                            reads=[self.onesb, sb_], writes=[psb])
                K.act.op(lambda e: e.activation(out=rs[:, :w], in_=ps[:, :w], func=AF.Sqrt, scale=1.0 / cfg.D, bias=eps[:, 0:1]),
                         reads=[psb, epsb], writes=[rsb])
                K.dve.op(lambda e: e.reciprocal(out=rs[:, :w], in_=rs[:, :w]), reads=[rsb], writes=[rsb])
                for kc in range(KC):
                    eng = K.dve
                    eng.op(lambda e, kc=kc: e.scalar_tensor_tensor(out=ot[:, kc, :w], in0=xt[:, kc, :w], scalar=self.pv[:, L, o_g + kc:o_g + kc + 1],
                                                                  in1=rs[:, :w], op0=ALU.mult, op1=ALU.mult),
                           reads=[xtb, rsb, self.pvb], writes=[otb])
                K.pool.dma(self.out[:, :, t0 - CTX:t0 - CTX + w].rearrange("c p t -> p c t"), ot[:, :, :w], otb, load=False)


_CACHE = {}


def kernel(**inputs):
    cfg = Cfg()
    inp = {k: np.asarray(v) for k, v in inputs.items()}
    if "nc" not in _CACHE:
        _CACHE["nc"] = Prog(cfg).build()
    nc = _CACHE["nc"]
    shared = prep_shared(cfg, inp)
    n = 8
    in_maps = []
    for c in range(n):
        m = dict(shared)
        m.update(prep_inputs(cfg, inp, c))
        in_maps.append(m)
    res = run_bass_kernel_spmd(nc, in_maps, core_ids=list(range(n)))
    out = np.empty((cfg.BATCH, cfg.SEQ, cfg.D), np.float32)
    for b in range(cfg.BATCH):
        o = np.asarray(res.results[b]["out"])
        out[b] = o.reshape(cfg.D, cfg.SEQ).T
    return out
```

```python
import math
import numpy as np
import concourse.bass as bass
import concourse.mybir as mybir
from concourse.bass_utils import run_bass_kernel_spmd

F32 = mybir.dt.float32
BF16 = mybir.dt.bfloat16
AF = mybir.ActivationFunctionType
ALU = mybir.AluOpType
AX = mybir.AxisListType


class Cfg:
    def __init__(self, D=2048, SEQ=8192, CTX=256, DEPTH=4, GRID_W=64, T=512, BATCH=4):
        self.D, self.SEQ, self.CTX, self.DEPTH, self.GRID_W, self.T, self.BATCH = D, SEQ, CTX, DEPTH, GRID_W, T, BATCH
        self.DB = D // 4
        self.KC = D // 128
        self.BC = self.DB // 128
        self.FF = 4 * D
        self.FFC = self.FF // 128
        self.HQ = self.DB // 64
        self.QPK = self.HQ // 2
        self.G = self.DB // 16
        self.NPAIR = self.G // 2
        self.NT = CTX + SEQ
        self.IN_COLS = 2 * self.DB + 2 * self.DB + self.DB + 256 + self.DB
        self.CONV_W = 31
        self.EPS = 1e-6
        BC = self.BC
        self.c_u = 0
        self.c_v = BC
        self.c_a = 2 * BC
        self.c_g = 3 * BC
        self.c_q = 4 * BC
        self.c_k = 5 * BC
        self.c_vv = 5 * BC + 1
        self.c_d = 5 * BC + 2
        self.NCI = 6 * BC + 2
        tiles = []
        s = 0
        while s < CTX:
            w = min(T, CTX - s)
            tiles.append((s, w, True))
            s += w
        while s < self.NT:
            w = min(T, self.NT - s)
            tiles.append((s, w, False))
            s += w
        self.tiles = tiles
        self.YPAD = 16
        self.NY = self.NT + 4 * self.YPAD


class Buf:
    __slots__ = ("name", "w", "r", "dsem", "dcnt", "dq")

    def __init__(self, name):
        self.name = name
        self.w = None
        self.r = []
        self.dsem = None
        self.dcnt = 0


class Eng:
    def __init__(self, K, eng, name, sem):
        self.K, self.eng, self.name, self.sem = K, eng, name, sem
        self.count = 0
        self.waited = {}

    def _wait(self, ev):
        if ev is None:
            return
        sem, val = ev
        if sem is self.sem and not self.K.same_engine_sync:
            return
        if sem is self.sem and self.name == "pe":
            return
        if self.waited.get(id(sem), 0) >= val:
            return
        self.eng.wait_ge(sem, val)
        self.waited[id(sem)] = val

    def deps(self, reads, writes):
        evs = {}

        def add(ev):
            if ev is None:
                return
            k = id(ev[0])
            if k not in evs or evs[k][1] < ev[1]:
                evs[k] = ev
        for b in reads:
            add(b.w)
        for b in writes:
            add(b.w)
            for ev in b.r:
                add(ev)
        for ev in evs.values():
            self._wait(ev)

    def op(self, fn, reads=(), writes=()):
        self.deps(reads, writes)
        ins = fn(self.eng)
        self.count += 1
        ins.then_inc(self.sem, 1)
        ev = (self.sem, self.count)
        for b in reads:
            b.r = [x for x in b.r if x[0] is not ev[0]] + [ev]
        for b in writes:
            b.w = ev
            b.r = []
        return ins

    def dma(self, out, in_, sbuf, reads=(), writes=(), load=True, **kw):
        K = self.K
        if sbuf.dsem is None:
            fl = K.free_sems.setdefault(self.name, [])
            if fl:
                sbuf.dsem, sbuf.dcnt = fl.pop()
            else:
                sbuf.dsem, sbuf.dcnt = K.new_sem(f"dsem{len(K._stack)}"), 0
            sbuf.dq = self.name
        assert sbuf.dq == self.name, "a buffer's DMAs must stay on one queue"
        rd = list(reads) + ([] if load else [sbuf])
        wr = list(writes) + ([sbuf] if load else [])
        self.deps(rd, wr)
        ins = self.eng.dma_start(out=out, in_=in_, **kw)
        sbuf.dcnt += 1
        ins.then_inc(sbuf.dsem, 16)
        ev = (sbuf.dsem, 16 * sbuf.dcnt)
        for b in rd:
            b.r = [x for x in b.r if x[0] is not ev[0]] + [ev]
        for b in wr:
            b.w = ev
            b.r = []
        K.dma_bufs[id(sbuf)] = sbuf
        return ins


class Kern:
    def __init__(self, nc, same_engine_sync=True):
        self.nc = nc
        self.same_engine_sync = same_engine_sync
        self.sems = []
        self.dma_bufs = {}
        self.free_sems = {}
        self._stack = []

    def new_sem(self, name):
        cm = self.nc.semaphore(name)
        h = cm.__enter__()
        self._stack.append(cm)
        return h

    def make_engines(self, block_engs):
        self.pe = Eng(self, block_engs["tensor"], "pe", self.new_sem("s_pe"))
        self.act = Eng(self, block_engs["scalar"], "act", self.new_sem("s_act"))
        self.dve = Eng(self, block_engs["vector"], "dve", self.new_sem("s_dve"))
        self.pool = Eng(self, block_engs["gpsimd"], "pool", self.new_sem("s_pool"))
        self.sp = Eng(self, block_engs["sync"], "sp", self.new_sem("s_sp"))
        self.engs = [self.pe, self.act, self.dve, self.pool, self.sp]

    def barrier(self):
        for e in self.engs:
            for o in self.engs:
                if o is not e and o.count > 0:
                    e._wait((o.sem, o.count))
            for b in self.dma_bufs.values():
                if b.dcnt:
                    e._wait((b.dsem, 16 * b.dcnt))
        for e in self.engs:
            if e.count > 20000:
                e.sem = self.new_sem(f"s_{e.name}_{len(self._stack)}")
                e.count = 0
        for b in self.dma_bufs.values():
            self.free_sems[b.dq].append((b.dsem, b.dcnt))
            b.dsem = None
        self.dma_bufs = {}

    def close(self):
        for cm in reversed(self._stack):
            cm.__exit__(None, None, None)


def lhsT_chunks(W, ncols=128):
    K, N = W.shape
    return np.ascontiguousarray(W.reshape(K // 128, 128, N // ncols, ncols).transpose(1, 2, 0, 3))


def weight_plan(cfg):
    KC, BC, FFC, DB = cfg.KC, cfg.BC, cfg.FFC, cfg.DB
    plan = {}
    off = 0

    def add(name, n):
        nonlocal off
        plan[name] = off
        off += n
    add("inF", (6 * BC + 2) * KC * 128)
    add("inV1", KC * DB)
    add("inV2", KC * 128)
    add("wsT", BC * 128)
    add("glu", 2 * BC * BC * 128)
    add("gb", KC * (4 * KC * 128 + 4 * BC * 128))
    add("out", KC * KC * 128)
    add("ff1", FFC * KC * 128)
    add("ff2", KC * FFC * 128)
    tot = off
    tot = (tot + 8191) // 8192 * 8192
    plan["_total"] = tot
    return plan


def swap16(W):
    K, N = W.shape
    return np.ascontiguousarray(W.reshape(K, N // 32, 2, 16)[:, :, ::-1, :].reshape(K, N))


def pack_layer_weights(cfg, inp, l):
    KC, BC, FFC, DB, D = cfg.KC, cfg.BC, cfg.FFC, cfg.DB, cfg.D
    plan = weight_plan(cfg)
    flat = np.zeros((128, plan["_total"]), np.float32)

    def put(name, arr):
        a = arr.reshape(128, -1)
        flat[:, plan[name]:plan[name] + a.shape[1]] = a
    w_in = inp["w_in"][l]
    cA, cB, cQ, cK, cV, cD = 0, 2 * DB, 4 * DB, 5 * DB, 5 * DB + 128, 5 * DB + 256
    cols = []
    for i in range(BC):
        cols.append(w_in[:, cA + i * 128: cA + (i + 1) * 128])
    for i in range(BC):
        cols.append(w_in[:, cB + DB + i * 128: cB + DB + (i + 1) * 128])
        cols.append(w_in[:, cB + i * 128: cB + (i + 1) * 128])
    wq = w_in[:, cQ:cQ + DB]
    wqs = swap16(wq)
    for i in range(BC):
        cols.append(wq[:, i * 128:(i + 1) * 128])
        cols.append(wqs[:, i * 128:(i + 1) * 128])
    wk = w_in[:, cK:cK + 128]
    cols.append(wk)
    cols.append(swap16(wk))
    for i in range(BC):
        cols.append(w_in[:, cD + i * 128: cD + (i + 1) * 128])
    put("inF", lhsT_chunks(np.concatenate(cols, axis=1)))
    put("inV1", lhsT_chunks(w_in[:, cA + DB: cA + 2 * DB], ncols=DB))
    put("inV2", lhsT_chunks(w_in[:, cV:cV + 128]))
    put("wsT", np.ascontiguousarray(inp["gmlp_ws"][l].transpose(2, 0, 1)))
    put("glu", lhsT_chunks(inp["s5_w_glu"][l]))
    g4 = np.stack([lhsT_chunks(inp["w_gate"][l, k]) for k in range(4)], axis=2)
    b4 = np.stack([lhsT_chunks(inp["w_branch"][l, k]) for k in range(4)], axis=2)
    gb = np.concatenate([g4.reshape(128, KC, -1), b4.reshape(128, KC, -1)], axis=2)
    put("gb", gb)
    put("out", lhsT_chunks(inp["w_out"][l]))
    put("ff1", lhsT_chunks(inp["w_ff1"][l]))
    put("ff2", lhsT_chunks(inp["w_ff2"][l]))
    return flat


def fm(v):
    return np.ascontiguousarray(v.reshape(-1, 128).T)


def pvec_plan(cfg):
    KC, BC = cfg.KC, cfg.BC
    plan = {}
    off = 0
    for name, n in (("b_mod", 6 * KC), ("n1g", KC), ("n2g", KC), ("b_gate", 4 * KC), ("conv_w", BC * 31),
                    ("conv_b", BC), ("cln_g", BC), ("cln_b", BC), ("s5_d", BC)):
        plan[name] = (off, n)
        off += n
    plan["_total"] = off
    return plan


def rowv_plan(cfg):
    DB, BC = cfg.DB, cfg.BC
    plan = {}
    off = 0
    for name, n in (("gln_g", DB), ("gln_b", DB), ("gbs", BC * 128), ("sink", cfg.HQ)):
        plan[name] = (off, n)
        off += n
    plan["_total"] = off
    return plan


def prep_inputs(cfg, inp, core):
    L, KC, BC, D, DB = cfg.DEPTH, cfg.KC, cfg.BC, cfg.D, cfg.DB
    b = core % cfg.BATCH
    m = {}
    xcat = np.concatenate([inp["ctx"][b], inp["x"][b]], axis=0)
    m["xT"] = np.ascontiguousarray(xcat.T.reshape(KC, 128, cfg.NT))
    cond = np.stack([fm(inp["c_ctx"]), fm(inp["c"][b])], axis=2)
    m["cond"] = np.ascontiguousarray(cond)
    return m


def prep_shared(cfg, inp):
    L, KC, BC, D, DB = cfg.DEPTH, cfg.KC, cfg.BC, cfg.D, cfg.DB
    m = {}
    m["wall"] = np.stack([pack_layer_weights(cfg, inp, l) for l in range(L)], axis=0)
    m["wmod"] = np.stack([lhsT_chunks(inp["w_mod"][l]) for l in range(L)], axis=0)
    pp = pvec_plan(cfg)
    pv = np.zeros((128, L + 1, pp["_total"]), np.float32)
    for l in range(L):
        def put(name, a):
            o, n = pp[name]
            pv[:, l, o:o + n] = a.reshape(128, n)
        put("b_mod", fm(inp["b_mod"][l]))
        put("n1g", fm(inp["norm1_g"][l]))
        put("n2g", fm(inp["norm2_g"][l]))
        put("b_gate", np.stack([fm(inp["b_gate"][l, k]) for k in range(4)], axis=1))
        cw = inp["conv_w"][l]
        put("conv_w", np.ascontiguousarray(cw.T.reshape(BC, 128, 31).transpose(1, 0, 2)))
        put("conv_b", fm(inp["conv_b"][l]))
        put("cln_g", fm(inp["conv_ln_g"][l]))
        put("cln_b", fm(inp["conv_ln_b"][l]))
        put("s5_d", fm(inp["s5_d"][l]))
    o, n = pp["n1g"]
    pv[:, L, o:o + n] = fm(inp["final_g"])
    m["pvec"] = pv
    rp = rowv_plan(cfg)
    rv = np.zeros((128, L, rp["_total"]), np.float32)
    for l in range(L):
        for name, a in (("gln_g", inp["gmlp_ln_g"][l]), ("gln_b", inp["gmlp_ln_b"][l]),
                        ("gbs", inp["gmlp_bs"][l].reshape(-1)), ("sink", inp["attn_sink"][l])):
            o, n = rp[name]
            rv[:, l, o:o + n] = np.broadcast_to(a.reshape(1, n), (128, n))
    m["rowv"] = rv
    half = 32
    inv = (10000.0 ** (-np.arange(0, half, 2, dtype=np.float32) / half)).astype(np.float32)
    pos = np.arange(cfg.SEQ)
    row = (pos // cfg.GRID_W).astype(np.float32)
    col = (pos % cfg.GRID_W).astype(np.float32)
    cosT = np.ones((128, cfg.NT), np.float32)
    sinT = np.zeros((128, cfg.NT), np.float32)
    for p in range(128):
        d = p % 64
        blk, j = d // 32, d % 32
        ang = ((row if blk == 0 else col) * inv[j % 16]).astype(np.float32)
        cosT[p, cfg.CTX:] = np.cos(ang)
        sinT[p, cfg.CTX:] = np.sin(ang) * (-1.0 if j < 16 else 1.0)
    m["ropec"] = cosT
    m["ropes"] = sinT
    kl = np.arange(128)[:, None]
    ql = np.arange(128)[None, :]
    m["mask_prev"] = (kl >= ql).astype(np.float32)
    m["mask_next"] = (kl <= ql).astype(np.float32)
    m["ident"] = np.eye(128, dtype=np.float32)
    NP = cfg.NPAIR
    A = np.zeros((128, L, 2, NP, 3), np.float32)
    Bm = np.zeros((128, L, 2, NP, 32), np.float32)
    Cm = np.zeros((128, L, 2, NP, 2, 64), np.float32)
    for g2 in range(2):
        rows = slice(g2 * 64, (g2 + 1) * 64)
        gidx = 2 * np.arange(NP) + g2
        A[rows, :, :, :, 0] = inp["s5_a_re"][:, :, gidx, :].transpose(3, 0, 1, 2)
        A[rows, :, :, :, 1] = inp["s5_a_im"][:, :, gidx, :].transpose(3, 0, 1, 2)
        A[rows, :, :, :, 2] = np.broadcast_to(inp["s5_log_step"][:, :, gidx][None], (64, L, 2, NP))
        cs_ = slice(g2 * 16, (g2 + 1) * 16)
        Bm[rows, :, 0, :, cs_] = inp["s5_b_re"][:, gidx, :, :].transpose(2, 0, 1, 3)
        Bm[rows, :, 1, :, cs_] = inp["s5_b_im"][:, gidx, :, :].transpose(2, 0, 1, 3)
        for k in range(NP):
            cc_ = slice(32 * (k % 2) + g2 * 16, 32 * (k % 2) + (g2 + 1) * 16)
            Cm[rows, :, :, k, 0, cc_] = inp["s5_c_re"][:, :, gidx[k], :, :].transpose(3, 0, 1, 2)
            Cm[rows, :, :, k, 1, cc_] = inp["s5_c_im"][:, :, gidx[k], :, :].transpose(3, 0, 1, 2)
    m["s5A"], m["s5B"], m["s5C"] = A, Bm, Cm
    return m


from contextlib import ExitStack

WSLOT = 8192
NWSLOT = 3


class Prog:
    def __init__(self, cfg, debug=False, n_layers=None, stop_after=None, same_engine_sync=True):
        self.cfg = cfg
        self.debug = debug
        self.L = cfg.DEPTH if n_layers is None else n_layers
        self.stop_after = stop_after
        nc = bass.Bass("TRN2", target_bir_lowering=False)
        self.nc = nc
        self.K = Kern(nc, same_engine_sync)
        self.K.make_engines({"tensor": nc.tensor, "scalar": nc.scalar, "vector": nc.vector,
                             "gpsimd": nc.gpsimd, "sync": nc.sync})
        self.wplan = weight_plan(cfg)
        self.pplan = pvec_plan(cfg)
        self.rplan = rowv_plan(cfg)
        self.psum_i = 0

    def din(self, name, shape, dt=F32):
        return self.nc.dram_tensor(name, list(shape), dt, kind="ExternalInput").ap()

    def dscratch(self, name, shape, dt):
        kind = "ExternalOutput" if self.debug else "Internal"
        return self.nc.dram_tensor(name, list(shape), dt, kind=kind).ap()

    def sb(self, st, name, shape, dt):
        self._uid = getattr(self, "_uid", 0) + 1
        name = f"{name}_{self._uid}"
        t = st.enter_context(self.nc.sbuf_tensor(name, list(shape), dt))
        return t, Buf(name)

    def next_psum(self):
        i = self.psum_i
        self.psum_i = (i + 1) % 6
        return self.ps[i], self.psb[i]

    def declare(self):
        cfg, L = self.cfg, self.cfg.DEPTH
        KC, BC, NT = cfg.KC, cfg.BC, cfg.NT
        self.xT = self.din("xT", [KC, 128, NT])
        self.cond = self.din("cond", [128, KC, 2])
        self.wall = self.din("wall", [L, 128, self.wplan["_total"]])
        self.wmod = self.din("wmod", [L, 128, 6 * KC, KC, 128])
        self.pvec = self.din("pvec", [128, L + 1, self.pplan["_total"]])
        self.rowv = self.din("rowv", [128, L, self.rplan["_total"]])
        self.ropec = self.din("ropec", [128, NT])
        self.ropes = self.din("ropes", [128, NT])
        self.mask_prev = self.din("mask_prev", [128, 128])
        self.mask_next = self.din("mask_next", [128, 128])
        self.ident = self.din("ident", [128, 128])
        self.s5A = self.din("s5A", [128, L, 2, cfg.NPAIR, 3])
        self.s5B = self.din("s5B", [128, L, 2, cfg.NPAIR, 32])
        self.s5C = self.din("s5C", [128, L, 2, cfg.NPAIR, 2, 64])
        self.out = self.nc.dram_tensor("out", [KC, 128, cfg.SEQ], F32, kind="ExternalOutput").ap()
        self.wb = [self.dscratch(f"wb{l}", [128, self.wplan["_total"]], BF16) for l in range(L)]
        self.X = self.dscratch("X", [KC, 128, NT], F32)
        self.H = self.dscratch("H", [KC, 128, NT], BF16)
        self.BR = self.dscratch("BR", [4, BC, 128, NT], BF16)
        self.Y = self.dscratch("Y", [BC, 128, cfg.NY], F32)
        self.Q = self.dscratch("Q", [BC, 128, NT], BF16)
        self.KT = self.dscratch("KT", [2, 128, NT], BF16)
        self.V = self.dscratch("V", [NT, 2, 128], BF16)
        self.U = self.dscratch("U", [BC, 128, NT], F32)
        self.YB = self.dscratch("YB", [BC, 128, NT], F32)

    def ypos(self, t):
        cfg = self.cfg
        return t + cfg.YPAD if t < cfg.CTX else t + 3 * cfg.YPAD

    def wload(self, l, off, n):
        i = self.wslot_i
        self.wslot_i = (i + 1) % len(self.wslots)
        t, b = self.wslots[i]
        self.K.sp.dma(t[:, 0:n], self.wb[l][:, off:off + n], b)
        return t, b

    def build(self):
        cfg, K, nc = self.cfg, self.K, self.nc
        self.declare()
        with ExitStack() as st:
            self.ps, self.psb = [], []
            for i in range(8):
                p = st.enter_context(nc.psum_tensor(f"ps{i}", [128, 512], F32))
                self.ps.append(p)
                self.psb.append(Buf(f"ps{i}"))
            self.pv, self.pvb = self.sb(st, "pv", [128, cfg.DEPTH + 1, self.pplan["_total"]], F32)
            self.modv, self.modb = self.sb(st, "modv", [128, cfg.DEPTH, 6 * cfg.KC, 2], F32)
            self.ones, self.onesb = self.sb(st, "ones", [128, 128], F32)
            self.identt, self.identb = self.sb(st, "identt", [128, 128], F32)
            K.pool.dma(self.pv[:], self.pvec[:, :, :], self.pvb)
            K.pool.dma(self.identt[:], self.ident[:, :], self.identb)
            K.dve.op(lambda e: e.memset(self.ones[:], 1.0), writes=[self.onesb])
            self.phase_w()
            K.barrier()
            self.phase_m()
            K.barrier()
            for l in range(self.L):
                self.phase1(l)
                K.barrier()
                if self.stop_after == ("p1", l):
                    break
                self.phase_s(l)
                K.barrier()
                if self.stop_after == ("ps", l):
                    break
                self.phase2(l)
                K.barrier()
            else:
                self.phase_final()
                K.barrier()
        K.close()
        return nc

    def phase_w(self):
        cfg, K, nc = self.cfg, self.K, self.nc
        tot = self.wplan["_total"]
        CH = 8192
        with ExitStack() as st:
            s32 = [self.sb(st, f"w32_{i}", [128, CH], F32) for i in range(2)]
            s16 = [self.sb(st, f"w16_{i}", [128, CH], BF16) for i in range(2)]
            it = 0
            for l in range(self.L):
                for off in range(0, tot, CH):
                    a, ab = s32[it % 2]
                    o, ob = s16[it % 2]
                    K.sp.dma(a[:], self.wall[l, :, off:off + CH], ab)
                    eng = (K.dve, K.act, K.pool)[it % 3]
                    if eng is K.act:
                        eng.op(lambda e: e.copy(out=o[:], in_=a[:]), reads=[ab], writes=[ob])
                    else:
                        eng.op(lambda e: e.tensor_copy(out=o[:], in_=a[:]), reads=[ab], writes=[ob])
                    K.pool.dma(self.wb[l][:, off:off + CH], o[:], ob, load=False)
                    it += 1

    def phase_m(self):
        cfg, K, nc = self.cfg, self.K, self.nc
        KC = cfg.KC
        NJ = 6 * KC
        JB = max(d for d in range(1, NJ + 1) if NJ % d == 0 and d * KC * 128 <= 8192)
        with ExitStack() as st:
            ct, cb = self.sb(st, "condt", [128, KC, 2], F32)
            sc, scb = self.sb(st, "scond", [128, KC, 2], F32)
            ws = [self.sb(st, f"wm_{i}", [128, JB, KC, 128], F32) for i in range(2)]
            K.pool.dma(ct[:], self.cond[:, :, :], cb)
            K.act.op(lambda e: e.activation(out=sc[:], in_=ct[:], func=AF.Silu), reads=[cb], writes=[scb])
            it = 0
            o_b, _ = self.pplan["b_mod"]
            for l in range(self.L):
                for j0 in range(0, NJ, JB):
                    w, wbuf = ws[it % 2]
                    it += 1
                    K.sp.dma(w[:], self.wmod[l, :, j0:j0 + JB, :, :], wbuf)
                    for j in range(j0, j0 + JB):
                        ps, psb = self.next_psum()
                        for kc in range(KC):
                            K.pe.op(lambda e, kc=kc, j=j: e.matmul(ps[:, 0:2], lhsT=w[:, j - j0, kc, :], rhs=sc[:, kc, :],
                                                                    start=(kc == 0), stop=(kc == KC - 1)),
                                    reads=[wbuf, scb], writes=[psb])
                        K.dve.op(lambda e, j=j: e.tensor_tensor(
                            out=self.modv[:, l, j, :], in0=ps[:, 0:2],
                            in1=self.pv[:, l, o_b + j:o_b + j + 1].to_broadcast([128, 2]), op=ALU.add),
                            reads=[psb, self.pvb], writes=[self.modb])

    def rms_to_h(self, st_tiles, xt, xtb, w, mod_scale_idx, mod_shift_idx, gname, l, which, ht, htb):
        cfg, K = self.cfg, self.K
        KC = cfg.KC
        sq = st_tiles["sq"]
        rs, rsb = st_tiles["rstd"]
        ab, abb = st_tiles["ab"]
        hf = st_tiles["hf"]
        o_g, _ = self.pplan[gname]
        K.dve.op(lambda e: e.tensor_scalar(out=ab[:, :, 0], in0=self.modv[:, l, mod_scale_idx * KC:(mod_scale_idx + 1) * KC, which],
                                           scalar1=1.0, scalar2=None, op0=ALU.add),
                 reads=[self.modb], writes=[abb])
        K.dve.op(lambda e: e.tensor_tensor(out=ab[:, :, 0], in0=ab[:, :, 0], in1=self.pv[:, l, o_g:o_g + KC], op=ALU.mult),
                 reads=[abb, self.pvb], writes=[abb])
        K.dve.op(lambda e: e.tensor_copy(out=ab[:, :, 1], in_=self.modv[:, l, mod_shift_idx * KC:(mod_shift_idx + 1) * KC, which]),
                 reads=[self.modb], writes=[abb])
        ps, psb = self.next_psum()
        for kc in range(KC):
            s, sb_ = sq[kc % 2]
            K.act.op(lambda e, kc=kc, s=s: e.activation(out=s[:, :w], in_=xt[:, kc, :w], func=AF.Square),
                     reads=[xtb], writes=[sb_])
            K.pe.op(lambda e, kc=kc, s=s: e.matmul(ps[:, :w], lhsT=self.ones[:], rhs=s[:, :w],
                                                   start=(kc == 0), stop=(kc == KC - 1)),
                    reads=[self.onesb, sb_], writes=[psb])
        K.act.op(lambda e: e.activation(out=rs[:, :w], in_=ps[:, :w], func=AF.Sqrt, scale=1.0 / cfg.D, bias=self.epsc[:, 0:1]),
                 reads=[psb, self.epsb], writes=[rsb])
        K.dve.op(lambda e: e.reciprocal(out=rs[:, :w], in_=rs[:, :w]), reads=[rsb], writes=[rsb])
        for kc in range(KC):
            f, fb = hf[kc % 2]
            K.dve.op(lambda e, kc=kc, f=f: e.tensor_tensor(out=f[:, :w], in0=xt[:, kc, :w], in1=rs[:, :w], op=ALU.mult),
                     reads=[xtb, rsb], writes=[fb])
            K.act.op(lambda e, kc=kc, f=f: e.activation(out=ht[:, kc, :w], in_=f[:, :w], func=AF.Identity,
                                                        scale=ab[:, kc, 0:1], bias=ab[:, kc, 1:2]),
                     reads=[fb, abb], writes=[htb])

    def mm_group(self, ps_ap, psb, lhs_list, rhs_list, reads):
        n = len(lhs_list)
        for i in range(n):
            self.K.pe.op(lambda e, i=i: e.matmul(ps_ap, lhsT=lhs_list[i], rhs=rhs_list[i], start=(i == 0), stop=(i == n - 1)),
                         reads=reads, writes=[psb])

    def phase1(self, l):
        cfg, K, nc = self.cfg, self.K, self.nc
        KC, BC, DB, T = cfg.KC, cfg.BC, cfg.DB, cfg.T
        xsrc = self.xT if l == 0 else self.X
        wp = self.wplan
        CW = KC * 128
        with ExitStack() as st:
            xt, xtb = self.sb(st, "p1_xt", [128, KC, T], F32)
            ht, htb = self.sb(st, "p1_ht", [128, KC, T], BF16)
            tl = {
                "sq": [self.sb(st, f"p1_sq{i}", [128, T], F32) for i in range(2)],
                "hf": [self.sb(st, f"p1_hf{i}", [128, T], F32) for i in range(2)],
                "rstd": self.sb(st, "p1_rstd", [128, T], F32),
                "ab": self.sb(st, "p1_ab", [128, KC, 2], F32),
            }
            self.epsc, self.epsb = self.sb(st, "p1_eps", [128, 1], F32)
            K.dve.op(lambda e: e.memset(self.epsc[:], cfg.EPS), writes=[self.epsb])
            wv1, wv1b = self.sb(st, "p1_wv1", [128, KC, DB], BF16)
            wv2, wv2b = self.sb(st, "p1_wv2", [128, KC, 128], BF16)
            wst, wstb = self.sb(st, "p1_wst", [128, BC, 128], BF16)
            self.wslots = [self.sb(st, f"p1_ws{i}", [128, WSLOT], BF16) for i in range(NWSLOT)]
            self.wslot_i = 0
            rv, rvb = self.sb(st, "p1_rv", [128, self.rplan["_total"]], F32)
            ut, utb = self.sb(st, "p1_ut", [128, BC, T], F32)
            yt, ytb = self.sb(st, "p1_yt", [128, BC, T], F32)
            qt, qtb = self.sb(st, "p1_qt", [128, BC, T], BF16)
            kt, ktb = self.sb(st, "p1_kt", [128, T], BF16)
            dt_, dtb = self.sb(st, "p1_dt", [128, BC, T], F32)
            bra, brab = self.sb(st, "p1_bra", [128, BC, T], BF16)
            cs, csb = self.sb(st, "p1_cos", [128, T], F32)
            sn, snb = self.sb(st, "p1_sin", [128, T], F32)
            sg, sgb = self.sb(st, "p1_sg", [128, T], F32)
            t1 = [self.sb(st, f"p1_t1{i}", [128, T], F32) for i in range(2)]
            t2 = [self.sb(st, f"p1_t2{i}", [128, T], F32) for i in range(2)]
            vg, vgb = self.sb(st, "p1_vg", [128, DB], F32)
            vn, vnb = self.sb(st, "p1_vn", [128, DB], F32)
            vnh, vnhb = self.sb(st, "p1_vnh", [128, DB], BF16)
            stt, sttb = self.sb(st, "p1_stats", [128, 8, 6], F32)
            mv, mvb = self.sb(st, "p1_mv", [128, 4], F32)
            mx, mxb = self.sb(st, "p1_mx", [128, BC, 128], F32)
            vv, vvb = self.sb(st, "p1_vv", [128, 2, 2, 64], BF16)
            o_glg, _ = self.rplan["gln_g"]
            o_glb, _ = self.rplan["gln_b"]
            o_gbs, _ = self.rplan["gbs"]
            K.pool.dma(rv[:], self.rowv[:, l, :], rvb)
            K.sp.dma(wv1[:], self.wb[l][:, wp["inV1"]:wp["inV1"] + KC * DB], wv1b)
            K.sp.dma(wv2[:], self.wb[l][:, wp["inV2"]:wp["inV2"] + KC * 128], wv2b)
            K.sp.dma(wst[:], self.wb[l][:, wp["wsT"]:wp["wsT"] + BC * 128], wstb)
            for (t0, w, is_ctx) in cfg.tiles:
                which = 0 if is_ctx else 1
                K.pool.dma(xt[:, :, :w], xsrc[:, :, t0:t0 + w].rearrange("c p t -> p c t"), xtb)
                K.pool.dma(cs[:, :w], self.ropec[:, t0:t0 + w], csb)
                K.pool.dma(sn[:, :w], self.ropes[:, t0:t0 + w], snb)
                self.rms_to_h(tl, xt, xtb, w, 1, 0, "n1g", l, which, ht, htb)
                K.pool.dma(self.H[:, :, t0:t0 + w].rearrange("c p t -> p c t"), ht[:, :, :w], htb, load=False)
                nchunks = 6 * BC + 2
                chunk_kind = []
                for i in range(BC):
                    chunk_kind.append(("u", i))
                for i in range(BC):
                    chunk_kind.append(("g", i))
                    chunk_kind.append(("a", i))
                for i in range(BC):
                    chunk_kind.append(("q", i))
                    chunk_kind.append(("qs", i))
                chunk_kind.append(("k", 0))
                chunk_kind.append(("ks", 0))
                for i in range(BC):
                    chunk_kind.append(("d", i))
                CPS = max(1, WSLOT // CW)
                wt = wtb = None
                pending = {}
                for ci, (kind, i) in enumerate(chunk_kind):
                    if ci % CPS == 0:
                        n = min(CPS, nchunks - ci)
                        wt, wtb = self.wload(l, wp["inF"] + ci * CW, n * CW)
                    base = (ci % CPS) * CW
                    ps, psb = self.next_psum()
                    self.mm_group(ps[:, :w], psb,
                                  [wt[:, base + kc * 128: base + (kc + 1) * 128] for kc in range(KC)],
                                  [ht[:, kc, :w] for kc in range(KC)], [wtb, htb])
                    if kind == "u":
                        K.act.op(lambda e, i=i, ps=ps: e.activation(out=ut[:, i, :w], in_=ps[:, :w], func=AF.Gelu_apprx_tanh),
                                 reads=[psb], writes=[utb])
                    elif kind == "g":
                        K.act.op(lambda e, ps=ps: e.activation(out=sg[:, :w], in_=ps[:, :w], func=AF.Sigmoid),
                                 reads=[psb], writes=[sgb])
                    elif kind == "a":
                        K.dve.op(lambda e, i=i, ps=ps: e.tensor_tensor(out=yt[:, i, :w], in0=ps[:, :w], in1=sg[:, :w], op=ALU.mult),
                                 reads=[psb, sgb], writes=[ytb])
                    elif kind in ("q", "k"):
                        a, ab_ = t1[ci % 2 if False else (ci // 2) % 2]
                        K.dve.op(lambda e, ps=ps, a=a: e.tensor_tensor(out=a[:, :w], in0=ps[:, :w], in1=cs[:, :w], op=ALU.mult),
                                 reads=[psb, csb], writes=[ab_])
                        pending["t1"] = (a, ab_)
                    elif kind in ("qs", "ks"):
                        a, ab_ = pending["t1"]
                        b2, b2b = t2[(ci // 2) % 2]
                        K.dve.op(lambda e, ps=ps, b2=b2: e.tensor_tensor(out=b2[:, :w], in0=ps[:, :w], in1=sn[:, :w], op=ALU.mult),
                                 reads=[psb, snb], writes=[b2b])
                        if kind == "qs":
                            K.pool.op(lambda e, i=i, a=a, b2=b2: e.tensor_tensor(out=qt[:, i, :w], in0=a[:, :w], in1=b2[:, :w], op=ALU.add),
                                      reads=[ab_, b2b], writes=[qtb])
                        else:
                            K.pool.op(lambda e, a=a, b2=b2: e.tensor_tensor(out=kt[:, :w], in0=a[:, :w], in1=b2[:, :w], op=ALU.add),
                                      reads=[ab_, b2b], writes=[ktb])
                    elif kind == "d":
                        K.act.op(lambda e, i=i, ps=ps: e.copy(out=dt_[:, i, :w], in_=ps[:, :w]), reads=[psb], writes=[dtb])
                y0 = self.ypos(t0)
                K.pool.dma(self.Y[:, :, y0:y0 + w].rearrange("c p t -> p c t"), yt[:, :, :w], ytb, load=False)
                K.pool.dma(self.Q[:, :, t0:t0 + w].rearrange("c p t -> p c t"), qt[:, :, :w], qtb, load=False)
                for hk in range(2):
                    for dup in range(2):
                        K.pool.dma(self.KT[hk, dup * 64:(dup + 1) * 64, t0:t0 + w], kt[hk * 64:(hk + 1) * 64, :w], ktb, load=False)
                K.pool.dma(self.U[:, :, t0:t0 + w].rearrange("c p t -> p c t"), dt_[:, :, :w], dtb, load=False)
                for j in range(w // 128):
                    tk = slice(j * 128, (j + 1) * 128)
                    ps, psb = self.next_psum()
                    self.mm_group(ps[:, :DB], psb, [ht[:, kc, tk] for kc in range(KC)],
                                  [wv1[:, kc, :] for kc in range(KC)], [htb, wv1b])
                    K.act.op(lambda e, ps=ps: e.activation(out=vg[:, :], in_=ps[:, :DB], func=AF.Gelu_apprx_tanh),
                             reads=[psb], writes=[vgb])
                    FMAX = 512
                    nst = (DB + FMAX - 1) // FMAX
                    for s_ in range(nst):
                        K.dve.op(lambda e, s_=s_: e.bn_stats(out=stt[:, s_, :], in_=vg[:, s_ * FMAX:min(DB, (s_ + 1) * FMAX)]),
                                 reads=[vgb], writes=[sttb])
                    K.dve.op(lambda e: e.bn_aggr(out=mv[:, 0:2], in_=stt[:, 0:nst, :]), reads=[sttb], writes=[mvb])
                    K.act.op(lambda e: e.activation(out=mv[:, 2:3], in_=mv[:, 1:2], func=AF.Sqrt, bias=self.epsc[:, 0:1]),
                             reads=[mvb, self.epsb], writes=[mvb])
                    K.dve.op(lambda e: e.reciprocal(out=mv[:, 2:3], in_=mv[:, 2:3]), reads=[mvb], writes=[mvb])
                    K.dve.op(lambda e: e.tensor_scalar(out=vn[:, :], in0=vg[:, :], scalar1=mv[:, 0:1], scalar2=mv[:, 2:3],
                                                       op0=ALU.subtract, op1=ALU.mult),
                             reads=[vgb, mvb], writes=[vnb])
                    K.pool.op(lambda e: e.tensor_tensor(out=vn[:, :], in0=vn[:, :], in1=rv[:, o_glg:o_glg + DB], op=ALU.mult),
                              reads=[vnb, rvb], writes=[vnb])
                    K.pool.op(lambda e: e.tensor_tensor(out=vnh[:, :], in0=vn[:, :], in1=rv[:, o_glb:o_glb + DB], op=ALU.add),
                              reads=[vnb, rvb], writes=[vnhb])
                    ps2, ps2b = self.next_psum()
                    for gi in range(BC):
                        K.pe.op(lambda e, gi=gi, ps2=ps2: e.matmul(ps2[:, gi * 128:(gi + 1) * 128], lhsT=vnh[:, gi * 128:(gi + 1) * 128],
                                                                  rhs=wst[:, gi, :], start=True, stop=True),
                                reads=[vnhb, wstb], writes=[ps2b])
                    K.dve.op(lambda e, ps2=ps2: e.tensor_tensor(out=mx[:, :, :], in0=ps2[:, :BC * 128].rearrange("p (g q) -> p g q", g=BC),
                                                               in1=rv[:, o_gbs:o_gbs + BC * 128].rearrange("p (g q) -> p g q", g=BC), op=ALU.add),
                             reads=[ps2b, rvb], writes=[mxb])
                    K.pool.op(lambda e, tk=tk: e.tensor_tensor(out=bra[:, :, tk], in0=mx[:, :, :], in1=ut[:, :, tk], op=ALU.mult),
                              reads=[mxb, utb], writes=[brab])
                    ps3, ps3b = self.next_psum()
                    self.mm_group(ps3[:, :128], ps3b, [ht[:, kc, tk] for kc in range(KC)],
                                  [wv2[:, kc, :] for kc in range(KC)], [htb, wv2b])
                    for dup in range(2):
                        K.act.op(lambda e, ps3=ps3, dup=dup: e.copy(out=vv[:, :, dup, :], in_=ps3[:, :128].rearrange("p (h d) -> p h d", h=2)),
                                 reads=[ps3b], writes=[vvb])
                    K.pool.dma(self.V[t0 + j * 128:t0 + (j + 1) * 128, :, :], vv[:].rearrange("p h u d -> p h (u d)"), vvb, load=False)
                K.pool.dma(self.BR[0, :, :, t0:t0 + w].rearrange("c p t -> p c t"), bra[:, :, :w], brab, load=False)


    def phase2(self, l):
        cfg, K, nc = self.cfg, self.K, self.nc
        KC, BC, DB, T, FFC, CTX, NT = cfg.KC, cfg.BC, cfg.DB, cfg.T, cfg.FFC, cfg.CTX, cfg.NT
        QPK = cfg.QPK
        xsrc = self.xT if l == 0 else self.X
        wp = self.wplan
        pp = self.pplan
        last = (l == cfg.DEPTH - 1)
        NCC = CTX // 128
        HC = min(FFC, KC)
        GW_ = 4 * KC * 128
        BW_ = 4 * BC * 128
        with ExitStack() as st:
            xt, xtb = self.sb(st, "p2_xt", [128, KC, T], F32)
            ht, htb = self.sb(st, "p2_ht", [128, KC, T], BF16)
            big, bigb = self.sb(st, "p2_big", [128, HC, T], BF16)
            tl = {
                "sq": [self.sb(st, f"p2_sq{i}", [128, T], F32) for i in range(2)],
                "hf": [self.sb(st, f"p2_hf{i}", [128, T], F32) for i in range(2)],
                "rstd": self.sb(st, "p2_rstd", [128, T], F32),
                "ab": self.sb(st, "p2_ab", [128, KC, 2], F32),
            }
            self.epsc, self.epsb = self.sb(st, "p2_eps", [128, 1], F32)
            K.dve.op(lambda e: e.memset(self.epsc[:], cfg.EPS), writes=[self.epsb])
            self.wslots = [self.sb(st, f"p2_ws{i}", [128, 8192], BF16) for i in range(3)]
            self.wslot_i = 0
            bws = [self.sb(st, f"p2_bw{i}", [128, BW_], BF16) for i in range(2)]
            brt = [self.sb(st, f"p2_br{k}", [128, BC, T], BF16) for k in range(4)]
            rv, rvb = self.sb(st, "p2_rv", [128, cfg.HQ], F32)
            esk, eskb = self.sb(st, "p2_esk", [128, cfg.HQ], F32)
            ywin, ywinb = self.sb(st, "p2_ywin", [128, BC, T + 32], F32)
            acc, accb = self.sb(st, "p2_acc", [128, BC, T], F32)
            cst = [self.sb(st, f"p2_cst{i}", [128, T], F32) for i in range(3)]
            qt, qtb = self.sb(st, "p2_qt", [128, BC, T], BF16)
            kwin, kwinb = self.sb(st, "p2_kwin", [128, 2, T + 256], BF16)
            vwin, vwinb = self.sb(st, "p2_vwin", [128, (T + 256) // 128, 2, 128], BF16)
            kctx, kctxb = self.sb(st, "p2_kctx", [128, 2, CTX], BF16)
            vctx, vctxb = self.sb(st, "p2_vctx", [128, NCC, 2, 128], BF16)
            mk = [self.sb(st, f"p2_mk{i}", [128, 128], BF16) for i in range(2)]
            mk32, mk32b = self.sb(st, "p2_mk32", [128, 2, 128], F32)
            onesh, oneshb = self.sb(st, "p2_onesh", [128, 128], BF16)
            pts = [self.sb(st, f"p2_pt{i}", [128, QPK * 128], BF16) for i in range(3)]
            rden, rdenb = self.sb(st, "p2_rden", [128, QPK * 128], F32)
            sgs = tl["sq"]
            prods = [self.sb(st, f"p2_pr{i}", [128, T], F32) for i in range(2)] + tl["hf"]
            rl = tl["sq"]
            zt, ztb = self.sb(st, "p2_zero", [128, 32], F32)
            o_sk, _ = self.rplan["sink"]
            K.pool.dma(rv[:], self.rowv[:, l, o_sk:o_sk + cfg.HQ], rvb)
            K.act.op(lambda e: e.activation(out=esk[:], in_=rv[:, :], func=AF.Exp), reads=[rvb], writes=[eskb])
            K.pool.dma(mk32[:, 0, :], self.mask_prev[:, :], mk32b)
            K.pool.dma(mk32[:, 1, :], self.mask_next[:, :], mk32b)
            for i in range(2):
                K.dve.op(lambda e, i=i: e.tensor_copy(out=mk[i][0][:], in_=mk32[:, i, :]), reads=[mk32b], writes=[mk[i][1]])
            K.dve.op(lambda e: e.memset(onesh[:], 1.0), writes=[oneshb])
            K.dve.op(lambda e: e.memset(zt[:], 0.0), writes=[ztb])
            P_ = cfg.YPAD
            for c0 in (0, P_ + CTX, 2 * P_ + CTX, 3 * P_ + NT):
                for c in range(BC):
                    K.pool.dma(self.Y[c, :, c0:c0 + P_], zt[:, 0:P_], ztb, load=False)
            K.pool.dma(kctx[:], self.KT[:, :, 0:CTX].rearrange("h p t -> p h t"), kctxb)
            K.pool.dma(vctx[:], self.V[0:CTX, :, :].rearrange("(c p) h d -> p c h d", p=128), vctxb)
            K.barrier()
            o_cw, _ = pp["conv_w"]
            o_cb, _ = pp["conv_b"]
            o_lg, _ = pp["cln_g"]
            o_lb, _ = pp["cln_b"]
            o_bg, _ = pp["b_gate"]
            for (t0, w, is_ctx) in cfg.tiles:
                if is_ctx and last:
                    continue
                which = 0 if is_ctx else 1
                K.pool.dma(xt[:, :, :w], xsrc[:, :, t0:t0 + w].rearrange("c p t -> p c t"), xtb)
                K.pool.dma(ht[:, :, :w], self.H[:, :, t0:t0 + w].rearrange("c p t -> p c t"), htb)
                K.pool.dma(brt[0][0][:, :, :w], self.BR[0, :, :, t0:t0 + w].rearrange("c p t -> p c t"), brt[0][1])
                K.pool.dma(brt[3][0][:, :, :w], self.BR[3, :, :, t0:t0 + w].rearrange("c p t -> p c t"), brt[3][1])
                y0 = self.ypos(t0)
                K.pool.dma(ywin[:, :, :w + 30], self.Y[:, :, y0 - 15:y0 + w + 15].rearrange("c p t -> p c t"), ywinb)
                K.pool.dma(qt[:, :, :w], self.Q[:, :, t0:t0 + w].rearrange("c p t -> p c t"), qtb)
                if not is_ctx:
                    k_lo = max(CTX, t0 - 128)
                    k_hi = min(NT, t0 + w + 128)
                    K.pool.dma(kwin[:, :, :k_hi - k_lo], self.KT[:, :, k_lo:k_hi].rearrange("h p t -> p h t"), kwinb)
                    K.pool.dma(vwin[:, :(k_hi - k_lo) // 128, :, :],
                               self.V[k_lo:k_hi, :, :].rearrange("(c p) h d -> p c h d", p=128), vwinb)
                for c in range(BC):
                    eng = K.dve
                    eng.op(lambda e, c=c: e.tensor_scalar(out=acc[:, c, :w], in0=ywin[:, c, 0:w],
                                                          scalar1=self.pv[:, l, o_cw + c * 31:o_cw + c * 31 + 1],
                                                          scalar2=self.pv[:, l, o_cb + c:o_cb + c + 1], op0=ALU.mult, op1=ALU.add),
                           reads=[ywinb, self.pvb], writes=[accb])
                    for j in range(1, 31):
                        eng.op(lambda e, c=c, j=j: e.scalar_tensor_tensor(
                            out=acc[:, c, :w], in0=ywin[:, c, j:j + w], scalar=self.pv[:, l, o_cw + c * 31 + j:o_cw + c * 31 + j + 1],
                            in1=acc[:, c, :w], op0=ALU.mult, op1=ALU.add), reads=[ywinb, self.pvb, accb], writes=[accb])
                ps1, ps1b = self.next_psum()
                ps2, ps2b = self.next_psum()
                for c in range(BC):
                    s_, sb_ = tl["sq"][c % 2]
                    K.pe.op(lambda e, c=c: e.matmul(ps1[:, :w], lhsT=self.ones[:], rhs=acc[:, c, :w], start=(c == 0), stop=(c == BC - 1)),
                            reads=[self.onesb, accb], writes=[ps1b])
                    K.act.op(lambda e, c=c, s_=s_: e.activation(out=s_[:, :w], in_=acc[:, c, :w], func=AF.Square), reads=[accb], writes=[sb_])
                    K.pe.op(lambda e, c=c, s_=s_: e.matmul(ps2[:, :w], lhsT=self.ones[:], rhs=s_[:, :w], start=(c == 0), stop=(c == BC - 1)),
                            reads=[self.onesb, sb_], writes=[ps2b])
                mean, meanb = cst[0]
                msq, msqb = cst[1]
                var, varb = cst[2]
                K.act.op(lambda e: e.mul(out=mean[:, :w], in_=ps1[:, :w], mul=1.0 / DB), reads=[ps1b], writes=[meanb])
                K.dve.op(lambda e: e.tensor_tensor(out=msq[:, :w], in0=mean[:, :w], in1=mean[:, :w], op=ALU.mult), reads=[meanb], writes=[msqb])
                K.dve.op(lambda e: e.scalar_tensor_tensor(out=var[:, :w], in0=ps2[:, :w], scalar=1.0 / DB, in1=msq[:, :w],
                                                          op0=ALU.mult, op1=ALU.subtract), reads=[ps2b, msqb], writes=[varb])
                K.act.op(lambda e: e.activation(out=var[:, :w], in_=var[:, :w], func=AF.Sqrt, bias=self.epsc[:, 0:1]),
                         reads=[varb, self.epsb], writes=[varb])
                K.dve.op(lambda e: e.reciprocal(out=var[:, :w], in_=var[:, :w]), reads=[varb], writes=[varb])
                for c in range(BC):
                    eng = K.dve if c % 2 == 0 else K.pool
                    eng.op(lambda e, c=c: e.tensor_tensor(out=acc[:, c, :w], in0=acc[:, c, :w], in1=mean[:, :w], op=ALU.subtract),
                           reads=[accb, meanb], writes=[accb])
                    eng.op(lambda e, c=c: e.tensor_tensor(out=acc[:, c, :w], in0=acc[:, c, :w], in1=var[:, :w], op=ALU.mult),
                           reads=[accb, varb], writes=[accb])
                    K.act.op(lambda e, c=c: e.activation(out=brt[1][0][:, c, :w], in_=acc[:, c, :w], func=AF.Silu,
                                                         scale=self.pv[:, l, o_lg + c:o_lg + c + 1], bias=self.pv[:, l, o_lb + c:o_lb + c + 1]),
                             reads=[accb, self.pvb], writes=[brt[1][1]])
                brc, brcb = brt[2]
                for j in range(w // 128):
                    tq0 = t0 + j * 128
                    qs = slice(j * 128, (j + 1) * 128)
                    chunks = []
                    if not is_ctx:
                        for rel_, mi in ((-128, 0), (0, None), (128, 1)):
                            ks = tq0 + rel_
                            if ks < CTX or ks >= NT:
                                continue
                            o = ks - k_lo
                            chunks.append((lambda hk, par, o=o: kwin[par * 64:(par + 1) * 64, hk, o:o + 128],
                                           lambda hk, o=o: vwin[:, o // 128, hk, :], mi, [kwinb], [vwinb]))
                    for cc in range(NCC):
                        chunks.append((lambda hk, par, cc=cc: kctx[par * 64:(par + 1) * 64, hk, cc * 128:(cc + 1) * 128],
                                       lambda hk, cc=cc: vctx[:, cc, hk, :], None, [kctxb], [vctxb]))
                    for hk in range(2):
                        pso, psob = self.ps[6], self.psb[6]
                        psd, psdb = self.ps[7], self.psb[7]
                        for ci, (kf, vf, mi, krd, vrd) in enumerate(chunks):
                            pt, ptb = pts[ci % 3]
                            for par in range(2):
                                heads = [i for i in range(QPK) if (hk * QPK + i) % 2 == par]
                                if not heads:
                                    continue
                                pss, pssb = self.next_psum()
                                for i in heads:
                                    ch = (hk * QPK + i) // 2
                                    K.pe.op(lambda e, i=i, par=par, ch=ch, kf=kf, pss=pss: e.matmul(
                                        pss[:, i * 128:(i + 1) * 128], lhsT=kf(hk, par), rhs=qt[par * 64:(par + 1) * 64, ch, qs],
                                        start=True, stop=True), reads=krd + [qtb], writes=[pssb])
                                for i in heads:
                                    K.act.op(lambda e, pss=pss, pt=pt, i=i: e.activation(out=pt[:, i * 128:(i + 1) * 128], in_=pss[:, i * 128:(i + 1) * 128],
                                                                                    func=AF.Exp, scale=0.125), reads=[pssb], writes=[ptb])
                            if mi is not None:
                                K.pool.op(lambda e, pt=pt, mi=mi: e.tensor_tensor(
                                    out=pt[:, :].rearrange("p (h q) -> p h q", h=QPK), in0=pt[:, :].rearrange("p (h q) -> p h q", h=QPK),
                                    in1=mk[mi][0][:, :].unsqueeze(1).to_broadcast([128, QPK, 128]), op=ALU.mult),
                                    reads=[ptb, mk[mi][1]], writes=[ptb])
                            first, lastc = (ci == 0), (ci == len(chunks) - 1)
                            K.pe.op(lambda e, vf=vf, pt=pt, first=first, lastc=lastc: e.matmul(
                                pso[:, :QPK * 128], lhsT=vf(hk), rhs=pt[:, :], start=first, stop=lastc), reads=vrd + [ptb], writes=[psob])
                            K.pe.op(lambda e, pt=pt, first=first, lastc=lastc: e.matmul(
                                psd[:, :QPK * 128], lhsT=onesh[:, :], rhs=pt[:, :], start=first, stop=lastc), reads=[oneshb, ptb], writes=[psdb])
                        K.dve.op(lambda e: e.tensor_tensor(
                            out=rden[:, :].rearrange("p (h q) -> p h q", h=QPK), in0=psd[:, :QPK * 128].rearrange("p (h q) -> p h q", h=QPK),
                            in1=esk[:, hk * QPK:(hk + 1) * QPK].unsqueeze(2).to_broadcast([128, QPK, 128]), op=ALU.add),
                            reads=[psdb, eskb], writes=[rdenb])
                        K.dve.op(lambda e: e.reciprocal(out=rden[:, :], in_=rden[:, :]), reads=[rdenb], writes=[rdenb])
                        for i in range(QPK):
                            hq = hk * QPK + i
                            par, ch = hq % 2, hq // 2
                            pr = slice(par * 64, (par + 1) * 64)
                            K.dve.op(lambda e, i=i, pr=pr, ch=ch: e.tensor_tensor(
                                out=brc[pr, ch, qs], in0=pso[pr, i * 128:(i + 1) * 128], in1=rden[pr, i * 128:(i + 1) * 128], op=ALU.mult),
                                reads=[psob, rdenb], writes=[brcb])
                for oc in range(KC):
                    gw, gwb = self.wload(l, wp["gb"] + oc * (GW_ + BW_), GW_)
                    bw, bwb = bws[oc % 2]
                    K.sp.dma(bw[:, :], self.wb[l][:, wp["gb"] + oc * (GW_ + BW_) + GW_: wp["gb"] + (oc + 1) * (GW_ + BW_)], bwb)
                    for k in range(4):
                        psg, psgb = self.next_psum()
                        self.mm_group(psg[:, :w], psgb, [gw[:, (k * KC + kc) * 128:(k * KC + kc + 1) * 128] for kc in range(KC)],
                                      [ht[:, kc, :w] for kc in range(KC)], [gwb, htb])
                        psb_, psbb = self.next_psum()
                        self.mm_group(psb_[:, :w], psbb, [bw[:, (k * BC + bc) * 128:(k * BC + bc + 1) * 128] for bc in range(BC)],
                                      [brt[k][0][:, bc, :w] for bc in range(BC)], [bwb, brt[k][1]])
                        sg, sgb = sgs[k % 2]
                        K.act.op(lambda e, psg=psg, sg=sg, k=k: e.activation(out=sg[:, :w], in_=psg[:, :w], func=AF.Sigmoid,
                                                                        bias=self.pv[:, l, o_bg + k * KC + oc:o_bg + k * KC + oc + 1]),
                                 reads=[psgb, self.pvb], writes=[sgb])
                        pr_, prb = prods[k]
                        K.dve.op(lambda e, psb_=psb_, sg=sg, pr_=pr_: e.tensor_tensor(out=pr_[:, :w], in0=psb_[:, :w], in1=sg[:, :w], op=ALU.mult),
                                 reads=[psbb, sgb], writes=[prb])
                    K.pool.op(lambda e: e.tensor_tensor(out=prods[0][0][:, :w], in0=prods[0][0][:, :w], in1=prods[1][0][:, :w], op=ALU.add),
                              reads=[prods[0][1], prods[1][1]], writes=[prods[0][1]])
                    K.pool.op(lambda e: e.tensor_tensor(out=prods[2][0][:, :w], in0=prods[2][0][:, :w], in1=prods[3][0][:, :w], op=ALU.add),
                              reads=[prods[2][1], prods[3][1]], writes=[prods[2][1]])
                    K.pool.op(lambda e, oc=oc: e.tensor_tensor(out=big[:, oc, :w], in0=prods[0][0][:, :w], in1=prods[2][0][:, :w], op=ALU.add),
                              reads=[prods[0][1], prods[2][1]], writes=[bigb])
                CW = KC * 128
                CPS = max(1, 8192 // CW)
                for oc in range(KC):
                    if oc % CPS == 0:
                        n = min(CPS, KC - oc)
                        wt, wtb = self.wload(l, wp["out"] + oc * CW, n * CW)
                    base = (oc % CPS) * CW
                    ps, psb = self.next_psum()
                    self.mm_group(ps[:, :w], psb, [wt[:, base + kc * 128:base + (kc + 1) * 128] for kc in range(KC)],
                                  [big[:, kc, :w] for kc in range(KC)], [wtb, bigb])
                    K.dve.op(lambda e, oc=oc, ps=ps: e.scalar_tensor_tensor(
                        out=xt[:, oc, :w], in0=ps[:, :w], scalar=self.modv[:, l, 2 * KC + oc, which:which + 1], in1=xt[:, oc, :w],
                        op0=ALU.mult, op1=ALU.add), reads=[psb, self.modb, xtb], writes=[xtb])
                self.rms_to_h(tl, xt, xtb, w, 4, 3, "n2g", l, which, ht, htb)
                for hp in range(FFC // HC):
                    for fcl in range(HC):
                        fc = hp * HC + fcl
                        if fcl % CPS == 0:
                            n = min(CPS, HC - fcl)
                            wt, wtb = self.wload(l, wp["ff1"] + fc * CW, n * CW)
                        base = (fcl % CPS) * CW
                        ps, psb = self.next_psum()
                        self.mm_group(ps[:, :w], psb, [wt[:, base + kc * 128:base + (kc + 1) * 128] for kc in range(KC)],
                                      [ht[:, kc, :w] for kc in range(KC)], [wtb, htb])
                        r_, rb = rl[fc % 2]
                        K.act.op(lambda e, ps=ps, r_=r_: e.activation(out=r_[:, :w], in_=ps[:, :w], func=AF.Relu), reads=[psb], writes=[rb])
                        eng = K.pool if fc % 2 == 0 else K.dve
                        eng.op(lambda e, r_=r_, fcl=fcl: e.tensor_tensor(out=big[:, fcl, :w], in0=r_[:, :w], in1=r_[:, :w], op=ALU.mult),
                               reads=[rb], writes=[bigb])
                    FW = FFC * 128
                    for oc in range(KC):
                        ps, psb = self.next_psum()
                        nload = HC * 128
                        for s0 in range(0, nload, 8192):
                            n = min(8192, nload - s0)
                            wt, wtb = self.wload(l, wp["ff2"] + oc * FW + hp * HC * 128 + s0, n)
                            nk = n // 128
                            for kk in range(nk):
                                fcl = s0 // 128 + kk
                                K.pe.op(lambda e, wt=wt, kk=kk, fcl=fcl, ps=ps: e.matmul(
                                    ps[:, :w], lhsT=wt[:, kk * 128:(kk + 1) * 128], rhs=big[:, fcl, :w],
                                    start=(fcl == 0), stop=(fcl == HC - 1)), reads=[wtb, bigb], writes=[psb])
                        K.dve.op(lambda e, oc=oc, ps=ps: e.scalar_tensor_tensor(
                            out=xt[:, oc, :w], in0=ps[:, :w], scalar=self.modv[:, l, 5 * KC + oc, which:which + 1], in1=xt[:, oc, :w],
                            op0=ALU.mult, op1=ALU.add), reads=[psb, self.modb, xtb], writes=[xtb])
                K.pool.dma(self.X[:, :, t0:t0 + w].rearrange("c p t -> p c t"), xt[:, :, :w], xtb, load=False)

    def phase_s(self, l):
        cfg, K, nc = self.cfg, self.K, self.nc
        BC, NP, NT, CTX = cfg.BC, cfg.NPAIR, cfg.NT, cfg.CTX
        TS = 256
        NQ = NP // 4
        PI = math.pi
        wp, pp = self.wplan, self.pplan
        LOG = int(math.log2(TS))
        with ExitStack() as st:
            W, Wb = self.sb(st, "s_W", [128, 24, 2, NP], F32)
            Bt, Btb = self.sb(st, "s_Bt", [128, 2, 2, 2, NQ, 128], BF16)
            Ct, Ctb = self.sb(st, "s_Ct", [128, 2, NP, 2, 64], BF16)
            R, Rb = self.sb(st, "s_R", [128, 2 * NP, 2, TS], F32)
            carry, carryb_ = self.sb(st, "s_carry", [128, 2, NP, 2], F32)
            st2 = ExitStack()
            At, Atb = self.sb(st2, "s_A", [128, 2, NP, 3], F32)
            Bt32, Bt32b = self.sb(st2, "s_B32", [128, 2, NP, 32], F32)
            Ct32, Ct32b = self.sb(st2, "s_C32", [128, 2, NP, 2, 64], F32)
            Wi, Wib = self.sb(st2, "s_Wi", [128, 2, NP], mybir.dt.int32)
            bbar, bbarb = self.sb(st2, "s_bbar", [128, 2, 2, NP, 32], F32)
            tb, tbb = self.sb(st2, "s_tb", [128, 2, NP, 32], F32)
            bbv, bbvb = self.sb(st2, "s_bbv", [128, 2, 2, 2, NP, 32], F32)
            E, Eb = self.sb(st2, "s_E", [128, 4, 2 * NP], F32)
            rt1, rt1b = self.sb(st2, "s_rt1", [128, 2 * NP, TS // 2], F32)
            rt2, rt2b = self.sb(st2, "s_rt2", [128, 2 * NP, TS // 2], F32)
            carryb = [[Buf(f"carry{d}_{k}") for k in range(NP)] for d in range(2)]
            K.pool.dma(At[:], self.s5A[:, l], Atb)
            K.pool.dma(Bt32[:], self.s5B[:, l], Bt32b)
            K.pool.dma(Ct32[:], self.s5C[:, l], Ct32b)
            a_re, a_im, ls = At[:, :, :, 0], At[:, :, :, 1], At[:, :, :, 2]
            (DT, XR, XI, MAG, T0, TF, RS, RC, SIN, COS, LR, LI, NR, DEN, CR, CI, TA, TB_, XC) = [W[:, i] for i in range(19)]

            def dv(fn, rd=(Atb, Wb), wr=(Wb,)):
                K.dve.op(fn, reads=list(rd), writes=list(wr))

            def ac(fn, rd=(Atb, Wb), wr=(Wb,)):
                K.act.op(fn, reads=list(rd), writes=list(wr))
            ac(lambda e: e.activation(out=DT, in_=ls, func=AF.Exp))
            dv(lambda e: e.tensor_tensor(out=XR, in0=a_re, in1=DT, op=ALU.mult))
            dv(lambda e: e.tensor_tensor(out=XI, in0=a_im, in1=DT, op=ALU.mult))
            ac(lambda e: e.activation(out=MAG, in_=XR, func=AF.Exp))

            def reduce_sin(dst, x_ap):
                dv(lambda e: e.tensor_scalar(out=T0, in0=x_ap, scalar1=1.0 / (2 * PI), scalar2=None, op0=ALU.mult))
                dv(lambda e: e.tensor_copy(out=Wi[:], in_=T0), wr=(Wib,))
                dv(lambda e: e.tensor_copy(out=TF, in_=Wi[:]), rd=(Wib,))
                dv(lambda e: e.scalar_tensor_tensor(out=RS, in0=TF, scalar=-2 * PI, in1=x_ap, op0=ALU.mult, op1=ALU.add))
                dv(lambda e: e.tensor_scalar(out=RS, in0=RS, scalar1=-PI, scalar2=PI, op0=ALU.max, op1=ALU.min))
                ac(lambda e: e.activation(out=dst, in_=RS, func=AF.Sin))
            reduce_sin(SIN, XI)
            dv(lambda e: e.tensor_scalar(out=XC, in0=XI, scalar1=PI / 2, scalar2=None, op0=ALU.add))
            reduce_sin(COS, XC)
            dv(lambda e: e.tensor_tensor(out=LR, in0=MAG, in1=COS, op=ALU.mult))
            dv(lambda e: e.tensor_tensor(out=LI, in0=MAG, in1=SIN, op=ALU.mult))
            dv(lambda e: e.tensor_scalar(out=NR, in0=LR, scalar1=-1.0, scalar2=None, op0=ALU.add))
            dv(lambda e: e.tensor_tensor(out=DEN, in0=a_re, in1=a_re, op=ALU.mult))
            dv(lambda e: e.tensor_tensor(out=TA, in0=a_im, in1=a_im, op=ALU.mult))
            dv(lambda e: e.tensor_tensor(out=DEN, in0=DEN, in1=TA, op=ALU.add))
            dv(lambda e: e.reciprocal(out=DEN, in_=DEN))
            dv(lambda e: e.tensor_tensor(out=TA, in0=NR, in1=a_re, op=ALU.mult))
            dv(lambda e: e.tensor_tensor(out=TB_, in0=LI, in1=a_im, op=ALU.mult))
            dv(lambda e: e.tensor_tensor(out=CR, in0=TA, in1=TB_, op=ALU.add))
            dv(lambda e: e.tensor_tensor(out=CR, in0=CR, in1=DEN, op=ALU.mult))
            dv(lambda e: e.tensor_tensor(out=TA, in0=LI, in1=a_re, op=ALU.mult))
            dv(lambda e: e.tensor_tensor(out=TB_, in0=NR, in1=a_im, op=ALU.mult))
            dv(lambda e: e.tensor_tensor(out=CI, in0=TA, in1=TB_, op=ALU.subtract))
            dv(lambda e: e.tensor_tensor(out=CI, in0=CI, in1=DEN, op=ALU.mult))
            b_re, b_im = Bt32[:, 0], Bt32[:, 1]
            for d in range(2):
                crb = CR[:, d, :].unsqueeze(2).to_broadcast([128, NP, 32])
                cib = CI[:, d, :].unsqueeze(2).to_broadcast([128, NP, 32])
                rd = (Bt32b, Wb, bbarb, tbb)
                dv(lambda e, d=d, crb=crb: e.tensor_tensor(out=bbar[:, d, 0], in0=b_re, in1=crb, op=ALU.mult), rd=rd, wr=(bbarb,))
                dv(lambda e, cib=cib: e.tensor_tensor(out=tb[:, 0], in0=b_im, in1=cib, op=ALU.mult), rd=rd, wr=(tbb,))
                dv(lambda e, d=d: e.tensor_tensor(out=bbar[:, d, 0], in0=bbar[:, d, 0], in1=tb[:, 0], op=ALU.subtract), rd=rd, wr=(bbarb,))
                dv(lambda e, d=d, crb=crb: e.tensor_tensor(out=bbar[:, d, 1], in0=b_im, in1=crb, op=ALU.mult), rd=rd, wr=(bbarb,))
                dv(lambda e, cib=cib: e.tensor_tensor(out=tb[:, 1], in0=b_re, in1=cib, op=ALU.mult), rd=rd, wr=(tbb,))
                dv(lambda e, d=d: e.tensor_tensor(out=bbar[:, d, 1], in0=bbar[:, d, 1], in1=tb[:, 1], op=ALU.add), rd=rd, wr=(bbarb,))
            K.dve.op(lambda e: e.memset(bbv[:], 0.0), writes=[bbvb])
            for v in range(2):
                K.dve.op(lambda e, v=v: e.tensor_copy(out=bbv[:, v, :, :, v::2, :], in_=bbar[:, :, :, v::2, :]), reads=[bbarb, bbvb], writes=[bbvb])
            for v in range(2):
              for d in range(2):
                for ri in range(2):
                    for q in range(NQ):
                        ps, psb = self.next_psum()
                        K.pe.op(lambda e, v=v, d=d, ri=ri, q=q, ps=ps: e.transpose(
                            out=ps[:, :128], in_=bbv[:, v, d, ri, q * 4:(q + 1) * 4, :].rearrange("p a b -> p (a b)"), identity=self.identt[:]),
                            reads=[bbvb, self.identb], writes=[psb])
                        K.act.op(lambda e, v=v, d=d, ri=ri, q=q, ps=ps: e.copy(out=Bt[:, v, d, ri, q, :], in_=ps[:, :128]), reads=[psb], writes=[Btb])
            K.act.op(lambda e: e.copy(out=Ct[:, :, :, 0, :], in_=Ct32[:, :, :, 0, :]), reads=[Ct32b], writes=[Ctb])
            K.act.op(lambda e: e.mul(out=Ct[:, :, :, 1, :], in_=Ct32[:, :, :, 1, :], mul=-1.0), reads=[Ct32b], writes=[Ctb])
            N2 = 2 * NP
            cosf = COS.rearrange("p d k -> p (d k)")
            sinf = SIN.rearrange("p d k -> p (d k)")
            dv(lambda e: e.tensor_copy(out=E[:, 0, :], in_=cosf), wr=(Eb,))
            dv(lambda e: e.tensor_copy(out=E[:, 1, :], in_=sinf), wr=(Eb,))
            dv(lambda e: e.tensor_copy(out=R[:, :, 0, 0], in_=cosf), wr=(Rb,))
            dv(lambda e: e.tensor_copy(out=R[:, :, 1, 0], in_=sinf), wr=(Rb,))
            rdR = (Rb, Eb, rt1b, rt2b)
            for kk in range(LOG):
                n = 1 << kk
                er = E[:, 0, :].unsqueeze(2).to_broadcast([128, N2, n])
                ei = E[:, 1, :].unsqueeze(2).to_broadcast([128, N2, n])
                sre, sim_ = R[:, :, 0, 0:n], R[:, :, 1, 0:n]
                dre, dim_ = R[:, :, 0, n:2 * n], R[:, :, 1, n:2 * n]
                a1, a2 = rt1[:, :, 0:n], rt2[:, :, 0:n]
                dv(lambda e, a1=a1, sre=sre, er=er: e.tensor_tensor(out=a1, in0=sre, in1=er, op=ALU.mult), rd=rdR, wr=(rt1b,))
                dv(lambda e, a2=a2, sim_=sim_, ei=ei: e.tensor_tensor(out=a2, in0=sim_, in1=ei, op=ALU.mult), rd=rdR, wr=(rt2b,))
                dv(lambda e, a1=a1, a2=a2, dre=dre: e.tensor_tensor(out=dre, in0=a1, in1=a2, op=ALU.subtract), rd=rdR, wr=(Rb,))
                dv(lambda e, a1=a1, sre=sre, ei=ei: e.tensor_tensor(out=a1, in0=sre, in1=ei, op=ALU.mult), rd=rdR, wr=(rt1b,))
                dv(lambda e, a2=a2, sim_=sim_, er=er: e.tensor_tensor(out=a2, in0=sim_, in1=er, op=ALU.mult), rd=rdR, wr=(rt2b,))
                dv(lambda e, a1=a1, a2=a2, dim_=dim_: e.tensor_tensor(out=dim_, in0=a1, in1=a2, op=ALU.add), rd=rdR, wr=(Rb,))
                dv(lambda e: e.tensor_tensor(out=E[:, 2, :], in0=E[:, 0, :], in1=E[:, 0, :], op=ALU.mult), rd=(Eb,), wr=(Eb,))
                dv(lambda e: e.tensor_tensor(out=E[:, 3, :], in0=E[:, 1, :], in1=E[:, 1, :], op=ALU.mult), rd=(Eb,), wr=(Eb,))
                dv(lambda e: e.tensor_tensor(out=E[:, 1, :], in0=E[:, 0, :], in1=E[:, 1, :], op=ALU.mult), rd=(Eb,), wr=(Eb,))
                dv(lambda e: e.tensor_scalar(out=E[:, 1, :], in0=E[:, 1, :], scalar1=2.0, scalar2=None, op0=ALU.mult), rd=(Eb,), wr=(Eb,))
                dv(lambda e: e.tensor_tensor(out=E[:, 0, :], in0=E[:, 2, :], in1=E[:, 3, :], op=ALU.subtract), rd=(Eb,), wr=(Eb,))
            K.dve.op(lambda e: e.memset(carry[:], 0.0), writes=[b for row in carryb for b in row])
            K.barrier()
            st2.close()
            wglu, wglub = self.sb(st, "s_wglu", [128, 2 * BC, BC, 128], BF16)
            ubufs = [(self.sb(st, f"s_u32{i}", [128, BC, TS], F32), self.sb(st, f"s_ubf{i}", [128, BC, TS], BF16),
                      self.sb(st, f"s_yb{i}", [128, BC, TS], F32)) for i in range(2)]
            yt, ytb = self.sb(st, "s_yt", [128, BC, TS], F32)
            yg, ygb = self.sb(st, "s_yg", [128, BC, TS], BF16)
            brd, brdb = self.sb(st, "s_brd", [128, BC, TS], BF16)
            sig, sigb = self.sb(st, "s_sig", [128, TS], F32)
            sets = []
            for i in range(6):
                d = {}
                for nm in ("t1", "t2", "bt", "st", "s32"):
                    d[nm] = self.sb(st, f"s_{nm}{i}", [128, 2, TS], F32)
                d["sbf"] = self.sb(st, f"s_sbf{i}", [128, 2, TS], BF16)
                sets.append(d)
            K.sp.dma(wglu[:], self.wb[l][:, wp["glu"]:wp["glu"] + 2 * BC * BC * 128], wglub)
            ctx_subs = [(t, TS) for t in range(0, CTX, TS)]
            lat_subs = [(t, TS) for t in range(CTX, NT, TS)]
            o_d, _ = pp["s5_d"]
            NSET = len(sets)
            ybd = {t: Buf(f"ybd{t}") for (t, _) in ctx_subs + lat_subs}
            items = []
            subs_all = []
            for dirn in (1, 0):
                order = (ctx_subs + lat_subs) if dirn == 0 else (ctx_subs[::-1] + lat_subs[::-1])
                for (t0, w) in order:
                    si = len(subs_all)
                    subs_all.append((dirn, t0))
                    for k in range(NP):
                        items.append((dirn, si, t0, k, k == 0, k == NP - 1))

            def sub_bufs(si):
                return ubufs[si % 2]

            def prologue(si):
                dirn, t0 = subs_all[si]
                (u32, u32b), (ubf, ubfb), (ybt, ybtb) = sub_bufs(si)
                K.pool.dma(u32[:], self.U[:, :, t0:t0 + TS].rearrange("c p t -> p c t"), u32b)
                if dirn == 0:
                    K.pool.dma(ybt[:], self.YB[:, :, t0:t0 + TS].rearrange("c p t -> p c t"), ybtb, reads=[ybd[t0]])
                    K.act.op(lambda e: e.copy(out=ubf[:], in_=u32[:]), reads=[u32b], writes=[ubfb])
                else:
                    K.act.op(lambda e: e.copy(out=ubf[:], in_=u32[:, :, ::-1]), reads=[u32b], writes=[ubfb])

            def psy_of(si):
                b = 4 + 2 * (si % 2)
                return [self.ps[b], self.ps[b + 1]], [self.psb[b], self.psb[b + 1]]

            def stage(sn, idx):
                dirn, si, t0, k, first, lastk = items[idx]
                (u32, u32b), (ubf, ubfb), (ybt, ybtb) = sub_bufs(si)
                q, slot = k // 4, k % 4
                rows = slice(64 * (slot // 2), 64 * (slot // 2) + 64)
                S_ = sets[idx % NSET]
                (t1, t1b), (t2, t2b), (bt, btb), (stt, sttb) = S_["t1"], S_["t2"], S_["bt"], S_["st"]
                (t3, t3b), (t4, t4b), (s32, s32b), (sbf, sbfb) = S_["t1"], S_["t2"], S_["s32"], S_["sbf"]
                rr = R[:, dirn * NP + k, 0, :].unsqueeze(1).to_broadcast([128, 2, TS])
                rim = R[:, dirn * NP + k, 1, :].unsqueeze(1).to_broadcast([128, 2, TS])
                if sn == 0:
                    if first and si == 0:
                        prologue(0)
                    if k == min(NP - 1, 4) and si + 1 < len(subs_all):
                        prologue(si + 1)
                    i = self.psum_i
                    self.psum_i = (i + 1) % 4
                    ps, psb = self.ps[i], self.psb[i]
                    for ri in range(2):
                        K.pe.op(lambda e, ri=ri: e.matmul(
                            ps[:, ri * TS:(ri + 1) * TS], lhsT=Bt[rows, slot % 2, dirn, ri, q, :], rhs=ubf[rows, q, :], start=True, stop=True),
                            reads=[Btb, ubfb], writes=[psb])
                    bu = ps[:, :2 * TS].rearrange("p (r t) -> p r t", r=2)
                    K.dve.op(lambda e: e.tensor_tensor(out=t1[:], in0=bu, in1=rr, op=ALU.mult), reads=[psb, Rb], writes=[t1b])
                    K.dve.op(lambda e: e.tensor_tensor(out=t2[:], in0=bu[:, ::-1, :], in1=rim, op=ALU.mult), reads=[psb, Rb], writes=[t2b])
                elif sn == 1:
                    K.pool.op(lambda e: e.tensor_tensor(out=bt[:, 0, :], in0=t1[:, 0, :], in1=t2[:, 0, :], op=ALU.add), reads=[t1b, t2b], writes=[btb])
                    K.pool.op(lambda e: e.tensor_tensor(out=bt[:, 1, :], in0=t1[:, 1, :], in1=t2[:, 1, :], op=ALU.subtract), reads=[t1b, t2b], writes=[btb])
                elif sn == 2:
                    dec = W[:, 3, dirn, k:k + 1].to_broadcast([128, TS])
                    for ri in range(2):
                        K.dve.op(lambda e, ri=ri: e.tensor_tensor_scan(
                            out=stt[:, ri, :], data0=dec, data1=bt[:, ri, :], initial=carry[:, dirn, k, ri:ri + 1], op0=ALU.mult, op1=ALU.add),
                            reads=[btb, Wb, carryb[dirn][k]], writes=[sttb])
                elif sn == 3:
                    eD1 = K.dve if getattr(self, "s5_demod_dve", True) else K.pool
                    eD2 = K.dve if getattr(self, "s5_comb_dve", True) else K.pool
                    eD1.op(lambda e: e.tensor_tensor(out=t3[:], in0=stt[:], in1=rr, op=ALU.mult), reads=[sttb, Rb], writes=[t3b])
                    eD1.op(lambda e: e.tensor_tensor(out=t4[:], in0=stt[:, ::-1, :], in1=rim, op=ALU.mult), reads=[sttb, Rb], writes=[t4b])
                    eD2.op(lambda e: e.tensor_tensor(out=s32[:, 0, :], in0=t3[:, 0, :], in1=t4[:, 0, :], op=ALU.subtract), reads=[t3b, t4b], writes=[s32b])
                    eD2.op(lambda e: e.tensor_tensor(out=s32[:, 1, :], in0=t3[:, 1, :], in1=t4[:, 1, :], op=ALU.add), reads=[t3b, t4b], writes=[s32b])
                elif sn == 4:
                    K.act.op(lambda e: e.copy(out=carry[:, dirn, k, :], in_=s32[:, :, TS - 1]), reads=[s32b], writes=[carryb[dirn][k]])
                    if dirn == 0:
                        K.act.op(lambda e: e.copy(out=sbf[:], in_=s32[:]), reads=[s32b], writes=[sbfb])
                    else:
                        K.act.op(lambda e: e.copy(out=sbf[:], in_=s32[:, :, ::-1]), reads=[s32b], writes=[sbfb])
                    psy, psyb = psy_of(si)
                    py, pyb = psy[q // 2], psyb[q // 2]
                    c0 = (q % 2) * TS
                    for ri in range(2):
                        K.pe.op(lambda e, ri=ri: e.matmul(
                            py[rows, c0:c0 + TS], lhsT=Ct[:, dirn, k, ri, :], rhs=sbf[:, ri, :],
                            start=(ri == 0 and slot % 2 == 0), stop=(ri == 1 and slot % 2 == 1)), reads=[Ctb, sbfb], writes=[pyb])
                    if lastk:
                        epilogue(si)

            def epilogue(si):
                dirn, t0 = subs_all[si]
                (u32, u32b), (ubf, ubfb), (ybt, ybtb) = sub_bufs(si)
                psy, psyb = psy_of(si)
                if dirn == 1:
                    for q in range(BC):
                        K.act.op(lambda e, q=q: e.copy(out=ybt[:, q, :], in_=psy[q // 2][:, (q % 2) * TS:(q % 2 + 1) * TS]),
                                 reads=[psyb[q // 2]], writes=[ybtb])
                    K.pool.dma(self.YB[:, :, t0:t0 + TS].rearrange("c p t -> p c t"), ybt[:], ybtb, load=False, writes=[ybd[t0]])
                else:
                    for q in range(BC):
                        K.dve.op(lambda e, q=q: e.tensor_tensor(out=yt[:, q, :], in0=psy[q // 2][:, (q % 2) * TS:(q % 2 + 1) * TS],
                                                                in1=ybt[:, q, :], op=ALU.add), reads=[psyb[q // 2], ybtb], writes=[ytb])
                        K.dve.op(lambda e, q=q: e.scalar_tensor_tensor(out=yt[:, q, :], in0=u32[:, q, :], scalar=self.pv[:, l, o_d + q:o_d + q + 1],
                                                                       in1=yt[:, q, :], op0=ALU.mult, op1=ALU.add),
                                 reads=[u32b, self.pvb, ytb], writes=[ytb])
                        K.act.op(lambda e, q=q: e.activation(out=yg[:, q, :], in_=yt[:, q, :], func=AF.Gelu_apprx_tanh), reads=[ytb], writes=[ygb])
                    for oc in range(BC):
                        i = self.psum_i
                        self.psum_i = (i + 1) % 4
                        psg, psgb = self.ps[i], self.psb[i]
                        self.mm_group(psg[:, :TS], psgb, [wglu[:, BC + oc, bc, :] for bc in range(BC)], [yg[:, bc, :] for bc in range(BC)], [wglub, ygb])
                        K.act.op(lambda e, psg=psg: e.activation(out=sig[:, :], in_=psg[:, :TS], func=AF.Sigmoid), reads=[psgb], writes=[sigb])
                        i = self.psum_i
                        self.psum_i = (i + 1) % 4
                        psa, psab = self.ps[i], self.psb[i]
                        self.mm_group(psa[:, :TS], psab, [wglu[:, oc, bc, :] for bc in range(BC)], [yg[:, bc, :] for bc in range(BC)], [wglub, ygb])
                        K.dve.op(lambda e, oc=oc, psa=psa: e.tensor_tensor(out=brd[:, oc, :], in0=psa[:, :TS], in1=sig[:, :], op=ALU.mult),
                                 reads=[psab, sigb], writes=[brdb])
                    K.pool.dma(self.BR[3, :, :, t0:t0 + TS].rearrange("c p t -> p c t"), brd[:], brdb, load=False)

            self.psum_i = 0
            NST = 5
            for step in range(len(items) + NST - 1):
                for sn in reversed(range(NST)):
                    idx = step - sn
                    if 0 <= idx < len(items):
                        stage(sn, idx)
            self.psum_i = 0

    def phase_final(self):
        cfg, K, nc = self.cfg, self.K, self.nc
        KC, T, CTX = cfg.KC, cfg.T, cfg.CTX
        L = cfg.DEPTH
        o_g, _ = self.pplan["n1g"]
        with ExitStack() as st:
            xt, xtb = self.sb(st, "f_xt", [128, KC, T], F32)
            ot, otb = self.sb(st, "f_ot", [128, KC, T], F32)
            sq = [self.sb(st, f"f_sq{i}", [128, T], F32) for i in range(2)]
            rs, rsb = self.sb(st, "f_rs", [128, T], F32)
            eps, epsb = self.sb(st, "f_eps", [128, 1], F32)
            K.dve.op(lambda e: e.memset(eps[:], cfg.EPS), writes=[epsb])
            for (t0, w, is_ctx) in cfg.tiles:
                if is_ctx:
                    continue
                K.pool.dma(xt[:, :, :w], self.X[:, :, t0:t0 + w].rearrange("c p t -> p c t"), xtb)
                ps, psb = self.next_psum()
                for kc in range(KC):
                    s_, sb_ = sq[kc % 2]
                    K.act.op(lambda e, kc=kc, s_=s_: e.activation(out=s_[:, :w], in_=xt[:, kc, :w], func=AF.Square), reads=[xtb], writes=[sb_])
                    K.pe.op(lambda e, kc=kc, s_=s_: e.matmul(ps[:, :w], lhsT=self.ones[:], rhs=s_[:, :w], start=(kc == 0), stop=(kc == KC - 1)),
                            reads=[self.onesb, sb_], writes=[psb])
                K.act.op(lambda e: e.activation(out=rs[:, :w], in_=ps[:, :w], func=AF.Sqrt, scale=1.0 / cfg.D, bias=eps[:, 0:1]),
                         reads=[psb, epsb], writes=[rsb])
                K.dve.op(lambda e: e.reciprocal(out=rs[:, :w], in_=rs[:, :w]), reads=[rsb], writes=[rsb])
                for kc in range(KC):
                    eng = K.dve
                    eng.op(lambda e, kc=kc: e.scalar_tensor_tensor(out=ot[:, kc, :w], in0=xt[:, kc, :w], scalar=self.pv[:, L, o_g + kc:o_g + kc + 1],
                                                                  in1=rs[:, :w], op0=ALU.mult, op1=ALU.mult),
                           reads=[xtb, rsb, self.pvb], writes=[otb])
                K.pool.dma(self.out[:, :, t0 - CTX:t0 - CTX + w].rearrange("c p t -> p c t"), ot[:, :, :w], otb, load=False)


_CACHE = {}


def kernel(**inputs):
    cfg = Cfg()
    inp = {k: np.asarray(v) for k, v in inputs.items()}
    if "nc" not in _CACHE:
        _CACHE["nc"] = Prog(cfg).build()
    nc = _CACHE["nc"]
    shared = prep_shared(cfg, inp)
    n = 8
    in_maps = []
    for c in range(n):
        m = dict(shared)
        m.update(prep_inputs(cfg, inp, c))
        in_maps.append(m)
    res = run_bass_kernel_spmd(nc, in_maps, core_ids=list(range(n)))
    out = np.empty((cfg.BATCH, cfg.SEQ, cfg.D), np.float32)
    for b in range(cfg.BATCH):
        o = np.asarray(res.results[b]["out"])
        out[b] = o.reshape(cfg.D, cfg.SEQ).T
    return out
```

```python
import math
import numpy as np
import concourse.bass as bass
import concourse.mybir as mybir
from concourse.bass_utils import run_bass_kernel_spmd

F32 = mybir.dt.float32
BF16 = mybir.dt.bfloat16
AF = mybir.ActivationFunctionType
ALU = mybir.AluOpType
AX = mybir.AxisListType


class Cfg:
    def __init__(self, D=2048, SEQ=8192, CTX=256, DEPTH=4, GRID_W=64, T=512, BATCH=4):
        self.D, self.SEQ, self.CTX, self.DEPTH, self.GRID_W, self.T, self.BATCH = D, SEQ, CTX, DEPTH, GRID_W, T, BATCH
        self.DB = D // 4
        self.KC = D // 128
        self.BC = self.DB // 128
        self.FF = 4 * D
        self.FFC = self.FF // 128
        self.HQ = self.DB // 64
        self.QPK = self.HQ // 2
        self.G = self.DB // 16
        self.NPAIR = self.G // 2
        self.NT = CTX + SEQ
        self.IN_COLS = 2 * self.DB + 2 * self.DB + self.DB + 256 + self.DB
        self.CONV_W = 31
        self.EPS = 1e-6
        BC = self.BC
        self.c_u = 0
        self.c_v = BC
        self.c_a = 2 * BC
        self.c_g = 3 * BC
        self.c_q = 4 * BC
        self.c_k = 5 * BC
        self.c_vv = 5 * BC + 1
        self.c_d = 5 * BC + 2
        self.NCI = 6 * BC + 2
        tiles = []
        s = 0
        while s < CTX:
            w = min(T, CTX - s)
            tiles.append((s, w, True))
            s += w
        while s < self.NT:
            w = min(T, self.NT - s)
            tiles.append((s, w, False))
            s += w
        self.tiles = tiles
        self.YPAD = 16
        self.NY = self.NT + 4 * self.YPAD


class Buf:
    __slots__ = ("name", "w", "r", "dsem", "dcnt", "dq")

    def __init__(self, name):
        self.name = name
        self.w = None
        self.r = []
        self.dsem = None
        self.dcnt = 0


class Eng:
    def __init__(self, K, eng, name, sem):
        self.K, self.eng, self.name, self.sem = K, eng, name, sem
        self.count = 0
        self.waited = {}

    def _wait(self, ev):
        if ev is None:
            return
        sem, val = ev
        if sem is self.sem and not self.K.same_engine_sync:
            return
        if sem is self.sem and self.name == "pe":
            return
        if self.waited.get(id(sem), 0) >= val:
            return
        self.eng.wait_ge(sem, val)
        self.waited[id(sem)] = val

    def deps(self, reads, writes):
        evs = {}

        def add(ev):
            if ev is None:
                return
            k = id(ev[0])
            if k not in evs or evs[k][1] < ev[1]:
                evs[k] = ev
        for b in reads:
            add(b.w)
        for b in writes:
            add(b.w)
            for ev in b.r:
                add(ev)
        for ev in evs.values():
            self._wait(ev)

    def op(self, fn, reads=(), writes=()):
        self.deps(reads, writes)
        ins = fn(self.eng)
        self.count += 1
        ins.then_inc(self.sem, 1)
        ev = (self.sem, self.count)
        for b in reads:
            b.r = [x for x in b.r if x[0] is not ev[0]] + [ev]
        for b in writes:
            b.w = ev
            b.r = []
        return ins

    def dma(self, out, in_, sbuf, reads=(), writes=(), load=True, **kw):
        K = self.K
        if sbuf.dsem is None:
            fl = K.free_sems.setdefault(self.name, [])
            if fl:
                sbuf.dsem, sbuf.dcnt = fl.pop()
            else:
                sbuf.dsem, sbuf.dcnt = K.new_sem(f"dsem{len(K._stack)}"), 0
            sbuf.dq = self.name
        assert sbuf.dq == self.name, "a buffer's DMAs must stay on one queue"
        rd = list(reads) + ([] if load else [sbuf])
        wr = list(writes) + ([sbuf] if load else [])
        self.deps(rd, wr)
        ins = self.eng.dma_start(out=out, in_=in_, **kw)
        sbuf.dcnt += 1
        ins.then_inc(sbuf.dsem, 16)
        ev = (sbuf.dsem, 16 * sbuf.dcnt)
        for b in rd:
            b.r = [x for x in b.r if x[0] is not ev[0]] + [ev]
        for b in wr:
            b.w = ev
            b.r = []
        K.dma_bufs[id(sbuf)] = sbuf
        return ins


class Kern:
    def __init__(self, nc, same_engine_sync=True):
        self.nc = nc
        self.same_engine_sync = same_engine_sync
        self.sems = []
        self.dma_bufs = {}
        self.free_sems = {}
        self._stack = []

    def new_sem(self, name):
        cm = self.nc.semaphore(name)
        h = cm.__enter__()
        self._stack.append(cm)
        return h

    def make_engines(self, block_engs):
        self.pe = Eng(self, block_engs["tensor"], "pe", self.new_sem("s_pe"))
        self.act = Eng(self, block_engs["scalar"], "act", self.new_sem("s_act"))
        self.dve = Eng(self, block_engs["vector"], "dve", self.new_sem("s_dve"))
        self.pool = Eng(self, block_engs["gpsimd"], "pool", self.new_sem("s_pool"))
        self.sp = Eng(self, block_engs["sync"], "sp", self.new_sem("s_sp"))
        self.engs = [self.pe, self.act, self.dve, self.pool, self.sp]

    def barrier(self):
        for e in self.engs:
            for o in self.engs:
                if o is not e and o.count > 0:
                    e._wait((o.sem, o.count))
            for b in self.dma_bufs.values():
                if b.dcnt:
                    e._wait((b.dsem, 16 * b.dcnt))
        for e in self.engs:
            if e.count > 20000:
                e.sem = self.new_sem(f"s_{e.name}_{len(self._stack)}")
                e.count = 0
        for b in self.dma_bufs.values():
            self.free_sems[b.dq].append((b.dsem, b.dcnt))
            b.dsem = None
        self.dma_bufs = {}

    def close(self):
        for cm in reversed(self._stack):
            cm.__exit__(None, None, None)


def lhsT_chunks(W, ncols=128):
    K, N = W.shape
    return np.ascontiguousarray(W.reshape(K // 128, 128, N // ncols, ncols).transpose(1, 2, 0, 3))


def weight_plan(cfg):
    KC, BC, FFC, DB = cfg.KC, cfg.BC, cfg.FFC, cfg.DB
    plan = {}
    off = 0

    def add(name, n):
        nonlocal off
        plan[name] = off
        off += n
    add("inF", (6 * BC + 2) * KC * 128)
    add("inV1", KC * DB)
    add("inV2", KC * 128)
    add("wsT", BC * 128)
    add("glu", 2 * BC * BC * 128)
    add("gb", KC * (4 * KC * 128 + 4 * BC * 128))
    add("out", KC * KC * 128)
    add("ff1", FFC * KC * 128)
    add("ff2", KC * FFC * 128)
    tot = off
    tot = (tot + 8191) // 8192 * 8192
    plan["_total"] = tot
    return plan


def swap16(W):
    K, N = W.shape
    return np.ascontiguousarray(W.reshape(K, N // 32, 2, 16)[:, :, ::-1, :].reshape(K, N))


def pack_layer_weights(cfg, inp, l):
    KC, BC, FFC, DB, D = cfg.KC, cfg.BC, cfg.FFC, cfg.DB, cfg.D
    plan = weight_plan(cfg)
    flat = np.zeros((128, plan["_total"]), np.float32)

    def put(name, arr):
        a = arr.reshape(128, -1)
        flat[:, plan[name]:plan[name] + a.shape[1]] = a
    w_in = inp["w_in"][l]
    cA, cB, cQ, cK, cV, cD = 0, 2 * DB, 4 * DB, 5 * DB, 5 * DB + 128, 5 * DB + 256
    cols = []
    for i in range(BC):
        cols.append(w_in[:, cA + i * 128: cA + (i + 1) * 128])
    for i in range(BC):
        cols.append(w_in[:, cB + DB + i * 128: cB + DB + (i + 1) * 128])
        cols.append(w_in[:, cB + i * 128: cB + (i + 1) * 128])
    wq = w_in[:, cQ:cQ + DB]
    wqs = swap16(wq)
    for i in range(BC):
        cols.append(wq[:, i * 128:(i + 1) * 128])
        cols.append(wqs[:, i * 128:(i + 1) * 128])
    wk = w_in[:, cK:cK + 128]
    cols.append(wk)
    cols.append(swap16(wk))
    for i in range(BC):
        cols.append(w_in[:, cD + i * 128: cD + (i + 1) * 128])
    put("inF", lhsT_chunks(np.concatenate(cols, axis=1)))
    put("inV1", lhsT_chunks(w_in[:, cA + DB: cA + 2 * DB], ncols=DB))
    put("inV2", lhsT_chunks(w_in[:, cV:cV + 128]))
    put("wsT", np.ascontiguousarray(inp["gmlp_ws"][l].transpose(2, 0, 1)))
    put("glu", lhsT_chunks(inp["s5_w_glu"][l]))
    g4 = np.stack([lhsT_chunks(inp["w_gate"][l, k]) for k in range(4)], axis=2)
    b4 = np.stack([lhsT_chunks(inp["w_branch"][l, k]) for k in range(4)], axis=2)
    gb = np.concatenate([g4.reshape(128, KC, -1), b4.reshape(128, KC, -1)], axis=2)
    put("gb", gb)
    put("out", lhsT_chunks(inp["w_out"][l]))
    put("ff1", lhsT_chunks(inp["w_ff1"][l]))
    put("ff2", lhsT_chunks(inp["w_ff2"][l]))
    return flat


def fm(v):
    return np.ascontiguousarray(v.reshape(-1, 128).T)


def pvec_plan(cfg):
    KC, BC = cfg.KC, cfg.BC
    plan = {}
    off = 0
    for name, n in (("b_mod", 6 * KC), ("n1g", KC), ("n2g", KC), ("b_gate", 4 * KC), ("conv_w", BC * 31),
                    ("conv_b", BC), ("cln_g", BC), ("cln_b", BC), ("s5_d", BC)):
        plan[name] = (off, n)
        off += n
    plan["_total"] = off
    return plan


def rowv_plan(cfg):
    DB, BC = cfg.DB, cfg.BC
    plan = {}
    off = 0
    for name, n in (("gln_g", DB), ("gln_b", DB), ("gbs", BC * 128), ("sink", cfg.HQ)):
        plan[name] = (off, n)
        off += n
    plan["_total"] = off
    return plan


def prep_inputs(cfg, inp, core):
    L, KC, BC, D, DB = cfg.DEPTH, cfg.KC, cfg.BC, cfg.D, cfg.DB
    b = core % cfg.BATCH
    m = {}
    xcat = np.concatenate([inp["ctx"][b], inp["x"][b]], axis=0)
    m["xT"] = np.ascontiguousarray(xcat.T.reshape(KC, 128, cfg.NT))
    cond = np.stack([fm(inp["c_ctx"]), fm(inp["c"][b])], axis=2)
    m["cond"] = np.ascontiguousarray(cond)
    return m


def prep_shared(cfg, inp):
    L, KC, BC, D, DB = cfg.DEPTH, cfg.KC, cfg.BC, cfg.D, cfg.DB
    m = {}
    m["wall"] = np.stack([pack_layer_weights(cfg, inp, l) for l in range(L)], axis=0)
    m["wmod"] = np.stack([lhsT_chunks(inp["w_mod"][l]) for l in range(L)], axis=0)
    pp = pvec_plan(cfg)
    pv = np.zeros((128, L + 1, pp["_total"]), np.float32)
    for l in range(L):
        def put(name, a):
            o, n = pp[name]
            pv[:, l, o:o + n] = a.reshape(128, n)
        put("b_mod", fm(inp["b_mod"][l]))
        put("n1g", fm(inp["norm1_g"][l]))
        put("n2g", fm(inp["norm2_g"][l]))
        put("b_gate", np.stack([fm(inp["b_gate"][l, k]) for k in range(4)], axis=1))
        cw = inp["conv_w"][l]
        put("conv_w", np.ascontiguousarray(cw.T.reshape(BC, 128, 31).transpose(1, 0, 2)))
        put("conv_b", fm(inp["conv_b"][l]))
        put("cln_g", fm(inp["conv_ln_g"][l]))
        put("cln_b", fm(inp["conv_ln_b"][l]))
        put("s5_d", fm(inp["s5_d"][l]))
    o, n = pp["n1g"]
    pv[:, L, o:o + n] = fm(inp["final_g"])
    m["pvec"] = pv
    rp = rowv_plan(cfg)
    rv = np.zeros((128, L, rp["_total"]), np.float32)
    for l in range(L):
        for name, a in (("gln_g", inp["gmlp_ln_g"][l]), ("gln_b", inp["gmlp_ln_b"][l]),
                        ("gbs", inp["gmlp_bs"][l].reshape(-1)), ("sink", inp["attn_sink"][l])):
            o, n = rp[name]
            rv[:, l, o:o + n] = np.broadcast_to(a.reshape(1, n), (128, n))
    m["rowv"] = rv
    half = 32
    inv = (10000.0 ** (-np.arange(0, half, 2, dtype=np.float32) / half)).astype(np.float32)
    pos = np.arange(cfg.SEQ)
    row = (pos // cfg.GRID_W).astype(np.float32)
    col = (pos % cfg.GRID_W).astype(np.float32)
    cosT = np.ones((128, cfg.NT), np.float32)
    sinT = np.zeros((128, cfg.NT), np.float32)
    for p in range(128):
        d = p % 64
        blk, j = d // 32, d % 32
        ang = ((row if blk == 0 else col) * inv[j % 16]).astype(np.float32)
        cosT[p, cfg.CTX:] = np.cos(ang)
        sinT[p, cfg.CTX:] = np.sin(ang) * (-1.0 if j < 16 else 1.0)
    m["ropec"] = cosT
    m["ropes"] = sinT
    kl = np.arange(128)[:, None]
    ql = np.arange(128)[None, :]
    m["mask_prev"] = (kl >= ql).astype(np.float32)
    m["mask_next"] = (kl <= ql).astype(np.float32)
    m["ident"] = np.eye(128, dtype=np.float32)
    NP = cfg.NPAIR
    A = np.zeros((128, L, 2, NP, 3), np.float32)
    Bm = np.zeros((128, L, 2, NP, 32), np.float32)
    Cm = np.zeros((128, L, 2, NP, 2, 64), np.float32)
    for g2 in range(2):
        rows = slice(g2 * 64, (g2 + 1) * 64)
        gidx = 2 * np.arange(NP) + g2
        A[rows, :, :, :, 0] = inp["s5_a_re"][:, :, gidx, :].transpose(3, 0, 1, 2)
        A[rows, :, :, :, 1] = inp["s5_a_im"][:, :, gidx, :].transpose(3, 0, 1, 2)
        A[rows, :, :, :, 2] = np.broadcast_to(inp["s5_log_step"][:, :, gidx][None], (64, L, 2, NP))
        cs_ = slice(g2 * 16, (g2 + 1) * 16)
        Bm[rows, :, 0, :, cs_] = inp["s5_b_re"][:, gidx, :, :].transpose(2, 0, 1, 3)
        Bm[rows, :, 1, :, cs_] = inp["s5_b_im"][:, gidx, :, :].transpose(2, 0, 1, 3)
        for k in range(NP):
            cc_ = slice(32 * (k % 2) + g2 * 16, 32 * (k % 2) + (g2 + 1) * 16)
            Cm[rows, :, :, k, 0, cc_] = inp["s5_c_re"][:, :, gidx[k], :, :].transpose(3, 0, 1, 2)
            Cm[rows, :, :, k, 1, cc_] = inp["s5_c_im"][:, :, gidx[k], :, :].transpose(3, 0, 1, 2)
    m["s5A"], m["s5B"], m["s5C"] = A, Bm, Cm
    return m


from contextlib import ExitStack

WSLOT = 8192
NWSLOT = 3


class Prog:
    def __init__(self, cfg, debug=False, n_layers=None, stop_after=None, same_engine_sync=True):
        self.cfg = cfg
        self.debug = debug
        self.L = cfg.DEPTH if n_layers is None else n_layers
        self.stop_after = stop_after
        nc = bass.Bass("TRN2", target_bir_lowering=False)
        self.nc = nc
        self.K = Kern(nc, same_engine_sync)
        self.K.make_engines({"tensor": nc.tensor, "scalar": nc.scalar, "vector": nc.vector,
                             "gpsimd": nc.gpsimd, "sync": nc.sync})
        self.wplan = weight_plan(cfg)
        self.pplan = pvec_plan(cfg)
        self.rplan = rowv_plan(cfg)
        self.psum_i = 0

    def din(self, name, shape, dt=F32):
        return self.nc.dram_tensor(name, list(shape), dt, kind="ExternalInput").ap()

    def dscratch(self, name, shape, dt):
        kind = "ExternalOutput" if self.debug else "Internal"
        return self.nc.dram_tensor(name, list(shape), dt, kind=kind).ap()

    def sb(self, st, name, shape, dt):
        self._uid = getattr(self, "_uid", 0) + 1
        name = f"{name}_{self._uid}"
        t = st.enter_context(self.nc.sbuf_tensor(name, list(shape), dt))
        return t, Buf(name)

    def next_psum(self):
        i = self.psum_i
        self.psum_i = (i + 1) % 6
        return self.ps[i], self.psb[i]

    def declare(self):
        cfg, L = self.cfg, self.cfg.DEPTH
        KC, BC, NT = cfg.KC, cfg.BC, cfg.NT
        self.xT = self.din("xT", [KC, 128, NT])
        self.cond = self.din("cond", [128, KC, 2])
        self.wall = self.din("wall", [L, 128, self.wplan["_total"]])
        self.wmod = self.din("wmod", [L, 128, 6 * KC, KC, 128])
        self.pvec = self.din("pvec", [128, L + 1, self.pplan["_total"]])
        self.rowv = self.din("rowv", [128, L, self.rplan["_total"]])
        self.ropec = self.din("ropec", [128, NT])
        self.ropes = self.din("ropes", [128, NT])
        self.mask_prev = self.din("mask_prev", [128, 128])
        self.mask_next = self.din("mask_next", [128, 128])
        self.ident = self.din("ident", [128, 128])
        self.s5A = self.din("s5A", [128, L, 2, cfg.NPAIR, 3])
        self.s5B = self.din("s5B", [128, L, 2, cfg.NPAIR, 32])
        self.s5C = self.din("s5C", [128, L, 2, cfg.NPAIR, 2, 64])
        self.out = self.nc.dram_tensor("out", [KC, 128, cfg.SEQ], F32, kind="ExternalOutput").ap()
        self.wb = [self.dscratch(f"wb{l}", [128, self.wplan["_total"]], BF16) for l in range(L)]
        self.X = self.dscratch("X", [KC, 128, NT], F32)
        self.H = self.dscratch("H", [KC, 128, NT], BF16)
        self.BR = self.dscratch("BR", [4, BC, 128, NT], BF16)
        self.Y = self.dscratch("Y", [BC, 128, cfg.NY], F32)
        self.Q = self.dscratch("Q", [BC, 128, NT], BF16)
        self.KT = self.dscratch("KT", [2, 128, NT], BF16)
        self.V = self.dscratch("V", [NT, 2, 128], BF16)
        self.U = self.dscratch("U", [BC, 128, NT], F32)
        self.YB = self.dscratch("YB", [BC, 128, NT], F32)

    def ypos(self, t):
        cfg = self.cfg
        return t + cfg.YPAD if t < cfg.CTX else t + 3 * cfg.YPAD

    def wload(self, l, off, n):
        i = self.wslot_i
        self.wslot_i = (i + 1) % len(self.wslots)
        t, b = self.wslots[i]
        self.K.sp.dma(t[:, 0:n], self.wb[l][:, off:off + n], b)
        return t, b

    def build(self):
        cfg, K, nc = self.cfg, self.K, self.nc
        self.declare()
        with ExitStack() as st:
            self.ps, self.psb = [], []
            for i in range(8):
                p = st.enter_context(nc.psum_tensor(f"ps{i}", [128, 512], F32))
                self.ps.append(p)
                self.psb.append(Buf(f"ps{i}"))
            self.pv, self.pvb = self.sb(st, "pv", [128, cfg.DEPTH + 1, self.pplan["_total"]], F32)
            self.modv, self.modb = self.sb(st, "modv", [128, cfg.DEPTH, 6 * cfg.KC, 2], F32)
            self.ones, self.onesb = self.sb(st, "ones", [128, 128], F32)
            self.identt, self.identb = self.sb(st, "identt", [128, 128], F32)
            K.pool.dma(self.pv[:], self.pvec[:, :, :], self.pvb)
            K.pool.dma(self.identt[:], self.ident[:, :], self.identb)
            K.dve.op(lambda e: e.memset(self.ones[:], 1.0), writes=[self.onesb])
            self.phase_w()
            K.barrier()
            self.phase_m()
            K.barrier()
            for l in range(self.L):
                self.phase1(l)
                K.barrier()
                if self.stop_after == ("p1", l):
                    break
                self.phase_s(l)
                K.barrier()
                if self.stop_after == ("ps", l):
                    break
                self.phase2(l)
                K.barrier()
            else:
                self.phase_final()
                K.barrier()
        K.close()
        return nc

    def phase_w(self):
        cfg, K, nc = self.cfg, self.K, self.nc
        tot = self.wplan["_total"]
        CH = 8192
        with ExitStack() as st:
            s32 = [self.sb(st, f"w32_{i}", [128, CH], F32) for i in range(2)]
            s16 = [self.sb(st, f"w16_{i}", [128, CH], BF16) for i in range(2)]
            it = 0
            for l in range(self.L):
                for off in range(0, tot, CH):
                    a, ab = s32[it % 2]
                    o, ob = s16[it % 2]
                    K.sp.dma(a[:], self.wall[l, :, off:off + CH], ab)
                    eng = (K.dve, K.act, K.pool)[it % 3]
                    if eng is K.act:
                        eng.op(lambda e: e.copy(out=o[:], in_=a[:]), reads=[ab], writes=[ob])
                    else:
                        eng.op(lambda e: e.tensor_copy(out=o[:], in_=a[:]), reads=[ab], writes=[ob])
                    K.pool.dma(self.wb[l][:, off:off + CH], o[:], ob, load=False)
                    it += 1

    def phase_m(self):
        cfg, K, nc = self.cfg, self.K, self.nc
        KC = cfg.KC
        NJ = 6 * KC
        JB = max(d for d in range(1, NJ + 1) if NJ % d == 0 and d * KC * 128 <= 8192)
        with ExitStack() as st:
            ct, cb = self.sb(st, "condt", [128, KC, 2], F32)
            sc, scb = self.sb(st, "scond", [128, KC, 2], F32)
            ws = [self.sb(st, f"wm_{i}", [128, JB, KC, 128], F32) for i in range(2)]
            K.pool.dma(ct[:], self.cond[:, :, :], cb)
            K.act.op(lambda e: e.activation(out=sc[:], in_=ct[:], func=AF.Silu), reads=[cb], writes=[scb])
            it = 0
            o_b, _ = self.pplan["b_mod"]
            for l in range(self.L):
                for j0 in range(0, NJ, JB):
                    w, wbuf = ws[it % 2]
                    it += 1
                    K.sp.dma(w[:], self.wmod[l, :, j0:j0 + JB, :, :], wbuf)
                    for j in range(j0, j0 + JB):
                        ps, psb = self.next_psum()
                        for kc in range(KC):
                            K.pe.op(lambda e, kc=kc, j=j: e.matmul(ps[:, 0:2], lhsT=w[:, j - j0, kc, :], rhs=sc[:, kc, :],
                                                                    start=(kc == 0), stop=(kc == KC - 1)),
                                    reads=[wbuf, scb], writes=[psb])
                        K.dve.op(lambda e, j=j: e.tensor_tensor(
                            out=self.modv[:, l, j, :], in0=ps[:, 0:2],
                            in1=self.pv[:, l, o_b + j:o_b + j + 1].to_broadcast([128, 2]), op=ALU.add),
                            reads=[psb, self.pvb], writes=[self.modb])

    def rms_to_h(self, st_tiles, xt, xtb, w, mod_scale_idx, mod_shift_idx, gname, l, which, ht, htb):
        cfg, K = self.cfg, self.K
        KC = cfg.KC
        sq = st_tiles["sq"]
        rs, rsb = st_tiles["rstd"]
        ab, abb = st_tiles["ab"]
        hf = st_tiles["hf"]
        o_g, _ = self.pplan[gname]
        K.dve.op(lambda e: e.tensor_scalar(out=ab[:, :, 0], in0=self.modv[:, l, mod_scale_idx * KC:(mod_scale_idx + 1) * KC, which],
                                           scalar1=1.0, scalar2=None, op0=ALU.add),
                 reads=[self.modb], writes=[abb])
        K.dve.op(lambda e: e.tensor_tensor(out=ab[:, :, 0], in0=ab[:, :, 0], in1=self.pv[:, l, o_g:o_g + KC], op=ALU.mult),
                 reads=[abb, self.pvb], writes=[abb])
        K.dve.op(lambda e: e.tensor_copy(out=ab[:, :, 1], in_=self.modv[:, l, mod_shift_idx * KC:(mod_shift_idx + 1) * KC, which]),
                 reads=[self.modb], writes=[abb])
        ps, psb = self.next_psum()
        for kc in range(KC):
            s, sb_ = sq[kc % 2]
            K.act.op(lambda e, kc=kc, s=s: e.activation(out=s[:, :w], in_=xt[:, kc, :w], func=AF.Square),
                     reads=[xtb], writes=[sb_])
            K.pe.op(lambda e, kc=kc, s=s: e.matmul(ps[:, :w], lhsT=self.ones[:], rhs=s[:, :w],
                                                   start=(kc == 0), stop=(kc == KC - 1)),
                    reads=[self.onesb, sb_], writes=[psb])
        K.act.op(lambda e: e.activation(out=rs[:, :w], in_=ps[:, :w], func=AF.Sqrt, scale=1.0 / cfg.D, bias=self.epsc[:, 0:1]),
                 reads=[psb, self.epsb], writes=[rsb])
        K.dve.op(lambda e: e.reciprocal(out=rs[:, :w], in_=rs[:, :w]), reads=[rsb], writes=[rsb])
        for kc in range(KC):
            f, fb = hf[kc % 2]
            K.dve.op(lambda e, kc=kc, f=f: e.tensor_tensor(out=f[:, :w], in0=xt[:, kc, :w], in1=rs[:, :w], op=ALU.mult),
                     reads=[xtb, rsb], writes=[fb])
            K.act.op(lambda e, kc=kc, f=f: e.activation(out=ht[:, kc, :w], in_=f[:, :w], func=AF.Identity,
                                                        scale=ab[:, kc, 0:1], bias=ab[:, kc, 1:2]),
                     reads=[fb, abb], writes=[htb])

    def mm_group(self, ps_ap, psb, lhs_list, rhs_list, reads):
        n = len(lhs_list)
        for i in range(n):
            self.K.pe.op(lambda e, i=i: e.matmul(ps_ap, lhsT=lhs_list[i], rhs=rhs_list[i], start=(i == 0), stop=(i == n - 1)),
                         reads=reads, writes=[psb])

    def phase1(self, l):
        cfg, K, nc = self.cfg, self.K, self.nc
        KC, BC, DB, T = cfg.KC, cfg.BC, cfg.DB, cfg.T
        xsrc = self.xT if l == 0 else self.X
        wp = self.wplan
        CW = KC * 128
        with ExitStack() as st:
            xt, xtb = self.sb(st, "p1_xt", [128, KC, T], F32)
            ht, htb = self.sb(st, "p1_ht", [128, KC, T], BF16)
            tl = {
                "sq": [self.sb(st, f"p1_sq{i}", [128, T], F32) for i in range(2)],
                "hf": [self.sb(st, f"p1_hf{i}", [128, T], F32) for i in range(2)],
                "rstd": self.sb(st, "p1_rstd", [128, T], F32),
                "ab": self.sb(st, "p1_ab", [128, KC, 2], F32),
            }
            self.epsc, self.epsb = self.sb(st, "p1_eps", [128, 1], F32)
            K.dve.op(lambda e: e.memset(self.epsc[:], cfg.EPS), writes=[self.epsb])
            wv1, wv1b = self.sb(st, "p1_wv1", [128, KC, DB], BF16)
            wv2, wv2b = self.sb(st, "p1_wv2", [128, KC, 128], BF16)
            wst, wstb = self.sb(st, "p1_wst", [128, BC, 128], BF16)
            self.wslots = [self.sb(st, f"p1_ws{i}", [128, WSLOT], BF16) for i in range(NWSLOT)]
            self.wslot_i = 0
            rv, rvb = self.sb(st, "p1_rv", [128, self.rplan["_total"]], F32)
            ut, utb = self.sb(st, "p1_ut", [128, BC, T], F32)
            yt, ytb = self.sb(st, "p1_yt", [128, BC, T], F32)
            qt, qtb = self.sb(st, "p1_qt", [128, BC, T], BF16)
            kt, ktb = self.sb(st, "p1_kt", [128, T], BF16)
            dt_, dtb = self.sb(st, "p1_dt", [128, BC, T], F32)
            bra, brab = self.sb(st, "p1_bra", [128, BC, T], BF16)
            cs, csb = self.sb(st, "p1_cos", [128, T], F32)
            sn, snb = self.sb(st, "p1_sin", [128, T], F32)
            sg, sgb = self.sb(st, "p1_sg", [128, T], F32)
            t1 = [self.sb(st, f"p1_t1{i}", [128, T], F32) for i in range(2)]
            t2 = [self.sb(st, f"p1_t2{i}", [128, T], F32) for i in range(2)]
            vg, vgb = self.sb(st, "p1_vg", [128, DB], F32)
            vn, vnb = self.sb(st, "p1_vn", [128, DB], F32)
            vnh, vnhb = self.sb(st, "p1_vnh", [128, DB], BF16)
            stt, sttb = self.sb(st, "p1_stats", [128, 8, 6], F32)
            mv, mvb = self.sb(st, "p1_mv", [128, 4], F32)
            mx, mxb = self.sb(st, "p1_mx", [128, BC, 128], F32)
            vv, vvb = self.sb(st, "p1_vv", [128, 2, 2, 64], BF16)
            o_glg, _ = self.rplan["gln_g"]
            o_glb, _ = self.rplan["gln_b"]
            o_gbs, _ = self.rplan["gbs"]
            K.pool.dma(rv[:], self.rowv[:, l, :], rvb)
            K.sp.dma(wv1[:], self.wb[l][:, wp["inV1"]:wp["inV1"] + KC * DB], wv1b)
            K.sp.dma(wv2[:], self.wb[l][:, wp["inV2"]:wp["inV2"] + KC * 128], wv2b)
            K.sp.dma(wst[:], self.wb[l][:, wp["wsT"]:wp["wsT"] + BC * 128], wstb)
            for (t0, w, is_ctx) in cfg.tiles:
                which = 0 if is_ctx else 1
                K.pool.dma(xt[:, :, :w], xsrc[:, :, t0:t0 + w].rearrange("c p t -> p c t"), xtb)
                K.pool.dma(cs[:, :w], self.ropec[:, t0:t0 + w], csb)
                K.pool.dma(sn[:, :w], self.ropes[:, t0:t0 + w], snb)
                self.rms_to_h(tl, xt, xtb, w, 1, 0, "n1g", l, which, ht, htb)
                K.pool.dma(self.H[:, :, t0:t0 + w].rearrange("c p t -> p c t"), ht[:, :, :w], htb, load=False)
                nchunks = 6 * BC + 2
                chunk_kind = []
                for i in range(BC):
                    chunk_kind.append(("u", i))
                for i in range(BC):
                    chunk_kind.append(("g", i))
                    chunk_kind.append(("a", i))
                for i in range(BC):
                    chunk_kind.append(("q", i))
                    chunk_kind.append(("qs", i))
                chunk_kind.append(("k", 0))
                chunk_kind.append(("ks", 0))
                for i in range(BC):
                    chunk_kind.append(("d", i))
                CPS = max(1, WSLOT // CW)
                wt = wtb = None
                pending = {}
                for ci, (kind, i) in enumerate(chunk_kind):
                    if ci % CPS == 0:
                        n = min(CPS, nchunks - ci)
                        wt, wtb = self.wload(l, wp["inF"] + ci * CW, n * CW)
                    base = (ci % CPS) * CW
                    ps, psb = self.next_psum()
                    self.mm_group(ps[:, :w], psb,
                                  [wt[:, base + kc * 128: base + (kc + 1) * 128] for kc in range(KC)],
                                  [ht[:, kc, :w] for kc in range(KC)], [wtb, htb])
                    if kind == "u":
                        K.act.op(lambda e, i=i, ps=ps: e.activation(out=ut[:, i, :w], in_=ps[:, :w], func=AF.Gelu_apprx_tanh),
                                 reads=[psb], writes=[utb])
                    elif kind == "g":
                        K.act.op(lambda e, ps=ps: e.activation(out=sg[:, :w], in_=ps[:, :w], func=AF.Sigmoid),
                                 reads=[psb], writes=[sgb])
                    elif kind == "a":
                        K.dve.op(lambda e, i=i, ps=ps: e.tensor_tensor(out=yt[:, i, :w], in0=ps[:, :w], in1=sg[:, :w], op=ALU.mult),
                                 reads=[psb, sgb], writes=[ytb])
                    elif kind in ("q", "k"):
                        a, ab_ = t1[ci % 2 if False else (ci // 2) % 2]
                        K.dve.op(lambda e, ps=ps, a=a: e.tensor_tensor(out=a[:, :w], in0=ps[:, :w], in1=cs[:, :w], op=ALU.mult),
                                 reads=[psb, csb], writes=[ab_])
                        pending["t1"] = (a, ab_)
                    elif kind in ("qs", "ks"):
                        a, ab_ = pending["t1"]
                        b2, b2b = t2[(ci // 2) % 2]
                        K.dve.op(lambda e, ps=ps, b2=b2: e.tensor_tensor(out=b2[:, :w], in0=ps[:, :w], in1=sn[:, :w], op=ALU.mult),
                                 reads=[psb, snb], writes=[b2b])
                        if kind == "qs":
                            K.pool.op(lambda e, i=i, a=a, b2=b2: e.tensor_tensor(out=qt[:, i, :w], in0=a[:, :w], in1=b2[:, :w], op=ALU.add),
                                      reads=[ab_, b2b], writes=[qtb])
                        else:
                            K.pool.op(lambda e, a=a, b2=b2: e.tensor_tensor(out=kt[:, :w], in0=a[:, :w], in1=b2[:, :w], op=ALU.add),
                                      reads=[ab_, b2b], writes=[ktb])
                    elif kind == "d":
                        K.act.op(lambda e, i=i, ps=ps: e.copy(out=dt_[:, i, :w], in_=ps[:, :w]), reads=[psb], writes=[dtb])
                y0 = self.ypos(t0)
                K.pool.dma(self.Y[:, :, y0:y0 + w].rearrange("c p t -> p c t"), yt[:, :, :w], ytb, load=False)
                K.pool.dma(self.Q[:, :, t0:t0 + w].rearrange("c p t -> p c t"), qt[:, :, :w], qtb, load=False)
                for hk in range(2):
                    for dup in range(2):
                        K.pool.dma(self.KT[hk, dup * 64:(dup + 1) * 64, t0:t0 + w], kt[hk * 64:(hk + 1) * 64, :w], ktb, load=False)
                K.pool.dma(self.U[:, :, t0:t0 + w].rearrange("c p t -> p c t"), dt_[:, :, :w], dtb, load=False)
                for j in range(w // 128):
                    tk = slice(j * 128, (j + 1) * 128)
                    ps, psb = self.next_psum()
                    self.mm_group(ps[:, :DB], psb, [ht[:, kc, tk] for kc in range(KC)],
                                  [wv1[:, kc, :] for kc in range(KC)], [htb, wv1b])
                    K.act.op(lambda e, ps=ps: e.activation(out=vg[:, :], in_=ps[:, :DB], func=AF.Gelu_apprx_tanh),
                             reads=[psb], writes=[vgb])
                    FMAX = 512
                    nst = (DB + FMAX - 1) // FMAX
                    for s_ in range(nst):
                        K.dve.op(lambda e, s_=s_: e.bn_stats(out=stt[:, s_, :], in_=vg[:, s_ * FMAX:min(DB, (s_ + 1) * FMAX)]),
                                 reads=[vgb], writes=[sttb])
                    K.dve.op(lambda e: e.bn_aggr(out=mv[:, 0:2], in_=stt[:, 0:nst, :]), reads=[sttb], writes=[mvb])
                    K.act.op(lambda e: e.activation(out=mv[:, 2:3], in_=mv[:, 1:2], func=AF.Sqrt, bias=self.epsc[:, 0:1]),
                             reads=[mvb, self.epsb], writes=[mvb])
                    K.dve.op(lambda e: e.reciprocal(out=mv[:, 2:3], in_=mv[:, 2:3]), reads=[mvb], writes=[mvb])
                    K.dve.op(lambda e: e.tensor_scalar(out=vn[:, :], in0=vg[:, :], scalar1=mv[:, 0:1], scalar2=mv[:, 2:3],
                                                       op0=ALU.subtract, op1=ALU.mult),
                             reads=[vgb, mvb], writes=[vnb])
                    K.pool.op(lambda e: e.tensor_tensor(out=vn[:, :], in0=vn[:, :], in1=rv[:, o_glg:o_glg + DB], op=ALU.mult),
                              reads=[vnb, rvb], writes=[vnb])
                    K.pool.op(lambda e: e.tensor_tensor(out=vnh[:, :], in0=vn[:, :], in1=rv[:, o_glb:o_glb + DB], op=ALU.add),
                              reads=[vnb, rvb], writes=[vnhb])
                    ps2, ps2b = self.next_psum()
                    for gi in range(BC):
                        K.pe.op(lambda e, gi=gi, ps2=ps2: e.matmul(ps2[:, gi * 128:(gi + 1) * 128], lhsT=vnh[:, gi * 128:(gi + 1) * 128],
                                                                  rhs=wst[:, gi, :], start=True, stop=True),
                                reads=[vnhb, wstb], writes=[ps2b])
                    K.dve.op(lambda e, ps2=ps2: e.tensor_tensor(out=mx[:, :, :], in0=ps2[:, :BC * 128].rearrange("p (g q) -> p g q", g=BC),
                                                               in1=rv[:, o_gbs:o_gbs + BC * 128].rearrange("p (g q) -> p g q", g=BC), op=ALU.add),
                             reads=[ps2b, rvb], writes=[mxb])
                    K.pool.op(lambda e, tk=tk: e.tensor_tensor(out=bra[:, :, tk], in0=mx[:, :, :], in1=ut[:, :, tk], op=ALU.mult),
                              reads=[mxb, utb], writes=[brab])
                    ps3, ps3b = self.next_psum()
                    self.mm_group(ps3[:, :128], ps3b, [ht[:, kc, tk] for kc in range(KC)],
                                  [wv2[:, kc, :] for kc in range(KC)], [htb, wv2b])
                    for dup in range(2):
                        K.act.op(lambda e, ps3=ps3, dup=dup: e.copy(out=vv[:, :, dup, :], in_=ps3[:, :128].rearrange("p (h d) -> p h d", h=2)),
                                 reads=[ps3b], writes=[vvb])
                    K.pool.dma(self.V[t0 + j * 128:t0 + (j + 1) * 128, :, :], vv[:].rearrange("p h u d -> p h (u d)"), vvb, load=False)
                K.pool.dma(self.BR[0, :, :, t0:t0 + w].rearrange("c p t -> p c t"), bra[:, :, :w], brab, load=False)


    def phase2(self, l):
        cfg, K, nc = self.cfg, self.K, self.nc
        KC, BC, DB, T, FFC, CTX, NT = cfg.KC, cfg.BC, cfg.DB, cfg.T, cfg.FFC, cfg.CTX, cfg.NT
        QPK = cfg.QPK
        xsrc = self.xT if l == 0 else self.X
        wp = self.wplan
        pp = self.pplan
        last = (l == cfg.DEPTH - 1)
        NCC = CTX // 128
        HC = min(FFC, KC)
        GW_ = 4 * KC * 128
        BW_ = 4 * BC * 128
        with ExitStack() as st:
            xt, xtb = self.sb(st, "p2_xt", [128, KC, T], F32)
            ht, htb = self.sb(st, "p2_ht", [128, KC, T], BF16)
            big, bigb = self.sb(st, "p2_big", [128, HC, T], BF16)
            tl = {
                "sq": [self.sb(st, f"p2_sq{i}", [128, T], F32) for i in range(2)],
                "hf": [self.sb(st, f"p2_hf{i}", [128, T], F32) for i in range(2)],
                "rstd": self.sb(st, "p2_rstd", [128, T], F32),
                "ab": self.sb(st, "p2_ab", [128, KC, 2], F32),
            }
            self.epsc, self.epsb = self.sb(st, "p2_eps", [128, 1], F32)
            K.dve.op(lambda e: e.memset(self.epsc[:], cfg.EPS), writes=[self.epsb])
            self.wslots = [self.sb(st, f"p2_ws{i}", [128, 8192], BF16) for i in range(3)]
            self.wslot_i = 0
            bws = [self.sb(st, f"p2_bw{i}", [128, BW_], BF16) for i in range(2)]
            brt = {0: self.sb(st, "p2_br0", [128, BC, T], BF16), 3: self.sb(st, "p2_br3", [128, BC, T], BF16)}
            brBs = [self.sb(st, f"p2_brB{i}", [128, BC, T], BF16) for i in range(2)]
            brCs = [self.sb(st, f"p2_brC{i}", [128, BC, T], BF16) for i in range(2)]
            rv, rvb = self.sb(st, "p2_rv", [128, cfg.HQ], F32)
            esk, eskb = self.sb(st, "p2_esk", [128, cfg.HQ], F32)
            ywins = [self.sb(st, f"p2_ywin{i}", [128, BC, T + 32], F32) for i in range(1)] * 2
            acc, accb = self.sb(st, "p2_acc", [128, BC, T], F32)
            cst = [self.sb(st, f"p2_cst{i}", [128, T], F32) for i in range(3)]
            qts = [self.sb(st, f"p2_qt{i}", [128, BC, T], BF16) for i in range(1)] * 2
            kwins = [self.sb(st, f"p2_kwin{i}", [128, 2, T + 256], BF16) for i in range(1)] * 2
            vwins = [self.sb(st, f"p2_vwin{i}", [128, (T + 256) // 128, 2, 128], BF16) for i in range(1)] * 2
            kctx, kctxb = self.sb(st, "p2_kctx", [128, 2, CTX], BF16)
            vctx, vctxb = self.sb(st, "p2_vctx", [128, NCC, 2, 128], BF16)
            mk = [self.sb(st, f"p2_mk{i}", [128, 128], BF16) for i in range(2)]
            mk32, mk32b = self.sb(st, "p2_mk32", [128, 2, 128], F32)
            onesh, oneshb = self.sb(st, "p2_onesh", [128, 128], BF16)
            pts = [self.sb(st, f"p2_pt{i}", [128, QPK * 128], BF16) for i in range(3)]
            rden, rdenb = self.sb(st, "p2_rden", [128, QPK * 128], F32)
            sgs = tl["sq"]
            prods = [cst[1], cst[2]] + tl["hf"]
            rl = tl["sq"]
            zt, ztb = self.sb(st, "p2_zero", [128, 32], F32)
            o_sk, _ = self.rplan["sink"]
            K.pool.dma(rv[:], self.rowv[:, l, o_sk:o_sk + cfg.HQ], rvb)
            K.act.op(lambda e: e.activation(out=esk[:], in_=rv[:, :], func=AF.Exp), reads=[rvb], writes=[eskb])
            K.pool.dma(mk32[:, 0, :], self.mask_prev[:, :], mk32b)
            K.pool.dma(mk32[:, 1, :], self.mask_next[:, :], mk32b)
            for i in range(2):
                K.dve.op(lambda e, i=i: e.tensor_copy(out=mk[i][0][:], in_=mk32[:, i, :]), reads=[mk32b], writes=[mk[i][1]])
            K.dve.op(lambda e: e.memset(onesh[:], 1.0), writes=[oneshb])
            K.dve.op(lambda e: e.memset(zt[:], 0.0), writes=[ztb])
            P_ = cfg.YPAD
            for c0 in (0, P_ + CTX, 2 * P_ + CTX, 3 * P_ + NT):
                for c in range(BC):
                    K.pool.dma(self.Y[c, :, c0:c0 + P_], zt[:, 0:P_], ztb, load=False)
            K.pool.dma(kctx[:], self.KT[:, :, 0:CTX].rearrange("h p t -> p h t"), kctxb)
            K.pool.dma(vctx[:], self.V[0:CTX, :, :].rearrange("(c p) h d -> p c h d", p=128), vctxb)
            K.barrier()
            o_cw, _ = pp["conv_w"]
            o_cb, _ = pp["conv_b"]
            o_lg, _ = pp["cln_g"]
            o_lb, _ = pp["cln_b"]
            o_bg, _ = pp["b_gate"]
            tiles2 = [t_ for t_ in cfg.tiles if not (t_[2] and last)]

            def front(ti):
                (t0, w, is_ctx) = tiles2[ti]
                par = ti % 2
                (ywin, ywinb), (qt, qtb), (kwin, kwinb), (vwin, vwinb) = ywins[par], qts[par], kwins[par], vwins[par]
                brB, brBb = brBs[par]
                y0 = self.ypos(t0)
                K.pool.dma(ywin[:, :, :w + 30], self.Y[:, :, y0 - 15:y0 + w + 15].rearrange("c p t -> p c t"), ywinb)
                K.pool.dma(qt[:, :, :w], self.Q[:, :, t0:t0 + w].rearrange("c p t -> p c t"), qtb)
                if not is_ctx:
                    k_lo = max(CTX, t0 - 128)
                    k_hi = min(NT, t0 + w + 128)
                    K.pool.dma(kwin[:, :, :k_hi - k_lo], self.KT[:, :, k_lo:k_hi].rearrange("h p t -> p h t"), kwinb)
                    K.pool.dma(vwin[:, :(k_hi - k_lo) // 128, :, :],
                               self.V[k_lo:k_hi, :, :].rearrange("(c p) h d -> p c h d", p=128), vwinb)
                yield
                for c in range(BC):
                    eng = K.dve
                    eng.op(lambda e, c=c: e.tensor_scalar(out=acc[:, c, :w], in0=ywin[:, c, 0:w],
                                                          scalar1=self.pv[:, l, o_cw + c * 31:o_cw + c * 31 + 1],
                                                          scalar2=self.pv[:, l, o_cb + c:o_cb + c + 1], op0=ALU.mult, op1=ALU.add),
                           reads=[ywinb, self.pvb], writes=[accb])
                    for j in range(1, 31):
                        if j % 10 == 0:
                            yield
                        eng.op(lambda e, c=c, j=j: e.scalar_tensor_tensor(
                            out=acc[:, c, :w], in0=ywin[:, c, j:j + w], scalar=self.pv[:, l, o_cw + c * 31 + j:o_cw + c * 31 + j + 1],
                            in1=acc[:, c, :w], op0=ALU.mult, op1=ALU.add), reads=[ywinb, self.pvb, accb], writes=[accb])
                yield
                ps1, ps1b = self.next_psum()
                ps2, ps2b = self.next_psum()
                for c in range(BC):
                    s_, sb_ = tl["sq"][c % 2]
                    K.pe.op(lambda e, c=c: e.matmul(ps1[:, :w], lhsT=self.ones[:], rhs=acc[:, c, :w], start=(c == 0), stop=(c == BC - 1)),
                            reads=[self.onesb, accb], writes=[ps1b])
                    K.act.op(lambda e, c=c, s_=s_: e.activation(out=s_[:, :w], in_=acc[:, c, :w], func=AF.Square), reads=[accb], writes=[sb_])
                    K.pe.op(lambda e, c=c, s_=s_: e.matmul(ps2[:, :w], lhsT=self.ones[:], rhs=s_[:, :w], start=(c == 0), stop=(c == BC - 1)),
                            reads=[self.onesb, sb_], writes=[ps2b])
                mean, meanb = cst[0]
                msq, msqb = cst[1]
                var, varb = cst[2]
                K.act.op(lambda e: e.mul(out=mean[:, :w], in_=ps1[:, :w], mul=1.0 / DB), reads=[ps1b], writes=[meanb])
                K.dve.op(lambda e: e.tensor_tensor(out=msq[:, :w], in0=mean[:, :w], in1=mean[:, :w], op=ALU.mult), reads=[meanb], writes=[msqb])
                K.dve.op(lambda e: e.scalar_tensor_tensor(out=var[:, :w], in0=ps2[:, :w], scalar=1.0 / DB, in1=msq[:, :w],
                                                          op0=ALU.mult, op1=ALU.subtract), reads=[ps2b, msqb], writes=[varb])
                K.act.op(lambda e: e.activation(out=var[:, :w], in_=var[:, :w], func=AF.Sqrt, bias=self.epsc[:, 0:1]),
                         reads=[varb, self.epsb], writes=[varb])
                K.dve.op(lambda e: e.reciprocal(out=var[:, :w], in_=var[:, :w]), reads=[varb], writes=[varb])
                for c in range(BC):
                    eng = K.dve if c % 2 == 0 else K.pool
                    eng.op(lambda e, c=c: e.tensor_tensor(out=acc[:, c, :w], in0=acc[:, c, :w], in1=mean[:, :w], op=ALU.subtract),
                           reads=[accb, meanb], writes=[accb])
                    eng.op(lambda e, c=c: e.tensor_tensor(out=acc[:, c, :w], in0=acc[:, c, :w], in1=var[:, :w], op=ALU.mult),
                           reads=[accb, varb], writes=[accb])
                    K.act.op(lambda e, c=c: e.activation(out=brB[:, c, :w], in_=acc[:, c, :w], func=AF.Silu,
                                                         scale=self.pv[:, l, o_lg + c:o_lg + c + 1], bias=self.pv[:, l, o_lb + c:o_lb + c + 1]),
                             reads=[accb, self.pvb], writes=[brBb])
                brc, brcb = brCs[par]
                yield
                for j in range(w // 128):
                    tq0 = t0 + j * 128
                    qs = slice(j * 128, (j + 1) * 128)
                    chunks = []
                    if not is_ctx:
                        for rel_, mi in ((-128, 0), (0, None), (128, 1)):
                            ks = tq0 + rel_
                            if ks < CTX or ks >= NT:
                                continue
                            o = ks - k_lo
                            chunks.append((lambda hk, par, o=o: kwin[par * 64:(par + 1) * 64, hk, o:o + 128],
                                           lambda hk, o=o: vwin[:, o // 128, hk, :], mi, [kwinb], [vwinb]))
                    for cc in range(NCC):
                        chunks.append((lambda hk, par, cc=cc: kctx[par * 64:(par + 1) * 64, hk, cc * 128:(cc + 1) * 128],
                                       lambda hk, cc=cc: vctx[:, cc, hk, :], None, [kctxb], [vctxb]))
                    for hk in range(2):
                        pso, psob = self.ps[6], self.psb[6]
                        psd, psdb = self.ps[7], self.psb[7]
                        for ci, (kf, vf, mi, krd, vrd) in enumerate(chunks):
                            pt, ptb = pts[ci % 3]
                            for par in range(2):
                                heads = [i for i in range(QPK) if (hk * QPK + i) % 2 == par]
                                if not heads:
                                    continue
                                pss, pssb = self.next_psum()
                                for i in heads:
                                    ch = (hk * QPK + i) // 2
                                    K.pe.op(lambda e, i=i, par=par, ch=ch, kf=kf, pss=pss: e.matmul(
                                        pss[:, i * 128:(i + 1) * 128], lhsT=kf(hk, par), rhs=qt[par * 64:(par + 1) * 64, ch, qs],
                                        start=True, stop=True), reads=krd + [qtb], writes=[pssb])
                                for i in heads:
                                    K.act.op(lambda e, pss=pss, pt=pt, i=i: e.activation(out=pt[:, i * 128:(i + 1) * 128], in_=pss[:, i * 128:(i + 1) * 128],
                                                                                    func=AF.Exp, scale=0.125), reads=[pssb], writes=[ptb])
                            if mi is not None:
                                K.pool.op(lambda e, pt=pt, mi=mi: e.tensor_tensor(
                                    out=pt[:, :].rearrange("p (h q) -> p h q", h=QPK), in0=pt[:, :].rearrange("p (h q) -> p h q", h=QPK),
                                    in1=mk[mi][0][:, :].unsqueeze(1).to_broadcast([128, QPK, 128]), op=ALU.mult),
                                    reads=[ptb, mk[mi][1]], writes=[ptb])
                            first, lastc = (ci == 0), (ci == len(chunks) - 1)
                            K.pe.op(lambda e, vf=vf, pt=pt, first=first, lastc=lastc: e.matmul(
                                pso[:, :QPK * 128], lhsT=vf(hk), rhs=pt[:, :], start=first, stop=lastc), reads=vrd + [ptb], writes=[psob])
                            K.pe.op(lambda e, pt=pt, first=first, lastc=lastc: e.matmul(
                                psd[:, :QPK * 128], lhsT=onesh[:, :], rhs=pt[:, :], start=first, stop=lastc), reads=[oneshb, ptb], writes=[psdb])
                        K.dve.op(lambda e: e.tensor_tensor(
                            out=rden[:, :].rearrange("p (h q) -> p h q", h=QPK), in0=psd[:, :QPK * 128].rearrange("p (h q) -> p h q", h=QPK),
                            in1=esk[:, hk * QPK:(hk + 1) * QPK].unsqueeze(2).to_broadcast([128, QPK, 128]), op=ALU.add),
                            reads=[psdb, eskb], writes=[rdenb])
                        K.dve.op(lambda e: e.reciprocal(out=rden[:, :], in_=rden[:, :]), reads=[rdenb], writes=[rdenb])
                        for i in range(QPK):
                            hq = hk * QPK + i
                            par, ch = hq % 2, hq // 2
                            pr = slice(par * 64, (par + 1) * 64)
                            K.dve.op(lambda e, i=i, pr=pr, ch=ch: e.tensor_tensor(
                                out=brc[pr, ch, qs], in0=pso[pr, i * 128:(i + 1) * 128], in1=rden[pr, i * 128:(i + 1) * 128], op=ALU.mult),
                                reads=[psob, rdenb], writes=[brcb])
                        yield

            def back(ti, tick):
                (t0, w, is_ctx) = tiles2[ti]
                par = ti % 2
                which = 0 if is_ctx else 1
                brl = [brt[0], brBs[par], brCs[par], brt[3]]
                K.pool.dma(xt[:, :, :w], xsrc[:, :, t0:t0 + w].rearrange("c p t -> p c t"), xtb)
                K.pool.dma(ht[:, :, :w], self.H[:, :, t0:t0 + w].rearrange("c p t -> p c t"), htb)
                K.pool.dma(brt[0][0][:, :, :w], self.BR[0, :, :, t0:t0 + w].rearrange("c p t -> p c t"), brt[0][1])
                K.pool.dma(brt[3][0][:, :, :w], self.BR[3, :, :, t0:t0 + w].rearrange("c p t -> p c t"), brt[3][1])
                for oc in range(KC):
                    tick()
                    gw, gwb = self.wload(l, wp["gb"] + oc * (GW_ + BW_), GW_)
                    bw, bwb = bws[oc % 2]
                    K.sp.dma(bw[:, :], self.wb[l][:, wp["gb"] + oc * (GW_ + BW_) + GW_: wp["gb"] + (oc + 1) * (GW_ + BW_)], bwb)
                    for k in range(4):
                        psg, psgb = self.next_psum()
                        self.mm_group(psg[:, :w], psgb, [gw[:, (k * KC + kc) * 128:(k * KC + kc + 1) * 128] for kc in range(KC)],
                                      [ht[:, kc, :w] for kc in range(KC)], [gwb, htb])
                        psb_, psbb = self.next_psum()
                        self.mm_group(psb_[:, :w], psbb, [bw[:, (k * BC + bc) * 128:(k * BC + bc + 1) * 128] for bc in range(BC)],
                                      [brl[k][0][:, bc, :w] for bc in range(BC)], [bwb, brl[k][1]])
                        sg, sgb = sgs[k % 2]
                        K.act.op(lambda e, psg=psg, sg=sg, k=k: e.activation(out=sg[:, :w], in_=psg[:, :w], func=AF.Sigmoid,
                                                                        bias=self.pv[:, l, o_bg + k * KC + oc:o_bg + k * KC + oc + 1]),
                                 reads=[psgb, self.pvb], writes=[sgb])
                        pr_, prb = prods[k]
                        K.dve.op(lambda e, psb_=psb_, sg=sg, pr_=pr_: e.tensor_tensor(out=pr_[:, :w], in0=psb_[:, :w], in1=sg[:, :w], op=ALU.mult),
                                 reads=[psbb, sgb], writes=[prb])
                    K.pool.op(lambda e: e.tensor_tensor(out=prods[0][0][:, :w], in0=prods[0][0][:, :w], in1=prods[1][0][:, :w], op=ALU.add),
                              reads=[prods[0][1], prods[1][1]], writes=[prods[0][1]])
                    K.pool.op(lambda e: e.tensor_tensor(out=prods[2][0][:, :w], in0=prods[2][0][:, :w], in1=prods[3][0][:, :w], op=ALU.add),
                              reads=[prods[2][1], prods[3][1]], writes=[prods[2][1]])
                    K.pool.op(lambda e, oc=oc: e.tensor_tensor(out=big[:, oc, :w], in0=prods[0][0][:, :w], in1=prods[2][0][:, :w], op=ALU.add),
                              reads=[prods[0][1], prods[2][1]], writes=[bigb])
                CW = KC * 128
                CPS = max(1, 8192 // CW)
                for oc in range(KC):
                    tick()
                    if oc % CPS == 0:
                        n = min(CPS, KC - oc)
                        wt, wtb = self.wload(l, wp["out"] + oc * CW, n * CW)
                    base = (oc % CPS) * CW
                    ps, psb = self.next_psum()
                    self.mm_group(ps[:, :w], psb, [wt[:, base + kc * 128:base + (kc + 1) * 128] for kc in range(KC)],
                                  [big[:, kc, :w] for kc in range(KC)], [wtb, bigb])
                    K.dve.op(lambda e, oc=oc, ps=ps: e.scalar_tensor_tensor(
                        out=xt[:, oc, :w], in0=ps[:, :w], scalar=self.modv[:, l, 2 * KC + oc, which:which + 1], in1=xt[:, oc, :w],
                        op0=ALU.mult, op1=ALU.add), reads=[psb, self.modb, xtb], writes=[xtb])
                self.rms_to_h(tl, xt, xtb, w, 4, 3, "n2g", l, which, ht, htb)
                for hp in range(FFC // HC):
                    for fcl in range(HC):
                        fc = hp * HC + fcl
                        if fcl % 2 == 0:
                            tick()
                        if fcl % CPS == 0:
                            n = min(CPS, HC - fcl)
                            wt, wtb = self.wload(l, wp["ff1"] + fc * CW, n * CW)
                        base = (fcl % CPS) * CW
                        ps, psb = self.next_psum()
                        self.mm_group(ps[:, :w], psb, [wt[:, base + kc * 128:base + (kc + 1) * 128] for kc in range(KC)],
                                      [ht[:, kc, :w] for kc in range(KC)], [wtb, htb])
                        r_, rb = rl[fc % 2]
                        K.act.op(lambda e, ps=ps, r_=r_: e.activation(out=r_[:, :w], in_=ps[:, :w], func=AF.Relu), reads=[psb], writes=[rb])
                        eng = K.pool if fc % 2 == 0 else K.dve
                        eng.op(lambda e, r_=r_, fcl=fcl: e.tensor_tensor(out=big[:, fcl, :w], in0=r_[:, :w], in1=r_[:, :w], op=ALU.mult),
                               reads=[rb], writes=[bigb])
                    FW = FFC * 128
                    for oc in range(KC):
                        tick()
                        ps, psb = self.next_psum()
                        nload = HC * 128
                        for s0 in range(0, nload, 8192):
                            n = min(8192, nload - s0)
                            wt, wtb = self.wload(l, wp["ff2"] + oc * FW + hp * HC * 128 + s0, n)
                            nk = n // 128
                            for kk in range(nk):
                                fcl = s0 // 128 + kk
                                K.pe.op(lambda e, wt=wt, kk=kk, fcl=fcl, ps=ps: e.matmul(
                                    ps[:, :w], lhsT=wt[:, kk * 128:(kk + 1) * 128], rhs=big[:, fcl, :w],
                                    start=(fcl == 0), stop=(fcl == HC - 1)), reads=[wtb, bigb], writes=[psb])
                        K.dve.op(lambda e, oc=oc, ps=ps: e.scalar_tensor_tensor(
                            out=xt[:, oc, :w], in0=ps[:, :w], scalar=self.modv[:, l, 5 * KC + oc, which:which + 1], in1=xt[:, oc, :w],
                            op0=ALU.mult, op1=ALU.add), reads=[psb, self.modb, xtb], writes=[xtb])
                K.pool.dma(self.X[:, :, t0:t0 + w].rearrange("c p t -> p c t"), xt[:, :, :w], xtb, load=False)

            for _ in front(0):
                pass
            for ti in range(len(tiles2)):
                nxt = front(ti + 1) if ti + 1 < len(tiles2) else None
                cnt = [0]

                def tick(nxt=nxt, cnt=cnt):
                    cnt[0] += 1
                    if nxt is not None and cnt[0] % 3 == 0:
                        next(nxt, None)
                back(ti, tick)
                if nxt is not None:
                    for _ in nxt:
                        pass

    def phase_s(self, l):
        cfg, K, nc = self.cfg, self.K, self.nc
        BC, NP, NT, CTX = cfg.BC, cfg.NPAIR, cfg.NT, cfg.CTX
        TS = 256
        NQ = NP // 4
        PI = math.pi
        wp, pp = self.wplan, self.pplan
        LOG = int(math.log2(TS))
        with ExitStack() as st:
            W, Wb = self.sb(st, "s_W", [128, 24, 2, NP], F32)
            Bt, Btb = self.sb(st, "s_Bt", [128, 2, 2, 2, NQ, 128], BF16)
            Ct, Ctb = self.sb(st, "s_Ct", [128, 2, NP, 2, 64], BF16)
            R, Rb = self.sb(st, "s_R", [128, 2 * NP, 2, TS], F32)
            carry, carryb_ = self.sb(st, "s_carry", [128, 2, NP, 2], F32)
            st2 = ExitStack()
            At, Atb = self.sb(st2, "s_A", [128, 2, NP, 3], F32)
            Bt32, Bt32b = self.sb(st2, "s_B32", [128, 2, NP, 32], F32)
            Ct32, Ct32b = self.sb(st2, "s_C32", [128, 2, NP, 2, 64], F32)
            Wi, Wib = self.sb(st2, "s_Wi", [128, 2, NP], mybir.dt.int32)
            bbar, bbarb = self.sb(st2, "s_bbar", [128, 2, 2, NP, 32], F32)
            tb, tbb = self.sb(st2, "s_tb", [128, 2, NP, 32], F32)
            bbv, bbvb = self.sb(st2, "s_bbv", [128, 2, 2, 2, NP, 32], F32)
            E, Eb = self.sb(st2, "s_E", [128, 4, 2 * NP], F32)
            rt1, rt1b = self.sb(st2, "s_rt1", [128, 2 * NP, TS // 2], F32)
            rt2, rt2b = self.sb(st2, "s_rt2", [128, 2 * NP, TS // 2], F32)
            carryb = [[Buf(f"carry{d}_{k}") for k in range(NP)] for d in range(2)]
            K.pool.dma(At[:], self.s5A[:, l], Atb)
            K.pool.dma(Bt32[:], self.s5B[:, l], Bt32b)
            K.pool.dma(Ct32[:], self.s5C[:, l], Ct32b)
            a_re, a_im, ls = At[:, :, :, 0], At[:, :, :, 1], At[:, :, :, 2]
            (DT, XR, XI, MAG, T0, TF, RS, RC, SIN, COS, LR, LI, NR, DEN, CR, CI, TA, TB_, XC) = [W[:, i] for i in range(19)]

            def dv(fn, rd=(Atb, Wb), wr=(Wb,)):
                K.dve.op(fn, reads=list(rd), writes=list(wr))

            def ac(fn, rd=(Atb, Wb), wr=(Wb,)):
                K.act.op(fn, reads=list(rd), writes=list(wr))
            ac(lambda e: e.activation(out=DT, in_=ls, func=AF.Exp))
            dv(lambda e: e.tensor_tensor(out=XR, in0=a_re, in1=DT, op=ALU.mult))
            dv(lambda e: e.tensor_tensor(out=XI, in0=a_im, in1=DT, op=ALU.mult))
            ac(lambda e: e.activation(out=MAG, in_=XR, func=AF.Exp))

            def reduce_sin(dst, x_ap):
                dv(lambda e: e.tensor_scalar(out=T0, in0=x_ap, scalar1=1.0 / (2 * PI), scalar2=None, op0=ALU.mult))
                dv(lambda e: e.tensor_copy(out=Wi[:], in_=T0), wr=(Wib,))
                dv(lambda e: e.tensor_copy(out=TF, in_=Wi[:]), rd=(Wib,))
                dv(lambda e: e.scalar_tensor_tensor(out=RS, in0=TF, scalar=-2 * PI, in1=x_ap, op0=ALU.mult, op1=ALU.add))
                dv(lambda e: e.tensor_scalar(out=RS, in0=RS, scalar1=-PI, scalar2=PI, op0=ALU.max, op1=ALU.min))
                ac(lambda e: e.activation(out=dst, in_=RS, func=AF.Sin))
            reduce_sin(SIN, XI)
            dv(lambda e: e.tensor_scalar(out=XC, in0=XI, scalar1=PI / 2, scalar2=None, op0=ALU.add))
            reduce_sin(COS, XC)
            dv(lambda e: e.tensor_tensor(out=LR, in0=MAG, in1=COS, op=ALU.mult))
            dv(lambda e: e.tensor_tensor(out=LI, in0=MAG, in1=SIN, op=ALU.mult))
            dv(lambda e: e.tensor_scalar(out=NR, in0=LR, scalar1=-1.0, scalar2=None, op0=ALU.add))
            dv(lambda e: e.tensor_tensor(out=DEN, in0=a_re, in1=a_re, op=ALU.mult))
            dv(lambda e: e.tensor_tensor(out=TA, in0=a_im, in1=a_im, op=ALU.mult))
            dv(lambda e: e.tensor_tensor(out=DEN, in0=DEN, in1=TA, op=ALU.add))
            dv(lambda e: e.reciprocal(out=DEN, in_=DEN))
            dv(lambda e: e.tensor_tensor(out=TA, in0=NR, in1=a_re, op=ALU.mult))
            dv(lambda e: e.tensor_tensor(out=TB_, in0=LI, in1=a_im, op=ALU.mult))
            dv(lambda e: e.tensor_tensor(out=CR, in0=TA, in1=TB_, op=ALU.add))
            dv(lambda e: e.tensor_tensor(out=CR, in0=CR, in1=DEN, op=ALU.mult))
            dv(lambda e: e.tensor_tensor(out=TA, in0=LI, in1=a_re, op=ALU.mult))
            dv(lambda e: e.tensor_tensor(out=TB_, in0=NR, in1=a_im, op=ALU.mult))
            dv(lambda e: e.tensor_tensor(out=CI, in0=TA, in1=TB_, op=ALU.subtract))
            dv(lambda e: e.tensor_tensor(out=CI, in0=CI, in1=DEN, op=ALU.mult))
            b_re, b_im = Bt32[:, 0], Bt32[:, 1]
            for d in range(2):
                crb = CR[:, d, :].unsqueeze(2).to_broadcast([128, NP, 32])
                cib = CI[:, d, :].unsqueeze(2).to_broadcast([128, NP, 32])
                rd = (Bt32b, Wb, bbarb, tbb)
                dv(lambda e, d=d, crb=crb: e.tensor_tensor(out=bbar[:, d, 0], in0=b_re, in1=crb, op=ALU.mult), rd=rd, wr=(bbarb,))
                dv(lambda e, cib=cib: e.tensor_tensor(out=tb[:, 0], in0=b_im, in1=cib, op=ALU.mult), rd=rd, wr=(tbb,))
                dv(lambda e, d=d: e.tensor_tensor(out=bbar[:, d, 0], in0=bbar[:, d, 0], in1=tb[:, 0], op=ALU.subtract), rd=rd, wr=(bbarb,))
                dv(lambda e, d=d, crb=crb: e.tensor_tensor(out=bbar[:, d, 1], in0=b_im, in1=crb, op=ALU.mult), rd=rd, wr=(bbarb,))
                dv(lambda e, cib=cib: e.tensor_tensor(out=tb[:, 1], in0=b_re, in1=cib, op=ALU.mult), rd=rd, wr=(tbb,))
                dv(lambda e, d=d: e.tensor_tensor(out=bbar[:, d, 1], in0=bbar[:, d, 1], in1=tb[:, 1], op=ALU.add), rd=rd, wr=(bbarb,))
            K.dve.op(lambda e: e.memset(bbv[:], 0.0), writes=[bbvb])
            for v in range(2):
                K.dve.op(lambda e, v=v: e.tensor_copy(out=bbv[:, v, :, :, v::2, :], in_=bbar[:, :, :, v::2, :]), reads=[bbarb, bbvb], writes=[bbvb])
            for v in range(2):
              for d in range(2):
                for ri in range(2):
                    for q in range(NQ):
                        ps, psb = self.next_psum()
                        K.pe.op(lambda e, v=v, d=d, ri=ri, q=q, ps=ps: e.transpose(
                            out=ps[:, :128], in_=bbv[:, v, d, ri, q * 4:(q + 1) * 4, :].rearrange("p a b -> p (a b)"), identity=self.identt[:]),
                            reads=[bbvb, self.identb], writes=[psb])
                        K.act.op(lambda e, v=v, d=d, ri=ri, q=q, ps=ps: e.copy(out=Bt[:, v, d, ri, q, :], in_=ps[:, :128]), reads=[psb], writes=[Btb])
            K.act.op(lambda e: e.copy(out=Ct[:, :, :, 0, :], in_=Ct32[:, :, :, 0, :]), reads=[Ct32b], writes=[Ctb])
            K.act.op(lambda e: e.mul(out=Ct[:, :, :, 1, :], in_=Ct32[:, :, :, 1, :], mul=-1.0), reads=[Ct32b], writes=[Ctb])
            N2 = 2 * NP
            cosf = COS.rearrange("p d k -> p (d k)")
            sinf = SIN.rearrange("p d k -> p (d k)")
            dv(lambda e: e.tensor_copy(out=E[:, 0, :], in_=cosf), wr=(Eb,))
            dv(lambda e: e.tensor_copy(out=E[:, 1, :], in_=sinf), wr=(Eb,))
            dv(lambda e: e.tensor_copy(out=R[:, :, 0, 0], in_=cosf), wr=(Rb,))
            dv(lambda e: e.tensor_copy(out=R[:, :, 1, 0], in_=sinf), wr=(Rb,))
            rdR = (Rb, Eb, rt1b, rt2b)
            for kk in range(LOG):
                n = 1 << kk
                er = E[:, 0, :].unsqueeze(2).to_broadcast([128, N2, n])
                ei = E[:, 1, :].unsqueeze(2).to_broadcast([128, N2, n])
                sre, sim_ = R[:, :, 0, 0:n], R[:, :, 1, 0:n]
                dre, dim_ = R[:, :, 0, n:2 * n], R[:, :, 1, n:2 * n]
                a1, a2 = rt1[:, :, 0:n], rt2[:, :, 0:n]
                dv(lambda e, a1=a1, sre=sre, er=er: e.tensor_tensor(out=a1, in0=sre, in1=er, op=ALU.mult), rd=rdR, wr=(rt1b,))
                dv(lambda e, a2=a2, sim_=sim_, ei=ei: e.tensor_tensor(out=a2, in0=sim_, in1=ei, op=ALU.mult), rd=rdR, wr=(rt2b,))
                dv(lambda e, a1=a1, a2=a2, dre=dre: e.tensor_tensor(out=dre, in0=a1, in1=a2, op=ALU.subtract), rd=rdR, wr=(Rb,))
                dv(lambda e, a1=a1, sre=sre, ei=ei: e.tensor_tensor(out=a1, in0=sre, in1=ei, op=ALU.mult), rd=rdR, wr=(rt1b,))
                dv(lambda e, a2=a2, sim_=sim_, er=er: e.tensor_tensor(out=a2, in0=sim_, in1=er, op=ALU.mult), rd=rdR, wr=(rt2b,))
                dv(lambda e, a1=a1, a2=a2, dim_=dim_: e.tensor_tensor(out=dim_, in0=a1, in1=a2, op=ALU.add), rd=rdR, wr=(Rb,))
                dv(lambda e: e.tensor_tensor(out=E[:, 2, :], in0=E[:, 0, :], in1=E[:, 0, :], op=ALU.mult), rd=(Eb,), wr=(Eb,))
                dv(lambda e: e.tensor_tensor(out=E[:, 3, :], in0=E[:, 1, :], in1=E[:, 1, :], op=ALU.mult), rd=(Eb,), wr=(Eb,))
                dv(lambda e: e.tensor_tensor(out=E[:, 1, :], in0=E[:, 0, :], in1=E[:, 1, :], op=ALU.mult), rd=(Eb,), wr=(Eb,))
                dv(lambda e: e.tensor_scalar(out=E[:, 1, :], in0=E[:, 1, :], scalar1=2.0, scalar2=None, op0=ALU.mult), rd=(Eb,), wr=(Eb,))
                dv(lambda e: e.tensor_tensor(out=E[:, 0, :], in0=E[:, 2, :], in1=E[:, 3, :], op=ALU.subtract), rd=(Eb,), wr=(Eb,))
            K.dve.op(lambda e: e.memset(carry[:], 0.0), writes=[b for row in carryb for b in row])
            K.barrier()
            st2.close()
            wglu, wglub = self.sb(st, "s_wglu", [128, 2 * BC, BC, 128], BF16)
            ubufs = [(self.sb(st, f"s_u32{i}", [128, BC, TS], F32), self.sb(st, f"s_ubf{i}", [128, BC, TS], BF16),
                      self.sb(st, f"s_yb{i}", [128, BC, TS], F32)) for i in range(2)]
            yt, ytb = self.sb(st, "s_yt", [128, BC, TS], F32)
            yg, ygb = self.sb(st, "s_yg", [128, BC, TS], BF16)
            brd, brdb = self.sb(st, "s_brd", [128, BC, TS], BF16)
            sig, sigb = self.sb(st, "s_sig", [128, TS], F32)
            sets = []
            for i in range(6):
                d = {}
                for nm in ("t1", "t2", "bt", "st", "s32"):
                    d[nm] = self.sb(st, f"s_{nm}{i}", [128, 2, TS], F32)
                d["sbf"] = self.sb(st, f"s_sbf{i}", [128, 2, TS], BF16)
                sets.append(d)
            K.sp.dma(wglu[:], self.wb[l][:, wp["glu"]:wp["glu"] + 2 * BC * BC * 128], wglub)
            ctx_subs = [(t, TS) for t in range(0, CTX, TS)]
            lat_subs = [(t, TS) for t in range(CTX, NT, TS)]
            o_d, _ = pp["s5_d"]
            NSET = len(sets)
            ybd = {t: Buf(f"ybd{t}") for (t, _) in ctx_subs + lat_subs}
            items = []
            subs_all = []
            for dirn in (1, 0):
                order = (ctx_subs + lat_subs) if dirn == 0 else (ctx_subs[::-1] + lat_subs[::-1])
                for (t0, w) in order:
                    si = len(subs_all)
                    subs_all.append((dirn, t0))
                    for k in range(NP):
                        items.append((dirn, si, t0, k, k == 0, k == NP - 1))

            def sub_bufs(si):
                return ubufs[si % 2]

            def prologue(si):
                dirn, t0 = subs_all[si]
                (u32, u32b), (ubf, ubfb), (ybt, ybtb) = sub_bufs(si)
                K.pool.dma(u32[:], self.U[:, :, t0:t0 + TS].rearrange("c p t -> p c t"), u32b)
                if dirn == 0:
                    K.pool.dma(ybt[:], self.YB[:, :, t0:t0 + TS].rearrange("c p t -> p c t"), ybtb, reads=[ybd[t0]])
                    K.act.op(lambda e: e.copy(out=ubf[:], in_=u32[:]), reads=[u32b], writes=[ubfb])
                else:
                    K.act.op(lambda e: e.copy(out=ubf[:], in_=u32[:, :, ::-1]), reads=[u32b], writes=[ubfb])

            def psy_of(si):
                b = 4 + 2 * (si % 2)
                return [self.ps[b], self.ps[b + 1]], [self.psb[b], self.psb[b + 1]]

            def stage(sn, idx):
                dirn, si, t0, k, first, lastk = items[idx]
                (u32, u32b), (ubf, ubfb), (ybt, ybtb) = sub_bufs(si)
                q, slot = k // 4, k % 4
                rows = slice(64 * (slot // 2), 64 * (slot // 2) + 64)
                S_ = sets[idx % NSET]
                (t1, t1b), (t2, t2b), (bt, btb), (stt, sttb) = S_["t1"], S_["t2"], S_["bt"], S_["st"]
                (t3, t3b), (t4, t4b), (s32, s32b), (sbf, sbfb) = S_["t1"], S_["t2"], S_["s32"], S_["sbf"]
                rr = R[:, dirn * NP + k, 0, :].unsqueeze(1).to_broadcast([128, 2, TS])
                rim = R[:, dirn * NP + k, 1, :].unsqueeze(1).to_broadcast([128, 2, TS])
                if sn == 0:
                    if first and si == 0:
                        prologue(0)
                    if k == min(NP - 1, 4) and si + 1 < len(subs_all):
                        prologue(si + 1)
                    i = self.psum_i
                    self.psum_i = (i + 1) % 4
                    ps, psb = self.ps[i], self.psb[i]
                    for ri in range(2):
                        K.pe.op(lambda e, ri=ri: e.matmul(
                            ps[:, ri * TS:(ri + 1) * TS], lhsT=Bt[rows, slot % 2, dirn, ri, q, :], rhs=ubf[rows, q, :], start=True, stop=True),
                            reads=[Btb, ubfb], writes=[psb])
                    bu = ps[:, :2 * TS].rearrange("p (r t) -> p r t", r=2)
                    K.dve.op(lambda e: e.tensor_tensor(out=t1[:], in0=bu, in1=rr, op=ALU.mult), reads=[psb, Rb], writes=[t1b])
                    K.dve.op(lambda e: e.tensor_tensor(out=t2[:], in0=bu[:, ::-1, :], in1=rim, op=ALU.mult), reads=[psb, Rb], writes=[t2b])
                elif sn == 1:
                    K.pool.op(lambda e: e.tensor_tensor(out=bt[:, 0, :], in0=t1[:, 0, :], in1=t2[:, 0, :], op=ALU.add), reads=[t1b, t2b], writes=[btb])
                    K.pool.op(lambda e: e.tensor_tensor(out=bt[:, 1, :], in0=t1[:, 1, :], in1=t2[:, 1, :], op=ALU.subtract), reads=[t1b, t2b], writes=[btb])
                elif sn == 2:
                    dec = W[:, 3, dirn, k:k + 1].to_broadcast([128, TS])
                    for ri in range(2):
                        K.dve.op(lambda e, ri=ri: e.tensor_tensor_scan(
                            out=stt[:, ri, :], data0=dec, data1=bt[:, ri, :], initial=carry[:, dirn, k, ri:ri + 1], op0=ALU.mult, op1=ALU.add),
                            reads=[btb, Wb, carryb[dirn][k]], writes=[sttb])
                elif sn == 3:
                    eD1 = K.dve if getattr(self, "s5_demod_dve", True) else K.pool
                    eD2 = K.dve if getattr(self, "s5_comb_dve", True) else K.pool
                    eD1.op(lambda e: e.tensor_tensor(out=t3[:], in0=stt[:], in1=rr, op=ALU.mult), reads=[sttb, Rb], writes=[t3b])
                    eD1.op(lambda e: e.tensor_tensor(out=t4[:], in0=stt[:, ::-1, :], in1=rim, op=ALU.mult), reads=[sttb, Rb], writes=[t4b])
                    eD2.op(lambda e: e.tensor_tensor(out=s32[:, 0, :], in0=t3[:, 0, :], in1=t4[:, 0, :], op=ALU.subtract), reads=[t3b, t4b], writes=[s32b])
                    eD2.op(lambda e: e.tensor_tensor(out=s32[:, 1, :], in0=t3[:, 1, :], in1=t4[:, 1, :], op=ALU.add), reads=[t3b, t4b], writes=[s32b])
                elif sn == 4:
                    K.act.op(lambda e: e.copy(out=carry[:, dirn, k, :], in_=s32[:, :, TS - 1]), reads=[s32b], writes=[carryb[dirn][k]])
                    if dirn == 0:
                        K.act.op(lambda e: e.copy(out=sbf[:], in_=s32[:]), reads=[s32b], writes=[sbfb])
                    else:
                        K.act.op(lambda e: e.copy(out=sbf[:], in_=s32[:, :, ::-1]), reads=[s32b], writes=[sbfb])
                    psy, psyb = psy_of(si)
                    py, pyb = psy[q // 2], psyb[q // 2]
                    c0 = (q % 2) * TS
                    for ri in range(2):
                        K.pe.op(lambda e, ri=ri: e.matmul(
                            py[rows, c0:c0 + TS], lhsT=Ct[:, dirn, k, ri, :], rhs=sbf[:, ri, :],
                            start=(ri == 0 and slot % 2 == 0), stop=(ri == 1 and slot % 2 == 1)), reads=[Ctb, sbfb], writes=[pyb])
                    if lastk:
                        epilogue(si)

            def epilogue(si):
                dirn, t0 = subs_all[si]
                (u32, u32b), (ubf, ubfb), (ybt, ybtb) = sub_bufs(si)
                psy, psyb = psy_of(si)
                if dirn == 1:
                    for q in range(BC):
                        K.act.op(lambda e, q=q: e.copy(out=ybt[:, q, :], in_=psy[q // 2][:, (q % 2) * TS:(q % 2 + 1) * TS]),
                                 reads=[psyb[q // 2]], writes=[ybtb])
                    K.pool.dma(self.YB[:, :, t0:t0 + TS].rearrange("c p t -> p c t"), ybt[:], ybtb, load=False, writes=[ybd[t0]])
                else:
                    for q in range(BC):
                        K.dve.op(lambda e, q=q: e.tensor_tensor(out=yt[:, q, :], in0=psy[q // 2][:, (q % 2) * TS:(q % 2 + 1) * TS],
                                                                in1=ybt[:, q, :], op=ALU.add), reads=[psyb[q // 2], ybtb], writes=[ytb])
                        K.dve.op(lambda e, q=q: e.scalar_tensor_tensor(out=yt[:, q, :], in0=u32[:, q, :], scalar=self.pv[:, l, o_d + q:o_d + q + 1],
                                                                       in1=yt[:, q, :], op0=ALU.mult, op1=ALU.add),
                                 reads=[u32b, self.pvb, ytb], writes=[ytb])
                        K.act.op(lambda e, q=q: e.activation(out=yg[:, q, :], in_=yt[:, q, :], func=AF.Gelu_apprx_tanh), reads=[ytb], writes=[ygb])
                    for oc in range(BC):
                        i = self.psum_i
                        self.psum_i = (i + 1) % 4
                        psg, psgb = self.ps[i], self.psb[i]
                        self.mm_group(psg[:, :TS], psgb, [wglu[:, BC + oc, bc, :] for bc in range(BC)], [yg[:, bc, :] for bc in range(BC)], [wglub, ygb])
                        K.act.op(lambda e, psg=psg: e.activation(out=sig[:, :], in_=psg[:, :TS], func=AF.Sigmoid), reads=[psgb], writes=[sigb])
                        i = self.psum_i
                        self.psum_i = (i + 1) % 4
                        psa, psab = self.ps[i], self.psb[i]
                        self.mm_group(psa[:, :TS], psab, [wglu[:, oc, bc, :] for bc in range(BC)], [yg[:, bc, :] for bc in range(BC)], [wglub, ygb])
                        K.dve.op(lambda e, oc=oc, psa=psa: e.tensor_tensor(out=brd[:, oc, :], in0=psa[:, :TS], in1=sig[:, :], op=ALU.mult),
                                 reads=[psab, sigb], writes=[brdb])
                    K.pool.dma(self.BR[3, :, :, t0:t0 + TS].rearrange("c p t -> p c t"), brd[:], brdb, load=False)

            self.psum_i = 0
            NST = 5
            for step in range(len(items) + NST - 1):
                for sn in reversed(range(NST)):
                    idx = step - sn
                    if 0 <= idx < len(items):
                        stage(sn, idx)
            self.psum_i = 0

    def phase_final(self):
        cfg, K, nc = self.cfg, self.K, self.nc
        KC, T, CTX = cfg.KC, cfg.T, cfg.CTX
        L = cfg.DEPTH
        o_g, _ = self.pplan["n1g"]
        with ExitStack() as st:
            xt, xtb = self.sb(st, "f_xt", [128, KC, T], F32)
            ot, otb = self.sb(st, "f_ot", [128, KC, T], F32)
            sq = [self.sb(st, f"f_sq{i}", [128, T], F32) for i in range(2)]
            rs, rsb = self.sb(st, "f_rs", [128, T], F32)
            eps, epsb = self.sb(st, "f_eps", [128, 1], F32)
            K.dve.op(lambda e: e.memset(eps[:], cfg.EPS), writes=[epsb])
            for (t0, w, is_ctx) in cfg.tiles:
                if is_ctx:
                    continue
                K.pool.dma(xt[:, :, :w], self.X[:, :, t0:t0 + w].rearrange("c p t -> p c t"), xtb)
                ps, psb = self.next_psum()
                for kc in range(KC):
                    s_, sb_ = sq[kc % 2]
                    K.act.op(lambda e, kc=kc, s_=s_: e.activation(out=s_[:, :w], in_=xt[:, kc, :w], func=AF.Square), reads=[xtb], writes=[sb_])
                    K.pe.op(lambda e, kc=kc, s_=s_: e.matmul(ps[:, :w], lhsT=self.ones[:], rhs=s_[:, :w], start=(kc == 0), stop=(kc == KC - 1)),
                            reads=[self.onesb, sb_], writes=[psb])
                K.act.op(lambda e: e.activation(out=rs[:, :w], in_=ps[:, :w], func=AF.Sqrt, scale=1.0 / cfg.D, bias=eps[:, 0:1]),
                         reads=[psb, epsb], writes=[rsb])
                K.dve.op(lambda e: e.reciprocal(out=rs[:, :w], in_=rs[:, :w]), reads=[rsb], writes=[rsb])
                for kc in range(KC):
                    eng = K.dve
                    eng.op(lambda e, kc=kc: e.scalar_tensor_tensor(out=ot[:, kc, :w], in0=xt[:, kc, :w], scalar=self.pv[:, L, o_g + kc:o_g + kc + 1],
                                                                  in1=rs[:, :w], op0=ALU.mult, op1=ALU.mult),
                           reads=[xtb, rsb, self.pvb], writes=[otb])
                K.pool.dma(self.out[:, :, t0 - CTX:t0 - CTX + w].rearrange("c p t -> p c t"), ot[:, :, :w], otb, load=False)


_CACHE = {}


def kernel(**inputs):
    cfg = Cfg()
    inp = {k: np.asarray(v) for k, v in inputs.items()}
    if "nc" not in _CACHE:
        _CACHE["nc"] = Prog(cfg).build()
    nc = _CACHE["nc"]
    shared = prep_shared(cfg, inp)
    n = 8
    in_maps = []
    for c in range(n):
        m = dict(shared)
        m.update(prep_inputs(cfg, inp, c))
        in_maps.append(m)
    res = run_bass_kernel_spmd(nc, in_maps, core_ids=list(range(n)))
    out = np.empty((cfg.BATCH, cfg.SEQ, cfg.D), np.float32)
    for b in range(cfg.BATCH):
        o = np.asarray(res.results[b]["out"])
        out[b] = o.reshape(cfg.D, cfg.SEQ).T
    return out
```

```python
import math
import numpy as np
import concourse.bass as bass
import concourse.mybir as mybir
from concourse.bass_utils import run_bass_kernel_spmd

F32 = mybir.dt.float32
BF16 = mybir.dt.bfloat16
AF = mybir.ActivationFunctionType
ALU = mybir.AluOpType
AX = mybir.AxisListType


class Cfg:
    def __init__(self, D=2048, SEQ=8192, CTX=256, DEPTH=4, GRID_W=64, T=512, BATCH=4):
        self.D, self.SEQ, self.CTX, self.DEPTH, self.GRID_W, self.T, self.BATCH = D, SEQ, CTX, DEPTH, GRID_W, T, BATCH
        self.DB = D // 4
        self.KC = D // 128
        self.BC = self.DB // 128
        self.FF = 4 * D
        self.FFC = self.FF // 128
        self.HQ = self.DB // 64
        self.QPK = self.HQ // 2
        self.G = self.DB // 16
        self.NPAIR = self.G // 2
        self.NT = CTX + SEQ
        self.IN_COLS = 2 * self.DB + 2 * self.DB + self.DB + 256 + self.DB
        self.CONV_W = 31
        self.EPS = 1e-6
        BC = self.BC
        self.c_u = 0
        self.c_v = BC
        self.c_a = 2 * BC
        self.c_g = 3 * BC
        self.c_q = 4 * BC
        self.c_k = 5 * BC
        self.c_vv = 5 * BC + 1
        self.c_d = 5 * BC + 2
        self.NCI = 6 * BC + 2
        tiles = []
        s = 0
        while s < CTX:
            w = min(T, CTX - s)
            tiles.append((s, w, True))
            s += w
        while s < self.NT:
            w = min(T, self.NT - s)
            tiles.append((s, w, False))
            s += w
        self.tiles = tiles
        self.YPAD = 16
        self.NY = self.NT + 4 * self.YPAD


class Buf:
    __slots__ = ("name", "w", "r", "dsem", "dcnt", "dq")

    def __init__(self, name):
        self.name = name
        self.w = None
        self.r = []
        self.dsem = None
        self.dcnt = 0


class Eng:
    def __init__(self, K, eng, name, sem):
        self.K, self.eng, self.name, self.sem = K, eng, name, sem
        self.count = 0
        self.waited = {}

    def _wait(self, ev):
        if ev is None:
            return
        sem, val = ev
        if sem is self.sem and not self.K.same_engine_sync:
            return
        if sem is self.sem and self.name == "pe":
            return
        if self.waited.get(id(sem), 0) >= val:
            return
        self.eng.wait_ge(sem, val)
        self.waited[id(sem)] = val

    def deps(self, reads, writes):
        evs = {}

        def add(ev):
            if ev is None:
                return
            k = id(ev[0])
            if k not in evs or evs[k][1] < ev[1]:
                evs[k] = ev
        for b in reads:
            add(b.w)
        for b in writes:
            add(b.w)
            for ev in b.r:
                add(ev)
        for ev in evs.values():
            self._wait(ev)

    def op(self, fn, reads=(), writes=()):
        self.deps(reads, writes)
        ins = fn(self.eng)
        self.count += 1
        ins.then_inc(self.sem, 1)
        ev = (self.sem, self.count)
        for b in reads:
            b.r = [x for x in b.r if x[0] is not ev[0]] + [ev]
        for b in writes:
            b.w = ev
            b.r = []
        return ins

    def dma(self, out, in_, sbuf, reads=(), writes=(), load=True, **kw):
        K = self.K
        if sbuf.dsem is None:
            fl = K.free_sems.setdefault(self.name, [])
            if fl:
                sbuf.dsem, sbuf.dcnt = fl.pop()
            else:
                sbuf.dsem, sbuf.dcnt = K.new_sem(f"dsem{len(K._stack)}"), 0
            sbuf.dq = self.name
        assert sbuf.dq == self.name, "a buffer's DMAs must stay on one queue"
        rd = list(reads) + ([] if load else [sbuf])
        wr = list(writes) + ([sbuf] if load else [])
        self.deps(rd, wr)
        ins = self.eng.dma_start(out=out, in_=in_, **kw)
        sbuf.dcnt += 1
        ins.then_inc(sbuf.dsem, 16)
        ev = (sbuf.dsem, 16 * sbuf.dcnt)
        for b in rd:
            b.r = [x for x in b.r if x[0] is not ev[0]] + [ev]
        for b in wr:
            b.w = ev
            b.r = []
        K.dma_bufs[id(sbuf)] = sbuf
        return ins


class Kern:
    def __init__(self, nc, same_engine_sync=True):
        self.nc = nc
        self.same_engine_sync = same_engine_sync
        self.sems = []
        self.dma_bufs = {}
        self.free_sems = {}
        self._stack = []

    def new_sem(self, name):
        cm = self.nc.semaphore(name)
        h = cm.__enter__()
        self._stack.append(cm)
        return h

    def make_engines(self, block_engs):
        self.pe = Eng(self, block_engs["tensor"], "pe", self.new_sem("s_pe"))
        self.act = Eng(self, block_engs["scalar"], "act", self.new_sem("s_act"))
        self.dve = Eng(self, block_engs["vector"], "dve", self.new_sem("s_dve"))
        self.pool = Eng(self, block_engs["gpsimd"], "pool", self.new_sem("s_pool"))
        self.sp = Eng(self, block_engs["sync"], "sp", self.new_sem("s_sp"))
        self.engs = [self.pe, self.act, self.dve, self.pool, self.sp]

    def barrier(self):
        for e in self.engs:
            for o in self.engs:
                if o is not e and o.count > 0:
                    e._wait((o.sem, o.count))
            for b in self.dma_bufs.values():
                if b.dcnt:
                    e._wait((b.dsem, 16 * b.dcnt))
        for e in self.engs:
            if e.count > 20000:
                e.sem = self.new_sem(f"s_{e.name}_{len(self._stack)}")
                e.count = 0
        for b in self.dma_bufs.values():
            self.free_sems[b.dq].append((b.dsem, b.dcnt))
            b.dsem = None
        self.dma_bufs = {}

    def close(self):
        for cm in reversed(self._stack):
            cm.__exit__(None, None, None)


def lhsT_chunks(W, ncols=128):
    K, N = W.shape
    return np.ascontiguousarray(W.reshape(K // 128, 128, N // ncols, ncols).transpose(1, 2, 0, 3))


def weight_plan(cfg):
    KC, BC, FFC, DB = cfg.KC, cfg.BC, cfg.FFC, cfg.DB
    plan = {}
    off = 0

    def add(name, n):
        nonlocal off
        plan[name] = off
        off += n
    add("inF", (6 * BC + 2) * KC * 128)
    add("inV1", KC * DB)
    add("inV2", KC * 128)
    add("wsT", BC * 128)
    add("glu", 2 * BC * BC * 128)
    add("gb", KC * (4 * KC * 128 + 4 * BC * 128))
    add("out", KC * KC * 128)
    add("ff1", FFC * KC * 128)
    add("ff2", KC * FFC * 128)
    tot = off
    tot = (tot + 8191) // 8192 * 8192
    plan["_total"] = tot
    return plan


def swap16(W):
    K, N = W.shape
    return np.ascontiguousarray(W.reshape(K, N // 32, 2, 16)[:, :, ::-1, :].reshape(K, N))


def pack_layer_weights(cfg, inp, l):
    KC, BC, FFC, DB, D = cfg.KC, cfg.BC, cfg.FFC, cfg.DB, cfg.D
    plan = weight_plan(cfg)
    flat = np.zeros((128, plan["_total"]), np.float32)

    def put(name, arr):
        a = arr.reshape(128, -1)
        flat[:, plan[name]:plan[name] + a.shape[1]] = a
    w_in = inp["w_in"][l]
    cA, cB, cQ, cK, cV, cD = 0, 2 * DB, 4 * DB, 5 * DB, 5 * DB + 128, 5 * DB + 256
    cols = []
    for i in range(BC):
        cols.append(w_in[:, cA + i * 128: cA + (i + 1) * 128])
    for i in range(BC):
        cols.append(w_in[:, cB + DB + i * 128: cB + DB + (i + 1) * 128])
        cols.append(w_in[:, cB + i * 128: cB + (i + 1) * 128])
    wq = w_in[:, cQ:cQ + DB]
    wqs = swap16(wq)
    for i in range(BC):
        cols.append(wq[:, i * 128:(i + 1) * 128])
        cols.append(wqs[:, i * 128:(i + 1) * 128])
    wk = w_in[:, cK:cK + 128]
    cols.append(wk)
    cols.append(swap16(wk))
    for i in range(BC):
        cols.append(w_in[:, cD + i * 128: cD + (i + 1) * 128])
    put("inF", lhsT_chunks(np.concatenate(cols, axis=1)))
    put("inV1", lhsT_chunks(w_in[:, cA + DB: cA + 2 * DB], ncols=DB))
    put("inV2", lhsT_chunks(w_in[:, cV:cV + 128]))
    put("wsT", np.ascontiguousarray(inp["gmlp_ws"][l].transpose(2, 0, 1)))
    put("glu", lhsT_chunks(inp["s5_w_glu"][l]))
    g4 = np.stack([lhsT_chunks(inp["w_gate"][l, k]) for k in range(4)], axis=2)
    b4 = np.stack([lhsT_chunks(inp["w_branch"][l, k]) for k in range(4)], axis=2)
    gb = np.concatenate([g4.reshape(128, KC, -1), b4.reshape(128, KC, -1)], axis=2)
    put("gb", gb)
    put("out", lhsT_chunks(inp["w_out"][l]))
    put("ff1", lhsT_chunks(inp["w_ff1"][l]))
    put("ff2", lhsT_chunks(inp["w_ff2"][l]))
    return flat


def fm(v):
    return np.ascontiguousarray(v.reshape(-1, 128).T)


def pvec_plan(cfg):
    KC, BC = cfg.KC, cfg.BC
    plan = {}
    off = 0
    for name, n in (("b_mod", 6 * KC), ("n1g", KC), ("n2g", KC), ("b_gate", 4 * KC), ("conv_w", BC * 31),
                    ("conv_b", BC), ("cln_g", BC), ("cln_b", BC), ("s5_d", BC)):
        plan[name] = (off, n)
        off += n
    plan["_total"] = off
    return plan


def rowv_plan(cfg):
    DB, BC = cfg.DB, cfg.BC
    plan = {}
    off = 0
    for name, n in (("gln_g", DB), ("gln_b", DB), ("gbs", BC * 128), ("sink", cfg.HQ)):
        plan[name] = (off, n)
        off += n
    plan["_total"] = off
    return plan


def prep_inputs(cfg, inp, core):
    L, KC, BC, D, DB = cfg.DEPTH, cfg.KC, cfg.BC, cfg.D, cfg.DB
    b = core % cfg.BATCH
    m = {}
    xcat = np.concatenate([inp["ctx"][b], inp["x"][b]], axis=0)
    m["xT"] = np.ascontiguousarray(xcat.T.reshape(KC, 128, cfg.NT))
    cond = np.stack([fm(inp["c_ctx"]), fm(inp["c"][b])], axis=2)
    m["cond"] = np.ascontiguousarray(cond)
    return m


def prep_shared(cfg, inp):
    L, KC, BC, D, DB = cfg.DEPTH, cfg.KC, cfg.BC, cfg.D, cfg.DB
    m = {}
    m["wall"] = np.stack([pack_layer_weights(cfg, inp, l) for l in range(L)], axis=0)
    m["wmod"] = np.stack([lhsT_chunks(inp["w_mod"][l]) for l in range(L)], axis=0)
    pp = pvec_plan(cfg)
    pv = np.zeros((128, L + 1, pp["_total"]), np.float32)
    for l in range(L):
        def put(name, a):
            o, n = pp[name]
            pv[:, l, o:o + n] = a.reshape(128, n)
        put("b_mod", fm(inp["b_mod"][l]))
        put("n1g", fm(inp["norm1_g"][l]))
        put("n2g", fm(inp["norm2_g"][l]))
        put("b_gate", np.stack([fm(inp["b_gate"][l, k]) for k in range(4)], axis=1))
        cw = inp["conv_w"][l]
        put("conv_w", np.ascontiguousarray(cw.T.reshape(BC, 128, 31).transpose(1, 0, 2)))
        put("conv_b", fm(inp["conv_b"][l]))
        put("cln_g", fm(inp["conv_ln_g"][l]))
        put("cln_b", fm(inp["conv_ln_b"][l]))
        put("s5_d", fm(inp["s5_d"][l]))
    o, n = pp["n1g"]
    pv[:, L, o:o + n] = fm(inp["final_g"])
    m["pvec"] = pv
    rp = rowv_plan(cfg)
    rv = np.zeros((128, L, rp["_total"]), np.float32)
    for l in range(L):
        for name, a in (("gln_g", inp["gmlp_ln_g"][l]), ("gln_b", inp["gmlp_ln_b"][l]),
                        ("gbs", inp["gmlp_bs"][l].reshape(-1)), ("sink", inp["attn_sink"][l])):
            o, n = rp[name]
            rv[:, l, o:o + n] = np.broadcast_to(a.reshape(1, n), (128, n))
    m["rowv"] = rv
    half = 32
    inv = (10000.0 ** (-np.arange(0, half, 2, dtype=np.float32) / half)).astype(np.float32)
    pos = np.arange(cfg.SEQ)
    row = (pos // cfg.GRID_W).astype(np.float32)
    col = (pos % cfg.GRID_W).astype(np.float32)
    cosT = np.ones((128, cfg.NT), np.float32)
    sinT = np.zeros((128, cfg.NT), np.float32)
    for p in range(128):
        d = p % 64
        blk, j = d // 32, d % 32
        ang = ((row if blk == 0 else col) * inv[j % 16]).astype(np.float32)
        cosT[p, cfg.CTX:] = np.cos(ang)
        sinT[p, cfg.CTX:] = np.sin(ang) * (-1.0 if j < 16 else 1.0)
    m["ropec"] = cosT
    m["ropes"] = sinT
    kl = np.arange(128)[:, None]
    ql = np.arange(128)[None, :]
    m["mask_prev"] = (kl >= ql).astype(np.float32)
    m["mask_next"] = (kl <= ql).astype(np.float32)
    m["ident"] = np.eye(128, dtype=np.float32)
    NP = cfg.NPAIR
    A = np.zeros((128, L, 2, NP, 3), np.float32)
    Bm = np.zeros((128, L, 2, NP, 32), np.float32)
    Cm = np.zeros((128, L, 2, NP, 2, 64), np.float32)
    for g2 in range(2):
        rows = slice(g2 * 64, (g2 + 1) * 64)
        gidx = 2 * np.arange(NP) + g2
        A[rows, :, :, :, 0] = inp["s5_a_re"][:, :, gidx, :].transpose(3, 0, 1, 2)
        A[rows, :, :, :, 1] = inp["s5_a_im"][:, :, gidx, :].transpose(3, 0, 1, 2)
        A[rows, :, :, :, 2] = np.broadcast_to(inp["s5_log_step"][:, :, gidx][None], (64, L, 2, NP))
        cs_ = slice(g2 * 16, (g2 + 1) * 16)
        Bm[rows, :, 0, :, cs_] = inp["s5_b_re"][:, gidx, :, :].transpose(2, 0, 1, 3)
        Bm[rows, :, 1, :, cs_] = inp["s5_b_im"][:, gidx, :, :].transpose(2, 0, 1, 3)
        for k in range(NP):
            cc_ = slice(32 * (k % 2) + g2 * 16, 32 * (k % 2) + (g2 + 1) * 16)
            Cm[rows, :, :, k, 0, cc_] = inp["s5_c_re"][:, :, gidx[k], :, :].transpose(3, 0, 1, 2)
            Cm[rows, :, :, k, 1, cc_] = inp["s5_c_im"][:, :, gidx[k], :, :].transpose(3, 0, 1, 2)
    m["s5A"], m["s5B"], m["s5C"] = A, Bm, Cm
    return m


from contextlib import ExitStack

WSLOT = 8192
NWSLOT = 3


class Prog:
    def __init__(self, cfg, debug=False, n_layers=None, stop_after=None, same_engine_sync=True):
        self.cfg = cfg
        self.debug = debug
        self.L = cfg.DEPTH if n_layers is None else n_layers
        self.stop_after = stop_after
        nc = bass.Bass("TRN2", target_bir_lowering=False)
        self.nc = nc
        self.K = Kern(nc, same_engine_sync)
        self.K.make_engines({"tensor": nc.tensor, "scalar": nc.scalar, "vector": nc.vector,
                             "gpsimd": nc.gpsimd, "sync": nc.sync})
        self.wplan = weight_plan(cfg)
        self.pplan = pvec_plan(cfg)
        self.rplan = rowv_plan(cfg)
        self.psum_i = 0

    def din(self, name, shape, dt=F32):
        return self.nc.dram_tensor(name, list(shape), dt, kind="ExternalInput").ap()

    def dscratch(self, name, shape, dt):
        kind = "ExternalOutput" if self.debug else "Internal"
        return self.nc.dram_tensor(name, list(shape), dt, kind=kind).ap()

    def sb(self, st, name, shape, dt):
        self._uid = getattr(self, "_uid", 0) + 1
        name = f"{name}_{self._uid}"
        t = st.enter_context(self.nc.sbuf_tensor(name, list(shape), dt))
        return t, Buf(name)

    def next_psum(self):
        i = self.psum_i
        self.psum_i = (i + 1) % 6
        return self.ps[i], self.psb[i]

    def declare(self):
        cfg, L = self.cfg, self.cfg.DEPTH
        KC, BC, NT = cfg.KC, cfg.BC, cfg.NT
        self.xT = self.din("xT", [KC, 128, NT])
        self.cond = self.din("cond", [128, KC, 2])
        self.wall = self.din("wall", [L, 128, self.wplan["_total"]])
        self.wmod = self.din("wmod", [L, 128, 6 * KC, KC, 128])
        self.pvec = self.din("pvec", [128, L + 1, self.pplan["_total"]])
        self.rowv = self.din("rowv", [128, L, self.rplan["_total"]])
        self.ropec = self.din("ropec", [128, NT])
        self.ropes = self.din("ropes", [128, NT])
        self.mask_prev = self.din("mask_prev", [128, 128])
        self.mask_next = self.din("mask_next", [128, 128])
        self.ident = self.din("ident", [128, 128])
        self.s5A = self.din("s5A", [128, L, 2, cfg.NPAIR, 3])
        self.s5B = self.din("s5B", [128, L, 2, cfg.NPAIR, 32])
        self.s5C = self.din("s5C", [128, L, 2, cfg.NPAIR, 2, 64])
        self.out = self.nc.dram_tensor("out", [KC, 128, cfg.SEQ], F32, kind="ExternalOutput").ap()
        self.wb = [self.dscratch(f"wb{l}", [128, self.wplan["_total"]], BF16) for l in range(L)]
        self.X = self.dscratch("X", [KC, 128, NT], F32)
        self.H = self.dscratch("H", [KC, 128, NT], BF16)
        self.BR = self.dscratch("BR", [4, BC, 128, NT], BF16)
        self.Y = self.dscratch("Y", [BC, 128, cfg.NY], F32)
        self.Q = self.dscratch("Q", [BC, 128, NT], BF16)
        self.KT = self.dscratch("KT", [2, 128, NT], BF16)
        self.V = self.dscratch("V", [NT, 2, 128], BF16)
        self.U = self.dscratch("U", [BC, 128, NT], F32)
        self.YB = self.dscratch("YB", [BC, 128, NT], F32)

    def ypos(self, t):
        cfg = self.cfg
        return t + cfg.YPAD if t < cfg.CTX else t + 3 * cfg.YPAD

    def wload(self, l, off, n):
        i = self.wslot_i
        self.wslot_i = (i + 1) % len(self.wslots)
        t, b = self.wslots[i]
        self.K.sp.dma(t[:, 0:n], self.wb[l][:, off:off + n], b)
        return t, b

    def build(self):
        cfg, K, nc = self.cfg, self.K, self.nc
        self.declare()
        with ExitStack() as st:
            self.ps, self.psb = [], []
            for i in range(8):
                p = st.enter_context(nc.psum_tensor(f"ps{i}", [128, 512], F32))
                self.ps.append(p)
                self.psb.append(Buf(f"ps{i}"))
            self.pv, self.pvb = self.sb(st, "pv", [128, cfg.DEPTH + 1, self.pplan["_total"]], F32)
            self.modv, self.modb = self.sb(st, "modv", [128, cfg.DEPTH, 6 * cfg.KC, 2], F32)
            self.ones, self.onesb = self.sb(st, "ones", [128, 128], F32)
            self.identt, self.identb = self.sb(st, "identt", [128, 128], F32)
            K.pool.dma(self.pv[:], self.pvec[:, :, :], self.pvb)
            K.pool.dma(self.identt[:], self.ident[:, :], self.identb)
            K.dve.op(lambda e: e.memset(self.ones[:], 1.0), writes=[self.onesb])
            self.phase_w()
            K.barrier()
            self.phase_m()
            K.barrier()
            for l in range(self.L):
                self.phase1(l)
                K.barrier()
                if self.stop_after == ("p1", l):
                    break
                self.phase_s(l)
                K.barrier()
                if self.stop_after == ("ps", l):
                    break
                self.phase2(l)
                K.barrier()
            else:
                self.phase_final()
                K.barrier()
        K.close()
        return nc

    def phase_w(self):
        cfg, K, nc = self.cfg, self.K, self.nc
        tot = self.wplan["_total"]
        CH = 8192
        with ExitStack() as st:
            s32 = [self.sb(st, f"w32_{i}", [128, CH], F32) for i in range(2)]
            s16 = [self.sb(st, f"w16_{i}", [128, CH], BF16) for i in range(2)]
            it = 0
            for l in range(self.L):
                for off in range(0, tot, CH):
                    a, ab = s32[it % 2]
                    o, ob = s16[it % 2]
                    K.sp.dma(a[:], self.wall[l, :, off:off + CH], ab)
                    eng = (K.dve, K.act, K.pool)[it % 3]
                    if eng is K.act:
                        eng.op(lambda e: e.copy(out=o[:], in_=a[:]), reads=[ab], writes=[ob])
                    else:
                        eng.op(lambda e: e.tensor_copy(out=o[:], in_=a[:]), reads=[ab], writes=[ob])
                    K.pool.dma(self.wb[l][:, off:off + CH], o[:], ob, load=False)
                    it += 1

    def phase_m(self):
        cfg, K, nc = self.cfg, self.K, self.nc
        KC = cfg.KC
        NJ = 6 * KC
        JB = max(d for d in range(1, NJ + 1) if NJ % d == 0 and d * KC * 128 <= 8192)
        with ExitStack() as st:
            ct, cb = self.sb(st, "condt", [128, KC, 2], F32)
            sc, scb = self.sb(st, "scond", [128, KC, 2], F32)
            ws = [self.sb(st, f"wm_{i}", [128, JB, KC, 128], F32) for i in range(2)]
            K.pool.dma(ct[:], self.cond[:, :, :], cb)
            K.act.op(lambda e: e.activation(out=sc[:], in_=ct[:], func=AF.Silu), reads=[cb], writes=[scb])
            it = 0
            o_b, _ = self.pplan["b_mod"]
            for l in range(self.L):
                for j0 in range(0, NJ, JB):
                    w, wbuf = ws[it % 2]
                    it += 1
                    K.sp.dma(w[:], self.wmod[l, :, j0:j0 + JB, :, :], wbuf)
                    for j in range(j0, j0 + JB):
                        ps, psb = self.next_psum()
                        for kc in range(KC):
                            K.pe.op(lambda e, kc=kc, j=j: e.matmul(ps[:, 0:2], lhsT=w[:, j - j0, kc, :], rhs=sc[:, kc, :],
                                                                    start=(kc == 0), stop=(kc == KC - 1)),
                                    reads=[wbuf, scb], writes=[psb])
                        K.dve.op(lambda e, j=j: e.tensor_tensor(
                            out=self.modv[:, l, j, :], in0=ps[:, 0:2],
                            in1=self.pv[:, l, o_b + j:o_b + j + 1].to_broadcast([128, 2]), op=ALU.add),
                            reads=[psb, self.pvb], writes=[self.modb])

    def rms_to_h(self, st_tiles, xt, xtb, w, mod_scale_idx, mod_shift_idx, gname, l, which, ht, htb):
        cfg, K = self.cfg, self.K
        KC = cfg.KC
        sq = st_tiles["sq"]
        rs, rsb = st_tiles["rstd"]
        ab, abb = st_tiles["ab"]
        hf = st_tiles["hf"]
        o_g, _ = self.pplan[gname]
        K.dve.op(lambda e: e.tensor_scalar(out=ab[:, :, 0], in0=self.modv[:, l, mod_scale_idx * KC:(mod_scale_idx + 1) * KC, which],
                                           scalar1=1.0, scalar2=None, op0=ALU.add),
                 reads=[self.modb], writes=[abb])
        K.dve.op(lambda e: e.tensor_tensor(out=ab[:, :, 0], in0=ab[:, :, 0], in1=self.pv[:, l, o_g:o_g + KC], op=ALU.mult),
                 reads=[abb, self.pvb], writes=[abb])
        K.dve.op(lambda e: e.tensor_copy(out=ab[:, :, 1], in_=self.modv[:, l, mod_shift_idx * KC:(mod_shift_idx + 1) * KC, which]),
                 reads=[self.modb], writes=[abb])
        ps, psb = self.next_psum()
        for kc in range(KC):
            s, sb_ = sq[kc % 2]
            K.act.op(lambda e, kc=kc, s=s: e.activation(out=s[:, :w], in_=xt[:, kc, :w], func=AF.Square),
                     reads=[xtb], writes=[sb_])
            K.pe.op(lambda e, kc=kc, s=s: e.matmul(ps[:, :w], lhsT=self.ones[:], rhs=s[:, :w],
                                                   start=(kc == 0), stop=(kc == KC - 1)),
                    reads=[self.onesb, sb_], writes=[psb])
        K.act.op(lambda e: e.activation(out=rs[:, :w], in_=ps[:, :w], func=AF.Sqrt, scale=1.0 / cfg.D, bias=self.epsc[:, 0:1]),
                 reads=[psb, self.epsb], writes=[rsb])
        K.dve.op(lambda e: e.reciprocal(out=rs[:, :w], in_=rs[:, :w]), reads=[rsb], writes=[rsb])
        for kc in range(KC):
            f, fb = hf[kc % 2]
            K.dve.op(lambda e, kc=kc, f=f: e.tensor_tensor(out=f[:, :w], in0=xt[:, kc, :w], in1=rs[:, :w], op=ALU.mult),
                     reads=[xtb, rsb], writes=[fb])
            K.act.op(lambda e, kc=kc, f=f: e.activation(out=ht[:, kc, :w], in_=f[:, :w], func=AF.Identity,
                                                        scale=ab[:, kc, 0:1], bias=ab[:, kc, 1:2]),
                     reads=[fb, abb], writes=[htb])

    def mm_group(self, ps_ap, psb, lhs_list, rhs_list, reads):
        n = len(lhs_list)
        for i in range(n):
            self.K.pe.op(lambda e, i=i: e.matmul(ps_ap, lhsT=lhs_list[i], rhs=rhs_list[i], start=(i == 0), stop=(i == n - 1)),
                         reads=reads, writes=[psb])

    def phase1(self, l):
        cfg, K, nc = self.cfg, self.K, self.nc
        KC, BC, DB, T = cfg.KC, cfg.BC, cfg.DB, cfg.T
        xsrc = self.xT if l == 0 else self.X
        wp = self.wplan
        CW = KC * 128
        with ExitStack() as st:
            xt, xtb = self.sb(st, "p1_xt", [128, KC, T], F32)
            ht, htb = self.sb(st, "p1_ht", [128, KC, T], BF16)
            tl = {
                "sq": [self.sb(st, f"p1_sq{i}", [128, T], F32) for i in range(2)],
                "hf": [self.sb(st, f"p1_hf{i}", [128, T], F32) for i in range(2)],
                "rstd": self.sb(st, "p1_rstd", [128, T], F32),
                "ab": self.sb(st, "p1_ab", [128, KC, 2], F32),
            }
            self.epsc, self.epsb = self.sb(st, "p1_eps", [128, 1], F32)
            K.dve.op(lambda e: e.memset(self.epsc[:], cfg.EPS), writes=[self.epsb])
            wv1, wv1b = self.sb(st, "p1_wv1", [128, KC, DB], BF16)
            wv2, wv2b = self.sb(st, "p1_wv2", [128, KC, 128], BF16)
            wst, wstb = self.sb(st, "p1_wst", [128, BC, 128], BF16)
            self.wslots = [self.sb(st, f"p1_ws{i}", [128, WSLOT], BF16) for i in range(NWSLOT)]
            self.wslot_i = 0
            rv, rvb = self.sb(st, "p1_rv", [128, self.rplan["_total"]], F32)
            ut, utb = self.sb(st, "p1_ut", [128, BC, T], F32)
            yt, ytb = self.sb(st, "p1_yt", [128, BC, T], F32)
            qt, qtb = self.sb(st, "p1_qt", [128, BC, T], BF16)
            kt, ktb = self.sb(st, "p1_kt", [128, T], BF16)
            dt_, dtb = self.sb(st, "p1_dt", [128, BC, T], F32)
            bra, brab = self.sb(st, "p1_bra", [128, BC, T], BF16)
            cs, csb = self.sb(st, "p1_cos", [128, T], F32)
            sn, snb = self.sb(st, "p1_sin", [128, T], F32)
            sg, sgb = self.sb(st, "p1_sg", [128, T], F32)
            t1 = [self.sb(st, f"p1_t1{i}", [128, T], F32) for i in range(2)]
            t2 = [self.sb(st, f"p1_t2{i}", [128, T], F32) for i in range(2)]
            vg, vgb = self.sb(st, "p1_vg", [128, DB], F32)
            vn, vnb = self.sb(st, "p1_vn", [128, DB], F32)
            vnh, vnhb = self.sb(st, "p1_vnh", [128, DB], BF16)
            stt, sttb = self.sb(st, "p1_stats", [128, 8, 6], F32)
            mv, mvb = self.sb(st, "p1_mv", [128, 4], F32)
            mx, mxb = self.sb(st, "p1_mx", [128, BC, 128], F32)
            vv, vvb = self.sb(st, "p1_vv", [128, 2, 2, 64], BF16)
            o_glg, _ = self.rplan["gln_g"]
            o_glb, _ = self.rplan["gln_b"]
            o_gbs, _ = self.rplan["gbs"]
            K.pool.dma(rv[:], self.rowv[:, l, :], rvb)
            K.sp.dma(wv1[:], self.wb[l][:, wp["inV1"]:wp["inV1"] + KC * DB], wv1b)
            K.sp.dma(wv2[:], self.wb[l][:, wp["inV2"]:wp["inV2"] + KC * 128], wv2b)
            K.sp.dma(wst[:], self.wb[l][:, wp["wsT"]:wp["wsT"] + BC * 128], wstb)
            for (t0, w, is_ctx) in cfg.tiles:
                which = 0 if is_ctx else 1
                K.pool.dma(xt[:, :, :w], xsrc[:, :, t0:t0 + w].rearrange("c p t -> p c t"), xtb)
                K.pool.dma(cs[:, :w], self.ropec[:, t0:t0 + w], csb)
                K.pool.dma(sn[:, :w], self.ropes[:, t0:t0 + w], snb)
                self.rms_to_h(tl, xt, xtb, w, 1, 0, "n1g", l, which, ht, htb)
                K.pool.dma(self.H[:, :, t0:t0 + w].rearrange("c p t -> p c t"), ht[:, :, :w], htb, load=False)
                nchunks = 6 * BC + 2
                chunk_kind = []
                for i in range(BC):
                    chunk_kind.append(("u", i))
                for i in range(BC):
                    chunk_kind.append(("g", i))
                    chunk_kind.append(("a", i))
                for i in range(BC):
                    chunk_kind.append(("q", i))
                    chunk_kind.append(("qs", i))
                chunk_kind.append(("k", 0))
                chunk_kind.append(("ks", 0))
                for i in range(BC):
                    chunk_kind.append(("d", i))
                CPS = max(1, WSLOT // CW)
                wt = wtb = None
                pending = {}
                for ci, (kind, i) in enumerate(chunk_kind):
                    if ci % CPS == 0:
                        n = min(CPS, nchunks - ci)
                        wt, wtb = self.wload(l, wp["inF"] + ci * CW, n * CW)
                    base = (ci % CPS) * CW
                    ps, psb = self.next_psum()
                    self.mm_group(ps[:, :w], psb,
                                  [wt[:, base + kc * 128: base + (kc + 1) * 128] for kc in range(KC)],
                                  [ht[:, kc, :w] for kc in range(KC)], [wtb, htb])
                    if kind == "u":
                        K.act.op(lambda e, i=i, ps=ps: e.activation(out=ut[:, i, :w], in_=ps[:, :w], func=AF.Gelu_apprx_tanh),
                                 reads=[psb], writes=[utb])
                    elif kind == "g":
                        K.act.op(lambda e, ps=ps: e.activation(out=sg[:, :w], in_=ps[:, :w], func=AF.Sigmoid),
                                 reads=[psb], writes=[sgb])
                    elif kind == "a":
                        K.dve.op(lambda e, i=i, ps=ps: e.tensor_tensor(out=yt[:, i, :w], in0=ps[:, :w], in1=sg[:, :w], op=ALU.mult),
                                 reads=[psb, sgb], writes=[ytb])
                    elif kind in ("q", "k"):
                        a, ab_ = t1[ci % 2 if False else (ci // 2) % 2]
                        K.dve.op(lambda e, ps=ps, a=a: e.tensor_tensor(out=a[:, :w], in0=ps[:, :w], in1=cs[:, :w], op=ALU.mult),
                                 reads=[psb, csb], writes=[ab_])
                        pending["t1"] = (a, ab_)
                    elif kind in ("qs", "ks"):
                        a, ab_ = pending["t1"]
                        b2, b2b = t2[(ci // 2) % 2]
                        K.dve.op(lambda e, ps=ps, b2=b2: e.tensor_tensor(out=b2[:, :w], in0=ps[:, :w], in1=sn[:, :w], op=ALU.mult),
                                 reads=[psb, snb], writes=[b2b])
                        if kind == "qs":
                            K.pool.op(lambda e, i=i, a=a, b2=b2: e.tensor_tensor(out=qt[:, i, :w], in0=a[:, :w], in1=b2[:, :w], op=ALU.add),
                                      reads=[ab_, b2b], writes=[qtb])
                        else:
                            K.pool.op(lambda e, a=a, b2=b2: e.tensor_tensor(out=kt[:, :w], in0=a[:, :w], in1=b2[:, :w], op=ALU.add),
                                      reads=[ab_, b2b], writes=[ktb])
                    elif kind == "d":
                        K.act.op(lambda e, i=i, ps=ps: e.copy(out=dt_[:, i, :w], in_=ps[:, :w]), reads=[psb], writes=[dtb])
                y0 = self.ypos(t0)
                K.pool.dma(self.Y[:, :, y0:y0 + w].rearrange("c p t -> p c t"), yt[:, :, :w], ytb, load=False)
                K.pool.dma(self.Q[:, :, t0:t0 + w].rearrange("c p t -> p c t"), qt[:, :, :w], qtb, load=False)
                for hk in range(2):
                    for dup in range(2):
                        K.pool.dma(self.KT[hk, dup * 64:(dup + 1) * 64, t0:t0 + w], kt[hk * 64:(hk + 1) * 64, :w], ktb, load=False)
                K.pool.dma(self.U[:, :, t0:t0 + w].rearrange("c p t -> p c t"), dt_[:, :, :w], dtb, load=False)
                for j in range(w // 128):
                    tk = slice(j * 128, (j + 1) * 128)
                    ps, psb = self.next_psum()
                    self.mm_group(ps[:, :DB], psb, [ht[:, kc, tk] for kc in range(KC)],
                                  [wv1[:, kc, :] for kc in range(KC)], [htb, wv1b])
                    K.act.op(lambda e, ps=ps: e.activation(out=vg[:, :], in_=ps[:, :DB], func=AF.Gelu_apprx_tanh),
                             reads=[psb], writes=[vgb])
                    FMAX = 512
                    nst = (DB + FMAX - 1) // FMAX
                    for s_ in range(nst):
                        K.dve.op(lambda e, s_=s_: e.bn_stats(out=stt[:, s_, :], in_=vg[:, s_ * FMAX:min(DB, (s_ + 1) * FMAX)]),
                                 reads=[vgb], writes=[sttb])
                    K.dve.op(lambda e: e.bn_aggr(out=mv[:, 0:2], in_=stt[:, 0:nst, :]), reads=[sttb], writes=[mvb])
                    K.act.op(lambda e: e.activation(out=mv[:, 2:3], in_=mv[:, 1:2], func=AF.Sqrt, bias=self.epsc[:, 0:1]),
                             reads=[mvb, self.epsb], writes=[mvb])
                    K.dve.op(lambda e: e.reciprocal(out=mv[:, 2:3], in_=mv[:, 2:3]), reads=[mvb], writes=[mvb])
                    K.dve.op(lambda e: e.tensor_scalar(out=vn[:, :], in0=vg[:, :], scalar1=mv[:, 0:1], scalar2=mv[:, 2:3],
                                                       op0=ALU.subtract, op1=ALU.mult),
                             reads=[vgb, mvb], writes=[vnb])
                    K.pool.op(lambda e: e.tensor_tensor(out=vn[:, :], in0=vn[:, :], in1=rv[:, o_glg:o_glg + DB], op=ALU.mult),
                              reads=[vnb, rvb], writes=[vnb])
                    K.pool.op(lambda e: e.tensor_tensor(out=vnh[:, :], in0=vn[:, :], in1=rv[:, o_glb:o_glb + DB], op=ALU.add),
                              reads=[vnb, rvb], writes=[vnhb])
                    ps2, ps2b = self.next_psum()
                    for gi in range(BC):
                        K.pe.op(lambda e, gi=gi, ps2=ps2: e.matmul(ps2[:, gi * 128:(gi + 1) * 128], lhsT=vnh[:, gi * 128:(gi + 1) * 128],
                                                                  rhs=wst[:, gi, :], start=True, stop=True),
                                reads=[vnhb, wstb], writes=[ps2b])
                    K.dve.op(lambda e, ps2=ps2: e.tensor_tensor(out=mx[:, :, :], in0=ps2[:, :BC * 128].rearrange("p (g q) -> p g q", g=BC),
                                                               in1=rv[:, o_gbs:o_gbs + BC * 128].rearrange("p (g q) -> p g q", g=BC), op=ALU.add),
                             reads=[ps2b, rvb], writes=[mxb])
                    K.pool.op(lambda e, tk=tk: e.tensor_tensor(out=bra[:, :, tk], in0=mx[:, :, :], in1=ut[:, :, tk], op=ALU.mult),
                              reads=[mxb, utb], writes=[brab])
                    ps3, ps3b = self.next_psum()
                    self.mm_group(ps3[:, :128], ps3b, [ht[:, kc, tk] for kc in range(KC)],
                                  [wv2[:, kc, :] for kc in range(KC)], [htb, wv2b])
                    for dup in range(2):
                        K.act.op(lambda e, ps3=ps3, dup=dup: e.copy(out=vv[:, :, dup, :], in_=ps3[:, :128].rearrange("p (h d) -> p h d", h=2)),
                                 reads=[ps3b], writes=[vvb])
                    K.pool.dma(self.V[t0 + j * 128:t0 + (j + 1) * 128, :, :], vv[:].rearrange("p h u d -> p h (u d)"), vvb, load=False)
                K.pool.dma(self.BR[0, :, :, t0:t0 + w].rearrange("c p t -> p c t"), bra[:, :, :w], brab, load=False)


    def phase2(self, l):
        cfg, K, nc = self.cfg, self.K, self.nc
        KC, BC, DB, T, FFC, CTX, NT = cfg.KC, cfg.BC, cfg.DB, cfg.T, cfg.FFC, cfg.CTX, cfg.NT
        QPK = cfg.QPK
        xsrc = self.xT if l == 0 else self.X
        wp = self.wplan
        pp = self.pplan
        last = (l == cfg.DEPTH - 1)
        NCC = CTX // 128
        HC = min(FFC, KC)
        GW_ = 4 * KC * 128
        BW_ = 4 * BC * 128
        with ExitStack() as st:
            xt, xtb = self.sb(st, "p2_xt", [128, KC, T], F32)
            ht, htb = self.sb(st, "p2_ht", [128, KC, T], BF16)
            big, bigb = self.sb(st, "p2_big", [128, HC, T], BF16)
            tl = {
                "sq": [self.sb(st, f"p2_sq{i}", [128, T], F32) for i in range(2)],
                "hf": [self.sb(st, f"p2_hf{i}", [128, T], F32) for i in range(2)],
                "rstd": self.sb(st, "p2_rstd", [128, T], F32),
                "ab": self.sb(st, "p2_ab", [128, KC, 2], F32),
            }
            self.epsc, self.epsb = self.sb(st, "p2_eps", [128, 1], F32)
            K.dve.op(lambda e: e.memset(self.epsc[:], cfg.EPS), writes=[self.epsb])
            self.wslots = [self.sb(st, f"p2_ws{i}", [128, 8192], BF16) for i in range(3)]
            self.wslot_i = 0
            bws = [self.sb(st, f"p2_bw{i}", [128, BW_], BF16) for i in range(2)]
            brt = {0: self.sb(st, "p2_br0", [128, BC, T], BF16), 3: self.sb(st, "p2_br3", [128, BC, T], BF16)}
            brBs = [self.sb(st, f"p2_brB{i}", [128, BC, T], BF16) for i in range(2)]
            brCs = [self.sb(st, f"p2_brC{i}", [128, BC, T], BF16) for i in range(2)]
            rv, rvb = self.sb(st, "p2_rv", [128, cfg.HQ], F32)
            esk, eskb = self.sb(st, "p2_esk", [128, cfg.HQ], F32)
            ywins = [self.sb(st, f"p2_ywin{i}", [128, BC, T + 32], F32) for i in range(1)] * 2
            acc, accb = self.sb(st, "p2_acc", [128, BC, T], F32)
            cst = [self.sb(st, f"p2_cst{i}", [128, T], F32) for i in range(3)]
            qts = [self.sb(st, f"p2_qt{i}", [128, BC, T], BF16) for i in range(1)] * 2
            kwins = [self.sb(st, f"p2_kwin{i}", [128, 2, T + 256], BF16) for i in range(1)] * 2
            vwins = [self.sb(st, f"p2_vwin{i}", [128, (T + 256) // 128, 2, 128], BF16) for i in range(1)] * 2
            kctx, kctxb = self.sb(st, "p2_kctx", [128, 2, CTX], BF16)
            vctx, vctxb = self.sb(st, "p2_vctx", [128, NCC, 2, 128], BF16)
            mk = [self.sb(st, f"p2_mk{i}", [128, 128], BF16) for i in range(2)]
            mk32, mk32b = self.sb(st, "p2_mk32", [128, 2, 128], F32)
            onesh, oneshb = self.sb(st, "p2_onesh", [128, 128], BF16)
            pts = [self.sb(st, f"p2_pt{i}", [128, QPK * 128], BF16) for i in range(3)]
            rden, rdenb = self.sb(st, "p2_rden", [128, QPK * 128], F32)
            sgs = tl["sq"]
            prods = [cst[1], cst[2]] + tl["hf"]
            rl = tl["sq"]
            zt, ztb = self.sb(st, "p2_zero", [128, 32], F32)
            o_sk, _ = self.rplan["sink"]
            K.pool.dma(rv[:], self.rowv[:, l, o_sk:o_sk + cfg.HQ], rvb)
            K.act.op(lambda e: e.activation(out=esk[:], in_=rv[:, :], func=AF.Exp), reads=[rvb], writes=[eskb])
            K.pool.dma(mk32[:, 0, :], self.mask_prev[:, :], mk32b)
            K.pool.dma(mk32[:, 1, :], self.mask_next[:, :], mk32b)
            for i in range(2):
                K.dve.op(lambda e, i=i: e.tensor_copy(out=mk[i][0][:], in_=mk32[:, i, :]), reads=[mk32b], writes=[mk[i][1]])
            K.dve.op(lambda e: e.memset(onesh[:], 1.0), writes=[oneshb])
            K.dve.op(lambda e: e.memset(zt[:], 0.0), writes=[ztb])
            P_ = cfg.YPAD
            for c0 in (0, P_ + CTX, 2 * P_ + CTX, 3 * P_ + NT):
                for c in range(BC):
                    K.pool.dma(self.Y[c, :, c0:c0 + P_], zt[:, 0:P_], ztb, load=False)
            K.pool.dma(kctx[:], self.KT[:, :, 0:CTX].rearrange("h p t -> p h t"), kctxb)
            K.pool.dma(vctx[:], self.V[0:CTX, :, :].rearrange("(c p) h d -> p c h d", p=128), vctxb)
            K.barrier()
            o_cw, _ = pp["conv_w"]
            o_cb, _ = pp["conv_b"]
            o_lg, _ = pp["cln_g"]
            o_lb, _ = pp["cln_b"]
            o_bg, _ = pp["b_gate"]
            tiles2 = [t_ for t_ in cfg.tiles if not (t_[2] and last)]

            def front(ti):
                (t0, w, is_ctx) = tiles2[ti]
                par = ti % 2
                (ywin, ywinb), (qt, qtb), (kwin, kwinb), (vwin, vwinb) = ywins[par], qts[par], kwins[par], vwins[par]
                brB, brBb = brBs[par]
                y0 = self.ypos(t0)
                K.pool.dma(ywin[:, :, :w + 30], self.Y[:, :, y0 - 15:y0 + w + 15].rearrange("c p t -> p c t"), ywinb)
                K.pool.dma(qt[:, :, :w], self.Q[:, :, t0:t0 + w].rearrange("c p t -> p c t"), qtb)
                if not is_ctx:
                    k_lo = max(CTX, t0 - 128)
                    k_hi = min(NT, t0 + w + 128)
                    K.pool.dma(kwin[:, :, :k_hi - k_lo], self.KT[:, :, k_lo:k_hi].rearrange("h p t -> p h t"), kwinb)
                    K.pool.dma(vwin[:, :(k_hi - k_lo) // 128, :, :],
                               self.V[k_lo:k_hi, :, :].rearrange("(c p) h d -> p c h d", p=128), vwinb)
                yield
                for c in range(BC):
                    eng = K.dve
                    eng.op(lambda e, c=c: e.tensor_scalar(out=acc[:, c, :w], in0=ywin[:, c, 0:w],
                                                          scalar1=self.pv[:, l, o_cw + c * 31:o_cw + c * 31 + 1],
                                                          scalar2=self.pv[:, l, o_cb + c:o_cb + c + 1], op0=ALU.mult, op1=ALU.add),
                           reads=[ywinb, self.pvb], writes=[accb])
                    for j in range(1, 31):
                        if j % 10 == 0:
                            yield
                        eng.op(lambda e, c=c, j=j: e.scalar_tensor_tensor(
                            out=acc[:, c, :w], in0=ywin[:, c, j:j + w], scalar=self.pv[:, l, o_cw + c * 31 + j:o_cw + c * 31 + j + 1],
                            in1=acc[:, c, :w], op0=ALU.mult, op1=ALU.add), reads=[ywinb, self.pvb, accb], writes=[accb])
                yield
                ps1, ps1b = self.next_psum()
                ps2, ps2b = self.next_psum()
                for c in range(BC):
                    s_, sb_ = tl["sq"][c % 2]
                    K.pe.op(lambda e, c=c: e.matmul(ps1[:, :w], lhsT=self.ones[:], rhs=acc[:, c, :w], start=(c == 0), stop=(c == BC - 1)),
                            reads=[self.onesb, accb], writes=[ps1b])
                    K.act.op(lambda e, c=c, s_=s_: e.activation(out=s_[:, :w], in_=acc[:, c, :w], func=AF.Square), reads=[accb], writes=[sb_])
                    K.pe.op(lambda e, c=c, s_=s_: e.matmul(ps2[:, :w], lhsT=self.ones[:], rhs=s_[:, :w], start=(c == 0), stop=(c == BC - 1)),
                            reads=[self.onesb, sb_], writes=[ps2b])
                mean, meanb = cst[0]
                msq, msqb = cst[1]
                var, varb = cst[2]
                K.act.op(lambda e: e.mul(out=mean[:, :w], in_=ps1[:, :w], mul=1.0 / DB), reads=[ps1b], writes=[meanb])
                K.dve.op(lambda e: e.tensor_tensor(out=msq[:, :w], in0=mean[:, :w], in1=mean[:, :w], op=ALU.mult), reads=[meanb], writes=[msqb])
                K.dve.op(lambda e: e.scalar_tensor_tensor(out=var[:, :w], in0=ps2[:, :w], scalar=1.0 / DB, in1=msq[:, :w],
                                                          op0=ALU.mult, op1=ALU.subtract), reads=[ps2b, msqb], writes=[varb])
                K.act.op(lambda e: e.activation(out=var[:, :w], in_=var[:, :w], func=AF.Sqrt, bias=self.epsc[:, 0:1]),
                         reads=[varb, self.epsb], writes=[varb])
                K.dve.op(lambda e: e.reciprocal(out=var[:, :w], in_=var[:, :w]), reads=[varb], writes=[varb])
                for c in range(BC):
                    eng = K.dve if c % 2 == 0 else K.pool
                    eng.op(lambda e, c=c: e.tensor_tensor(out=acc[:, c, :w], in0=acc[:, c, :w], in1=mean[:, :w], op=ALU.subtract),
                           reads=[accb, meanb], writes=[accb])
                    eng.op(lambda e, c=c: e.tensor_tensor(out=acc[:, c, :w], in0=acc[:, c, :w], in1=var[:, :w], op=ALU.mult),
                           reads=[accb, varb], writes=[accb])
                    K.act.op(lambda e, c=c: e.activation(out=brB[:, c, :w], in_=acc[:, c, :w], func=AF.Silu,
                                                         scale=self.pv[:, l, o_lg + c:o_lg + c + 1], bias=self.pv[:, l, o_lb + c:o_lb + c + 1]),
                             reads=[accb, self.pvb], writes=[brBb])
                brc, brcb = brCs[par]
                yield
                for j in range(w // 128):
                    tq0 = t0 + j * 128
                    qs = slice(j * 128, (j + 1) * 128)
                    chunks = []
                    if not is_ctx:
                        for rel_, mi in ((-128, 0), (0, None), (128, 1)):
                            ks = tq0 + rel_
                            if ks < CTX or ks >= NT:
                                continue
                            o = ks - k_lo
                            chunks.append((lambda hk, par, o=o: kwin[par * 64:(par + 1) * 64, hk, o:o + 128],
                                           lambda hk, o=o: vwin[:, o // 128, hk, :], mi, [kwinb], [vwinb]))
                    for cc in range(NCC):
                        chunks.append((lambda hk, par, cc=cc: kctx[par * 64:(par + 1) * 64, hk, cc * 128:(cc + 1) * 128],
                                       lambda hk, cc=cc: vctx[:, cc, hk, :], None, [kctxb], [vctxb]))
                    for hk in range(2):
                        pso, psob = self.ps[6], self.psb[6]
                        psd, psdb = self.ps[7], self.psb[7]
                        for ci, (kf, vf, mi, krd, vrd) in enumerate(chunks):
                            pt, ptb = pts[ci % 3]
                            for par in range(2):
                                heads = [i for i in range(QPK) if (hk * QPK + i) % 2 == par]
                                if not heads:
                                    continue
                                pss, pssb = self.next_psum()
                                for i in heads:
                                    ch = (hk * QPK + i) // 2
                                    K.pe.op(lambda e, i=i, par=par, ch=ch, kf=kf, pss=pss: e.matmul(
                                        pss[:, i * 128:(i + 1) * 128], lhsT=kf(hk, par), rhs=qt[par * 64:(par + 1) * 64, ch, qs],
                                        start=True, stop=True), reads=krd + [qtb], writes=[pssb])
                                for i in heads:
                                    K.act.op(lambda e, pss=pss, pt=pt, i=i: e.activation(out=pt[:, i * 128:(i + 1) * 128], in_=pss[:, i * 128:(i + 1) * 128],
                                                                                    func=AF.Exp, scale=0.125), reads=[pssb], writes=[ptb])
                            if mi is not None:
                                K.pool.op(lambda e, pt=pt, mi=mi: e.tensor_tensor(
                                    out=pt[:, :].rearrange("p (h q) -> p h q", h=QPK), in0=pt[:, :].rearrange("p (h q) -> p h q", h=QPK),
                                    in1=mk[mi][0][:, :].unsqueeze(1).to_broadcast([128, QPK, 128]), op=ALU.mult),
                                    reads=[ptb, mk[mi][1]], writes=[ptb])
                            first, lastc = (ci == 0), (ci == len(chunks) - 1)
                            K.pe.op(lambda e, vf=vf, pt=pt, first=first, lastc=lastc: e.matmul(
                                pso[:, :QPK * 128], lhsT=vf(hk), rhs=pt[:, :], start=first, stop=lastc), reads=vrd + [ptb], writes=[psob])
                            K.pe.op(lambda e, pt=pt, first=first, lastc=lastc: e.matmul(
                                psd[:, :QPK * 128], lhsT=onesh[:, :], rhs=pt[:, :], start=first, stop=lastc), reads=[oneshb, ptb], writes=[psdb])
                        K.dve.op(lambda e: e.tensor_tensor(
                            out=rden[:, :].rearrange("p (h q) -> p h q", h=QPK), in0=psd[:, :QPK * 128].rearrange("p (h q) -> p h q", h=QPK),
                            in1=esk[:, hk * QPK:(hk + 1) * QPK].unsqueeze(2).to_broadcast([128, QPK, 128]), op=ALU.add),
                            reads=[psdb, eskb], writes=[rdenb])
                        K.dve.op(lambda e: e.reciprocal(out=rden[:, :], in_=rden[:, :]), reads=[rdenb], writes=[rdenb])
                        for i in range(QPK):
                            hq = hk * QPK + i
                            par, ch = hq % 2, hq // 2
                            pr = slice(par * 64, (par + 1) * 64)
                            K.dve.op(lambda e, i=i, pr=pr, ch=ch: e.tensor_tensor(
                                out=brc[pr, ch, qs], in0=pso[pr, i * 128:(i + 1) * 128], in1=rden[pr, i * 128:(i + 1) * 128], op=ALU.mult),
                                reads=[psob, rdenb], writes=[brcb])
                        yield

            def back(ti, tick):
                (t0, w, is_ctx) = tiles2[ti]
                par = ti % 2
                which = 0 if is_ctx else 1
                brl = [brt[0], brBs[par], brCs[par], brt[3]]
                K.pool.dma(xt[:, :, :w], xsrc[:, :, t0:t0 + w].rearrange("c p t -> p c t"), xtb)
                K.pool.dma(ht[:, :, :w], self.H[:, :, t0:t0 + w].rearrange("c p t -> p c t"), htb)
                K.pool.dma(brt[0][0][:, :, :w], self.BR[0, :, :, t0:t0 + w].rearrange("c p t -> p c t"), brt[0][1])
                K.pool.dma(brt[3][0][:, :, :w], self.BR[3, :, :, t0:t0 + w].rearrange("c p t -> p c t"), brt[3][1])
                for oc in range(KC):
                    tick()
                    gw, gwb = self.wload(l, wp["gb"] + oc * (GW_ + BW_), GW_)
                    bw, bwb = bws[oc % 2]
                    K.sp.dma(bw[:, :], self.wb[l][:, wp["gb"] + oc * (GW_ + BW_) + GW_: wp["gb"] + (oc + 1) * (GW_ + BW_)], bwb)
                    for k in range(4):
                        psg, psgb = self.next_psum()
                        self.mm_group(psg[:, :w], psgb, [gw[:, (k * KC + kc) * 128:(k * KC + kc + 1) * 128] for kc in range(KC)],
                                      [ht[:, kc, :w] for kc in range(KC)], [gwb, htb])
                        psb_, psbb = self.next_psum()
                        self.mm_group(psb_[:, :w], psbb, [bw[:, (k * BC + bc) * 128:(k * BC + bc + 1) * 128] for bc in range(BC)],
                                      [brl[k][0][:, bc, :w] for bc in range(BC)], [bwb, brl[k][1]])
                        sg, sgb = sgs[k % 2]
                        K.act.op(lambda e, psg=psg, sg=sg, k=k: e.activation(out=sg[:, :w], in_=psg[:, :w], func=AF.Sigmoid,
                                                                        bias=self.pv[:, l, o_bg + k * KC + oc:o_bg + k * KC + oc + 1]),
                                 reads=[psgb, self.pvb], writes=[sgb])
                        pr_, prb = prods[k]
                        K.dve.op(lambda e, psb_=psb_, sg=sg, pr_=pr_: e.tensor_tensor(out=pr_[:, :w], in0=psb_[:, :w], in1=sg[:, :w], op=ALU.mult),
                                 reads=[psbb, sgb], writes=[prb])
                    K.pool.op(lambda e: e.tensor_tensor(out=prods[0][0][:, :w], in0=prods[0][0][:, :w], in1=prods[1][0][:, :w], op=ALU.add),
                              reads=[prods[0][1], prods[1][1]], writes=[prods[0][1]])
                    K.pool.op(lambda e: e.tensor_tensor(out=prods[2][0][:, :w], in0=prods[2][0][:, :w], in1=prods[3][0][:, :w], op=ALU.add),
                              reads=[prods[2][1], prods[3][1]], writes=[prods[2][1]])
                    K.pool.op(lambda e, oc=oc: e.tensor_tensor(out=big[:, oc, :w], in0=prods[0][0][:, :w], in1=prods[2][0][:, :w], op=ALU.add),
                              reads=[prods[0][1], prods[2][1]], writes=[bigb])
                CW = KC * 128
                CPS = max(1, 8192 // CW)
                for oc in range(KC):
                    tick()
                    if oc % CPS == 0:
                        n = min(CPS, KC - oc)
                        wt, wtb = self.wload(l, wp["out"] + oc * CW, n * CW)
                    base = (oc % CPS) * CW
                    ps, psb = self.next_psum()
                    self.mm_group(ps[:, :w], psb, [wt[:, base + kc * 128:base + (kc + 1) * 128] for kc in range(KC)],
                                  [big[:, kc, :w] for kc in range(KC)], [wtb, bigb])
                    K.dve.op(lambda e, oc=oc, ps=ps: e.scalar_tensor_tensor(
                        out=xt[:, oc, :w], in0=ps[:, :w], scalar=self.modv[:, l, 2 * KC + oc, which:which + 1], in1=xt[:, oc, :w],
                        op0=ALU.mult, op1=ALU.add), reads=[psb, self.modb, xtb], writes=[xtb])
                self.rms_to_h(tl, xt, xtb, w, 4, 3, "n2g", l, which, ht, htb)
                for hp in range(FFC // HC):
                    for fcl in range(HC):
                        fc = hp * HC + fcl
                        if fcl % 2 == 0:
                            tick()
                        if fcl % CPS == 0:
                            n = min(CPS, HC - fcl)
                            wt, wtb = self.wload(l, wp["ff1"] + fc * CW, n * CW)
                        base = (fcl % CPS) * CW
                        ps, psb = self.next_psum()
                        self.mm_group(ps[:, :w], psb, [wt[:, base + kc * 128:base + (kc + 1) * 128] for kc in range(KC)],
                                      [ht[:, kc, :w] for kc in range(KC)], [wtb, htb])
                        r_, rb = rl[fc % 2]
                        K.act.op(lambda e, ps=ps, r_=r_: e.activation(out=r_[:, :w], in_=ps[:, :w], func=AF.Relu), reads=[psb], writes=[rb])
                        eng = K.pool if fc % 2 == 0 else K.dve
                        eng.op(lambda e, r_=r_, fcl=fcl: e.tensor_tensor(out=big[:, fcl, :w], in0=r_[:, :w], in1=r_[:, :w], op=ALU.mult),
                               reads=[rb], writes=[bigb])
                    FW = FFC * 128
                    for oc in range(KC):
                        tick()
                        ps, psb = self.next_psum()
                        nload = HC * 128
                        for s0 in range(0, nload, 8192):
                            n = min(8192, nload - s0)
                            wt, wtb = self.wload(l, wp["ff2"] + oc * FW + hp * HC * 128 + s0, n)
                            nk = n // 128
                            for kk in range(nk):
                                fcl = s0 // 128 + kk
                                K.pe.op(lambda e, wt=wt, kk=kk, fcl=fcl, ps=ps: e.matmul(
                                    ps[:, :w], lhsT=wt[:, kk * 128:(kk + 1) * 128], rhs=big[:, fcl, :w],
                                    start=(fcl == 0), stop=(fcl == HC - 1)), reads=[wtb, bigb], writes=[psb])
                        K.dve.op(lambda e, oc=oc, ps=ps: e.scalar_tensor_tensor(
                            out=xt[:, oc, :w], in0=ps[:, :w], scalar=self.modv[:, l, 5 * KC + oc, which:which + 1], in1=xt[:, oc, :w],
                            op0=ALU.mult, op1=ALU.add), reads=[psb, self.modb, xtb], writes=[xtb])
                K.pool.dma(self.X[:, :, t0:t0 + w].rearrange("c p t -> p c t"), xt[:, :, :w], xtb, load=False)

            for _ in front(0):
                pass
            for ti in range(len(tiles2)):
                nxt = front(ti + 1) if ti + 1 < len(tiles2) else None
                cnt = [0]

                def tick(nxt=nxt, cnt=cnt):
                    cnt[0] += 1
                    if nxt is not None and cnt[0] % 3 == 0:
                        next(nxt, None)
                back(ti, tick)
                if nxt is not None:
                    for _ in nxt:
                        pass

    def phase_s(self, l):
        cfg, K, nc = self.cfg, self.K, self.nc
        BC, NP, NT, CTX = cfg.BC, cfg.NPAIR, cfg.NT, cfg.CTX
        TS = 256
        NQ = NP // 4
        PI = math.pi
        wp, pp = self.wplan, self.pplan
        LOG = int(math.log2(TS))
        with ExitStack() as st:
            W, Wb = self.sb(st, "s_W", [128, 24, 2, NP], F32)
            Bt, Btb = self.sb(st, "s_Bt", [128, 2, 2, 2, NQ, 128], BF16)
            Ct, Ctb = self.sb(st, "s_Ct", [128, 2, NP, 2, 64], BF16)
            R, Rb = self.sb(st, "s_R", [128, 2 * NP, 2, TS], F32)
            carry, carryb_ = self.sb(st, "s_carry", [128, 2, NP, 2], F32)
            st2 = ExitStack()
            At, Atb = self.sb(st2, "s_A", [128, 2, NP, 3], F32)
            Bt32, Bt32b = self.sb(st2, "s_B32", [128, 2, NP, 32], F32)
            Ct32, Ct32b = self.sb(st2, "s_C32", [128, 2, NP, 2, 64], F32)
            Wi, Wib = self.sb(st2, "s_Wi", [128, 2, NP], mybir.dt.int32)
            bbar, bbarb = self.sb(st2, "s_bbar", [128, 2, 2, NP, 32], F32)
            tb, tbb = self.sb(st2, "s_tb", [128, 2, NP, 32], F32)
            bbv, bbvb = self.sb(st2, "s_bbv", [128, 2, 2, 2, NP, 32], F32)
            E, Eb = self.sb(st2, "s_E", [128, 4, 2 * NP], F32)
            rt1, rt1b = self.sb(st2, "s_rt1", [128, 2 * NP, TS // 2], F32)
            rt2, rt2b = self.sb(st2, "s_rt2", [128, 2 * NP, TS // 2], F32)
            carryb = [[Buf(f"carry{d}_{k}") for k in range(NP)] for d in range(2)]
            K.pool.dma(At[:], self.s5A[:, l], Atb)
            K.pool.dma(Bt32[:], self.s5B[:, l], Bt32b)
            K.pool.dma(Ct32[:], self.s5C[:, l], Ct32b)
            a_re, a_im, ls = At[:, :, :, 0], At[:, :, :, 1], At[:, :, :, 2]
            (DT, XR, XI, MAG, T0, TF, RS, RC, SIN, COS, LR, LI, NR, DEN, CR, CI, TA, TB_, XC) = [W[:, i] for i in range(19)]

            def dv(fn, rd=(Atb, Wb), wr=(Wb,)):
                K.dve.op(fn, reads=list(rd), writes=list(wr))

            def ac(fn, rd=(Atb, Wb), wr=(Wb,)):
                K.act.op(fn, reads=list(rd), writes=list(wr))
            ac(lambda e: e.activation(out=DT, in_=ls, func=AF.Exp))
            dv(lambda e: e.tensor_tensor(out=XR, in0=a_re, in1=DT, op=ALU.mult))
            dv(lambda e: e.tensor_tensor(out=XI, in0=a_im, in1=DT, op=ALU.mult))
            ac(lambda e: e.activation(out=MAG, in_=XR, func=AF.Exp))

            def reduce_sin(dst, x_ap):
                dv(lambda e: e.tensor_scalar(out=T0, in0=x_ap, scalar1=1.0 / (2 * PI), scalar2=None, op0=ALU.mult))
                dv(lambda e: e.tensor_copy(out=Wi[:], in_=T0), wr=(Wib,))
                dv(lambda e: e.tensor_copy(out=TF, in_=Wi[:]), rd=(Wib,))
                dv(lambda e: e.scalar_tensor_tensor(out=RS, in0=TF, scalar=-2 * PI, in1=x_ap, op0=ALU.mult, op1=ALU.add))
                dv(lambda e: e.tensor_scalar(out=RS, in0=RS, scalar1=-PI, scalar2=PI, op0=ALU.max, op1=ALU.min))
                ac(lambda e: e.activation(out=dst, in_=RS, func=AF.Sin))
            reduce_sin(SIN, XI)
            dv(lambda e: e.tensor_scalar(out=XC, in0=XI, scalar1=PI / 2, scalar2=None, op0=ALU.add))
            reduce_sin(COS, XC)
            dv(lambda e: e.tensor_tensor(out=LR, in0=MAG, in1=COS, op=ALU.mult))
            dv(lambda e: e.tensor_tensor(out=LI, in0=MAG, in1=SIN, op=ALU.mult))
            dv(lambda e: e.tensor_scalar(out=NR, in0=LR, scalar1=-1.0, scalar2=None, op0=ALU.add))
            dv(lambda e: e.tensor_tensor(out=DEN, in0=a_re, in1=a_re, op=ALU.mult))
            dv(lambda e: e.tensor_tensor(out=TA, in0=a_im, in1=a_im, op=ALU.mult))
            dv(lambda e: e.tensor_tensor(out=DEN, in0=DEN, in1=TA, op=ALU.add))
            dv(lambda e: e.reciprocal(out=DEN, in_=DEN))
            dv(lambda e: e.tensor_tensor(out=TA, in0=NR, in1=a_re, op=ALU.mult))
            dv(lambda e: e.tensor_tensor(out=TB_, in0=LI, in1=a_im, op=ALU.mult))
            dv(lambda e: e.tensor_tensor(out=CR, in0=TA, in1=TB_, op=ALU.add))
            dv(lambda e: e.tensor_tensor(out=CR, in0=CR, in1=DEN, op=ALU.mult))
            dv(lambda e: e.tensor_tensor(out=TA, in0=LI, in1=a_re, op=ALU.mult))
            dv(lambda e: e.tensor_tensor(out=TB_, in0=NR, in1=a_im, op=ALU.mult))
            dv(lambda e: e.tensor_tensor(out=CI, in0=TA, in1=TB_, op=ALU.subtract))
            dv(lambda e: e.tensor_tensor(out=CI, in0=CI, in1=DEN, op=ALU.mult))
            b_re, b_im = Bt32[:, 0], Bt32[:, 1]
            for d in range(2):
                crb = CR[:, d, :].unsqueeze(2).to_broadcast([128, NP, 32])
                cib = CI[:, d, :].unsqueeze(2).to_broadcast([128, NP, 32])
                rd = (Bt32b, Wb, bbarb, tbb)
                dv(lambda e, d=d, crb=crb: e.tensor_tensor(out=bbar[:, d, 0], in0=b_re, in1=crb, op=ALU.mult), rd=rd, wr=(bbarb,))
                dv(lambda e, cib=cib: e.tensor_tensor(out=tb[:, 0], in0=b_im, in1=cib, op=ALU.mult), rd=rd, wr=(tbb,))
                dv(lambda e, d=d: e.tensor_tensor(out=bbar[:, d, 0], in0=bbar[:, d, 0], in1=tb[:, 0], op=ALU.subtract), rd=rd, wr=(bbarb,))
                dv(lambda e, d=d, crb=crb: e.tensor_tensor(out=bbar[:, d, 1], in0=b_im, in1=crb, op=ALU.mult), rd=rd, wr=(bbarb,))
                dv(lambda e, cib=cib: e.tensor_tensor(out=tb[:, 1], in0=b_re, in1=cib, op=ALU.mult), rd=rd, wr=(tbb,))
                dv(lambda e, d=d: e.tensor_tensor(out=bbar[:, d, 1], in0=bbar[:, d, 1], in1=tb[:, 1], op=ALU.add), rd=rd, wr=(bbarb,))
            K.dve.op(lambda e: e.memset(bbv[:], 0.0), writes=[bbvb])
            for v in range(2):
                K.dve.op(lambda e, v=v: e.tensor_copy(out=bbv[:, v, :, :, v::2, :], in_=bbar[:, :, :, v::2, :]), reads=[bbarb, bbvb], writes=[bbvb])
            for v in range(2):
              for d in range(2):
                for ri in range(2):
                    for q in range(NQ):
                        ps, psb = self.next_psum()
                        K.pe.op(lambda e, v=v, d=d, ri=ri, q=q, ps=ps: e.transpose(
                            out=ps[:, :128], in_=bbv[:, v, d, ri, q * 4:(q + 1) * 4, :].rearrange("p a b -> p (a b)"), identity=self.identt[:]),
                            reads=[bbvb, self.identb], writes=[psb])
                        K.act.op(lambda e, v=v, d=d, ri=ri, q=q, ps=ps: e.copy(out=Bt[:, v, d, ri, q, :], in_=ps[:, :128]), reads=[psb], writes=[Btb])
            K.act.op(lambda e: e.copy(out=Ct[:, :, :, 0, :], in_=Ct32[:, :, :, 0, :]), reads=[Ct32b], writes=[Ctb])
            K.act.op(lambda e: e.mul(out=Ct[:, :, :, 1, :], in_=Ct32[:, :, :, 1, :], mul=-1.0), reads=[Ct32b], writes=[Ctb])
            N2 = 2 * NP
            cosf = COS.rearrange("p d k -> p (d k)")
            sinf = SIN.rearrange("p d k -> p (d k)")
            dv(lambda e: e.tensor_copy(out=E[:, 0, :], in_=cosf), wr=(Eb,))
            dv(lambda e: e.tensor_copy(out=E[:, 1, :], in_=sinf), wr=(Eb,))
            dv(lambda e: e.tensor_copy(out=R[:, :, 0, 0], in_=cosf), wr=(Rb,))
            dv(lambda e: e.tensor_copy(out=R[:, :, 1, 0], in_=sinf), wr=(Rb,))
            rdR = (Rb, Eb, rt1b, rt2b)
            for kk in range(LOG):
                n = 1 << kk
                er = E[:, 0, :].unsqueeze(2).to_broadcast([128, N2, n])
                ei = E[:, 1, :].unsqueeze(2).to_broadcast([128, N2, n])
                sre, sim_ = R[:, :, 0, 0:n], R[:, :, 1, 0:n]
                dre, dim_ = R[:, :, 0, n:2 * n], R[:, :, 1, n:2 * n]
                a1, a2 = rt1[:, :, 0:n], rt2[:, :, 0:n]
                dv(lambda e, a1=a1, sre=sre, er=er: e.tensor_tensor(out=a1, in0=sre, in1=er, op=ALU.mult), rd=rdR, wr=(rt1b,))
                dv(lambda e, a2=a2, sim_=sim_, ei=ei: e.tensor_tensor(out=a2, in0=sim_, in1=ei, op=ALU.mult), rd=rdR, wr=(rt2b,))
                dv(lambda e, a1=a1, a2=a2, dre=dre: e.tensor_tensor(out=dre, in0=a1, in1=a2, op=ALU.subtract), rd=rdR, wr=(Rb,))
                dv(lambda e, a1=a1, sre=sre, ei=ei: e.tensor_tensor(out=a1, in0=sre, in1=ei, op=ALU.mult), rd=rdR, wr=(rt1b,))
                dv(lambda e, a2=a2, sim_=sim_, er=er: e.tensor_tensor(out=a2, in0=sim_, in1=er, op=ALU.mult), rd=rdR, wr=(rt2b,))
                dv(lambda e, a1=a1, a2=a2, dim_=dim_: e.tensor_tensor(out=dim_, in0=a1, in1=a2, op=ALU.add), rd=rdR, wr=(Rb,))
                dv(lambda e: e.tensor_tensor(out=E[:, 2, :], in0=E[:, 0, :], in1=E[:, 0, :], op=ALU.mult), rd=(Eb,), wr=(Eb,))
                dv(lambda e: e.tensor_tensor(out=E[:, 3, :], in0=E[:, 1, :], in1=E[:, 1, :], op=ALU.mult), rd=(Eb,), wr=(Eb,))
                dv(lambda e: e.tensor_tensor(out=E[:, 1, :], in0=E[:, 0, :], in1=E[:, 1, :], op=ALU.mult), rd=(Eb,), wr=(Eb,))
                dv(lambda e: e.tensor_scalar(out=E[:, 1, :], in0=E[:, 1, :], scalar1=2.0, scalar2=None, op0=ALU.mult), rd=(Eb,), wr=(Eb,))
                dv(lambda e: e.tensor_tensor(out=E[:, 0, :], in0=E[:, 2, :], in1=E[:, 3, :], op=ALU.subtract), rd=(Eb,), wr=(Eb,))
            K.dve.op(lambda e: e.memset(carry[:], 0.0), writes=[b for row in carryb for b in row])
            K.barrier()
            st2.close()
            wglu, wglub = self.sb(st, "s_wglu", [128, 2 * BC, BC, 128], BF16)
            ubufs = [(self.sb(st, f"s_u32{i}", [128, BC, TS], F32), self.sb(st, f"s_ubf{i}", [128, BC, TS], BF16),
                      self.sb(st, f"s_yb{i}", [128, BC, TS], F32)) for i in range(2)]
            yt, ytb = self.sb(st, "s_yt", [128, BC, TS], F32)
            yg, ygb = self.sb(st, "s_yg", [128, BC, TS], BF16)
            brd, brdb = self.sb(st, "s_brd", [128, BC, TS], BF16)
            sig, sigb = self.sb(st, "s_sig", [128, TS], F32)
            sets = []
            for i in range(6):
                d = {}
                for nm in ("t1", "t2", "bt", "st", "s32"):
                    d[nm] = self.sb(st, f"s_{nm}{i}", [128, 2, TS], F32)
                d["sbf"] = self.sb(st, f"s_sbf{i}", [128, 2, TS], BF16)
                sets.append(d)
            K.sp.dma(wglu[:], self.wb[l][:, wp["glu"]:wp["glu"] + 2 * BC * BC * 128], wglub)
            ctx_subs = [(t, TS) for t in range(0, CTX, TS)]
            lat_subs = [(t, TS) for t in range(CTX, NT, TS)]
            o_d, _ = pp["s5_d"]
            NSET = len(sets)
            ybd = {t: Buf(f"ybd{t}") for (t, _) in ctx_subs + lat_subs}
            items = []
            subs_all = []
            for dirn in (1, 0):
                order = (ctx_subs + lat_subs) if dirn == 0 else (ctx_subs[::-1] + lat_subs[::-1])
                for (t0, w) in order:
                    si = len(subs_all)
                    subs_all.append((dirn, t0))
                    for k in range(NP):
                        items.append((dirn, si, t0, k, k == 0, k == NP - 1))

            def sub_bufs(si):
                return ubufs[si % 2]

            def prologue(si):
                dirn, t0 = subs_all[si]
                (u32, u32b), (ubf, ubfb), (ybt, ybtb) = sub_bufs(si)
                K.pool.dma(u32[:], self.U[:, :, t0:t0 + TS].rearrange("c p t -> p c t"), u32b)
                if dirn == 0:
                    K.pool.dma(ybt[:], self.YB[:, :, t0:t0 + TS].rearrange("c p t -> p c t"), ybtb, reads=[ybd[t0]])
                    K.act.op(lambda e: e.copy(out=ubf[:], in_=u32[:]), reads=[u32b], writes=[ubfb])
                else:
                    K.act.op(lambda e: e.copy(out=ubf[:], in_=u32[:, :, ::-1]), reads=[u32b], writes=[ubfb])

            def psy_of(si):
                b = 4 + 2 * (si % 2)
                return [self.ps[b], self.ps[b + 1]], [self.psb[b], self.psb[b + 1]]

            def stage(sn, idx):
                dirn, si, t0, k, first, lastk = items[idx]
                (u32, u32b), (ubf, ubfb), (ybt, ybtb) = sub_bufs(si)
                q, slot = k // 4, k % 4
                rows = slice(64 * (slot // 2), 64 * (slot // 2) + 64)
                S_ = sets[idx % NSET]
                (t1, t1b), (t2, t2b), (bt, btb), (stt, sttb) = S_["t1"], S_["t2"], S_["bt"], S_["st"]
                (t3, t3b), (t4, t4b), (s32, s32b), (sbf, sbfb) = S_["t1"], S_["t2"], S_["s32"], S_["sbf"]
                rr = R[:, dirn * NP + k, 0, :].unsqueeze(1).to_broadcast([128, 2, TS])
                rim = R[:, dirn * NP + k, 1, :].unsqueeze(1).to_broadcast([128, 2, TS])
                if sn == 0:
                    if first and si == 0:
                        prologue(0)
                    if k == min(NP - 1, 4) and si + 1 < len(subs_all):
                        prologue(si + 1)
                    i = self.psum_i
                    self.psum_i = (i + 1) % 4
                    ps, psb = self.ps[i], self.psb[i]
                    for ri in range(2):
                        K.pe.op(lambda e, ri=ri: e.matmul(
                            ps[:, ri * TS:(ri + 1) * TS], lhsT=Bt[rows, slot % 2, dirn, ri, q, :], rhs=ubf[rows, q, :], start=True, stop=True),
                            reads=[Btb, ubfb], writes=[psb])
                    bu = ps[:, :2 * TS].rearrange("p (r t) -> p r t", r=2)
                    K.dve.op(lambda e: e.tensor_tensor(out=t1[:], in0=bu, in1=rr, op=ALU.mult), reads=[psb, Rb], writes=[t1b])
                    K.dve.op(lambda e: e.tensor_tensor(out=t2[:], in0=bu[:, ::-1, :], in1=rim, op=ALU.mult), reads=[psb, Rb], writes=[t2b])
                elif sn == 1:
                    K.pool.op(lambda e: e.tensor_tensor(out=bt[:, 0, :], in0=t1[:, 0, :], in1=t2[:, 0, :], op=ALU.add), reads=[t1b, t2b], writes=[btb])
                    K.pool.op(lambda e: e.tensor_tensor(out=bt[:, 1, :], in0=t1[:, 1, :], in1=t2[:, 1, :], op=ALU.subtract), reads=[t1b, t2b], writes=[btb])
                elif sn == 2:
                    dec = W[:, 3, dirn, k:k + 1].to_broadcast([128, TS])
                    for ri in range(2):
                        K.dve.op(lambda e, ri=ri: e.tensor_tensor_scan(
                            out=stt[:, ri, :], data0=dec, data1=bt[:, ri, :], initial=carry[:, dirn, k, ri:ri + 1], op0=ALU.mult, op1=ALU.add),
                            reads=[btb, Wb, carryb[dirn][k]], writes=[sttb])
                elif sn == 3:
                    eD1 = K.dve if getattr(self, "s5_demod_dve", True) else K.pool
                    eD2 = K.dve if getattr(self, "s5_comb_dve", True) else K.pool
                    eD1.op(lambda e: e.tensor_tensor(out=t3[:], in0=stt[:], in1=rr, op=ALU.mult), reads=[sttb, Rb], writes=[t3b])
                    eD1.op(lambda e: e.tensor_tensor(out=t4[:], in0=stt[:, ::-1, :], in1=rim, op=ALU.mult), reads=[sttb, Rb], writes=[t4b])
                    eD2.op(lambda e: e.tensor_tensor(out=s32[:, 0, :], in0=t3[:, 0, :], in1=t4[:, 0, :], op=ALU.subtract), reads=[t3b, t4b], writes=[s32b])
                    eD2.op(lambda e: e.tensor_tensor(out=s32[:, 1, :], in0=t3[:, 1, :], in1=t4[:, 1, :], op=ALU.add), reads=[t3b, t4b], writes=[s32b])
                elif sn == 4:
                    K.act.op(lambda e: e.copy(out=carry[:, dirn, k, :], in_=s32[:, :, TS - 1]), reads=[s32b], writes=[carryb[dirn][k]])
                    if dirn == 0:
                        K.act.op(lambda e: e.copy(out=sbf[:], in_=s32[:]), reads=[s32b], writes=[sbfb])
                    else:
                        K.act.op(lambda e: e.copy(out=sbf[:], in_=s32[:, :, ::-1]), reads=[s32b], writes=[sbfb])
                    psy, psyb = psy_of(si)
                    py, pyb = psy[q // 2], psyb[q // 2]
                    c0 = (q % 2) * TS
                    for ri in range(2):
                        K.pe.op(lambda e, ri=ri: e.matmul(
                            py[rows, c0:c0 + TS], lhsT=Ct[:, dirn, k, ri, :], rhs=sbf[:, ri, :],
                            start=(ri == 0 and slot % 2 == 0), stop=(ri == 1 and slot % 2 == 1)), reads=[Ctb, sbfb], writes=[pyb])
                    if lastk:
                        epilogue(si)

            def epilogue(si):
                dirn, t0 = subs_all[si]
                (u32, u32b), (ubf, ubfb), (ybt, ybtb) = sub_bufs(si)
                psy, psyb = psy_of(si)
                if dirn == 1:
                    for q in range(BC):
                        K.act.op(lambda e, q=q: e.copy(out=ybt[:, q, :], in_=psy[q // 2][:, (q % 2) * TS:(q % 2 + 1) * TS]),
                                 reads=[psyb[q // 2]], writes=[ybtb])
                    K.pool.dma(self.YB[:, :, t0:t0 + TS].rearrange("c p t -> p c t"), ybt[:], ybtb, load=False, writes=[ybd[t0]])
                else:
                    for q in range(BC):
                        K.dve.op(lambda e, q=q: e.tensor_tensor(out=yt[:, q, :], in0=psy[q // 2][:, (q % 2) * TS:(q % 2 + 1) * TS],
                                                                in1=ybt[:, q, :], op=ALU.add), reads=[psyb[q // 2], ybtb], writes=[ytb])
                        K.dve.op(lambda e, q=q: e.scalar_tensor_tensor(out=yt[:, q, :], in0=u32[:, q, :], scalar=self.pv[:, l, o_d + q:o_d + q + 1],
                                                                       in1=yt[:, q, :], op0=ALU.mult, op1=ALU.add),
                                 reads=[u32b, self.pvb, ytb], writes=[ytb])
                        K.act.op(lambda e, q=q: e.activation(out=yg[:, q, :], in_=yt[:, q, :], func=AF.Gelu_apprx_tanh), reads=[ytb], writes=[ygb])
                    for oc in range(BC):
                        i = self.psum_i
                        self.psum_i = (i + 1) % 4
                        psg, psgb = self.ps[i], self.psb[i]
                        self.mm_group(psg[:, :TS], psgb, [wglu[:, BC + oc, bc, :] for bc in range(BC)], [yg[:, bc, :] for bc in range(BC)], [wglub, ygb])
                        K.act.op(lambda e, psg=psg: e.activation(out=sig[:, :], in_=psg[:, :TS], func=AF.Sigmoid), reads=[psgb], writes=[sigb])
                        i = self.psum_i
                        self.psum_i = (i + 1) % 4
                        psa, psab = self.ps[i], self.psb[i]
                        self.mm_group(psa[:, :TS], psab, [wglu[:, oc, bc, :] for bc in range(BC)], [yg[:, bc, :] for bc in range(BC)], [wglub, ygb])
                        K.dve.op(lambda e, oc=oc, psa=psa: e.tensor_tensor(out=brd[:, oc, :], in0=psa[:, :TS], in1=sig[:, :], op=ALU.mult),
                                 reads=[psab, sigb], writes=[brdb])
                    K.pool.dma(self.BR[3, :, :, t0:t0 + TS].rearrange("c p t -> p c t"), brd[:], brdb, load=False)

            self.psum_i = 0
            NST = 5
            for step in range(len(items) + NST - 1):
                for sn in reversed(range(NST)):
                    idx = step - sn
                    if 0 <= idx < len(items):
                        stage(sn, idx)
            self.psum_i = 0

    def phase_final(self):
        cfg, K, nc = self.cfg, self.K, self.nc
        KC, T, CTX = cfg.KC, cfg.T, cfg.CTX
        L = cfg.DEPTH
        o_g, _ = self.pplan["n1g"]
        with ExitStack() as st:
            xt, xtb = self.sb(st, "f_xt", [128, KC, T], F32)
            ot, otb = self.sb(st, "f_ot", [128, KC, T], F32)
            sq = [self.sb(st, f"f_sq{i}", [128, T], F32) for i in range(2)]
            rs, rsb = self.sb(st, "f_rs", [128, T], F32)
            eps, epsb = self.sb(st, "f_eps", [128, 1], F32)
            K.dve.op(lambda e: e.memset(eps[:], cfg.EPS), writes=[epsb])
            for (t0, w, is_ctx) in cfg.tiles:
                if is_ctx:
                    continue
                K.pool.dma(xt[:, :, :w], self.X[:, :, t0:t0 + w].rearrange("c p t -> p c t"), xtb)
                ps, psb = self.next_psum()
                for kc in range(KC):
                    s_, sb_ = sq[kc % 2]
                    K.act.op(lambda e, kc=kc, s_=s_: e.activation(out=s_[:, :w], in_=xt[:, kc, :w], func=AF.Square), reads=[xtb], writes=[sb_])
                    K.pe.op(lambda e, kc=kc, s_=s_: e.matmul(ps[:, :w], lhsT=self.ones[:], rhs=s_[:, :w], start=(kc == 0), stop=(kc == KC - 1)),
                            reads=[self.onesb, sb_], writes=[psb])
                K.act.op(lambda e: e.activation(out=rs[:, :w], in_=ps[:, :w], func=AF.Sqrt, scale=1.0 / cfg.D, bias=eps[:, 0:1]),
                         reads=[psb, epsb], writes=[rsb])
                K.dve.op(lambda e: e.reciprocal(out=rs[:, :w], in_=rs[:, :w]), reads=[rsb], writes=[rsb])
                for kc in range(KC):
                    eng = K.dve
                    eng.op(lambda e, kc=kc: e.scalar_tensor_tensor(out=ot[:, kc, :w], in0=xt[:, kc, :w], scalar=self.pv[:, L, o_g + kc:o_g + kc + 1],
                                                                  in1=rs[:, :w], op0=ALU.mult, op1=ALU.mult),
                           reads=[xtb, rsb, self.pvb], writes=[otb])
                K.pool.dma(self.out[:, :, t0 - CTX:t0 - CTX + w].rearrange("c p t -> p c t"), ot[:, :, :w], otb, load=False)


_CACHE = {}


def kernel(**inputs):
    cfg = Cfg()
    inp = {k: np.asarray(v) for k, v in inputs.items()}
    if "nc" not in _CACHE:
        _CACHE["nc"] = Prog(cfg).build()
    nc = _CACHE["nc"]
    shared = prep_shared(cfg, inp)
    n = 8
    owners = [0, 1, 4, 5]
    in_maps = []
    zero_cache = {}
    for c in range(n):
        m = dict(shared)
        if c in owners:
            m.update(prep_inputs(cfg, inp, owners.index(c)))
        else:
            if not zero_cache:
                ref_m = prep_inputs(cfg, inp, 0)
                for k_, v_ in ref_m.items():
                    zero_cache[k_] = np.zeros_like(v_)
                for k_ in ("pvec", "rowv"):
                    zero_cache[k_] = np.zeros_like(shared[k_])
            m.update(zero_cache)
        in_maps.append(m)
    res = run_bass_kernel_spmd(nc, in_maps, core_ids=list(range(n)))
    out = np.empty((cfg.BATCH, cfg.SEQ, cfg.D), np.float32)
    for b in range(cfg.BATCH):
        o = np.asarray(res.results[owners[b]]["out"])
        out[b] = o.reshape(cfg.D, cfg.SEQ).T
    return out
```

```python
import math
import numpy as np
import concourse.bass as bass
import concourse.mybir as mybir
from concourse.bass_utils import run_bass_kernel_spmd

F32 = mybir.dt.float32
BF16 = mybir.dt.bfloat16
AF = mybir.ActivationFunctionType
ALU = mybir.AluOpType
AX = mybir.AxisListType


class Cfg:
    def __init__(self, D=2048, SEQ=8192, CTX=256, DEPTH=4, GRID_W=64, T=512, BATCH=4):
        self.D, self.SEQ, self.CTX, self.DEPTH, self.GRID_W, self.T, self.BATCH = D, SEQ, CTX, DEPTH, GRID_W, T, BATCH
        self.DB = D // 4
        self.KC = D // 128
        self.BC = self.DB // 128
        self.FF = 4 * D
        self.FFC = self.FF // 128
        self.HQ = self.DB // 64
        self.QPK = self.HQ // 2
        self.G = self.DB // 16
        self.NPAIR = self.G // 2
        self.NT = CTX + SEQ
        self.IN_COLS = 2 * self.DB + 2 * self.DB + self.DB + 256 + self.DB
        self.CONV_W = 31
        self.EPS = 1e-6
        BC = self.BC
        self.c_u = 0
        self.c_v = BC
        self.c_a = 2 * BC
        self.c_g = 3 * BC
        self.c_q = 4 * BC
        self.c_k = 5 * BC
        self.c_vv = 5 * BC + 1
        self.c_d = 5 * BC + 2
        self.NCI = 6 * BC + 2
        tiles = []
        s = 0
        while s < CTX:
            w = min(T, CTX - s)
            tiles.append((s, w, True))
            s += w
        while s < self.NT:
            w = min(T, self.NT - s)
            tiles.append((s, w, False))
            s += w
        self.tiles = tiles
        self.YPAD = 16
        self.NY = self.NT + 4 * self.YPAD


class Buf:
    __slots__ = ("name", "w", "r", "dsem", "dcnt", "dq")

    def __init__(self, name):
        self.name = name
        self.w = None
        self.r = []
        self.dsem = None
        self.dcnt = 0


class Eng:
    def __init__(self, K, eng, name, sem):
        self.K, self.eng, self.name, self.sem = K, eng, name, sem
        self.count = 0
        self.waited = {}

    def _wait(self, ev):
        if ev is None:
            return
        sem, val = ev
        if sem is self.sem and not self.K.same_engine_sync:
            return
        if sem is self.sem and self.name == "pe":
            return
        if self.waited.get(id(sem), 0) >= val:
            return
        self.eng.wait_ge(sem, val)
        self.waited[id(sem)] = val

    def deps(self, reads, writes):
        evs = {}

        def add(ev):
            if ev is None:
                return
            k = id(ev[0])
            if k not in evs or evs[k][1] < ev[1]:
                evs[k] = ev
        for b in reads:
            add(b.w)
        for b in writes:
            add(b.w)
            for ev in b.r:
                add(ev)
        for ev in evs.values():
            self._wait(ev)

    def op(self, fn, reads=(), writes=()):
        self.deps(reads, writes)
        ins = fn(self.eng)
        self.count += 1
        ins.then_inc(self.sem, 1)
        ev = (self.sem, self.count)
        for b in reads:
            b.r = [x for x in b.r if x[0] is not ev[0]] + [ev]
        for b in writes:
            b.w = ev
            b.r = []
        return ins

    def dma(self, out, in_, sbuf, reads=(), writes=(), load=True, **kw):
        K = self.K
        if sbuf.dsem is None:
            fl = K.free_sems.setdefault(self.name, [])
            if fl:
                sbuf.dsem, sbuf.dcnt = fl.pop()
            else:
                sbuf.dsem, sbuf.dcnt = K.new_sem(f"dsem{len(K._stack)}"), 0
            sbuf.dq = self.name
        assert sbuf.dq == self.name, "a buffer's DMAs must stay on one queue"
        rd = list(reads) + ([] if load else [sbuf])
        wr = list(writes) + ([sbuf] if load else [])
        self.deps(rd, wr)
        ins = self.eng.dma_start(out=out, in_=in_, **kw)
        sbuf.dcnt += 1
        ins.then_inc(sbuf.dsem, 16)
        ev = (sbuf.dsem, 16 * sbuf.dcnt)
        for b in rd:
            b.r = [x for x in b.r if x[0] is not ev[0]] + [ev]
        for b in wr:
            b.w = ev
            b.r = []
        K.dma_bufs[id(sbuf)] = sbuf
        return ins


class Kern:
    def __init__(self, nc, same_engine_sync=True):
        self.nc = nc
        self.same_engine_sync = same_engine_sync
        self.sems = []
        self.dma_bufs = {}
        self.free_sems = {}
        self._stack = []

    def new_sem(self, name):
        cm = self.nc.semaphore(name)
        h = cm.__enter__()
        self._stack.append(cm)
        return h

    def make_engines(self, block_engs):
        self.pe = Eng(self, block_engs["tensor"], "pe", self.new_sem("s_pe"))
        self.act = Eng(self, block_engs["scalar"], "act", self.new_sem("s_act"))
        self.dve = Eng(self, block_engs["vector"], "dve", self.new_sem("s_dve"))
        self.pool = Eng(self, block_engs["gpsimd"], "pool", self.new_sem("s_pool"))
        self.sp = Eng(self, block_engs["sync"], "sp", self.new_sem("s_sp"))
        self.engs = [self.pe, self.act, self.dve, self.pool, self.sp]

    def barrier(self):
        for e in self.engs:
            for o in self.engs:
                if o is not e and o.count > 0:
                    e._wait((o.sem, o.count))
            for b in self.dma_bufs.values():
                if b.dcnt:
                    e._wait((b.dsem, 16 * b.dcnt))
        for e in self.engs:
            if e.count > 20000:
                e.sem = self.new_sem(f"s_{e.name}_{len(self._stack)}")
                e.count = 0
        for b in self.dma_bufs.values():
            self.free_sems[b.dq].append((b.dsem, b.dcnt))
            b.dsem = None
        self.dma_bufs = {}

    def close(self):
        for cm in reversed(self._stack):
            cm.__exit__(None, None, None)


def lhsT_chunks(W, ncols=128):
    K, N = W.shape
    return np.ascontiguousarray(W.reshape(K // 128, 128, N // ncols, ncols).transpose(1, 2, 0, 3))


def weight_plan(cfg):
    KC, BC, FFC, DB = cfg.KC, cfg.BC, cfg.FFC, cfg.DB
    plan = {}
    off = 0

    def add(name, n):
        nonlocal off
        plan[name] = off
        off += n
    add("inF", (6 * BC + 2) * KC * 128)
    add("inV1", KC * DB)
    add("inV2", KC * 128)
    add("wsT", BC * 128)
    add("glu", 2 * BC * BC * 128)
    add("gb", KC * (4 * KC * 128 + 4 * BC * 128))
    add("out", KC * KC * 128)
    add("ff1", FFC * KC * 128)
    add("ff2", KC * FFC * 128)
    tot = off
    tot = (tot + 8191) // 8192 * 8192
    plan["_total"] = tot
    return plan


def swap16(W):
    K, N = W.shape
    return np.ascontiguousarray(W.reshape(K, N // 32, 2, 16)[:, :, ::-1, :].reshape(K, N))


def pack_layer_weights(cfg, inp, l):
    KC, BC, FFC, DB, D = cfg.KC, cfg.BC, cfg.FFC, cfg.DB, cfg.D
    plan = weight_plan(cfg)
    flat = np.zeros((128, plan["_total"]), np.float32)

    def put(name, arr):
        a = arr.reshape(128, -1)
        flat[:, plan[name]:plan[name] + a.shape[1]] = a
    w_in = inp["w_in"][l]
    cA, cB, cQ, cK, cV, cD = 0, 2 * DB, 4 * DB, 5 * DB, 5 * DB + 128, 5 * DB + 256
    cols = []
    for i in range(BC):
        cols.append(w_in[:, cA + i * 128: cA + (i + 1) * 128])
    for i in range(BC):
        cols.append(w_in[:, cB + DB + i * 128: cB + DB + (i + 1) * 128])
        cols.append(w_in[:, cB + i * 128: cB + (i + 1) * 128])
    wq = w_in[:, cQ:cQ + DB]
    wqs = swap16(wq)
    for i in range(BC):
        cols.append(wq[:, i * 128:(i + 1) * 128])
        cols.append(wqs[:, i * 128:(i + 1) * 128])
    wk = w_in[:, cK:cK + 128]
    cols.append(wk)
    cols.append(swap16(wk))
    for i in range(BC):
        cols.append(w_in[:, cD + i * 128: cD + (i + 1) * 128])
    put("inF", lhsT_chunks(np.concatenate(cols, axis=1)))
    put("inV1", lhsT_chunks(w_in[:, cA + DB: cA + 2 * DB], ncols=DB))
    put("inV2", lhsT_chunks(w_in[:, cV:cV + 128]))
    put("wsT", np.ascontiguousarray(inp["gmlp_ws"][l].transpose(2, 0, 1)))
    put("glu", lhsT_chunks(inp["s5_w_glu"][l]))
    g4 = np.stack([lhsT_chunks(inp["w_gate"][l, k]) for k in range(4)], axis=2)
    b4 = np.stack([lhsT_chunks(inp["w_branch"][l, k]) for k in range(4)], axis=2)
    gb = np.concatenate([g4.reshape(128, KC, -1), b4.reshape(128, KC, -1)], axis=2)
    put("gb", gb)
    put("out", lhsT_chunks(inp["w_out"][l]))
    put("ff1", lhsT_chunks(inp["w_ff1"][l]))
    put("ff2", lhsT_chunks(inp["w_ff2"][l]))
    return flat


def fm(v):
    return np.ascontiguousarray(v.reshape(-1, 128).T)


def pvec_plan(cfg):
    KC, BC = cfg.KC, cfg.BC
    plan = {}
    off = 0
    for name, n in (("b_mod", 6 * KC), ("n1g", KC), ("n2g", KC), ("b_gate", 4 * KC), ("conv_w", BC * 31),
                    ("conv_b", BC), ("cln_g", BC), ("cln_b", BC), ("s5_d", BC)):
        plan[name] = (off, n)
        off += n
    plan["_total"] = off
    return plan


def rowv_plan(cfg):
    DB, BC = cfg.DB, cfg.BC
    plan = {}
    off = 0
    for name, n in (("gln_g", DB), ("gln_b", DB), ("gbs", BC * 128), ("sink", cfg.HQ)):
        plan[name] = (off, n)
        off += n
    plan["_total"] = off
    return plan


def prep_inputs(cfg, inp, core):
    L, KC, BC, D, DB = cfg.DEPTH, cfg.KC, cfg.BC, cfg.D, cfg.DB
    b = core % cfg.BATCH
    m = {}
    xcat = np.concatenate([inp["ctx"][b], inp["x"][b]], axis=0)
    m["xT"] = np.ascontiguousarray(xcat.T.reshape(KC, 128, cfg.NT))
    cond = np.stack([fm(inp["c_ctx"]), fm(inp["c"][b])], axis=2)
    m["cond"] = np.ascontiguousarray(cond)
    return m


def prep_shared(cfg, inp):
    L, KC, BC, D, DB = cfg.DEPTH, cfg.KC, cfg.BC, cfg.D, cfg.DB
    m = {}
    m["wall"] = np.stack([pack_layer_weights(cfg, inp, l) for l in range(L)], axis=0)
    m["wmod"] = np.stack([lhsT_chunks(inp["w_mod"][l]) for l in range(L)], axis=0)
    pp = pvec_plan(cfg)
    pv = np.zeros((128, L + 1, pp["_total"]), np.float32)
    for l in range(L):
        def put(name, a):
            o, n = pp[name]
            pv[:, l, o:o + n] = a.reshape(128, n)
        put("b_mod", fm(inp["b_mod"][l]))
        put("n1g", fm(inp["norm1_g"][l]))
        put("n2g", fm(inp["norm2_g"][l]))
        put("b_gate", np.stack([fm(inp["b_gate"][l, k]) for k in range(4)], axis=1))
        cw = inp["conv_w"][l]
        put("conv_w", np.ascontiguousarray(cw.T.reshape(BC, 128, 31).transpose(1, 0, 2)))
        put("conv_b", fm(inp["conv_b"][l]))
        put("cln_g", fm(inp["conv_ln_g"][l]))
        put("cln_b", fm(inp["conv_ln_b"][l]))
        put("s5_d", fm(inp["s5_d"][l]))
    o, n = pp["n1g"]
    pv[:, L, o:o + n] = fm(inp["final_g"])
    m["pvec"] = pv
    rp = rowv_plan(cfg)
    rv = np.zeros((128, L, rp["_total"]), np.float32)
    for l in range(L):
        for name, a in (("gln_g", inp["gmlp_ln_g"][l]), ("gln_b", inp["gmlp_ln_b"][l]),
                        ("gbs", inp["gmlp_bs"][l].reshape(-1)), ("sink", inp["attn_sink"][l])):
            o, n = rp[name]
            rv[:, l, o:o + n] = np.broadcast_to(a.reshape(1, n), (128, n))
    m["rowv"] = rv
    half = 32
    inv = (10000.0 ** (-np.arange(0, half, 2, dtype=np.float32) / half)).astype(np.float32)
    pos = np.arange(cfg.SEQ)
    row = (pos // cfg.GRID_W).astype(np.float32)
    col = (pos % cfg.GRID_W).astype(np.float32)
    cosT = np.ones((128, cfg.NT), np.float32)
    sinT = np.zeros((128, cfg.NT), np.float32)
    for p in range(128):
        d = p % 64
        blk, j = d // 32, d % 32
        ang = ((row if blk == 0 else col) * inv[j % 16]).astype(np.float32)
        cosT[p, cfg.CTX:] = np.cos(ang)
        sinT[p, cfg.CTX:] = np.sin(ang) * (-1.0 if j < 16 else 1.0)
    m["ropec"] = cosT
    m["ropes"] = sinT
    kl = np.arange(128)[:, None]
    ql = np.arange(128)[None, :]
    m["mask_prev"] = (kl >= ql).astype(np.float32)
    m["mask_next"] = (kl <= ql).astype(np.float32)
    m["ident"] = np.eye(128, dtype=np.float32)
    NP = cfg.NPAIR
    A = np.zeros((128, L, 2, NP, 3), np.float32)
    Bm = np.zeros((128, L, 2, NP, 32), np.float32)
    Cm = np.zeros((128, L, 2, NP, 2, 64), np.float32)
    for g2 in range(2):
        rows = slice(g2 * 64, (g2 + 1) * 64)
        gidx = 2 * np.arange(NP) + g2
        A[rows, :, :, :, 0] = inp["s5_a_re"][:, :, gidx, :].transpose(3, 0, 1, 2)
        A[rows, :, :, :, 1] = inp["s5_a_im"][:, :, gidx, :].transpose(3, 0, 1, 2)
        A[rows, :, :, :, 2] = np.broadcast_to(inp["s5_log_step"][:, :, gidx][None], (64, L, 2, NP))
        cs_ = slice(g2 * 16, (g2 + 1) * 16)
        Bm[rows, :, 0, :, cs_] = inp["s5_b_re"][:, gidx, :, :].transpose(2, 0, 1, 3)
        Bm[rows, :, 1, :, cs_] = inp["s5_b_im"][:, gidx, :, :].transpose(2, 0, 1, 3)
        for k in range(NP):
            cc_ = slice(32 * (k % 2) + g2 * 16, 32 * (k % 2) + (g2 + 1) * 16)
            Cm[rows, :, :, k, 0, cc_] = inp["s5_c_re"][:, :, gidx[k], :, :].transpose(3, 0, 1, 2)
            Cm[rows, :, :, k, 1, cc_] = inp["s5_c_im"][:, :, gidx[k], :, :].transpose(3, 0, 1, 2)
    m["s5A"], m["s5B"], m["s5C"] = A, Bm, Cm
    return m


from contextlib import ExitStack

WSLOT = 8192
NWSLOT = 3


class Prog:
    def __init__(self, cfg, debug=False, n_layers=None, stop_after=None, same_engine_sync=True):
        self.cfg = cfg
        self.debug = debug
        self.L = cfg.DEPTH if n_layers is None else n_layers
        self.stop_after = stop_after
        nc = bass.Bass("TRN2", target_bir_lowering=False)
        self.nc = nc
        self.K = Kern(nc, same_engine_sync)
        self.K.make_engines({"tensor": nc.tensor, "scalar": nc.scalar, "vector": nc.vector,
                             "gpsimd": nc.gpsimd, "sync": nc.sync})
        self.wplan = weight_plan(cfg)
        self.pplan = pvec_plan(cfg)
        self.rplan = rowv_plan(cfg)
        self.psum_i = 0

    def din(self, name, shape, dt=F32):
        return self.nc.dram_tensor(name, list(shape), dt, kind="ExternalInput").ap()

    def dscratch(self, name, shape, dt):
        kind = "ExternalOutput" if self.debug else "Internal"
        return self.nc.dram_tensor(name, list(shape), dt, kind=kind).ap()

    def sb(self, st, name, shape, dt):
        self._uid = getattr(self, "_uid", 0) + 1
        name = f"{name}_{self._uid}"
        t = st.enter_context(self.nc.sbuf_tensor(name, list(shape), dt))
        return t, Buf(name)

    def next_psum(self):
        i = self.psum_i
        self.psum_i = (i + 1) % 6
        return self.ps[i], self.psb[i]

    def declare(self):
        cfg, L = self.cfg, self.cfg.DEPTH
        KC, BC, NT = cfg.KC, cfg.BC, cfg.NT
        self.xT = self.din("xT", [KC, 128, NT])
        self.cond = self.din("cond", [128, KC, 2])
        self.wall = self.din("wall", [L, 128, self.wplan["_total"]])
        self.wmod = self.din("wmod", [L, 128, 6 * KC, KC, 128])
        self.pvec = self.din("pvec", [128, L + 1, self.pplan["_total"]])
        self.rowv = self.din("rowv", [128, L, self.rplan["_total"]])
        self.ropec = self.din("ropec", [128, NT])
        self.ropes = self.din("ropes", [128, NT])
        self.mask_prev = self.din("mask_prev", [128, 128])
        self.mask_next = self.din("mask_next", [128, 128])
        self.ident = self.din("ident", [128, 128])
        self.s5A = self.din("s5A", [128, L, 2, cfg.NPAIR, 3])
        self.s5B = self.din("s5B", [128, L, 2, cfg.NPAIR, 32])
        self.s5C = self.din("s5C", [128, L, 2, cfg.NPAIR, 2, 64])
        self.out = self.nc.dram_tensor("out", [KC, 128, cfg.SEQ], F32, kind="ExternalOutput").ap()
        self.wb = [self.dscratch(f"wb{l}", [128, self.wplan["_total"]], BF16) for l in range(L)]
        self.X = self.dscratch("X", [KC, 128, NT], F32)
        self.H = self.dscratch("H", [KC, 128, NT], BF16)
        self.BR = self.dscratch("BR", [4, BC, 128, NT], BF16)
        self.Y = self.dscratch("Y", [BC, 128, cfg.NY], F32)
        self.Q = self.dscratch("Q", [BC, 128, NT], BF16)
        self.KT = self.dscratch("KT", [2, 128, NT], BF16)
        self.V = self.dscratch("V", [NT, 2, 128], BF16)
        self.U = self.dscratch("U", [BC, 128, NT], F32)
        self.YB = self.dscratch("YB", [BC, 128, NT], F32)

    def ypos(self, t):
        cfg = self.cfg
        return t + cfg.YPAD if t < cfg.CTX else t + 3 * cfg.YPAD

    def wload(self, l, off, n):
        i = self.wslot_i
        self.wslot_i = (i + 1) % len(self.wslots)
        t, b = self.wslots[i]
        self.K.sp.dma(t[:, 0:n], self.wb[l][:, off:off + n], b)
        return t, b

    def build(self):
        cfg, K, nc = self.cfg, self.K, self.nc
        self.declare()
        with ExitStack() as st:
            self.ps, self.psb = [], []
            for i in range(8):
                p = st.enter_context(nc.psum_tensor(f"ps{i}", [128, 512], F32))
                self.ps.append(p)
                self.psb.append(Buf(f"ps{i}"))
            self.pv, self.pvb = self.sb(st, "pv", [128, cfg.DEPTH + 1, self.pplan["_total"]], F32)
            self.modv, self.modb = self.sb(st, "modv", [128, cfg.DEPTH, 6 * cfg.KC, 2], F32)
            self.ones, self.onesb = self.sb(st, "ones", [128, 128], F32)
            self.identt, self.identb = self.sb(st, "identt", [128, 128], F32)
            K.pool.dma(self.pv[:], self.pvec[:, :, :], self.pvb)
            K.pool.dma(self.identt[:], self.ident[:, :], self.identb)
            K.dve.op(lambda e: e.memset(self.ones[:], 1.0), writes=[self.onesb])
            self.phase_w()
            K.barrier()
            self.phase_m()
            K.barrier()
            for l in range(self.L):
                self.phase1(l)
                K.barrier()
                if self.stop_after == ("p1", l):
                    break
                self.phase_s(l)
                K.barrier()
                if self.stop_after == ("ps", l):
                    break
                self.phase2(l)
                K.barrier()
            else:
                self.phase_final()
                K.barrier()
        K.close()
        return nc

    def phase_w(self):
        cfg, K, nc = self.cfg, self.K, self.nc
        tot = self.wplan["_total"]
        CH = 8192
        with ExitStack() as st:
            s32 = [self.sb(st, f"w32_{i}", [128, CH], F32) for i in range(2)]
            s16 = [self.sb(st, f"w16_{i}", [128, CH], BF16) for i in range(2)]
            it = 0
            for l in range(self.L):
                for off in range(0, tot, CH):
                    a, ab = s32[it % 2]
                    o, ob = s16[it % 2]
                    K.sp.dma(a[:], self.wall[l, :, off:off + CH], ab)
                    eng = (K.dve, K.act, K.pool)[it % 3]
                    if eng is K.act:
                        eng.op(lambda e: e.copy(out=o[:], in_=a[:]), reads=[ab], writes=[ob])
                    else:
                        eng.op(lambda e: e.tensor_copy(out=o[:], in_=a[:]), reads=[ab], writes=[ob])
                    K.pool.dma(self.wb[l][:, off:off + CH], o[:], ob, load=False)
                    it += 1

    def phase_m(self):
        cfg, K, nc = self.cfg, self.K, self.nc
        KC = cfg.KC
        NJ = 6 * KC
        JB = max(d for d in range(1, NJ + 1) if NJ % d == 0 and d * KC * 128 <= 8192)
        with ExitStack() as st:
            ct, cb = self.sb(st, "condt", [128, KC, 2], F32)
            sc, scb = self.sb(st, "scond", [128, KC, 2], F32)
            ws = [self.sb(st, f"wm_{i}", [128, JB, KC, 128], F32) for i in range(2)]
            K.pool.dma(ct[:], self.cond[:, :, :], cb)
            K.act.op(lambda e: e.activation(out=sc[:], in_=ct[:], func=AF.Silu), reads=[cb], writes=[scb])
            it = 0
            o_b, _ = self.pplan["b_mod"]
            for l in range(self.L):
                for j0 in range(0, NJ, JB):
                    w, wbuf = ws[it % 2]
                    it += 1
                    K.sp.dma(w[:], self.wmod[l, :, j0:j0 + JB, :, :], wbuf)
                    for j in range(j0, j0 + JB):
                        ps, psb = self.next_psum()
                        for kc in range(KC):
                            K.pe.op(lambda e, kc=kc, j=j: e.matmul(ps[:, 0:2], lhsT=w[:, j - j0, kc, :], rhs=sc[:, kc, :],
                                                                    start=(kc == 0), stop=(kc == KC - 1)),
                                    reads=[wbuf, scb], writes=[psb])
                        K.dve.op(lambda e, j=j: e.tensor_tensor(
                            out=self.modv[:, l, j, :], in0=ps[:, 0:2],
                            in1=self.pv[:, l, o_b + j:o_b + j + 1].to_broadcast([128, 2]), op=ALU.add),
                            reads=[psb, self.pvb], writes=[self.modb])

    def rms_to_h(self, st_tiles, xt, xtb, w, mod_scale_idx, mod_shift_idx, gname, l, which, ht, htb):
        cfg, K = self.cfg, self.K
        KC = cfg.KC
        sq = st_tiles["sq"]
        rs, rsb = st_tiles["rstd"]
        ab, abb = st_tiles["ab"]
        hf = st_tiles["hf"]
        o_g, _ = self.pplan[gname]
        K.dve.op(lambda e: e.tensor_scalar(out=ab[:, :, 0], in0=self.modv[:, l, mod_scale_idx * KC:(mod_scale_idx + 1) * KC, which],
                                           scalar1=1.0, scalar2=None, op0=ALU.add),
                 reads=[self.modb], writes=[abb])
        K.dve.op(lambda e: e.tensor_tensor(out=ab[:, :, 0], in0=ab[:, :, 0], in1=self.pv[:, l, o_g:o_g + KC], op=ALU.mult),
                 reads=[abb, self.pvb], writes=[abb])
        K.dve.op(lambda e: e.tensor_copy(out=ab[:, :, 1], in_=self.modv[:, l, mod_shift_idx * KC:(mod_shift_idx + 1) * KC, which]),
                 reads=[self.modb], writes=[abb])
        ps, psb = self.next_psum()
        for kc in range(KC):
            s, sb_ = sq[kc % 2]
            K.act.op(lambda e, kc=kc, s=s: e.activation(out=s[:, :w], in_=xt[:, kc, :w], func=AF.Square),
                     reads=[xtb], writes=[sb_])
            K.pe.op(lambda e, kc=kc, s=s: e.matmul(ps[:, :w], lhsT=self.ones[:], rhs=s[:, :w],
                                                   start=(kc == 0), stop=(kc == KC - 1)),
                    reads=[self.onesb, sb_], writes=[psb])
        K.act.op(lambda e: e.activation(out=rs[:, :w], in_=ps[:, :w], func=AF.Sqrt, scale=1.0 / cfg.D, bias=self.epsc[:, 0:1]),
                 reads=[psb, self.epsb], writes=[rsb])
        K.dve.op(lambda e: e.reciprocal(out=rs[:, :w], in_=rs[:, :w]), reads=[rsb], writes=[rsb])
        for kc in range(KC):
            f, fb = hf[kc % 2]
            K.dve.op(lambda e, kc=kc, f=f: e.tensor_tensor(out=f[:, :w], in0=xt[:, kc, :w], in1=rs[:, :w], op=ALU.mult),
                     reads=[xtb, rsb], writes=[fb])
            K.act.op(lambda e, kc=kc, f=f: e.activation(out=ht[:, kc, :w], in_=f[:, :w], func=AF.Identity,
                                                        scale=ab[:, kc, 0:1], bias=ab[:, kc, 1:2]),
                     reads=[fb, abb], writes=[htb])

    def mm_group(self, ps_ap, psb, lhs_list, rhs_list, reads):
        n = len(lhs_list)
        for i in range(n):
            self.K.pe.op(lambda e, i=i: e.matmul(ps_ap, lhsT=lhs_list[i], rhs=rhs_list[i], start=(i == 0), stop=(i == n - 1)),
                         reads=reads, writes=[psb])

    def phase1(self, l):
        cfg, K, nc = self.cfg, self.K, self.nc
        KC, BC, DB, T = cfg.KC, cfg.BC, cfg.DB, cfg.T
        xsrc = self.xT if l == 0 else self.X
        wp = self.wplan
        CW = KC * 128
        with ExitStack() as st:
            xt, xtb = self.sb(st, "p1_xt", [128, KC, T], F32)
            ht, htb = self.sb(st, "p1_ht", [128, KC, T], BF16)
            tl = {
                "sq": [self.sb(st, f"p1_sq{i}", [128, T], F32) for i in range(2)],
                "hf": [self.sb(st, f"p1_hf{i}", [128, T], F32) for i in range(2)],
                "rstd": self.sb(st, "p1_rstd", [128, T], F32),
                "ab": self.sb(st, "p1_ab", [128, KC, 2], F32),
            }
            self.epsc, self.epsb = self.sb(st, "p1_eps", [128, 1], F32)
            K.dve.op(lambda e: e.memset(self.epsc[:], cfg.EPS), writes=[self.epsb])
            wv1, wv1b = self.sb(st, "p1_wv1", [128, KC, DB], BF16)
            wv2, wv2b = self.sb(st, "p1_wv2", [128, KC, 128], BF16)
            wst, wstb = self.sb(st, "p1_wst", [128, BC, 128], BF16)
            self.wslots = [self.sb(st, f"p1_ws{i}", [128, WSLOT], BF16) for i in range(NWSLOT)]
            self.wslot_i = 0
            rv, rvb = self.sb(st, "p1_rv", [128, self.rplan["_total"]], F32)
            ut, utb = self.sb(st, "p1_ut", [128, BC, T], F32)
            yt, ytb = self.sb(st, "p1_yt", [128, BC, T], F32)
            qt, qtb = self.sb(st, "p1_qt", [128, BC, T], BF16)
            kt, ktb = self.sb(st, "p1_kt", [128, T], BF16)
            dt_, dtb = self.sb(st, "p1_dt", [128, BC, T], F32)
            bra, brab = self.sb(st, "p1_bra", [128, BC, T], BF16)
            cs, csb = self.sb(st, "p1_cos", [128, T], F32)
            sn, snb = self.sb(st, "p1_sin", [128, T], F32)
            sg, sgb = self.sb(st, "p1_sg", [128, T], F32)
            t1 = [self.sb(st, f"p1_t1{i}", [128, T], F32) for i in range(2)]
            t2 = [self.sb(st, f"p1_t2{i}", [128, T], F32) for i in range(2)]
            vg, vgb = self.sb(st, "p1_vg", [128, DB], F32)
            vn, vnb = self.sb(st, "p1_vn", [128, DB], F32)
            vnh, vnhb = self.sb(st, "p1_vnh", [128, DB], BF16)
            stt, sttb = self.sb(st, "p1_stats", [128, 8, 6], F32)
            mv, mvb = self.sb(st, "p1_mv", [128, 4], F32)
            mx, mxb = self.sb(st, "p1_mx", [128, BC, 128], F32)
            vv, vvb = self.sb(st, "p1_vv", [128, 2, 2, 64], BF16)
            o_glg, _ = self.rplan["gln_g"]
            o_glb, _ = self.rplan["gln_b"]
            o_gbs, _ = self.rplan["gbs"]
            K.pool.dma(rv[:], self.rowv[:, l, :], rvb)
            K.sp.dma(wv1[:], self.wb[l][:, wp["inV1"]:wp["inV1"] + KC * DB], wv1b)
            K.sp.dma(wv2[:], self.wb[l][:, wp["inV2"]:wp["inV2"] + KC * 128], wv2b)
            K.sp.dma(wst[:], self.wb[l][:, wp["wsT"]:wp["wsT"] + BC * 128], wstb)
            for (t0, w, is_ctx) in cfg.tiles:
                which = 0 if is_ctx else 1
                K.pool.dma(xt[:, :, :w], xsrc[:, :, t0:t0 + w].rearrange("c p t -> p c t"), xtb)
                K.pool.dma(cs[:, :w], self.ropec[:, t0:t0 + w], csb)
                K.pool.dma(sn[:, :w], self.ropes[:, t0:t0 + w], snb)
                self.rms_to_h(tl, xt, xtb, w, 1, 0, "n1g", l, which, ht, htb)
                K.pool.dma(self.H[:, :, t0:t0 + w].rearrange("c p t -> p c t"), ht[:, :, :w], htb, load=False)
                nchunks = 6 * BC + 2
                chunk_kind = []
                for i in range(BC):
                    chunk_kind.append(("u", i))
                for i in range(BC):
                    chunk_kind.append(("g", i))
                    chunk_kind.append(("a", i))
                for i in range(BC):
                    chunk_kind.append(("q", i))
                    chunk_kind.append(("qs", i))
                chunk_kind.append(("k", 0))
                chunk_kind.append(("ks", 0))
                for i in range(BC):
                    chunk_kind.append(("d", i))
                CPS = max(1, WSLOT // CW)
                wt = wtb = None
                pending = {}
                for ci, (kind, i) in enumerate(chunk_kind):
                    if ci % CPS == 0:
                        n = min(CPS, nchunks - ci)
                        wt, wtb = self.wload(l, wp["inF"] + ci * CW, n * CW)
                    base = (ci % CPS) * CW
                    ps, psb = self.next_psum()
                    self.mm_group(ps[:, :w], psb,
                                  [wt[:, base + kc * 128: base + (kc + 1) * 128] for kc in range(KC)],
                                  [ht[:, kc, :w] for kc in range(KC)], [wtb, htb])
                    if kind == "u":
                        K.act.op(lambda e, i=i, ps=ps: e.activation(out=ut[:, i, :w], in_=ps[:, :w], func=AF.Gelu_apprx_tanh),
                                 reads=[psb], writes=[utb])
                    elif kind == "g":
                        K.act.op(lambda e, ps=ps: e.activation(out=sg[:, :w], in_=ps[:, :w], func=AF.Sigmoid),
                                 reads=[psb], writes=[sgb])
                    elif kind == "a":
                        K.dve.op(lambda e, i=i, ps=ps: e.tensor_tensor(out=yt[:, i, :w], in0=ps[:, :w], in1=sg[:, :w], op=ALU.mult),
                                 reads=[psb, sgb], writes=[ytb])
                    elif kind in ("q", "k"):
                        a, ab_ = t1[ci % 2 if False else (ci // 2) % 2]
                        K.dve.op(lambda e, ps=ps, a=a: e.tensor_tensor(out=a[:, :w], in0=ps[:, :w], in1=cs[:, :w], op=ALU.mult),
                                 reads=[psb, csb], writes=[ab_])
                        pending["t1"] = (a, ab_)
                    elif kind in ("qs", "ks"):
                        a, ab_ = pending["t1"]
                        b2, b2b = t2[(ci // 2) % 2]
                        K.dve.op(lambda e, ps=ps, b2=b2: e.tensor_tensor(out=b2[:, :w], in0=ps[:, :w], in1=sn[:, :w], op=ALU.mult),
                                 reads=[psb, snb], writes=[b2b])
                        if kind == "qs":
                            K.pool.op(lambda e, i=i, a=a, b2=b2: e.tensor_tensor(out=qt[:, i, :w], in0=a[:, :w], in1=b2[:, :w], op=ALU.add),
                                      reads=[ab_, b2b], writes=[qtb])
                        else:
                            K.pool.op(lambda e, a=a, b2=b2: e.tensor_tensor(out=kt[:, :w], in0=a[:, :w], in1=b2[:, :w], op=ALU.add),
                                      reads=[ab_, b2b], writes=[ktb])
                    elif kind == "d":
                        K.act.op(lambda e, i=i, ps=ps: e.copy(out=dt_[:, i, :w], in_=ps[:, :w]), reads=[psb], writes=[dtb])
                y0 = self.ypos(t0)
                K.pool.dma(self.Y[:, :, y0:y0 + w].rearrange("c p t -> p c t"), yt[:, :, :w], ytb, load=False)
                K.pool.dma(self.Q[:, :, t0:t0 + w].rearrange("c p t -> p c t"), qt[:, :, :w], qtb, load=False)
                for hk in range(2):
                    for dup in range(2):
                        K.pool.dma(self.KT[hk, dup * 64:(dup + 1) * 64, t0:t0 + w], kt[hk * 64:(hk + 1) * 64, :w], ktb, load=False)
                K.pool.dma(self.U[:, :, t0:t0 + w].rearrange("c p t -> p c t"), dt_[:, :, :w], dtb, load=False)
                for j in range(w // 128):
                    tk = slice(j * 128, (j + 1) * 128)
                    ps, psb = self.next_psum()
                    self.mm_group(ps[:, :DB], psb, [ht[:, kc, tk] for kc in range(KC)],
                                  [wv1[:, kc, :] for kc in range(KC)], [htb, wv1b])
                    K.act.op(lambda e, ps=ps: e.activation(out=vg[:, :], in_=ps[:, :DB], func=AF.Gelu_apprx_tanh),
                             reads=[psb], writes=[vgb])
                    FMAX = 512
                    nst = (DB + FMAX - 1) // FMAX
                    for s_ in range(nst):
                        K.dve.op(lambda e, s_=s_: e.bn_stats(out=stt[:, s_, :], in_=vg[:, s_ * FMAX:min(DB, (s_ + 1) * FMAX)]),
                                 reads=[vgb], writes=[sttb])
                    K.dve.op(lambda e: e.bn_aggr(out=mv[:, 0:2], in_=stt[:, 0:nst, :]), reads=[sttb], writes=[mvb])
                    K.act.op(lambda e: e.activation(out=mv[:, 2:3], in_=mv[:, 1:2], func=AF.Sqrt, bias=self.epsc[:, 0:1]),
                             reads=[mvb, self.epsb], writes=[mvb])
                    K.dve.op(lambda e: e.reciprocal(out=mv[:, 2:3], in_=mv[:, 2:3]), reads=[mvb], writes=[mvb])
                    K.dve.op(lambda e: e.tensor_scalar(out=vn[:, :], in0=vg[:, :], scalar1=mv[:, 0:1], scalar2=mv[:, 2:3],
                                                       op0=ALU.subtract, op1=ALU.mult),
                             reads=[vgb, mvb], writes=[vnb])
                    K.pool.op(lambda e: e.tensor_tensor(out=vn[:, :], in0=vn[:, :], in1=rv[:, o_glg:o_glg + DB], op=ALU.mult),
                              reads=[vnb, rvb], writes=[vnb])
                    K.pool.op(lambda e: e.tensor_tensor(out=vnh[:, :], in0=vn[:, :], in1=rv[:, o_glb:o_glb + DB], op=ALU.add),
                              reads=[vnb, rvb], writes=[vnhb])
                    ps2, ps2b = self.next_psum()
                    for gi in range(BC):
                        K.pe.op(lambda e, gi=gi, ps2=ps2: e.matmul(ps2[:, gi * 128:(gi + 1) * 128], lhsT=vnh[:, gi * 128:(gi + 1) * 128],
                                                                  rhs=wst[:, gi, :], start=True, stop=True),
                                reads=[vnhb, wstb], writes=[ps2b])
                    K.dve.op(lambda e, ps2=ps2: e.tensor_tensor(out=mx[:, :, :], in0=ps2[:, :BC * 128].rearrange("p (g q) -> p g q", g=BC),
                                                               in1=rv[:, o_gbs:o_gbs + BC * 128].rearrange("p (g q) -> p g q", g=BC), op=ALU.add),
                             reads=[ps2b, rvb], writes=[mxb])
                    K.pool.op(lambda e, tk=tk: e.tensor_tensor(out=bra[:, :, tk], in0=mx[:, :, :], in1=ut[:, :, tk], op=ALU.mult),
                              reads=[mxb, utb], writes=[brab])
                    ps3, ps3b = self.next_psum()
                    self.mm_group(ps3[:, :128], ps3b, [ht[:, kc, tk] for kc in range(KC)],
                                  [wv2[:, kc, :] for kc in range(KC)], [htb, wv2b])
                    for dup in range(2):
                        K.act.op(lambda e, ps3=ps3, dup=dup: e.copy(out=vv[:, :, dup, :], in_=ps3[:, :128].rearrange("p (h d) -> p h d", h=2)),
                                 reads=[ps3b], writes=[vvb])
                    K.pool.dma(self.V[t0 + j * 128:t0 + (j + 1) * 128, :, :], vv[:].rearrange("p h u d -> p h (u d)"), vvb, load=False)
                K.pool.dma(self.BR[0, :, :, t0:t0 + w].rearrange("c p t -> p c t"), bra[:, :, :w], brab, load=False)


    def phase2(self, l):
        cfg, K, nc = self.cfg, self.K, self.nc
        KC, BC, DB, T, FFC, CTX, NT = cfg.KC, cfg.BC, cfg.DB, cfg.T, cfg.FFC, cfg.CTX, cfg.NT
        QPK = cfg.QPK
        xsrc = self.xT if l == 0 else self.X
        wp = self.wplan
        pp = self.pplan
        last = (l == cfg.DEPTH - 1)
        NCC = CTX // 128
        HC = min(FFC, KC)
        GW_ = 4 * KC * 128
        BW_ = 4 * BC * 128
        with ExitStack() as st:
            xt, xtb = self.sb(st, "p2_xt", [128, KC, T], F32)
            ht, htb = self.sb(st, "p2_ht", [128, KC, T], BF16)
            big, bigb = self.sb(st, "p2_big", [128, HC, T], BF16)
            tl = {
                "sq": [self.sb(st, f"p2_sq{i}", [128, T], F32) for i in range(2)],
                "hf": [self.sb(st, f"p2_hf{i}", [128, T], F32) for i in range(2)],
                "rstd": self.sb(st, "p2_rstd", [128, T], F32),
                "ab": self.sb(st, "p2_ab", [128, KC, 2], F32),
            }
            self.epsc, self.epsb = self.sb(st, "p2_eps", [128, 1], F32)
            K.dve.op(lambda e: e.memset(self.epsc[:], cfg.EPS), writes=[self.epsb])
            self.wslots = [self.sb(st, f"p2_ws{i}", [128, 8192], BF16) for i in range(3)]
            self.wslot_i = 0
            bws = [self.sb(st, f"p2_bw{i}", [128, BW_], BF16) for i in range(2)]
            brt = {0: self.sb(st, "p2_br0", [128, BC, T], BF16), 3: self.sb(st, "p2_br3", [128, BC, T], BF16)}
            brBs = [self.sb(st, f"p2_brB{i}", [128, BC, T], BF16) for i in range(2)]
            brCs = [self.sb(st, f"p2_brC{i}", [128, BC, T], BF16) for i in range(2)]
            rv, rvb = self.sb(st, "p2_rv", [128, cfg.HQ], F32)
            esk, eskb = self.sb(st, "p2_esk", [128, cfg.HQ], F32)
            ywins = [self.sb(st, f"p2_ywin{i}", [128, BC, T + 32], F32) for i in range(1)] * 2
            acc, accb = self.sb(st, "p2_acc", [128, BC, T], F32)
            cst = [self.sb(st, f"p2_cst{i}", [128, T], F32) for i in range(3)]
            qts = [self.sb(st, f"p2_qt{i}", [128, BC, T], BF16) for i in range(1)] * 2
            kwins = [self.sb(st, f"p2_kwin{i}", [128, 2, T + 256], BF16) for i in range(1)] * 2
            vwins = [self.sb(st, f"p2_vwin{i}", [128, (T + 256) // 128, 2, 128], BF16) for i in range(1)] * 2
            kctx, kctxb = self.sb(st, "p2_kctx", [128, 2, CTX], BF16)
            vctx, vctxb = self.sb(st, "p2_vctx", [128, NCC, 2, 128], BF16)
            mk = [self.sb(st, f"p2_mk{i}", [128, 128], BF16) for i in range(2)]
            mk32, mk32b = self.sb(st, "p2_mk32", [128, 2, 128], F32)
            onesh, oneshb = self.sb(st, "p2_onesh", [128, 128], BF16)
            pts = [self.sb(st, f"p2_pt{i}", [128, QPK * 128], BF16) for i in range(3)]
            rden, rdenb = self.sb(st, "p2_rden", [128, QPK * 128], F32)
            sgs = tl["sq"]
            prods = [cst[1], cst[2]] + tl["hf"]
            rl = tl["sq"]
            zt, ztb = self.sb(st, "p2_zero", [128, 32], F32)
            o_sk, _ = self.rplan["sink"]
            K.pool.dma(rv[:], self.rowv[:, l, o_sk:o_sk + cfg.HQ], rvb)
            K.act.op(lambda e: e.activation(out=esk[:], in_=rv[:, :], func=AF.Exp), reads=[rvb], writes=[eskb])
            K.pool.dma(mk32[:, 0, :], self.mask_prev[:, :], mk32b)
            K.pool.dma(mk32[:, 1, :], self.mask_next[:, :], mk32b)
            for i in range(2):
                K.dve.op(lambda e, i=i: e.tensor_copy(out=mk[i][0][:], in_=mk32[:, i, :]), reads=[mk32b], writes=[mk[i][1]])
            K.dve.op(lambda e: e.memset(onesh[:], 1.0), writes=[oneshb])
            K.dve.op(lambda e: e.memset(zt[:], 0.0), writes=[ztb])
            P_ = cfg.YPAD
            for c0 in (0, P_ + CTX, 2 * P_ + CTX, 3 * P_ + NT):
                for c in range(BC):
                    K.pool.dma(self.Y[c, :, c0:c0 + P_], zt[:, 0:P_], ztb, load=False)
            K.pool.dma(kctx[:], self.KT[:, :, 0:CTX].rearrange("h p t -> p h t"), kctxb)
            K.pool.dma(vctx[:], self.V[0:CTX, :, :].rearrange("(c p) h d -> p c h d", p=128), vctxb)
            K.barrier()
            o_cw, _ = pp["conv_w"]
            o_cb, _ = pp["conv_b"]
            o_lg, _ = pp["cln_g"]
            o_lb, _ = pp["cln_b"]
            o_bg, _ = pp["b_gate"]
            tiles2 = [t_ for t_ in cfg.tiles if not (t_[2] and last)]

            def front(ti):
                (t0, w, is_ctx) = tiles2[ti]
                par = ti % 2
                (ywin, ywinb), (qt, qtb), (kwin, kwinb), (vwin, vwinb) = ywins[par], qts[par], kwins[par], vwins[par]
                brB, brBb = brBs[par]
                y0 = self.ypos(t0)
                K.pool.dma(ywin[:, :, :w + 30], self.Y[:, :, y0 - 15:y0 + w + 15].rearrange("c p t -> p c t"), ywinb)
                K.pool.dma(qt[:, :, :w], self.Q[:, :, t0:t0 + w].rearrange("c p t -> p c t"), qtb)
                if not is_ctx:
                    k_lo = max(CTX, t0 - 128)
                    k_hi = min(NT, t0 + w + 128)
                    K.pool.dma(kwin[:, :, :k_hi - k_lo], self.KT[:, :, k_lo:k_hi].rearrange("h p t -> p h t"), kwinb)
                    K.pool.dma(vwin[:, :(k_hi - k_lo) // 128, :, :],
                               self.V[k_lo:k_hi, :, :].rearrange("(c p) h d -> p c h d", p=128), vwinb)
                yield
                for c in range(BC):
                    eng = K.dve
                    eng.op(lambda e, c=c: e.tensor_scalar(out=acc[:, c, :w], in0=ywin[:, c, 0:w],
                                                          scalar1=self.pv[:, l, o_cw + c * 31:o_cw + c * 31 + 1],
                                                          scalar2=self.pv[:, l, o_cb + c:o_cb + c + 1], op0=ALU.mult, op1=ALU.add),
                           reads=[ywinb, self.pvb], writes=[accb])
                    for j in range(1, 31):
                        if j % 10 == 0:
                            yield
                        eng.op(lambda e, c=c, j=j: e.scalar_tensor_tensor(
                            out=acc[:, c, :w], in0=ywin[:, c, j:j + w], scalar=self.pv[:, l, o_cw + c * 31 + j:o_cw + c * 31 + j + 1],
                            in1=acc[:, c, :w], op0=ALU.mult, op1=ALU.add), reads=[ywinb, self.pvb, accb], writes=[accb])
                yield
                ps1, ps1b = self.next_psum()
                ps2, ps2b = self.next_psum()
                for c in range(BC):
                    s_, sb_ = tl["sq"][c % 2]
                    K.pe.op(lambda e, c=c: e.matmul(ps1[:, :w], lhsT=self.ones[:], rhs=acc[:, c, :w], start=(c == 0), stop=(c == BC - 1)),
                            reads=[self.onesb, accb], writes=[ps1b])
                    K.act.op(lambda e, c=c, s_=s_: e.activation(out=s_[:, :w], in_=acc[:, c, :w], func=AF.Square), reads=[accb], writes=[sb_])
                    K.pe.op(lambda e, c=c, s_=s_: e.matmul(ps2[:, :w], lhsT=self.ones[:], rhs=s_[:, :w], start=(c == 0), stop=(c == BC - 1)),
                            reads=[self.onesb, sb_], writes=[ps2b])
                mean, meanb = cst[0]
                msq, msqb = cst[1]
                var, varb = cst[2]
                K.act.op(lambda e: e.mul(out=mean[:, :w], in_=ps1[:, :w], mul=1.0 / DB), reads=[ps1b], writes=[meanb])
                K.dve.op(lambda e: e.tensor_tensor(out=msq[:, :w], in0=mean[:, :w], in1=mean[:, :w], op=ALU.mult), reads=[meanb], writes=[msqb])
                K.dve.op(lambda e: e.scalar_tensor_tensor(out=var[:, :w], in0=ps2[:, :w], scalar=1.0 / DB, in1=msq[:, :w],
                                                          op0=ALU.mult, op1=ALU.subtract), reads=[ps2b, msqb], writes=[varb])
                K.act.op(lambda e: e.activation(out=var[:, :w], in_=var[:, :w], func=AF.Sqrt, bias=self.epsc[:, 0:1]),
                         reads=[varb, self.epsb], writes=[varb])
                K.dve.op(lambda e: e.reciprocal(out=var[:, :w], in_=var[:, :w]), reads=[varb], writes=[varb])
                for c in range(BC):
                    eng = K.dve if c % 2 == 0 else K.pool
                    eng.op(lambda e, c=c: e.tensor_tensor(out=acc[:, c, :w], in0=acc[:, c, :w], in1=mean[:, :w], op=ALU.subtract),
                           reads=[accb, meanb], writes=[accb])
                    eng.op(lambda e, c=c: e.tensor_tensor(out=acc[:, c, :w], in0=acc[:, c, :w], in1=var[:, :w], op=ALU.mult),
                           reads=[accb, varb], writes=[accb])
                    K.act.op(lambda e, c=c: e.activation(out=brB[:, c, :w], in_=acc[:, c, :w], func=AF.Silu,
                                                         scale=self.pv[:, l, o_lg + c:o_lg + c + 1], bias=self.pv[:, l, o_lb + c:o_lb + c + 1]),
                             reads=[accb, self.pvb], writes=[brBb])
                brc, brcb = brCs[par]
                yield
                for j in range(w // 128):
                    tq0 = t0 + j * 128
                    qs = slice(j * 128, (j + 1) * 128)
                    chunks = []
                    if not is_ctx:
                        for rel_, mi in ((-128, 0), (0, None), (128, 1)):
                            ks = tq0 + rel_
                            if ks < CTX or ks >= NT:
                                continue
                            o = ks - k_lo
                            chunks.append((lambda hk, par, o=o: kwin[par * 64:(par + 1) * 64, hk, o:o + 128],
                                           lambda hk, o=o: vwin[:, o // 128, hk, :], mi, [kwinb], [vwinb]))
                    for cc in range(NCC):
                        chunks.append((lambda hk, par, cc=cc: kctx[par * 64:(par + 1) * 64, hk, cc * 128:(cc + 1) * 128],
                                       lambda hk, cc=cc: vctx[:, cc, hk, :], None, [kctxb], [vctxb]))
                    for hk in range(2):
                        pso, psob = self.ps[6], self.psb[6]
                        psd, psdb = self.ps[7], self.psb[7]
                        for ci, (kf, vf, mi, krd, vrd) in enumerate(chunks):
                            pt, ptb = pts[ci % 3]
                            for par in range(2):
                                heads = [i for i in range(QPK) if (hk * QPK + i) % 2 == par]
                                if not heads:
                                    continue
                                pss, pssb = self.next_psum()
                                for i in heads:
                                    ch = (hk * QPK + i) // 2
                                    K.pe.op(lambda e, i=i, par=par, ch=ch, kf=kf, pss=pss: e.matmul(
                                        pss[:, i * 128:(i + 1) * 128], lhsT=kf(hk, par), rhs=qt[par * 64:(par + 1) * 64, ch, qs],
                                        start=True, stop=True), reads=krd + [qtb], writes=[pssb])
                                for i in heads:
                                    K.act.op(lambda e, pss=pss, pt=pt, i=i: e.activation(out=pt[:, i * 128:(i + 1) * 128], in_=pss[:, i * 128:(i + 1) * 128],
                                                                                    func=AF.Exp, scale=0.125), reads=[pssb], writes=[ptb])
                            if mi is not None:
                                K.pool.op(lambda e, pt=pt, mi=mi: e.tensor_tensor(
                                    out=pt[:, :].rearrange("p (h q) -> p h q", h=QPK), in0=pt[:, :].rearrange("p (h q) -> p h q", h=QPK),
                                    in1=mk[mi][0][:, :].unsqueeze(1).to_broadcast([128, QPK, 128]), op=ALU.mult),
                                    reads=[ptb, mk[mi][1]], writes=[ptb])
                            first, lastc = (ci == 0), (ci == len(chunks) - 1)
                            K.pe.op(lambda e, vf=vf, pt=pt, first=first, lastc=lastc: e.matmul(
                                pso[:, :QPK * 128], lhsT=vf(hk), rhs=pt[:, :], start=first, stop=lastc), reads=vrd + [ptb], writes=[psob])
                            K.pe.op(lambda e, pt=pt, first=first, lastc=lastc: e.matmul(
                                psd[:, :QPK * 128], lhsT=onesh[:, :], rhs=pt[:, :], start=first, stop=lastc), reads=[oneshb, ptb], writes=[psdb])
                        K.dve.op(lambda e: e.tensor_tensor(
                            out=rden[:, :].rearrange("p (h q) -> p h q", h=QPK), in0=psd[:, :QPK * 128].rearrange("p (h q) -> p h q", h=QPK),
                            in1=esk[:, hk * QPK:(hk + 1) * QPK].unsqueeze(2).to_broadcast([128, QPK, 128]), op=ALU.add),
                            reads=[psdb, eskb], writes=[rdenb])
                        K.dve.op(lambda e: e.reciprocal(out=rden[:, :], in_=rden[:, :]), reads=[rdenb], writes=[rdenb])
                        for i in range(QPK):
                            hq = hk * QPK + i
                            par, ch = hq % 2, hq // 2
                            pr = slice(par * 64, (par + 1) * 64)
                            K.dve.op(lambda e, i=i, pr=pr, ch=ch: e.tensor_tensor(
                                out=brc[pr, ch, qs], in0=pso[pr, i * 128:(i + 1) * 128], in1=rden[pr, i * 128:(i + 1) * 128], op=ALU.mult),
                                reads=[psob, rdenb], writes=[brcb])
                        yield

            def back(ti, tick):
                (t0, w, is_ctx) = tiles2[ti]
                par = ti % 2
                which = 0 if is_ctx else 1
                brl = [brt[0], brBs[par], brCs[par], brt[3]]
                K.pool.dma(xt[:, :, :w], xsrc[:, :, t0:t0 + w].rearrange("c p t -> p c t"), xtb)
                K.pool.dma(ht[:, :, :w], self.H[:, :, t0:t0 + w].rearrange("c p t -> p c t"), htb)
                K.pool.dma(brt[0][0][:, :, :w], self.BR[0, :, :, t0:t0 + w].rearrange("c p t -> p c t"), brt[0][1])
                K.pool.dma(brt[3][0][:, :, :w], self.BR[3, :, :, t0:t0 + w].rearrange("c p t -> p c t"), brt[3][1])
                for oc in range(KC):
                    tick()
                    gw, gwb = self.wload(l, wp["gb"] + oc * (GW_ + BW_), GW_)
                    bw, bwb = bws[oc % 2]
                    K.sp.dma(bw[:, :], self.wb[l][:, wp["gb"] + oc * (GW_ + BW_) + GW_: wp["gb"] + (oc + 1) * (GW_ + BW_)], bwb)
                    for k in range(4):
                        psg, psgb = self.next_psum()
                        self.mm_group(psg[:, :w], psgb, [gw[:, (k * KC + kc) * 128:(k * KC + kc + 1) * 128] for kc in range(KC)],
                                      [ht[:, kc, :w] for kc in range(KC)], [gwb, htb])
                        psb_, psbb = self.next_psum()
                        self.mm_group(psb_[:, :w], psbb, [bw[:, (k * BC + bc) * 128:(k * BC + bc + 1) * 128] for bc in range(BC)],
                                      [brl[k][0][:, bc, :w] for bc in range(BC)], [bwb, brl[k][1]])
                        sg, sgb = sgs[k % 2]
                        K.act.op(lambda e, psg=psg, sg=sg, k=k: e.activation(out=sg[:, :w], in_=psg[:, :w], func=AF.Sigmoid,
                                                                        bias=self.pv[:, l, o_bg + k * KC + oc:o_bg + k * KC + oc + 1]),
                                 reads=[psgb, self.pvb], writes=[sgb])
                        pr_, prb = prods[k]
                        K.dve.op(lambda e, psb_=psb_, sg=sg, pr_=pr_: e.tensor_tensor(out=pr_[:, :w], in0=psb_[:, :w], in1=sg[:, :w], op=ALU.mult),
                                 reads=[psbb, sgb], writes=[prb])
                    K.pool.op(lambda e: e.tensor_tensor(out=prods[0][0][:, :w], in0=prods[0][0][:, :w], in1=prods[1][0][:, :w], op=ALU.add),
                              reads=[prods[0][1], prods[1][1]], writes=[prods[0][1]])
                    K.pool.op(lambda e: e.tensor_tensor(out=prods[2][0][:, :w], in0=prods[2][0][:, :w], in1=prods[3][0][:, :w], op=ALU.add),
                              reads=[prods[2][1], prods[3][1]], writes=[prods[2][1]])
                    K.pool.op(lambda e, oc=oc: e.tensor_tensor(out=big[:, oc, :w], in0=prods[0][0][:, :w], in1=prods[2][0][:, :w], op=ALU.add),
                              reads=[prods[0][1], prods[2][1]], writes=[bigb])
                CW = KC * 128
                CPS = max(1, 8192 // CW)
                for oc in range(KC):
                    tick()
                    if oc % CPS == 0:
                        n = min(CPS, KC - oc)
                        wt, wtb = self.wload(l, wp["out"] + oc * CW, n * CW)
                    base = (oc % CPS) * CW
                    ps, psb = self.next_psum()
                    self.mm_group(ps[:, :w], psb, [wt[:, base + kc * 128:base + (kc + 1) * 128] for kc in range(KC)],
                                  [big[:, kc, :w] for kc in range(KC)], [wtb, bigb])
                    K.dve.op(lambda e, oc=oc, ps=ps: e.scalar_tensor_tensor(
                        out=xt[:, oc, :w], in0=ps[:, :w], scalar=self.modv[:, l, 2 * KC + oc, which:which + 1], in1=xt[:, oc, :w],
                        op0=ALU.mult, op1=ALU.add), reads=[psb, self.modb, xtb], writes=[xtb])
                self.rms_to_h(tl, xt, xtb, w, 4, 3, "n2g", l, which, ht, htb)
                for hp in range(FFC // HC):
                    for fcl in range(HC):
                        fc = hp * HC + fcl
                        if fcl % 2 == 0:
                            tick()
                        if fcl % CPS == 0:
                            n = min(CPS, HC - fcl)
                            wt, wtb = self.wload(l, wp["ff1"] + fc * CW, n * CW)
                        base = (fcl % CPS) * CW
                        ps, psb = self.next_psum()
                        self.mm_group(ps[:, :w], psb, [wt[:, base + kc * 128:base + (kc + 1) * 128] for kc in range(KC)],
                                      [ht[:, kc, :w] for kc in range(KC)], [wtb, htb])
                        r_, rb = rl[fc % 2]
                        K.act.op(lambda e, ps=ps, r_=r_: e.activation(out=r_[:, :w], in_=ps[:, :w], func=AF.Relu), reads=[psb], writes=[rb])
                        eng = K.pool if fc % 2 == 0 else K.dve
                        eng.op(lambda e, r_=r_, fcl=fcl: e.tensor_tensor(out=big[:, fcl, :w], in0=r_[:, :w], in1=r_[:, :w], op=ALU.mult),
                               reads=[rb], writes=[bigb])
                    FW = FFC * 128
                    for oc in range(KC):
                        tick()
                        ps, psb = self.next_psum()
                        nload = HC * 128
                        for s0 in range(0, nload, 8192):
                            n = min(8192, nload - s0)
                            wt, wtb = self.wload(l, wp["ff2"] + oc * FW + hp * HC * 128 + s0, n)
                            nk = n // 128
                            for kk in range(nk):
                                fcl = s0 // 128 + kk
                                K.pe.op(lambda e, wt=wt, kk=kk, fcl=fcl, ps=ps: e.matmul(
                                    ps[:, :w], lhsT=wt[:, kk * 128:(kk + 1) * 128], rhs=big[:, fcl, :w],
                                    start=(fcl == 0), stop=(fcl == HC - 1)), reads=[wtb, bigb], writes=[psb])
                        K.dve.op(lambda e, oc=oc, ps=ps: e.scalar_tensor_tensor(
                            out=xt[:, oc, :w], in0=ps[:, :w], scalar=self.modv[:, l, 5 * KC + oc, which:which + 1], in1=xt[:, oc, :w],
                            op0=ALU.mult, op1=ALU.add), reads=[psb, self.modb, xtb], writes=[xtb])
                K.pool.dma(self.X[:, :, t0:t0 + w].rearrange("c p t -> p c t"), xt[:, :, :w], xtb, load=False)

            for _ in front(0):
                pass
            for ti in range(len(tiles2)):
                nxt = front(ti + 1) if ti + 1 < len(tiles2) else None
                cnt = [0]

                def tick(nxt=nxt, cnt=cnt):
                    cnt[0] += 1
                    if nxt is not None and cnt[0] % 3 == 0:
                        next(nxt, None)
                back(ti, tick)
                if nxt is not None:
                    for _ in nxt:
                        pass

    def phase_s(self, l):
        cfg, K, nc = self.cfg, self.K, self.nc
        BC, NP, NT, CTX = cfg.BC, cfg.NPAIR, cfg.NT, cfg.CTX
        TS = 256
        NQ = NP // 4
        PI = math.pi
        wp, pp = self.wplan, self.pplan
        LOG = int(math.log2(TS))
        with ExitStack() as st:
            W, Wb = self.sb(st, "s_W", [128, 24, 2, NP], F32)
            Bt, Btb = self.sb(st, "s_Bt", [128, 2, 2, 2, NQ, 128], BF16)
            Ct, Ctb = self.sb(st, "s_Ct", [128, 2, NP, 2, 64], BF16)
            R, Rb = self.sb(st, "s_R", [128, 2 * NP, 2, TS], F32)
            carry, carryb_ = self.sb(st, "s_carry", [128, 2, NP, 2], F32)
            st2 = ExitStack()
            At, Atb = self.sb(st2, "s_A", [128, 2, NP, 3], F32)
            Bt32, Bt32b = self.sb(st2, "s_B32", [128, 2, NP, 32], F32)
            Ct32, Ct32b = self.sb(st2, "s_C32", [128, 2, NP, 2, 64], F32)
            Wi, Wib = self.sb(st2, "s_Wi", [128, 2, NP], mybir.dt.int32)
            bbar, bbarb = self.sb(st2, "s_bbar", [128, 2, 2, NP, 32], F32)
            tb, tbb = self.sb(st2, "s_tb", [128, 2, NP, 32], F32)
            bbv, bbvb = self.sb(st2, "s_bbv", [128, 2, 2, 2, NP, 32], F32)
            E, Eb = self.sb(st2, "s_E", [128, 4, 2 * NP], F32)
            rt1, rt1b = self.sb(st2, "s_rt1", [128, 2 * NP, TS // 2], F32)
            rt2, rt2b = self.sb(st2, "s_rt2", [128, 2 * NP, TS // 2], F32)
            carryb = [[Buf(f"carry{d}_{k}") for k in range(NP)] for d in range(2)]
            K.pool.dma(At[:], self.s5A[:, l], Atb)
            K.pool.dma(Bt32[:], self.s5B[:, l], Bt32b)
            K.pool.dma(Ct32[:], self.s5C[:, l], Ct32b)
            a_re, a_im, ls = At[:, :, :, 0], At[:, :, :, 1], At[:, :, :, 2]
            (DT, XR, XI, MAG, T0, TF, RS, RC, SIN, COS, LR, LI, NR, DEN, CR, CI, TA, TB_, XC) = [W[:, i] for i in range(19)]

            def dv(fn, rd=(Atb, Wb), wr=(Wb,)):
                K.dve.op(fn, reads=list(rd), writes=list(wr))

            def ac(fn, rd=(Atb, Wb), wr=(Wb,)):
                K.act.op(fn, reads=list(rd), writes=list(wr))
            ac(lambda e: e.activation(out=DT, in_=ls, func=AF.Exp))
            dv(lambda e: e.tensor_tensor(out=XR, in0=a_re, in1=DT, op=ALU.mult))
            dv(lambda e: e.tensor_tensor(out=XI, in0=a_im, in1=DT, op=ALU.mult))
            ac(lambda e: e.activation(out=MAG, in_=XR, func=AF.Exp))

            def reduce_sin(dst, x_ap):
                dv(lambda e: e.tensor_scalar(out=T0, in0=x_ap, scalar1=1.0 / (2 * PI), scalar2=None, op0=ALU.mult))
                dv(lambda e: e.tensor_copy(out=Wi[:], in_=T0), wr=(Wib,))
                dv(lambda e: e.tensor_copy(out=TF, in_=Wi[:]), rd=(Wib,))
                dv(lambda e: e.scalar_tensor_tensor(out=RS, in0=TF, scalar=-2 * PI, in1=x_ap, op0=ALU.mult, op1=ALU.add))
                dv(lambda e: e.tensor_scalar(out=RS, in0=RS, scalar1=-PI, scalar2=PI, op0=ALU.max, op1=ALU.min))
                ac(lambda e: e.activation(out=dst, in_=RS, func=AF.Sin))
            reduce_sin(SIN, XI)
            dv(lambda e: e.tensor_scalar(out=XC, in0=XI, scalar1=PI / 2, scalar2=None, op0=ALU.add))
            reduce_sin(COS, XC)
            dv(lambda e: e.tensor_tensor(out=LR, in0=MAG, in1=COS, op=ALU.mult))
            dv(lambda e: e.tensor_tensor(out=LI, in0=MAG, in1=SIN, op=ALU.mult))
            dv(lambda e: e.tensor_scalar(out=NR, in0=LR, scalar1=-1.0, scalar2=None, op0=ALU.add))
            dv(lambda e: e.tensor_tensor(out=DEN, in0=a_re, in1=a_re, op=ALU.mult))
            dv(lambda e: e.tensor_tensor(out=TA, in0=a_im, in1=a_im, op=ALU.mult))
            dv(lambda e: e.tensor_tensor(out=DEN, in0=DEN, in1=TA, op=ALU.add))
            dv(lambda e: e.reciprocal(out=DEN, in_=DEN))
            dv(lambda e: e.tensor_tensor(out=TA, in0=NR, in1=a_re, op=ALU.mult))
            dv(lambda e: e.tensor_tensor(out=TB_, in0=LI, in1=a_im, op=ALU.mult))
            dv(lambda e: e.tensor_tensor(out=CR, in0=TA, in1=TB_, op=ALU.add))
            dv(lambda e: e.tensor_tensor(out=CR, in0=CR, in1=DEN, op=ALU.mult))
            dv(lambda e: e.tensor_tensor(out=TA, in0=LI, in1=a_re, op=ALU.mult))
            dv(lambda e: e.tensor_tensor(out=TB_, in0=NR, in1=a_im, op=ALU.mult))
            dv(lambda e: e.tensor_tensor(out=CI, in0=TA, in1=TB_, op=ALU.subtract))
            dv(lambda e: e.tensor_tensor(out=CI, in0=CI, in1=DEN, op=ALU.mult))
            b_re, b_im = Bt32[:, 0], Bt32[:, 1]
            for d in range(2):
                crb = CR[:, d, :].unsqueeze(2).to_broadcast([128, NP, 32])
                cib = CI[:, d, :].unsqueeze(2).to_broadcast([128, NP, 32])
                rd = (Bt32b, Wb, bbarb, tbb)
                dv(lambda e, d=d, crb=crb: e.tensor_tensor(out=bbar[:, d, 0], in0=b_re, in1=crb, op=ALU.mult), rd=rd, wr=(bbarb,))
                dv(lambda e, cib=cib: e.tensor_tensor(out=tb[:, 0], in0=b_im, in1=cib, op=ALU.mult), rd=rd, wr=(tbb,))
                dv(lambda e, d=d: e.tensor_tensor(out=bbar[:, d, 0], in0=bbar[:, d, 0], in1=tb[:, 0], op=ALU.subtract), rd=rd, wr=(bbarb,))
                dv(lambda e, d=d, crb=crb: e.tensor_tensor(out=bbar[:, d, 1], in0=b_im, in1=crb, op=ALU.mult), rd=rd, wr=(bbarb,))
                dv(lambda e, cib=cib: e.tensor_tensor(out=tb[:, 1], in0=b_re, in1=cib, op=ALU.mult), rd=rd, wr=(tbb,))
                dv(lambda e, d=d: e.tensor_tensor(out=bbar[:, d, 1], in0=bbar[:, d, 1], in1=tb[:, 1], op=ALU.add), rd=rd, wr=(bbarb,))
            K.dve.op(lambda e: e.memset(bbv[:], 0.0), writes=[bbvb])
            for v in range(2):
                K.dve.op(lambda e, v=v: e.tensor_copy(out=bbv[:, v, :, :, v::2, :], in_=bbar[:, :, :, v::2, :]), reads=[bbarb, bbvb], writes=[bbvb])
            for v in range(2):
              for d in range(2):
                for ri in range(2):
                    for q in range(NQ):
                        ps, psb = self.next_psum()
                        K.pe.op(lambda e, v=v, d=d, ri=ri, q=q, ps=ps: e.transpose(
                            out=ps[:, :128], in_=bbv[:, v, d, ri, q * 4:(q + 1) * 4, :].rearrange("p a b -> p (a b)"), identity=self.identt[:]),
                            reads=[bbvb, self.identb], writes=[psb])
                        K.act.op(lambda e, v=v, d=d, ri=ri, q=q, ps=ps: e.copy(out=Bt[:, v, d, ri, q, :], in_=ps[:, :128]), reads=[psb], writes=[Btb])
            K.act.op(lambda e: e.copy(out=Ct[:, :, :, 0, :], in_=Ct32[:, :, :, 0, :]), reads=[Ct32b], writes=[Ctb])
            K.act.op(lambda e: e.mul(out=Ct[:, :, :, 1, :], in_=Ct32[:, :, :, 1, :], mul=-1.0), reads=[Ct32b], writes=[Ctb])
            N2 = 2 * NP
            cosf = COS.rearrange("p d k -> p (d k)")
            sinf = SIN.rearrange("p d k -> p (d k)")
            dv(lambda e: e.tensor_copy(out=E[:, 0, :], in_=cosf), wr=(Eb,))
            dv(lambda e: e.tensor_copy(out=E[:, 1, :], in_=sinf), wr=(Eb,))
            dv(lambda e: e.tensor_copy(out=R[:, :, 0, 0], in_=cosf), wr=(Rb,))
            dv(lambda e: e.tensor_copy(out=R[:, :, 1, 0], in_=sinf), wr=(Rb,))
            rdR = (Rb, Eb, rt1b, rt2b)
            for kk in range(LOG):
                n = 1 << kk
                er = E[:, 0, :].unsqueeze(2).to_broadcast([128, N2, n])
                ei = E[:, 1, :].unsqueeze(2).to_broadcast([128, N2, n])
                sre, sim_ = R[:, :, 0, 0:n], R[:, :, 1, 0:n]
                dre, dim_ = R[:, :, 0, n:2 * n], R[:, :, 1, n:2 * n]
                a1, a2 = rt1[:, :, 0:n], rt2[:, :, 0:n]
                dv(lambda e, a1=a1, sre=sre, er=er: e.tensor_tensor(out=a1, in0=sre, in1=er, op=ALU.mult), rd=rdR, wr=(rt1b,))
                dv(lambda e, a2=a2, sim_=sim_, ei=ei: e.tensor_tensor(out=a2, in0=sim_, in1=ei, op=ALU.mult), rd=rdR, wr=(rt2b,))
                dv(lambda e, a1=a1, a2=a2, dre=dre: e.tensor_tensor(out=dre, in0=a1, in1=a2, op=ALU.subtract), rd=rdR, wr=(Rb,))
                dv(lambda e, a1=a1, sre=sre, ei=ei: e.tensor_tensor(out=a1, in0=sre, in1=ei, op=ALU.mult), rd=rdR, wr=(rt1b,))
                dv(lambda e, a2=a2, sim_=sim_, er=er: e.tensor_tensor(out=a2, in0=sim_, in1=er, op=ALU.mult), rd=rdR, wr=(rt2b,))
                dv(lambda e, a1=a1, a2=a2, dim_=dim_: e.tensor_tensor(out=dim_, in0=a1, in1=a2, op=ALU.add), rd=rdR, wr=(Rb,))
                dv(lambda e: e.tensor_tensor(out=E[:, 2, :], in0=E[:, 0, :], in1=E[:, 0, :], op=ALU.mult), rd=(Eb,), wr=(Eb,))
                dv(lambda e: e.tensor_tensor(out=E[:, 3, :], in0=E[:, 1, :], in1=E[:, 1, :], op=ALU.mult), rd=(Eb,), wr=(Eb,))
                dv(lambda e: e.tensor_tensor(out=E[:, 1, :], in0=E[:, 0, :], in1=E[:, 1, :], op=ALU.mult), rd=(Eb,), wr=(Eb,))
                dv(lambda e: e.tensor_scalar(out=E[:, 1, :], in0=E[:, 1, :], scalar1=2.0, scalar2=None, op0=ALU.mult), rd=(Eb,), wr=(Eb,))
                dv(lambda e: e.tensor_tensor(out=E[:, 0, :], in0=E[:, 2, :], in1=E[:, 3, :], op=ALU.subtract), rd=(Eb,), wr=(Eb,))
            K.dve.op(lambda e: e.memset(carry[:], 0.0), writes=[b for row in carryb for b in row])
            K.barrier()
            st2.close()
            wglu, wglub = self.sb(st, "s_wglu", [128, 2 * BC, BC, 128], BF16)
            ubufs = [(self.sb(st, f"s_u32{i}", [128, BC, TS], F32), self.sb(st, f"s_ubf{i}", [128, BC, TS], BF16),
                      self.sb(st, f"s_yb{i}", [128, BC, TS], F32)) for i in range(2)]
            yt, ytb = self.sb(st, "s_yt", [128, BC, TS], F32)
            yg, ygb = self.sb(st, "s_yg", [128, BC, TS], BF16)
            brd, brdb = self.sb(st, "s_brd", [128, BC, TS], BF16)
            sig, sigb = self.sb(st, "s_sig", [128, TS], F32)
            sets = []
            for i in range(6):
                d = {}
                for nm in ("t1", "t2", "bt", "st", "s32"):
                    d[nm] = self.sb(st, f"s_{nm}{i}", [128, 2, TS], F32)
                d["sbf"] = self.sb(st, f"s_sbf{i}", [128, 2, TS], BF16)
                sets.append(d)
            K.sp.dma(wglu[:], self.wb[l][:, wp["glu"]:wp["glu"] + 2 * BC * BC * 128], wglub)
            ctx_subs = [(t, TS) for t in range(0, CTX, TS)]
            lat_subs = [(t, TS) for t in range(CTX, NT, TS)]
            o_d, _ = pp["s5_d"]
            NSET = len(sets)
            ybd = {t: Buf(f"ybd{t}") for (t, _) in ctx_subs + lat_subs}
            items = []
            subs_all = []
            for dirn in (1, 0):
                order = (ctx_subs + lat_subs) if dirn == 0 else (ctx_subs[::-1] + lat_subs[::-1])
                for (t0, w) in order:
                    si = len(subs_all)
                    subs_all.append((dirn, t0))
                    for k in range(NP):
                        items.append((dirn, si, t0, k, k == 0, k == NP - 1))

            def sub_bufs(si):
                return ubufs[si % 2]

            def prologue(si):
                dirn, t0 = subs_all[si]
                (u32, u32b), (ubf, ubfb), (ybt, ybtb) = sub_bufs(si)
                K.pool.dma(u32[:], self.U[:, :, t0:t0 + TS].rearrange("c p t -> p c t"), u32b)
                if dirn == 0:
                    K.pool.dma(ybt[:], self.YB[:, :, t0:t0 + TS].rearrange("c p t -> p c t"), ybtb, reads=[ybd[t0]])
                    K.act.op(lambda e: e.copy(out=ubf[:], in_=u32[:]), reads=[u32b], writes=[ubfb])
                else:
                    K.act.op(lambda e: e.copy(out=ubf[:], in_=u32[:, :, ::-1]), reads=[u32b], writes=[ubfb])

            def psy_of(si):
                b = 4 + 2 * (si % 2)
                return [self.ps[b], self.ps[b + 1]], [self.psb[b], self.psb[b + 1]]

            def stage(sn, idx):
                dirn, si, t0, k, first, lastk = items[idx]
                (u32, u32b), (ubf, ubfb), (ybt, ybtb) = sub_bufs(si)
                q, slot = k // 4, k % 4
                rows = slice(64 * (slot // 2), 64 * (slot // 2) + 64)
                S_ = sets[idx % NSET]
                (t1, t1b), (t2, t2b), (bt, btb), (stt, sttb) = S_["t1"], S_["t2"], S_["bt"], S_["st"]
                (t3, t3b), (t4, t4b), (s32, s32b), (sbf, sbfb) = S_["t1"], S_["t2"], S_["s32"], S_["sbf"]
                rr = R[:, dirn * NP + k, 0, :].unsqueeze(1).to_broadcast([128, 2, TS])
                rim = R[:, dirn * NP + k, 1, :].unsqueeze(1).to_broadcast([128, 2, TS])
                if sn == 0:
                    if first and si == 0:
                        prologue(0)
                    if k == min(NP - 1, 4) and si + 1 < len(subs_all):
                        prologue(si + 1)
                    i = self.psum_i
                    self.psum_i = (i + 1) % 4
                    ps, psb = self.ps[i], self.psb[i]
                    for ri in range(2):
                        K.pe.op(lambda e, ri=ri: e.matmul(
                            ps[:, ri * TS:(ri + 1) * TS], lhsT=Bt[rows, slot % 2, dirn, ri, q, :], rhs=ubf[rows, q, :], start=True, stop=True),
                            reads=[Btb, ubfb], writes=[psb])
                    bu = ps[:, :2 * TS].rearrange("p (r t) -> p r t", r=2)
                    K.dve.op(lambda e: e.tensor_tensor(out=t1[:], in0=bu, in1=rr, op=ALU.mult), reads=[psb, Rb], writes=[t1b])
                    K.dve.op(lambda e: e.tensor_tensor(out=t2[:], in0=bu[:, ::-1, :], in1=rim, op=ALU.mult), reads=[psb, Rb], writes=[t2b])
                elif sn == 1:
                    K.pool.op(lambda e: e.tensor_tensor(out=bt[:, 0, :], in0=t1[:, 0, :], in1=t2[:, 0, :], op=ALU.add), reads=[t1b, t2b], writes=[btb])
                    K.pool.op(lambda e: e.tensor_tensor(out=bt[:, 1, :], in0=t1[:, 1, :], in1=t2[:, 1, :], op=ALU.subtract), reads=[t1b, t2b], writes=[btb])
                elif sn == 2:
                    dec = W[:, 3, dirn, k:k + 1].to_broadcast([128, TS])
                    for ri in range(2):
                        K.dve.op(lambda e, ri=ri: e.tensor_tensor_scan(
                            out=stt[:, ri, :], data0=dec, data1=bt[:, ri, :], initial=carry[:, dirn, k, ri:ri + 1], op0=ALU.mult, op1=ALU.add),
                            reads=[btb, Wb, carryb[dirn][k]], writes=[sttb])
                elif sn == 3:
                    eD1 = K.dve if getattr(self, "s5_demod_dve", True) else K.pool
                    eD2 = K.dve if getattr(self, "s5_comb_dve", True) else K.pool
                    eD1.op(lambda e: e.tensor_tensor(out=t3[:], in0=stt[:], in1=rr, op=ALU.mult), reads=[sttb, Rb], writes=[t3b])
                    K.pool.op(lambda e: e.tensor_tensor(out=t4[:], in0=stt[:, ::-1, :], in1=rim, op=ALU.mult), reads=[sttb, Rb], writes=[t4b])
                    eD2.op(lambda e: e.tensor_tensor(out=s32[:, 0, :], in0=t3[:, 0, :], in1=t4[:, 0, :], op=ALU.subtract), reads=[t3b, t4b], writes=[s32b])
                    eD2.op(lambda e: e.tensor_tensor(out=s32[:, 1, :], in0=t3[:, 1, :], in1=t4[:, 1, :], op=ALU.add), reads=[t3b, t4b], writes=[s32b])
                elif sn == 4:
                    K.act.op(lambda e: e.copy(out=carry[:, dirn, k, :], in_=s32[:, :, TS - 1]), reads=[s32b], writes=[carryb[dirn][k]])
                    if dirn == 0:
                        K.act.op(lambda e: e.copy(out=sbf[:], in_=s32[:]), reads=[s32b], writes=[sbfb])
                    else:
                        K.act.op(lambda e: e.copy(out=sbf[:], in_=s32[:, :, ::-1]), reads=[s32b], writes=[sbfb])
                    psy, psyb = psy_of(si)
                    py, pyb = psy[q // 2], psyb[q // 2]
                    c0 = (q % 2) * TS
                    for ri in range(2):
                        K.pe.op(lambda e, ri=ri: e.matmul(
                            py[rows, c0:c0 + TS], lhsT=Ct[:, dirn, k, ri, :], rhs=sbf[:, ri, :],
                            start=(ri == 0 and slot % 2 == 0), stop=(ri == 1 and slot % 2 == 1)), reads=[Ctb, sbfb], writes=[pyb])
                    if lastk:
                        epilogue(si)

            def epilogue(si):
                dirn, t0 = subs_all[si]
                (u32, u32b), (ubf, ubfb), (ybt, ybtb) = sub_bufs(si)
                psy, psyb = psy_of(si)
                if dirn == 1:
                    for q in range(BC):
                        K.act.op(lambda e, q=q: e.copy(out=ybt[:, q, :], in_=psy[q // 2][:, (q % 2) * TS:(q % 2 + 1) * TS]),
                                 reads=[psyb[q // 2]], writes=[ybtb])
                    K.pool.dma(self.YB[:, :, t0:t0 + TS].rearrange("c p t -> p c t"), ybt[:], ybtb, load=False, writes=[ybd[t0]])
                else:
                    for q in range(BC):
                        K.dve.op(lambda e, q=q: e.tensor_tensor(out=yt[:, q, :], in0=psy[q // 2][:, (q % 2) * TS:(q % 2 + 1) * TS],
                                                                in1=ybt[:, q, :], op=ALU.add), reads=[psyb[q // 2], ybtb], writes=[ytb])
                        K.dve.op(lambda e, q=q: e.scalar_tensor_tensor(out=yt[:, q, :], in0=u32[:, q, :], scalar=self.pv[:, l, o_d + q:o_d + q + 1],
                                                                       in1=yt[:, q, :], op0=ALU.mult, op1=ALU.add),
                                 reads=[u32b, self.pvb, ytb], writes=[ytb])
                        K.act.op(lambda e, q=q: e.activation(out=yg[:, q, :], in_=yt[:, q, :], func=AF.Gelu_apprx_tanh), reads=[ytb], writes=[ygb])
                    for oc in range(BC):
                        i = self.psum_i
                        self.psum_i = (i + 1) % 4
                        psg, psgb = self.ps[i], self.psb[i]
                        self.mm_group(psg[:, :TS], psgb, [wglu[:, BC + oc, bc, :] for bc in range(BC)], [yg[:, bc, :] for bc in range(BC)], [wglub, ygb])
                        K.act.op(lambda e, psg=psg: e.activation(out=sig[:, :], in_=psg[:, :TS], func=AF.Sigmoid), reads=[psgb], writes=[sigb])
                        i = self.psum_i
                        self.psum_i = (i + 1) % 4
                        psa, psab = self.ps[i], self.psb[i]
                        self.mm_group(psa[:, :TS], psab, [wglu[:, oc, bc, :] for bc in range(BC)], [yg[:, bc, :] for bc in range(BC)], [wglub, ygb])
                        K.dve.op(lambda e, oc=oc, psa=psa: e.tensor_tensor(out=brd[:, oc, :], in0=psa[:, :TS], in1=sig[:, :], op=ALU.mult),
                                 reads=[psab, sigb], writes=[brdb])
                    K.pool.dma(self.BR[3, :, :, t0:t0 + TS].rearrange("c p t -> p c t"), brd[:], brdb, load=False)

            self.psum_i = 0
            NST = 5
            for step in range(len(items) + NST - 1):
                for sn in reversed(range(NST)):
                    idx = step - sn
                    if 0 <= idx < len(items):
                        stage(sn, idx)
            self.psum_i = 0

    def phase_final(self):
        cfg, K, nc = self.cfg, self.K, self.nc
        KC, T, CTX = cfg.KC, cfg.T, cfg.CTX
        L = cfg.DEPTH
        o_g, _ = self.pplan["n1g"]
        with ExitStack() as st:
            xt, xtb = self.sb(st, "f_xt", [128, KC, T], F32)
            ot, otb = self.sb(st, "f_ot", [128, KC, T], F32)
            sq = [self.sb(st, f"f_sq{i}", [128, T], F32) for i in range(2)]
            rs, rsb = self.sb(st, "f_rs", [128, T], F32)
            eps, epsb = self.sb(st, "f_eps", [128, 1], F32)
            K.dve.op(lambda e: e.memset(eps[:], cfg.EPS), writes=[epsb])
            for (t0, w, is_ctx) in cfg.tiles:
                if is_ctx:
                    continue
                K.pool.dma(xt[:, :, :w], self.X[:, :, t0:t0 + w].rearrange("c p t -> p c t"), xtb)
                ps, psb = self.next_psum()
                for kc in range(KC):
                    s_, sb_ = sq[kc % 2]
                    K.act.op(lambda e, kc=kc, s_=s_: e.activation(out=s_[:, :w], in_=xt[:, kc, :w], func=AF.Square), reads=[xtb], writes=[sb_])
                    K.pe.op(lambda e, kc=kc, s_=s_: e.matmul(ps[:, :w], lhsT=self.ones[:], rhs=s_[:, :w], start=(kc == 0), stop=(kc == KC - 1)),
                            reads=[self.onesb, sb_], writes=[psb])
                K.act.op(lambda e: e.activation(out=rs[:, :w], in_=ps[:, :w], func=AF.Sqrt, scale=1.0 / cfg.D, bias=eps[:, 0:1]),
                         reads=[psb, epsb], writes=[rsb])
                K.dve.op(lambda e: e.reciprocal(out=rs[:, :w], in_=rs[:, :w]), reads=[rsb], writes=[rsb])
                for kc in range(KC):
                    eng = K.dve
                    eng.op(lambda e, kc=kc: e.scalar_tensor_tensor(out=ot[:, kc, :w], in0=xt[:, kc, :w], scalar=self.pv[:, L, o_g + kc:o_g + kc + 1],
                                                                  in1=rs[:, :w], op0=ALU.mult, op1=ALU.mult),
                           reads=[xtb, rsb, self.pvb], writes=[otb])
                K.pool.dma(self.out[:, :, t0 - CTX:t0 - CTX + w].rearrange("c p t -> p c t"), ot[:, :, :w], otb, load=False)


_CACHE = {}


def kernel(**inputs):
    cfg = Cfg()
    inp = {k: np.asarray(v) for k, v in inputs.items()}
    if "nc" not in _CACHE:
        _CACHE["nc"] = Prog(cfg).build()
    nc = _CACHE["nc"]
    shared = prep_shared(cfg, inp)
    n = 8
    owners = [0, 1, 4, 5]
    in_maps = []
    zero_cache = {}
    for c in range(n):
        m = dict(shared)
        if c in owners:
            m.update(prep_inputs(cfg, inp, owners.index(c)))
        else:
            if not zero_cache:
                ref_m = prep_inputs(cfg, inp, 0)
                for k_, v_ in ref_m.items():
                    zero_cache[k_] = np.zeros_like(v_)
                for k_ in ("pvec", "rowv"):
                    zero_cache[k_] = np.zeros_like(shared[k_])
            m.update(zero_cache)
        in_maps.append(m)
    res = run_bass_kernel_spmd(nc, in_maps, core_ids=list(range(n)))
    out = np.empty((cfg.BATCH, cfg.SEQ, cfg.D), np.float32)
    for b in range(cfg.BATCH):
        o = np.asarray(res.results[owners[b]]["out"])
        out[b] = o.reshape(cfg.D, cfg.SEQ).T
    return out
```
